# Optimizing a Trainium2 kernel written in Bass

```python
import math
import jax
import jax.numpy as jnp
from jax import lax
import numpy as np

D_MODEL = 2048
BATCH = 4
SEQ = 2048
DEPTH = 2
DEC_BATCH = 32
DEC_SEQ = 8
PAST_LEN = 8192
PAGE_SIZE = 128

F32 = jnp.float32
RMS_EPS = 1e-6
NEG_INF = -1e30

GDN_K_HEADS = 16
GDN_V_HEADS = 32
GDN_DK = 128
GDN_DV = 128
GDN_KDIM = GDN_K_HEADS * GDN_DK
GDN_VDIM = GDN_V_HEADS * GDN_DV
GDN_CONV_DIM = 2 * GDN_KDIM + GDN_VDIM
GDN_CONV_W = 4
GDN_CHUNK = 64
GDN_PROJ = GDN_CONV_DIM + GDN_VDIM + 2 * GDN_V_HEADS

NSA_HEADS = 16
NSA_KV_HEADS = 4
NSA_HD = D_MODEL // NSA_HEADS
NSA_GROUP = NSA_HEADS // NSA_KV_HEADS
NSA_QDIM = NSA_HEADS * NSA_HD
NSA_KVDIM = NSA_KV_HEADS * NSA_HD
NSA_PROJ = NSA_QDIM + 6 * NSA_KVDIM + 3 * NSA_HEADS
CMP_BLOCK = 32
CMP_STRIDE = 16
CMP_HIDDEN = 2 * NSA_HD
SEL_BLOCK = 64
N_SEL = 16
WINDOW = 512
FORCE_BONUS = 1e3
SEL_QBLOCK = 64
WIN_QBLOCK = 128

N_BUCKETS = 32
REL_MAX_DIST = 1024

D_FF = ((8 * D_MODEL // 3 + 127) // 128) * 128
FFN_CONV_W = 3

N_GDN = (DEPTH + 1) // 2
N_NSA = DEPTH // 2

kernel_name = "hybrid_gdn_nsa_convffn_step"


def rmsnorm(x, w):
    xf = x.astype(F32)
    y = xf * lax.rsqrt(jnp.mean(xf * xf, axis=-1, keepdims=True) + RMS_EPS)
    return (y * w.astype(F32)).astype(x.dtype)


def l2norm(x):
    xf = x.astype(F32)
    return xf * lax.rsqrt(jnp.sum(xf * xf, axis=-1, keepdims=True) + RMS_EPS)


def causal_dwconv(x, prefix, w, b=None):
    width = w.shape[-1]
    t = x.shape[1]
    xp = jnp.concatenate([prefix.astype(x.dtype), x], axis=1)
    y = xp[:, 0:t] * w[:, 0]
    for i in range(1, width):
        y = y + xp[:, i:i + t] * w[:, i]
    if b is not None:
        y = y + b
    return y, xp[:, t:]


def t5_bucket(dist):
    d = jnp.maximum(dist, 0)
    max_exact = N_BUCKETS // 2
    scale = (N_BUCKETS - max_exact) / math.log(REL_MAX_DIST / max_exact)
    large = max_exact + (jnp.log(jnp.maximum(d, 1).astype(F32) / max_exact) * scale).astype(jnp.int32)
    return jnp.where(d < max_exact, d, jnp.minimum(large, N_BUCKETS - 1))


def gated_delta_rule(q, k, v, g, beta, s0):
    b, t, h, _ = q.shape
    dv = v.shape[-1]
    L = math.gcd(t, GDN_CHUNK)
    nc = t // L

    def chunks(a):
        a = a.reshape((b, nc, L, h) + a.shape[3:])
        return jnp.moveaxis(a, (1, 3), (0, 2))

    qc, kc, vc, gc, bc = chunks(q), chunks(k), chunks(v), chunks(g), chunks(beta)
    G = jnp.cumsum(gc, axis=-1)
    diff = G[..., :, None] - G[..., None, :]
    incl = jnp.tril(jnp.ones((L, L), bool))
    strict = jnp.tril(jnp.ones((L, L), bool), -1)
    dec_incl = jnp.exp(jnp.where(incl, diff, -jnp.inf))
    dec_strict = jnp.exp(jnp.where(strict, diff, -jnp.inf))
    kk = jnp.einsum('cbhid,cbhjd->cbhij', kc, kc)
    a_mat = jnp.eye(L, dtype=F32) + bc[..., :, None] * kk * dec_strict
    u_eff = lax.linalg.triangular_solve(a_mat, bc[..., None] * vc, left_side=True, lower=True, unit_diagonal=True)
    w_k = lax.linalg.triangular_solve(a_mat, (bc * jnp.exp(G))[..., None] * kc, left_side=True, lower=True, unit_diagonal=True)
    a_qk = jnp.einsum('cbhid,cbhjd->cbhij', qc, kc) * dec_incl
    q_dec = qc * jnp.exp(G)[..., None]
    g_last = G[..., -1:]
    k_dec = kc * jnp.exp(g_last - G)[..., None]

    def step(s, xs):
        u_c, w_c, aqk_c, q_c, k_c, gl_c = xs
        u = u_c - jnp.einsum('bhik,bhkv->bhiv', w_c, s)
        o = jnp.einsum('bhik,bhkv->bhiv', q_c, s) + jnp.einsum('bhij,bhjv->bhiv', aqk_c, u)
        s = jnp.exp(gl_c)[..., None] * s + jnp.einsum('bhik,bhiv->bhkv', k_c, u)
        return s, o

    s_fin, o = lax.scan(step, s0, (u_eff, w_k, a_qk, q_dec, k_dec, g_last))
    o = jnp.moveaxis(o, (0, 2), (1, 3)).reshape(b, t, h, dv)
    return o, s_fin


def gdn_mixer(h, s0, conv_prefix, w_in, conv_w, a_log, dt_bias, norm_w, w_out):
    b, t, _ = h.shape
    proj = h @ w_in
    qkv, z, bl, al = jnp.split(proj, [GDN_CONV_DIM, GDN_CONV_DIM + GDN_VDIM, GDN_CONV_DIM + GDN_VDIM + GDN_V_HEADS], axis=-1)
    qkv, new_prefix = causal_dwconv(qkv, conv_prefix, conv_w)
    qkv = jax.nn.silu(qkv)
    q = qkv[..., :GDN_KDIM].reshape(b, t, GDN_K_HEADS, GDN_DK)
    k = qkv[..., GDN_KDIM:2 * GDN_KDIM].reshape(b, t, GDN_K_HEADS, GDN_DK)
    v = qkv[..., 2 * GDN_KDIM:].reshape(b, t, GDN_V_HEADS, GDN_DV).astype(F32)
    rep = GDN_V_HEADS // GDN_K_HEADS
    q = jnp.repeat(l2norm(q), rep, axis=2) * (GDN_DK ** -0.5)
    k = jnp.repeat(l2norm(k), rep, axis=2)
    beta = jax.nn.sigmoid(bl.astype(F32))
    g = -jnp.exp(a_log.astype(F32)) * jax.nn.softplus(al.astype(F32) + dt_bias.astype(F32))
    o, s_new = gated_delta_rule(q, k, v, g, beta, s0.astype(F32))
    o = rmsnorm(o, norm_w) * jax.nn.silu(z.reshape(b, t, GDN_V_HEADS, GDN_DV).astype(F32))
    y = o.reshape(b, t, GDN_VDIM).astype(h.dtype) @ w_out
    return y, s_new.astype(s0.dtype), new_prefix


def nsa_compress(rows, pe, w1, w2):
    b, n = rows.shape[:2]
    ns = n // CMP_STRIDE
    nsub = CMP_BLOCK // CMP_STRIDE
    nc = ns - nsub + 1
    r = rows[:, :ns * CMP_STRIDE].reshape(b, ns, CMP_STRIDE, NSA_KV_HEADS, NSA_HD)
    w1r = w1.reshape(nsub, CMP_STRIDE, NSA_HD, CMP_HIDDEN)
    hid = pe.reshape(-1) @ w1
    for i in range(nsub):
        hid = hid + jnp.einsum('bnsgd,sde->bnge', r[:, i:i + nc], w1r[i])
    return jax.nn.gelu(hid) @ w2


def nsa_mixer(h, past, win_prefix, win_keep, w_in, q_norm, k_norm, cmp_pe, cmp_w1, cmp_w2, rel_bias, w_out):
    b, t, _ = h.shape
    p_len = past.shape[1]
    n_all = p_len + t
    proj = h @ w_in
    q = proj[..., :NSA_QDIM].reshape(b, t, NSA_HEADS, NSA_HD)
    kv = proj[..., NSA_QDIM:NSA_QDIM + 6 * NSA_KVDIM].reshape(b, t, 6, NSA_KV_HEADS, NSA_HD)
    gates = jax.nn.sigmoid(proj[..., NSA_QDIM + 6 * NSA_KVDIM:].astype(F32)).reshape(b, t, 3, NSA_HEADS)
    q = rmsnorm(q, q_norm) * (NSA_HD ** -0.5)
    qg = q.reshape(b, t, NSA_KV_HEADS, NSA_GROUP, NSA_HD)
    new_rows = jnp.stack([kv[:, :, 0], kv[:, :, 1], rmsnorm(kv[:, :, 2], k_norm[1]), kv[:, :, 3]], axis=2)
    new_win = jnp.stack([rmsnorm(kv[:, :, 4], k_norm[2]), kv[:, :, 5]], axis=2)
    rows = jnp.concatenate([past.astype(h.dtype), new_rows], axis=1)
    qpos = p_len + jnp.arange(t)
    rel_h = rel_bias.astype(F32)

    kc = rmsnorm(nsa_compress(rows[:, :, 0], cmp_pe[0], cmp_w1[0], cmp_w2[0]), k_norm[0])
    vc = nsa_compress(rows[:, :, 1], cmp_pe[1], cmp_w1[1], cmp_w2[1])
    nc = kc.shape[1]
    c_end = jnp.arange(nc) * CMP_STRIDE + (CMP_BLOCK - 1)
    c_dist = qpos[:, None] - c_end[None, :]
    c_ok = (c_dist >= 0)[:, None, None, :]
    c_bias = rel_h[t5_bucket(c_dist)].reshape(t, nc, NSA_KV_HEADS, NSA_GROUP).transpose(0, 2, 3, 1)
    s = jnp.einsum('btgjd,bngd->btgjn', qg, kc, preferred_element_type=F32) + c_bias
    p_cmp = jax.nn.softmax(jnp.where(c_ok, s, NEG_INF), axis=-1) * c_ok
    o_cmp = jnp.einsum('btgjn,bngd->btgjd', p_cmp.astype(vc.dtype), vc)

    n_sb = -(-n_all // SEL_BLOCK)
    sb_start = jnp.arange(n_sb) * SEL_BLOCK
    c_start = c_end - (CMP_BLOCK - 1)
    ovl = jnp.maximum(jnp.minimum(c_end[:, None], sb_start[None, :] + SEL_BLOCK - 1)
                      - jnp.maximum(c_start[:, None], sb_start[None, :]) + 1, 0).astype(F32) / CMP_BLOCK
    imp = jnp.einsum('btgjn,nm->btgm', p_cmp, ovl)
    cur = qpos // SEL_BLOCK
    blk = jnp.arange(n_sb)
    sb_ok = sb_start[None, :] <= qpos[:, None]
    forced = (blk[None, :] == 0) | (blk[None, :] == cur[:, None]) | (blk[None, :] == cur[:, None] - 1)
    score = jnp.where(sb_ok[:, None, :], imp + jnp.where(forced, FORCE_BONUS, 0.0)[:, None, :], NEG_INF)
    n_pick = min(N_SEL, n_sb)
    top_v, top_i = lax.top_k(score, n_pick)
    top_ok = top_v > 0.5 * NEG_INF

    pad = n_sb * SEL_BLOCK - n_all
    ks = jnp.pad(rows[:, :, 2], ((0, 0), (0, pad), (0, 0), (0, 0))).reshape(
        b, n_sb, SEL_BLOCK, NSA_KV_HEADS, NSA_HD).transpose(0, 3, 1, 2, 4)
    vs = jnp.pad(rows[:, :, 3], ((0, 0), (0, pad), (0, 0), (0, 0))).reshape(
        b, n_sb, SEL_BLOCK, NSA_KV_HEADS, NSA_HD).transpose(0, 3, 1, 2, 4)
    rel_g = rel_h.reshape(N_BUCKETS, NSA_KV_HEADS, NSA_GROUP).transpose(1, 0, 2)
    b_ix = jnp.arange(b)[:, None, None, None]
    g_ix = jnp.arange(NSA_KV_HEADS)[None, None, :, None]
    qb_len = math.gcd(t, SEL_QBLOCK)
    nqb = t // qb_len

    def sel_attend(args):
        q_b, pos_b, idx_b, ok_b = args
        kg = ks[b_ix, g_ix, idx_b]
        vg = vs[b_ix, g_ix, idx_b]
        kpos = idx_b[..., None] * SEL_BLOCK + jnp.arange(SEL_BLOCK)
        dist = pos_b[None, :, None, None, None] - kpos
        mask = (ok_b[..., None] & (dist >= 0))[:, :, :, None]
        bias = jnp.moveaxis(rel_g[g_ix[..., None], t5_bucket(dist)], -1, 3)
        sc = jnp.einsum('bqgjd,bqgnkd->bqgjnk', q_b, kg, preferred_element_type=F32) + bias
        sc = jnp.where(mask, sc, NEG_INF)
        p = jax.nn.softmax(sc.reshape(sc.shape[:4] + (-1,)), axis=-1).reshape(sc.shape)
        return jnp.einsum('bqgjnk,bqgnkd->bqgjd', p.astype(vg.dtype), vg)

    def to_blocks(a):
        return jnp.moveaxis(a.reshape((a.shape[0], nqb, qb_len) + a.shape[2:]), 1, 0)

    o_sel = lax.map(sel_attend, (to_blocks(qg), qpos.reshape(nqb, qb_len), to_blocks(top_i), to_blocks(top_ok)))
    o_sel = jnp.moveaxis(o_sel, 0, 1).reshape(b, t, NSA_HEADS, NSA_HD)

    rows_w = jnp.concatenate([win_prefix.astype(h.dtype), new_win], axis=1)
    wpos = p_len - WINDOW + jnp.arange(WINDOW + t)
    wb = math.gcd(t, WIN_QBLOCK)
    nwb = t // wb
    w_idx = jnp.arange(nwb)[:, None] * wb + jnp.arange(WINDOW + wb)[None, :]
    slab = rows_w[:, w_idx]
    kpos = wpos[w_idx]
    w_dist = qpos.reshape(nwb, wb)[:, :, None] - kpos[:, None, :]
    w_ok = (w_dist >= 0) & (w_dist < WINDOW) & (kpos[:, None, :] >= 0)
    w_bias = rel_h[t5_bucket(w_dist)].reshape(nwb, wb, WINDOW + wb, NSA_KV_HEADS, NSA_GROUP).transpose(0, 3, 4, 1, 2)
    qw = qg.reshape(b, nwb, wb, NSA_KV_HEADS, NSA_GROUP, NSA_HD)
    sw = jnp.einsum('bnqgjd,bnkgd->bngjqk', qw, slab[:, :, :, 0], preferred_element_type=F32) + w_bias
    pw = jax.nn.softmax(jnp.where(w_ok[:, None, None], sw, NEG_INF), axis=-1)
    o_win = jnp.einsum('bngjqk,bnkgd->bnqgjd', pw.astype(slab.dtype), slab[:, :, :, 1]).reshape(b, t, NSA_HEADS, NSA_HD)

    o = (gates[:, :, 0, :, None] * o_cmp.reshape(b, t, NSA_HEADS, NSA_HD)
         + gates[:, :, 1, :, None] * o_sel
         + gates[:, :, 2, :, None] * o_win)
    y = o.reshape(b, t, NSA_QDIM).astype(h.dtype) @ w_out
    return y, new_rows, rows_w[:, rows_w.shape[1] - win_keep:]


def conv_ffn(h, prefix, w_up, conv_w, conv_b, w_down):
    up = h @ w_up
    gate, val = up[..., :D_FF], up[..., D_FF:]
    gate, new_prefix = causal_dwconv(gate, prefix, conv_w, conv_b)
    return (jax.nn.silu(gate) * val) @ w_down, new_prefix


def setup_inputs(seed: int = 0) -> dict:
    key = jax.random.key(seed)
    ks = jax.random.split(key, 32)
    n_pages = PAST_LEN // PAGE_SIZE
    n_used = DEC_BATCH * n_pages
    n_pool = (n_used * 5) // 4
    win_buf = min(WINDOW, PAST_LEN)

    def nrm(k, shape, scale=1.0):
        return jax.random.normal(k, shape, F32) * scale

    def gain(k, shape):
        return 1.0 + nrm(k, shape, 0.02)

    page_table = jax.random.permutation(ks[7], n_pool)[:n_used].reshape(DEC_BATCH, n_pages).astype(jnp.int32)
    return {
        "x_prompt": nrm(ks[0], (BATCH, SEQ, D_MODEL)),
        "x_sample": nrm(ks[1], (DEC_BATCH, DEC_SEQ, D_MODEL)),
        "state_gdn": nrm(ks[2], (N_GDN, DEC_BATCH, GDN_V_HEADS, GDN_DK, GDN_DV), GDN_DK ** -0.5),
        "state_gdn_conv": nrm(ks[3], (N_GDN, DEC_BATCH, GDN_CONV_W - 1, GDN_CONV_DIM)),
        "cache_nsa_kv": nrm(ks[4], (N_NSA, n_pool, PAGE_SIZE, 4, NSA_KV_HEADS, NSA_HD)),
        "state_nsa_win": nrm(ks[5], (N_NSA, DEC_BATCH, win_buf, 2, NSA_KV_HEADS, NSA_HD)),
        "state_ffn_conv": nrm(ks[6], (DEPTH, DEC_BATCH, FFN_CONV_W - 1, D_FF)),
        "page_table": page_table,
        "norm_mix": gain(ks[8], (DEPTH, D_MODEL)),
        "norm_ffn": gain(ks[9], (DEPTH, D_MODEL)),
        "gdn_w_in": nrm(ks[10], (N_GDN, D_MODEL, GDN_PROJ), D_MODEL ** -0.5),
        "gdn_conv_w": nrm(ks[11], (N_GDN, GDN_CONV_DIM, GDN_CONV_W), GDN_CONV_W ** -0.5),
        "gdn_A_log": jnp.log(jax.random.uniform(ks[12], (N_GDN, GDN_V_HEADS), F32, 1.0, 16.0)),
        "gdn_dt_bias": nrm(ks[13], (N_GDN, GDN_V_HEADS), 0.1),
        "gdn_norm": gain(ks[14], (N_GDN, GDN_DV)),
        "gdn_w_out": nrm(ks[15], (N_GDN, GDN_VDIM, D_MODEL), GDN_VDIM ** -0.5),
        "nsa_w_in": nrm(ks[16], (N_NSA, D_MODEL, NSA_PROJ), D_MODEL ** -0.5),
        "nsa_q_norm": gain(ks[17], (N_NSA, NSA_HD)),
        "nsa_k_norm": gain(ks[18], (N_NSA, 3, NSA_HD)),
        "nsa_cmp_pe": nrm(ks[19], (N_NSA, 2, CMP_BLOCK, NSA_HD), 0.1),
        "nsa_cmp_w1": nrm(ks[20], (N_NSA, 2, CMP_BLOCK * NSA_HD, CMP_HIDDEN), (CMP_BLOCK * NSA_HD) ** -0.5),
        "nsa_cmp_w2": nrm(ks[21], (N_NSA, 2, CMP_HIDDEN, NSA_HD), CMP_HIDDEN ** -0.5),
        "rel_bias": nrm(ks[22], (N_BUCKETS, NSA_HEADS), 0.5),
        "nsa_w_out": nrm(ks[23], (N_NSA, NSA_QDIM, D_MODEL), NSA_QDIM ** -0.5),
        "ffn_w_up": nrm(ks[24], (DEPTH, D_MODEL, 2 * D_FF), D_MODEL ** -0.5),
        "ffn_conv_w": nrm(ks[25], (DEPTH, D_FF, FFN_CONV_W), FFN_CONV_W ** -0.5),
        "ffn_conv_b": nrm(ks[26], (DEPTH, D_FF), 0.02),
        "ffn_w_down": nrm(ks[27], (DEPTH, D_FF, D_MODEL), D_FF ** -0.5),
    }


def reference(x_prompt, x_sample, state_gdn, state_gdn_conv, cache_nsa_kv, state_nsa_win, state_ffn_conv,
              page_table, norm_mix, norm_ffn, gdn_w_in, gdn_conv_w, gdn_A_log, gdn_dt_bias, gdn_norm, gdn_w_out,
              nsa_w_in, nsa_q_norm, nsa_k_norm, nsa_cmp_pe, nsa_cmp_w1, nsa_cmp_w2, rel_bias, nsa_w_out,
              ffn_w_up, ffn_conv_w, ffn_conv_b, ffn_w_down):
    n_pages = PAST_LEN // PAGE_SIZE
    win_buf = state_nsa_win.shape[2]
    hp, hs = x_prompt, x_sample
    gdn_p, gdnc_p, kv_p, win_p, ffn_p = [], [], [], [], []
    gdn_s, gdnc_s, kv_s, win_s, ffn_s = [], [], [], [], []
    for i in range(DEPTH):
        j = i // 2
        hn_p = rmsnorm(hp, norm_mix[i])
        hn_s = rmsnorm(hs, norm_mix[i])
        if i % 2 == 0:
            gw = (gdn_w_in[j], gdn_conv_w[j], gdn_A_log[j], gdn_dt_bias[j], gdn_norm[j], gdn_w_out[j])
            mp, st_p, cv_p = gdn_mixer(hn_p, jnp.zeros((BATCH, GDN_V_HEADS, GDN_DK, GDN_DV), hp.dtype),
                                       jnp.zeros((BATCH, GDN_CONV_W - 1, GDN_CONV_DIM), hp.dtype), *gw)
            ms, st_s, cv_s = gdn_mixer(hn_s, state_gdn[j], state_gdn_conv[j], *gw)
            gdn_p.append(st_p)
            gdnc_p.append(cv_p)
            gdn_s.append(st_s)
            gdnc_s.append(cv_s)
        else:
            nw = (nsa_w_in[j], nsa_q_norm[j], nsa_k_norm[j], nsa_cmp_pe[j], nsa_cmp_w1[j], nsa_cmp_w2[j], rel_bias, nsa_w_out[j])
            mp, rows_p, wst_p = nsa_mixer(hn_p, jnp.zeros((BATCH, 0, 4, NSA_KV_HEADS, NSA_HD), hp.dtype),
                                          jnp.zeros((BATCH, WINDOW, 2, NSA_KV_HEADS, NSA_HD), hp.dtype),
                                          min(WINDOW, SEQ), *nw)
            past = cache_nsa_kv[j][page_table].reshape(DEC_BATCH, n_pages * PAGE_SIZE, 4, NSA_KV_HEADS, NSA_HD)
            prefix = jnp.pad(state_nsa_win[j], ((0, 0), (WINDOW - win_buf, 0), (0, 0), (0, 0), (0, 0)))
            ms, rows_s, wst_s = nsa_mixer(hn_s, past, prefix, win_buf, *nw)
            kv_p.append(rows_p)
            win_p.append(wst_p)
            kv_s.append(rows_s)
            win_s.append(wst_s)
        hp = hp + mp
        hs = hs + ms
        fw = (ffn_w_up[i], ffn_conv_w[i], ffn_conv_b[i], ffn_w_down[i])
        fp, cp = conv_ffn(rmsnorm(hp, norm_ffn[i]), jnp.zeros((BATCH, FFN_CONV_W - 1, D_FF), hp.dtype), *fw)
        fs, cs = conv_ffn(rmsnorm(hs, norm_ffn[i]), state_ffn_conv[i], *fw)
        ffn_p.append(cp)
        ffn_s.append(cs)
        hp = hp + fp
        hs = hs + fs
    return (hp, hs,
            jnp.stack(gdn_p), jnp.stack(gdnc_p), jnp.stack(kv_p), jnp.stack(win_p), jnp.stack(ffn_p),
            jnp.stack(gdn_s), jnp.stack(gdnc_s), jnp.stack(kv_s), jnp.stack(win_s), jnp.stack(ffn_s))
```

```python
import math
import numpy as np
import ml_dtypes
import concourse.bass as bass
import concourse.mybir as mybir
from concourse.bass_utils import run_bass_kernel_spmd
from contextlib import ExitStack

F32 = mybir.dt.float32
BF16 = mybir.dt.bfloat16
I32 = mybir.dt.int32
AF = mybir.ActivationFunctionType
ALU = mybir.AluOpType
AX = mybir.AxisListType

ENGS = ("pe", "dve", "act", "pool", "sp")
DMA_RING = 8
RMS_EPS = 1e-6


class Buf:
    __slots__ = ("t", "w", "r", "name", "multi", "ws", "psum")

    def __init__(self, t, name="", multi=False):
        self.t = t
        self.w = None
        self.r = {}
        self.name = name
        self.multi = multi
        self.ws = {}
        self.psum = False

    def __getitem__(self, idx):
        return self.t[idx]


class K:
    def __init__(self, nc, es, same_engine_sync=True):
        self.nc = nc
        self.es = es
        self.eng = {"pe": nc.tensor, "dve": nc.vector, "act": nc.scalar,
                    "pool": nc.gpsimd, "sp": nc.sync}
        self.sems = {}
        self.cnt = {}
        for e in ("pe", "dve", "act", "pool"):
            self.sems[e] = es.enter_context(nc.semaphore("s_" + e))
            self.cnt[e] = 0
        self.ring = {}
        self.ring_n = {}
        for q in ("sp", "pool", "act"):
            self.ring[q] = [es.enter_context(nc.semaphore(f"d_{q}{i}")) for i in range(DMA_RING)]
            self.ring_n[q] = 0
        self.seen = {e: {} for e in ENGS}
        self.same_engine_sync = same_engine_sync
        self.n_inst = {e: 0 for e in ENGS}
        self.n_wait = {e: 0 for e in ENGS}
        self.rr = 0

    def sb(self, name, shape, dt=F32):
        self.uid = getattr(self, "uid", 0) + 1
        t = self.es.enter_context(self.nc.sbuf_tensor(f"sb{self.uid}_" + name, list(shape), dt))
        return Buf(t, name)

    def ps(self, name, shape, dt=F32):
        t = self.es.enter_context(self.nc.psum_tensor("ps_" + name, list(shape), dt))
        b = Buf(t, name)
        b.psum = True
        return b

    def _semobj(self, key):
        if isinstance(key, str):
            return self.sems[key]
        q, i = key
        return self.ring[q][i]

    def _wait(self, e, ev):
        if ev is None:
            return
        key, val = ev
        if key == e and (e == "pe" or not self.same_engine_sync):
            return
        if self.seen[e].get(key, 0) >= val:
            return
        self.eng[e].wait_ge(self._semobj(key), val)
        self.seen[e][key] = val
        self.n_wait[e] += 1

    def _deps(self, e, reads, writes):
        for b in reads:
            if b.multi:
                for key, val in b.ws.items():
                    self._wait(e, (key, val))
            else:
                self._wait(e, b.w)
            if b.psum:
                for key, val in b.r.items():
                    if key != e:
                        self._wait(e, (key, val))
        for b in writes:
            if not b.multi:
                self._wait(e, b.w)
            for key, val in b.r.items():
                self._wait(e, (key, val))

    def _commit(self, ev, reads, writes):
        k_, v = ev
        for b in reads:
            if b in writes:
                continue
            if b.r.get(k_, 0) < v:
                b.r[k_] = v
        for b in writes:
            if b.multi:
                if b.r:
                    b.ws = {}
                if b.ws.get(k_, 0) < v:
                    b.ws[k_] = v
            else:
                b.w = ev
            b.r = {}

    def op(self, e, fn, reads=(), writes=(), inc=True, dma=False):
        if dma:
            return self._dma_like(e, fn, reads, writes)
        self._deps(e, reads, writes)
        ins = fn(self.eng[e])
        self.n_inst[e] += 1
        if inc:
            ins.then_inc(self.sems[e], 1)
            self.cnt[e] += 1
            ev = (e, self.cnt[e])
        else:
            ev = (e, self.cnt[e] + 1)
        self._commit(ev, reads, writes)
        return ins

    def _dma_like(self, q, fn, reads, writes):
        self._deps(q, reads, writes)
        n = self.ring_n[q]
        slot = n % DMA_RING
        rnd = n // DMA_RING
        if rnd > 0:
            self._wait(q, ((q, slot), 16 * rnd))
        ins = fn(self.eng[q])
        ins.then_inc(self.ring[q][slot], 16)
        self.ring_n[q] = n + 1
        self.n_inst[q] += 1
        ev = ((q, slot), 16 * (rnd + 1))
        self._commit(ev, reads, writes)
        return ins

    def dma(self, q, out_ap, in_ap, reads=(), writes=(), **kw):
        self._deps(q, reads, writes)
        n = self.ring_n[q]
        slot = n % DMA_RING
        rnd = n // DMA_RING
        if rnd > 0:
            self._wait(q, ((q, slot), 16 * rnd))
        ins = self.eng[q].dma_start(out=out_ap, in_=in_ap, **kw)
        ins.then_inc(self.ring[q][slot], 16)
        self.ring_n[q] = n + 1
        self.n_inst[q] += 1
        ev = ((q, slot), 16 * (rnd + 1))
        self._commit(ev, reads, writes)
        return ins

    def finish(self):
        for q in ("sp", "pool", "act"):
            n = self.ring_n[q]
            for slot in range(DMA_RING):
                cnt = (n - slot + DMA_RING - 1) // DMA_RING if n > slot else 0
                if cnt > 0:
                    self._wait("sp", ((q, slot), 16 * cnt))

    def barrier(self):
        for e in ENGS:
            for o in ("pe", "dve", "act", "pool"):
                if o != e and self.cnt[o] > 0:
                    self._wait(e, (o, self.cnt[o]))
            for q in ("sp", "pool", "act"):
                n = self.ring_n[q]
                for slot in range(DMA_RING):
                    c = (n - slot + DMA_RING - 1) // DMA_RING if n > slot else 0
                    if c > 0:
                        self._wait(e, ((q, slot), 16 * c))

    def scope(self):
        kk = self

        class _S:
            def __enter__(s_):
                s_.old = kk.es
                s_.new = ExitStack()
                s_.new.__enter__()
                kk.es = s_.new
                return s_

            def __exit__(s_, *a):
                kk.barrier()
                kk.es = s_.old
                s_.new.__exit__(*a)
                return False
        return _S()

    def ev_eng(self):
        self.rr += 1
        return "act" if self.rr % 2 else "dve"


class Ring:
    def __init__(self, bufs):
        self.bufs = bufs
        self.i = 0

    def next(self):
        b = self.bufs[self.i % len(self.bufs)]
        self.i += 1
        return b


class Cfg:
    def __init__(self, TP=2048, NKH=16, D=2048, DFF=5504, NS=4, TS=8, stage=99, dbg=False, PAST=8192, NPOOL=2560):
        self.PAST = PAST
        self.NPOOL = NPOOL
        self.TP = TP
        self.NKH = NKH
        self.NVH = 2 * NKH
        self.D = D
        self.KT = D // 128
        self.DFF = DFF
        self.NS = NS
        self.TS = TS
        self.NSAMP = NS * TS
        self.NT = TP + self.NSAMP
        self.NBP = TP // 64
        self.NB = self.NBP + 1
        self.NCH = TP // 512
        self.KD = NKH * 128
        self.VD = self.NVH * 128
        self.CONVD = 2 * self.KD + self.VD
        self.PROJ = self.CONVD + self.VD + 2 * self.NVH
        self.stage = stage
        self.dbg = dbg

    def chunks(self):
        r = [(c * 512, 512) for c in range(self.NCH)]
        r.append((self.TP, self.NSAMP))
        return r

    def ttiles(self):
        r = [(t * 128, 128) for t in range(self.TP // 128)]
        r.append((self.TP, self.NSAMP))
        return r

    def blocks(self):
        r = [(b * 64, 64) for b in range(self.NBP)]
        r.append((self.TP, self.NSAMP))
        return r


def host_consts(cfg):
    c = {}
    c["ident_f"] = np.eye(128, dtype=np.float32)
    c["ident_b"] = np.eye(128).astype(ml_dtypes.bfloat16)
    c["ones_f"] = np.ones((128, 128), np.float32)
    c["ones_b"] = np.ones((128, 128)).astype(ml_dtypes.bfloat16)
    NEG = -30000.0
    j = np.arange(64)[:, None]
    i = np.arange(64)[None, :]
    def mk(blk, n):
        sj = (np.arange(n)[:, None] // blk)
        si = (np.arange(n)[None, :] // blk)
        same = (sj == si)
        jj = np.arange(n)[:, None]
        ii = np.arange(n)[None, :]
        ui = same & (ii >= jj)
        us = same & (ii > jj)
        ls = same & (ii < jj)
        out = np.zeros((5, 64, 64), np.float32)
        out[0, :n, :n] = ui
        out[1, :n, :n] = same
        out[2, :, :] = NEG
        out[2, :n, :n] = np.where(ui, 0.0, NEG)
        out[3, :, :] = NEG
        out[3, :n, :n] = np.where(us, 0.0, NEG)
        out[4, :, :] = -NEG
        out[4, :n, :n] = np.where(ls, 0.0, -NEG)
        return out
    c["masks_p"] = mk(64, 64)
    c["masks_s"] = mk(cfg.TS, cfg.NSAMP)
    rm = np.zeros((64, cfg.NS), np.float32)
    for s in range(cfg.NS):
        rm[s * cfg.TS:(s + 1) * cfg.TS, s] = 1.0
    c["rowmask_s"] = rm
    return c


def build(cfg):
    nc = bass.Bass("TRN2", target_bir_lowering=False)
    D, KT, NT, TP, NS, TS = cfg.D, cfg.KT, cfg.NT, cfg.TP, cfg.NS, cfg.TS
    NSAMP, NVH, NKH, NB = cfg.NSAMP, cfg.NVH, cfg.NKH, cfg.NB
    CONVD, PROJ, VD, KD = cfg.CONVD, cfg.PROJ, cfg.VD, cfg.KD

    def din(name, shape, dt=F32):
        return nc.dram_tensor(name, list(shape), dt, kind="ExternalInput").ap()

    def dout(name, shape, dt=F32):
        return nc.dram_tensor(name, list(shape), dt, kind="ExternalOutput").ap()

    def dscr(name, shape, dt=F32):
        return nc.dram_tensor(name, list(shape), dt, kind="Internal").ap()

    I = {}
    S = {}
    I["xin"] = din("xin", [NT, D])
    I["norm_mix"] = din("norm_mix", [2, D])
    I["norm_ffn"] = din("norm_ffn", [2, D])
    I["gdn_w_in"] = din("gdn_w_in", [D, PROJ])
    I["gdn_conv_w"] = din("gdn_conv_w", [CONVD, 4])
    I["gdn_A_log"] = din("gdn_A_log", [1, NVH])
    I["gdn_dt_bias"] = din("gdn_dt_bias", [1, NVH])
    I["gdn_norm"] = din("gdn_norm", [1, 128])
    I["gdn_w_out"] = din("gdn_w_out", [VD, D])
    I["st_gdn"] = din("st_gdn", [NS, NVH, 128, 128])
    I["st_gdn_conv"] = din("st_gdn_conv", [NS * 3, CONVD])
    DFF = cfg.DFF
    I["ffn_w_up"] = din("ffn_w_up", [2, D, 2 * DFF])
    I["ffn_conv_w"] = din("ffn_conv_w", [2, DFF, 3])
    I["ffn_conv_b"] = din("ffn_conv_b", [2, DFF])
    I["ffn_w_down"] = din("ffn_w_down", [2, DFF, D])
    I["st_ffn_conv"] = din("st_ffn_conv", [2, NS * 2, DFF])
    for nm, shp, dt in (("ident_f", [128, 128], F32), ("ident_b", [128, 128], BF16),
                        ("ones_f", [128, 128], F32), ("ones_b", [128, 128], BF16),
                        ("masks_p", [5, 64, 64], F32), ("masks_s", [5, 64, 64], F32),
                        ("rowmask_s", [64, NS], F32)):
        I[nm] = din(nm, shp, dt)
    O = {}
    O["o_gdn_p"] = dout("o_gdn_p", [NVH, 128, 128])
    O["o_gdn_s"] = dout("o_gdn_s", [NS, NVH, 128, 128])
    O["o_gdnc"] = dout("o_gdnc", [(1 + NS) * 3, CONVD])
    PAST, NPG, NPOOL = cfg.PAST, cfg.PAST // 128, cfg.NPOOL
    I["nsa_w_in"] = din("nsa_w_in", [D, 5168])
    I["nsa_q_norm"] = din("nsa_q_norm", [1, 128])
    I["nsa_k_norm"] = din("nsa_k_norm", [1, 3, 128])
    I["st_win"] = din("st_win", [NS, 512, 1024])
    I["nsa_cmp_pe"] = din("nsa_cmp_pe", [2, 32, 128])
    I["nsa_cmp_w1"] = din("nsa_cmp_w1", [2, 4096, 256])
    I["nsa_cmp_w2"] = din("nsa_cmp_w2", [2, 256, 128])
    I["rel_bias"] = din("rel_bias", [32, 16])
    I["nsa_w_out"] = din("nsa_w_out", [2048, 2048])
    I["cache"] = din("cache", [NPOOL * 128, 2048])
    S["past"] = dscr("past", [NS * PAST, 2048], BF16)
    I["page_table"] = din("page_table", [NS, NPG], I32)
    hc = nsa_host_consts(cfg, PAST)
    for nm, arr in hc.items():
        if nm.startswith("dims"):
            continue
        I[nm] = din(nm, list(arr.shape), {np.dtype(np.float32): F32, np.dtype(np.int32): I32}.get(arr.dtype, BF16))
    S["gvec"] = dscr("gvec", [16, NSA_GL])
    S["oT2"] = dscr("oT2", [2048, NT], BF16)
    O["o_kv"] = dout("o_kv", [NT, 2048])
    O["o_pwin"] = dout("o_pwin", [512, 1024])
    O["o_swin"] = dout("o_swin", [NS, 512, 1024])
    S["qT"] = dscr("qT", [2048, NT], BF16)
    S["win_new"] = dscr("win_new", [NT, 1024])
    S["gates"] = dscr("gates", [NT, 48])
    O["o_ffnc"] = dout("o_ffnc", [2, (1 + NS) * 2, DFF])
    O["y"] = dout("y", [NT, D])
    if cfg.dbg:
        O["dbg_h1"] = dout("dbg_h1", [NT, D])
        O["dbg_h2"] = dout("dbg_h2", [NT, D])
        O["dbg_h3"] = dout("dbg_h3", [NT, D])
        O["dbg_oT2"] = dout("dbg_oT2", [2048, NT], BF16)
        O["dbg_gvec"] = dout("dbg_gvec", [16, NSA_GL])
        O["dbg_qT"] = dout("dbg_qT", [2048, NT], BF16)
    if cfg.dbg:
        O["dbg_xn"] = dout("dbg_xn", [128, KT, NT], BF16)
        O["dbg_o"] = dout("dbg_o", [NVH * 128, NT], BF16)
    S["oT"] = dscr("oT", [NVH * 128, NT], BF16)
    S["tmS"] = dscr("tmS", [64, 5 * NVH * NB], F32)
    S["h1"] = dscr("h1", [NT, D])
    S["h2"] = dscr("h2", [NT, D])
    S["h3"] = dscr("h3", [NT, D])
    S["actT"] = dscr("actT", [DFF, NT], BF16)

    es = ExitStack()
    with es:
        k = K(nc, es)
        dbuf = {n: Buf(a, n, multi=True) for n, a in list(I.items()) + list(O.items()) + list(S.items())}

        ident_f = k.sb("ident_f", [128, 128], F32)
        ident_b = k.sb("ident_b", [128, 128], BF16)
        ones_f = k.sb("ones_f", [128, 128], F32)
        ones_b = k.sb("ones_b", [128, 128], BF16)
        masks_p = k.sb("masks_p", [64, 5, 64], F32)
        masks_s = k.sb("masks_s", [64, 5, 64], F32)
        rowmask_s = k.sb("rowmask_s", [64, NS], F32)
        for nm, t in (("ident_f", ident_f), ("ident_b", ident_b), ("ones_f", ones_f), ("ones_b", ones_b),
                      ("rowmask_s", rowmask_s)):
            k.dma("sp", t[:], I[nm], writes=[t])
        k.dma("sp", masks_p[:], I["masks_p"].rearrange("m j i -> j m i"), writes=[masks_p])
        k.dma("sp", masks_s[:], I["masks_s"].rearrange("m j i -> j m i"), writes=[masks_s])

        pf = Ring([k.ps(f"pf{i}", [128, 512], F32) for i in range(6)])
        pbf = Ring([k.ps(f"pb{i}", [128, 1024], BF16) for i in range(2)])

        def norm_fm(src_name, gain_ap, xn):
          with k.scope():
              gain_bc = k.sb(f"gain_{src_name}", [128, D], F32)
              k.dma("sp", gain_bc[:], gain_ap.partition_broadcast(128), writes=[gain_bc])
              xt_r = Ring([k.sb(f"xt{i}_{src_name}", [128, D], F32) for i in range(2)])
              sq = k.sb(f"sq_{src_name}", [128, D], F32)
              xs_r = Ring([k.sb(f"xs{i}_{src_name}", [128, D], BF16) for i in range(2)])
              ss_r = Ring([k.sb(f"ss{i}_{src_name}", [128, 2], F32) for i in range(2)])
              src = dbuf[src_name]
              for (r0, R) in cfg.ttiles():
                  xt = xt_r.next()
                  xs = xs_r.next()
                  ss = ss_r.next()
                  k.dma("sp", xt[0:R, :], src[r0:r0 + R, :], reads=[src], writes=[xt])
                  k.op("act", lambda e: e.activation(out=sq[0:R, :], in_=xt[0:R, :], func=AF.Square),
                       reads=[xt], writes=[sq])
                  k.op("dve", lambda e: e.reduce_sum(out=ss[0:R, 0:1], in_=sq[0:R, :], axis=AX.X),
                       reads=[sq], writes=[ss])
                  k.op("dve", lambda e: e.tensor_scalar(out=ss[0:R, 1:2], in0=ss[0:R, 0:1], scalar1=1.0 / D,
                                                        scalar2=RMS_EPS, op0=ALU.mult, op1=ALU.add),
                       reads=[ss], writes=[ss])
                  k.op("act", lambda e: e.activation(out=ss[0:R, 0:1], in_=ss[0:R, 1:2], func=AF.Sqrt),
                       reads=[ss], writes=[ss])
                  k.op("dve", lambda e: e.reciprocal(out=ss[0:R, 0:1], in_=ss[0:R, 0:1]),
                       reads=[ss], writes=[ss])
                  k.op("dve", lambda e: e.scalar_tensor_tensor(out=xs[0:R, :], in0=xt[0:R, :], scalar=ss[0:R, 0:1],
                                                               in1=gain_bc[0:R, :], op0=ALU.mult, op1=ALU.mult),
                       reads=[xt, ss, gain_bc], writes=[xs])
                  for g4 in range(KT // 4):
                      pb = pbf.next()
                      for q in range(4):
                          kt = g4 * 4 + q
                          k.op("pe", lambda e: e.transpose(out=pb[:, q * 128:q * 128 + R],
                                                           in_=xs[0:R, kt * 128:(kt + 1) * 128],
                                                           identity=ident_b[0:R, 0:R]),
                               reads=[xs, ident_b], writes=[pb], inc=(q == 3))
                      eng = k.ev_eng()
                      src_v = pb[:, 0:512].rearrange("p (q r) -> p q r", q=4)[:, :, 0:R]
                      dst_v = xn[:, g4 * 4:(g4 + 1) * 4, r0:r0 + R]
                      if eng == "act":
                          k.op("act", lambda e: e.copy(out=dst_v, in_=src_v), reads=[pb], writes=[xn])
                      else:
                          k.op("dve", lambda e: e.tensor_copy(out=dst_v, in_=src_v), reads=[pb], writes=[xn])

        CC = dict(ident_f=ident_f, ident_b=ident_b, ones_f=ones_f, ones_b=ones_b,
                  masks_p=masks_p, masks_s=masks_s, rowmask_s=rowmask_s)
        with k.scope():
            xn = k.sb("xn", [128, KT, NT], BF16)
            norm_fm("xin", I["norm_mix"][0:1, :], xn)
            if cfg.dbg:
                k.dma("sp", O["dbg_xn"], xn[:], reads=[xn], writes=[dbuf["dbg_xn"]])
            gdn_layer(k, cfg, I, O, S, dbuf, xn, pf, pbf, CC)
        if cfg.stage >= 2:
            linear_tm(k, cfg, pf, dbuf, "go", "oT", S["oT"], "gdn_w_out", I["gdn_w_out"], VD,
                      "xin", I["xin"], "h1", S["h1"])
            ffn_layer(k, cfg, I, O, S, dbuf, pf, pbf, CC, 0, "h1", S["h1"], "h2", S["h2"], norm_fm,
                      hook=(Unpager(k, cfg, I, S, dbuf) if cfg.stage >= 4 else None))
        if cfg.stage >= 3:
            nsa_proj(k, cfg, I, O, S, dbuf, pf, pbf, CC, norm_fm, "h2")
        if cfg.stage >= 4:
            nsa_attn(k, cfg, I, O, S, dbuf, pf, pbf, CC)
            linear_tm(k, cfg, pf, dbuf, "no", "oT2", S["oT2"], "nsa_w_out", I["nsa_w_out"], 2048,
                      "h2", S["h2"], "h3", S["h3"])
            ffn_layer(k, cfg, I, O, S, dbuf, pf, pbf, CC, 1, "h3", S["h3"], "y", O["y"], norm_fm)
            if cfg.dbg:
                k.dma("sp", O["dbg_h3"], S["h3"], reads=[dbuf["h3"]], writes=[dbuf["dbg_h3"]])
                k.dma("sp", O["dbg_oT2"], S["oT2"], reads=[dbuf["oT2"]], writes=[dbuf["dbg_oT2"]])
                k.dma("sp", O["dbg_gvec"], S["gvec"], reads=[dbuf["gvec"]], writes=[dbuf["dbg_gvec"]])
                k.dma("sp", O["dbg_qT"], S["qT"], reads=[dbuf["qT"]], writes=[dbuf["dbg_qT"]])
        if cfg.stage >= 2:
            if cfg.dbg:
                k.dma("sp", O["dbg_h1"], S["h1"], reads=[dbuf["h1"]], writes=[dbuf["dbg_h1"]])
                k.dma("sp", O["dbg_h2"], S["h2"], reads=[dbuf["h2"]], writes=[dbuf["dbg_h2"]])

        k.finish()
        print("inst", k.n_inst, "wait", k.n_wait)
    return nc


def gdn_layer(k, cfg, I, O, S, dbuf, xn, pf, pbf, C):
    D, KT, NT, TP, NS, TS = cfg.D, cfg.KT, cfg.NT, cfg.TP, cfg.NS, cfg.TS
    NSAMP, NVH, NKH, NB, NBP = cfg.NSAMP, cfg.NVH, cfg.NKH, cfg.NB, cfg.NBP
    CONVD, PROJ, VD, KD = cfg.CONVD, cfg.PROJ, cfg.VD, cfg.KD
    ident_f, ident_b, ones_f, ones_b = C["ident_f"], C["ident_b"], C["ones_f"], C["ones_b"]
    masks_p, masks_s, rowmask_s = C["masks_p"], C["masks_s"], C["rowmask_s"]
    NCT = CONVD // 128
    PADW = 3 + TP + NS * 11
    SOFF = 3 + TP
    blocks = cfg.blocks()
    chunks = cfg.chunks()

    cw = k.sb("g_cw", [128, NCT, 4], F32)
    k.dma("sp", cw[:], I["gdn_conv_w"].rearrange("(t p) w -> p t w", p=128), writes=[cw])
    alog = k.sb("g_alog", [64, NVH], F32)
    dtb = k.sb("g_dtb", [64, NVH], F32)
    k.dma("sp", alog[:], I["gdn_A_log"].partition_broadcast(64), writes=[alog])
    k.dma("sp", dtb[:], I["gdn_dt_bias"].partition_broadcast(64), writes=[dtb])
    gnorm = k.sb("g_norm", [128, 1], F32)
    k.dma("sp", gnorm[:], I["gdn_norm"].rearrange("o d -> d o"), writes=[gnorm])
    stc_r = Ring([k.sb(f"g_stc{i}", [NS * 3, 128], F32) for i in range(2)])
    negA = k.sb("g_negA", [64, NVH], F32)
    k.op("act", lambda e: e.activation(out=negA[:], in_=alog[:], func=AF.Exp), reads=[alog], writes=[negA])
    k.op("dve", lambda e: e.tensor_scalar(out=negA[:], in0=negA[:], scalar1=-1.0, scalar2=None, op0=ALU.mult),
         reads=[negA], writes=[negA])

    tm_scope = k.scope()
    tm_scope.__enter__()
    wbd = k.sb("g_wbd", [128, KT, 2 * NVH], BF16)
    k.dma("pool", wbd[:], I["gdn_w_in"][:, CONVD + VD:PROJ].rearrange("(kt p) c -> p kt c", p=128), writes=[wbd])
    NH2 = 2 * NVH
    bl_all = k.sb("g_bl", [64, NB, NH2], F32)
    k.op("pool", lambda e: e.memset(bl_all[:], 0.0), writes=[bl_all])
    per_bank = 512 // NH2
    for g0 in range(0, NB, per_bank):
        gb = blocks[g0:g0 + per_bank]
        ps = pf.next()
        for bi, (t0, L) in enumerate(gb):
            for kt in range(KT):
                k.op("pe", lambda e: e.matmul(ps[0:L, bi * NH2:(bi + 1) * NH2], lhsT=xn[:, kt, t0:t0 + L],
                                              rhs=wbd[:, kt, :], start=(kt == 0), stop=(kt == KT - 1)),
                     reads=[xn, wbd], writes=[ps], inc=(kt == KT - 1))
        nfull = sum(1 for (_, L) in gb if L == 64)
        if nfull:
            k.op("act", lambda e: e.copy(out=bl_all[:, g0:g0 + nfull, :],
                                         in_=ps[0:64, 0:nfull * NH2].rearrange("p (b c) -> p b c", c=NH2)),
                 reads=[ps], writes=[bl_all])
        if nfull < len(gb):
            bi = nfull
            k.op("act", lambda e: e.copy(out=bl_all[0:NSAMP, g0 + bi, :], in_=ps[0:NSAMP, bi * NH2:(bi + 1) * NH2]),
                 reads=[ps], writes=[bl_all])

    def tm(name):
        return k.sb(name, [64, NB, NVH], F32)
    beta_t, g_t, G_t, gl_t, c_t, kd_t, nb_t, tmp1, tmp2 = (tm("g_beta"), tm("g_g"), tm("g_G"), tm("g_gl"),
                                                          tm("g_c"), tm("g_kd"), tm("g_nb"), tm("g_t1"), tm("g_t2"))
    blv = bl_all[:, :, 0:NVH]
    alv = bl_all[:, :, NVH:NH2]
    dtb_b = dtb[:].unsqueeze(1).to_broadcast([64, NB, NVH])
    negA_b = negA[:].unsqueeze(1).to_broadcast([64, NB, NVH])
    k.op("act", lambda e: e.activation(out=beta_t[:], in_=blv, func=AF.Sigmoid), reads=[bl_all], writes=[beta_t])
    k.op("dve", lambda e: e.tensor_tensor(out=tmp1[:], in0=alv, in1=dtb_b, op=ALU.add), reads=[bl_all, dtb], writes=[tmp1])
    k.op("dve", lambda e: e.tensor_scalar(out=tmp2[:], in0=tmp1[:], scalar1=-1.0, scalar2=None, op0=ALU.mult),
         reads=[tmp1], writes=[tmp2])
    k.op("dve", lambda e: e.tensor_tensor(out=tmp2[:], in0=tmp2[:], in1=tmp1[:], op=ALU.min),
         reads=[tmp1, tmp2], writes=[tmp2])
    k.op("act", lambda e: e.activation(out=tmp2[:], in_=tmp2[:], func=AF.Exp), reads=[tmp2], writes=[tmp2])
    k.op("act", lambda e: e.activation(out=tmp2[:], in_=tmp2[:], func=AF.Ln, bias=1.0), reads=[tmp2], writes=[tmp2])
    k.op("dve", lambda e: e.scalar_tensor_tensor(out=tmp1[:], in0=tmp1[:], scalar=0.0, in1=tmp2[:],
                                                 op0=ALU.max, op1=ALU.add), reads=[tmp1, tmp2], writes=[tmp1])
    k.op("dve", lambda e: e.tensor_tensor(out=g_t[:], in0=tmp1[:], in1=negA_b, op=ALU.mult),
         reads=[tmp1, negA], writes=[g_t])
    per_bank = 512 // NVH
    for (dst, mi) in ((G_t, 0), (gl_t, 1)):
        for g0 in range(0, NB, per_bank):
            gb = blocks[g0:g0 + per_bank]
            ps = pf.next()
            for bi, (t0, L) in enumerate(gb):
                mk_ = masks_p if L == 64 else masks_s
                k.op("pe", lambda e: e.matmul(ps[0:L, bi * NVH:(bi + 1) * NVH], lhsT=mk_[0:L, mi, 0:L],
                                              rhs=g_t[0:L, g0 + bi, :], start=True, stop=True),
                     reads=[mk_, g_t], writes=[ps], inc=(bi == len(gb) - 1))
            k.op("dve", lambda e: e.tensor_copy(out=dst[:, g0:g0 + len(gb), :],
                                                in_=ps[0:64, 0:len(gb) * NVH].rearrange("p (b c) -> p b c", c=NVH)),
                 reads=[ps], writes=[dst])
    k.op("act", lambda e: e.activation(out=tmp1[:], in_=G_t[:], func=AF.Exp), reads=[G_t], writes=[tmp1])
    k.op("dve", lambda e: e.scalar_tensor_tensor(out=c_t[:], in0=tmp1[:], scalar=-1.0, in1=beta_t[:],
                                                 op0=ALU.mult, op1=ALU.mult), reads=[tmp1, beta_t], writes=[c_t])
    k.op("dve", lambda e: e.tensor_tensor(out=tmp2[:], in0=gl_t[:], in1=G_t[:], op=ALU.subtract),
         reads=[gl_t, G_t], writes=[tmp2])
    k.op("act", lambda e: e.activation(out=kd_t[:], in_=tmp2[:], func=AF.Exp), reads=[tmp2], writes=[kd_t])
    k.op("dve", lambda e: e.tensor_scalar(out=nb_t[:], in0=beta_t[:], scalar1=-1.0, scalar2=None, op0=ALU.mult),
         reads=[beta_t], writes=[nb_t])

    tmH = k.sb("g_tmH", [64, 5, NVH, NB], F32)
    for idx, src_t in enumerate((beta_t, G_t, c_t, kd_t, nb_t)):
        k.op("pool", lambda e: e.tensor_copy(out=tmH[:, idx].rearrange("p h b -> p b h"), in_=src_t[:]),
             reads=[src_t], writes=[tmH])
    k.dma("sp", S["tmS"], tmH[:].rearrange("p k h b -> p (k h b)"), reads=[tmH], writes=[dbuf["tmS"]])
    tm_scope.__exit__(None, None, None)
    hd_r = Ring([k.sb(f"g_hd{i}", [64, 5, NB], F32) for i in range(2)])

    def load_hd(hv):
        hd = hd_r.next()
        k.dma("sp", hd[:], S["tmS"].rearrange("p (k h b) -> p k h b", k=5, h=NVH)[:, :, hv, :],
              reads=[dbuf["tmS"]], writes=[hd])
        return hd

    wt_r = Ring([k.sb(f"g_wt{i}", [128, KT, 128], BF16) for i in range(2)])
    prep = k.sb("g_prep", [128, PADW], F32)
    k.op("pool", lambda e: e.memset(prep[:], 0.0), writes=[prep])
    cq = k.sb("g_cq", [128, NT], F32)
    sqb = k.sb("g_sqb", [128, NT], BF16)
    rinv = k.sb("g_rinv", [128, 512], F32)
    epsb = k.sb("g_epsb", [128, 1], F32)
    k.op("pool", lambda e: e.memset(epsb[:], RMS_EPS), writes=[epsb])
    kqfm = k.sb("g_kqfm", [128, NB, 128], BF16)
    k.op("pool", lambda e: e.memset(kqfm[:], 0.0), writes=[kqfm])
    k_tok = k.sb("g_ktok", [64, NB, 128], BF16)
    vs = k.sb("g_vs", [128, NT], BF16)
    zsil = [k.sb("g_zsil", [128, NT], BF16)] * 2
    vb_tok = [k.sb("g_vb", [64, NB, 128], BF16)] * 2
    convout = k.sb("g_convout", [128, NCT, 16], F32)
    k.op("pool", lambda e: e.memset(convout[:], 0.0), writes=[convout])

    def _load_w_now(col0):
        wt = wt_r.next()
        k.dma("pool", wt[:], I["gdn_w_in"][:, col0:col0 + 128].rearrange("(kt p) c -> p kt c", p=128),
              reads=[dbuf["gdn_w_in"]], writes=[wt])
        return wt

    col_order = []
    for j_ in range(NKH):
        col_order += [j_ * 128, KD + j_ * 128]
        for a_ in range(2):
            hv_ = 2 * j_ + a_
            col_order += [2 * KD + hv_ * 128, CONVD + hv_ * 128]
    pre = {"i": 0, "wt": None}

    def load_w(col0):
        i = pre["i"]
        assert col_order[i] == col0, (i, col0, col_order[i])
        wt = pre["wt"] if pre["wt"] is not None else _load_w_now(col0)
        pre["i"] = i + 1
        pre["wt"] = _load_w_now(col_order[i + 1]) if i + 1 < len(col_order) else None
        return wt

    def proj(wt, consumer):
        for ci, (t0, n) in enumerate(chunks):
            ps = pf.next()
            for kt in range(KT):
                k.op("pe", lambda e: e.matmul(ps[:, 0:n], lhsT=wt[:, kt, :], rhs=xn[:, kt, t0:t0 + n],
                                              start=(kt == 0), stop=(kt == KT - 1)),
                     reads=[wt, xn], writes=[ps], inc=(kt == KT - 1))
            consumer(ci, t0, n, ps)

    def samp_view(buf_ap_2d, w, lo, hi):
        return buf_ap_2d.rearrange("p (s w) -> p s w", w=w)[:, :, lo:hi]

    def proj_conv_silu(col0, dst, wt=None):
        ct = col0 // 128
        if wt is None:
            wt = load_w(col0)

        def cons(ci, t0, n, ps):
            eng = k.ev_eng()
            if n == 512:
                dv = prep[:, 3 + t0:3 + t0 + n]
                sv = ps[:, 0:n]
            else:
                dv = samp_view(prep[:, SOFF:SOFF + NS * 11], 11, 3, 11)
                sv = ps[:, 0:n].rearrange("p (s w) -> p s w", w=TS)
            if eng == "act":
                k.op("act", lambda e: e.copy(out=dv, in_=sv), reads=[ps], writes=[prep])
            else:
                k.op("dve", lambda e: e.tensor_copy(out=dv, in_=sv), reads=[ps], writes=[prep])
        pst = pf.next()
        stc = stc_r.next()
        k.dma("sp", stc[:], I["st_gdn_conv"][:, ct * 128:(ct + 1) * 128], writes=[stc])
        k.op("pe", lambda e: e.transpose(out=pst[:, 0:NS * 3], in_=stc[0:NS * 3, 0:128],
                                         identity=ident_f[0:NS * 3, 0:NS * 3]),
             reads=[stc, ident_f], writes=[pst])
        k.op("dve", lambda e: e.tensor_copy(out=samp_view(prep[:, SOFF:SOFF + NS * 11], 11, 0, 3),
                                            in_=pst[:, 0:NS * 3].rearrange("p (s w) -> p s w", w=3)),
             reads=[pst], writes=[prep])
        proj(wt, cons)
        k.op("pool", lambda e: e.tensor_copy(out=convout[:, ct, 0:3], in_=prep[:, 3 + TP - 3:3 + TP]),
             reads=[prep], writes=[convout])
        k.op("pool", lambda e: e.tensor_copy(out=convout[:, ct, 3:3 + NS * 3].rearrange("p (s w) -> p s w", w=3),
                                             in_=samp_view(prep[:, SOFF:SOFF + NS * 11], 11, 8, 11)),
             reads=[prep], writes=[convout])
        for (ov, mkv) in ((cq[:, 0:TP], lambda i: prep[:, i:i + TP]),
                          (cq[:, TP:NT].rearrange("p (s w) -> p s w", w=TS),
                           lambda i: samp_view(prep[:, SOFF:SOFF + NS * 11], 11, i, i + TS))):
            k.op("dve", lambda e: e.tensor_scalar(out=ov, in0=mkv(3), scalar1=cw[:, ct, 3:4], scalar2=None, op0=ALU.mult),
                 reads=[prep, cw], writes=[cq])
            for i in range(3):
                k.op("dve", lambda e: e.scalar_tensor_tensor(out=ov, in0=mkv(i), scalar=cw[:, ct, i:i + 1], in1=ov,
                                                             op0=ALU.mult, op1=ALU.add),
                     reads=[prep, cw, cq], writes=[cq])
        k.op("act", lambda e: e.activation(out=dst[:], in_=cq[:], func=AF.Silu), reads=[cq], writes=[dst])

    def l2norm_to(src, which, scale):
        k.op("act", lambda e: e.activation(out=sqb[:], in_=src[:], func=AF.Square), reads=[src], writes=[sqb])
        for ci, (t0, n) in enumerate(chunks):
            ps = pf.next()
            k.op("pe", lambda e: e.matmul(ps[:, 0:n], lhsT=ones_b[:, :], rhs=sqb[:, t0:t0 + n], start=True, stop=True),
                 reads=[ones_b, sqb], writes=[ps])
            k.op("act", lambda e: e.activation(out=rinv[:, 0:n], in_=ps[:, 0:n], func=AF.Sqrt, bias=epsb[:, 0:1]),
                 reads=[ps, epsb], writes=[rinv])
            k.op("dve", lambda e: e.reciprocal(out=rinv[:, 0:n], in_=rinv[:, 0:n]), reads=[rinv], writes=[rinv])
            if n == 512:
                b0 = t0 // 64
                dv = kqfm[:, b0:b0 + 8, which * 64:which * 64 + 64]
                sv = src[:, t0:t0 + n].rearrange("p (b i) -> p b i", i=64)
                rv = rinv[:, 0:n].rearrange("p (b i) -> p b i", i=64)
            else:
                dv = kqfm[:, NBP, which * 64:which * 64 + n]
                sv = src[:, t0:t0 + n]
                rv = rinv[:, 0:n]
            k.op("dve", lambda e: e.scalar_tensor_tensor(out=dv, in0=sv, scalar=scale, in1=rv, op0=ALU.mult, op1=ALU.mult),
                 reads=[src, rinv], writes=[kqfm])

    def to_tok(src_fn, evac):
        for g0 in range(0, NB, 8):
            gb = blocks[g0:g0 + 8]
            pb = pbf.next()
            for bi, (t0, L) in enumerate(gb):
                k.op("pe", lambda e: e.transpose(out=pb[0:L, bi * 128:(bi + 1) * 128], in_=src_fn(g0 + bi, t0, L),
                                                 identity=ident_b[:, :]),
                     reads=src_fn.reads + [ident_b], writes=[pb], inc=(bi == len(gb) - 1))
            evac(g0, gb, pb)

    NG = 4
    dgG = k.sb("g_dgG", [64, NG, 64], F32)
    dgB = k.sb("g_dgB", [64, NG, 64], F32)
    t1 = k.sb("g_t1b", [64, NG, 64], F32)
    tI = k.sb("g_tI", [64, NG, 64], F32)
    tS = k.sb("g_tS", [64, NG, 64], F32)
    tL = k.sb("g_tL", [64, NG, 64], F32)
    kkE = k.sb("g_kkE", [64, NG, 64], F32)
    kkL = k.sb("g_kkL", [64, NG, 64], F32)
    eGbc = k.sb("g_eGbc", [128, NG * 64], F32)
    PR = [k.sb(f"g_PR{i}", [64, NG, 128], BF16) for i in range(2)]
    PT = [k.sb(f"g_PT{i}", [64, NG, 64], BF16) for i in range(2)]
    heads = []
    for a in range(1):
        h = dict(
            AQT=k.sb(f"g_AQT{a}", [64, NB, 64], BF16), TT=k.sb(f"g_TT{a}", [64, NB, 64], BF16),
            qdec=k.sb(f"g_qdec{a}", [128, NB, 64], BF16), kdec=k.sb(f"g_kdec{a}", [64, 2, 128], BF16),
            egl=k.sb(f"g_egl{a}", [128, NB, 4], F32),
            S=k.sb(f"g_S{a}", [128, 128], F32), Sb=k.sb(f"g_Sb{a}", [128, 128], BF16),
            Ss=[k.sb(f"g_Ss{a}_{s}", [128, 128], F32) for s in range(NS)],
            Ssb=[k.sb(f"g_Ssb{a}_{s}", [128, 128], BF16) for s in range(NS)],
            r=k.sb(f"g_r{a}", [64, 128], BF16), u=k.sb(f"g_u{a}", [64, 128], BF16),
            otok=k.sb(f"g_otok{a}", [64, 8, 128], F32), osq=k.sb(f"g_osq{a}", [64, 8, 128], F32),
            oss=k.sb(f"g_oss{a}", [64, 16], F32), on=k.sb(f"g_on{a}", [64, 8, 128], BF16),
            oT=k.sb(f"g_oT{a}", [128, 512], BF16),
            kzp=k.sb(f"g_kzp{a}", [128, NS, NSAMP], BF16), qzp=k.sb(f"g_qzp{a}", [128, NS, NSAMP], BF16),
            kds=k.sb(f"g_kds{a}", [64, NS, 128], BF16),
        )
        heads.append(h)
    heads.append(heads[0])
    HD = {}

    def precompute(hv, hd):
        H = heads[0]
        groups = [(g0, blocks[g0:g0 + NG]) for g0 in range(0, NBP, NG)] + [(NBP, [blocks[NBP]])]
        for (g0, gb) in groups:
            nb = len(gb)
            L = gb[0][1]
            mk_ = masks_p if L == 64 else masks_s
            idb = ident_f[0:L, 0:64].unsqueeze(1).to_broadcast([L, nb, 64])
            k.op("dve", lambda e: e.tensor_tensor(out=dgG[0:L, 0:nb, :], in0=idb,
                                                  in1=hd[0:L, 1, g0:g0 + nb].unsqueeze(2).to_broadcast([L, nb, 64]),
                                                  op=ALU.mult), reads=[ident_f, hd], writes=[dgG])
            k.op("dve", lambda e: e.tensor_tensor(out=dgB[0:L, 0:nb, :], in0=idb,
                                                  in1=hd[0:L, 0, g0:g0 + nb].unsqueeze(2).to_broadcast([L, nb, 64]),
                                                  op=ALU.mult), reads=[ident_f, hd], writes=[dgB])
            pG = pf.next()
            k.op("pe", lambda e: e.matmul(pG[:, 0:nb * 64], lhsT=ones_f[0:L, :],
                                          rhs=dgG[0:L, 0:nb, :].rearrange("p b i -> p (b i)"), start=True, stop=True),
                 reads=[ones_f, dgG], writes=[pG])
            pB = pf.next()
            k.op("pe", lambda e: e.matmul(pB[0:64, 0:nb * 64], lhsT=ones_f[0:L, 0:64],
                                          rhs=dgB[0:L, 0:nb, :].rearrange("p b i -> p (b i)"), start=True, stop=True),
                 reads=[ones_f, dgB], writes=[pB])
            pGv = pG[0:L, 0:nb * 64].rearrange("p (b i) -> p b i", i=64)
            pBv = pB[0:L, 0:nb * 64].rearrange("p (b i) -> p b i", i=64)
            k.op("dve", lambda e: e.tensor_tensor(out=t1[0:L, 0:nb, :], in0=pGv,
                                                  in1=hd[0:L, 1, g0:g0 + nb].unsqueeze(2).to_broadcast([L, nb, 64]),
                                                  op=ALU.subtract), reads=[pG, hd], writes=[t1])
            for (dst, mi) in ((tI, 2), (tS, 3), (tL, 4)):
                k.op("pool", lambda e: e.tensor_tensor(out=dst[0:L, 0:nb, :], in0=t1[0:L, 0:nb, :],
                                                       in1=mk_[0:L, mi, :].unsqueeze(1).to_broadcast([L, nb, 64]),
                                                       op=ALU.add), reads=[t1, mk_], writes=[dst])
            k.op("act", lambda e: e.activation(out=tI[0:L, 0:nb, :], in_=tI[0:L, 0:nb, :], func=AF.Exp), reads=[tI], writes=[tI])
            k.op("act", lambda e: e.activation(out=tS[0:L, 0:nb, :], in_=tS[0:L, 0:nb, :], func=AF.Exp), reads=[tS], writes=[tS])
            k.op("act", lambda e: e.activation(out=tL[0:L, 0:nb, :], in_=tL[0:L, 0:nb, :], func=AF.Exp, scale=-1.0),
                 reads=[tL], writes=[tL])
            k.op("act", lambda e: e.activation(out=eGbc[:, 0:nb * 64], in_=pG[:, 0:nb * 64], func=AF.Exp),
                 reads=[pG], writes=[eGbc])
            pK = pf.next()
            for bi, (t0, Lb) in enumerate(gb):
                k.op("pe", lambda e: e.matmul(pK[0:L, bi * 128:(bi + 1) * 128], lhsT=kqfm[:, g0 + bi, 0:L],
                                              rhs=kqfm[:, g0 + bi, :], start=True, stop=True),
                     reads=[kqfm], writes=[pK], inc=(bi == nb - 1))
            pKv = pK[0:L, 0:nb * 128].rearrange("p (b c) -> p b c", c=128)
            k.op("dve", lambda e: e.tensor_tensor(out=H["AQT"][0:L, g0:g0 + nb, 0:L], in0=pKv[:, :, 64:64 + L],
                                                  in1=tI[0:L, 0:nb, 0:L], op=ALU.mult),
                 reads=[pK, tI], writes=[H["AQT"]])
            k.op("dve", lambda e: e.tensor_tensor(out=kkE[0:L, 0:nb, 0:L], in0=pKv[:, :, 0:L], in1=tS[0:L, 0:nb, 0:L],
                                                  op=ALU.mult), reads=[pK, tS], writes=[kkE])
            k.op("dve", lambda e: e.tensor_tensor(out=kkL[0:L, 0:nb, 0:L], in0=pKv[:, :, 0:L], in1=tL[0:L, 0:nb, 0:L],
                                                  op=ALU.mult), reads=[pK, tL], writes=[kkL])
            pr, pt = PR[0], PT[0]
            k.op("dve", lambda e: e.scalar_tensor_tensor(out=pr[0:L, 0:nb, 0:L], in0=kkE[0:L, 0:nb, 0:L], scalar=-1.0,
                                                         in1=pBv[:, :, 0:L], op0=ALU.mult, op1=ALU.mult),
                 reads=[kkE, pB], writes=[pr])
            k.op("pool", lambda e: e.tensor_copy(out=pr[0:L, 0:nb, 64:64 + L],
                                                 in_=ident_f[0:L, 0:L].unsqueeze(1).to_broadcast([L, nb, L])),
                 reads=[ident_f], writes=[pr])
            k.op("dve", lambda e: e.tensor_tensor(out=pt[0:L, 0:nb, 0:L], in0=kkL[0:L, 0:nb, 0:L],
                                                  in1=hd[0:L, 4, g0:g0 + nb].unsqueeze(2).to_broadcast([L, nb, L]),
                                                  op=ALU.mult), reads=[kkL, hd], writes=[pt])
            for s in range(6):
                pr, pt = PR[s % 2], PT[s % 2]
                prn, ptn = PR[(s + 1) % 2], PT[(s + 1) % 2]
                p1 = pf.next()
                last = (s == 5)
                for bi in range(nb):
                    k.op("pe", lambda e: e.matmul(p1[0:L, bi * 128:bi * 128 + 64 + L], lhsT=pt[0:L, bi, 0:L],
                                                  rhs=pr[0:L, bi, 0:64 + L], start=True, stop=True),
                         reads=[pt, pr], writes=[p1], inc=(bi == nb - 1))
                p1v = p1[0:L, 0:nb * 128].rearrange("p (b c) -> p b c", c=128)
                if not last:
                    p2 = pf.next()
                    for bi in range(nb):
                        k.op("pe", lambda e: e.matmul(p2[0:L, bi * 64:bi * 64 + L], lhsT=pr[0:L, bi, 0:L],
                                                      rhs=pt[0:L, bi, 0:L], start=True, stop=True),
                             reads=[pt, pr], writes=[p2], inc=(bi == nb - 1))
                    p2v = p2[0:L, 0:nb * 64].rearrange("p (b c) -> p b c", c=64)
                    k.op("act", lambda e: e.copy(out=prn[0:L, 0:nb, 0:L], in_=p1v[:, :, 0:L]), reads=[p1], writes=[prn])
                    k.op("act", lambda e: e.copy(out=ptn[0:L, 0:nb, 0:L], in_=p2v[:, :, 0:L]), reads=[p2], writes=[ptn])
                    k.op("dve", lambda e: e.tensor_tensor(out=prn[0:L, 0:nb, 64:64 + L], in0=p1v[:, :, 64:64 + L],
                                                          in1=pr[0:L, 0:nb, 64:64 + L], op=ALU.add),
                         reads=[p1, pr], writes=[prn])
                else:
                    k.op("dve", lambda e: e.tensor_tensor(out=H["TT"][0:L, g0:g0 + nb, 0:L], in0=p1v[:, :, 64:64 + L],
                                                          in1=pr[0:L, 0:nb, 64:64 + L], op=ALU.add),
                         reads=[p1, pr], writes=[H["TT"]])
            eGv = eGbc[:, 0:nb * 64].rearrange("p (b i) -> p b i", i=64)
            k.op("pool", lambda e: e.tensor_tensor(out=H["qdec"][:, g0:g0 + nb, 0:L], in0=kqfm[:, g0:g0 + nb, 64:64 + L],
                                                   in1=eGv[:, :, 0:L], op=ALU.mult),
                 reads=[kqfm, eGbc], writes=[H["qdec"]])
            if L == 64:
                k.op("pool", lambda e: e.tensor_copy(out=H["egl"][:, g0:g0 + nb, 0:1], in_=eGv[:, :, 63:64]),
                     reads=[eGbc], writes=[H["egl"]])
            else:
                k.op("pool", lambda e: e.tensor_copy(out=H["egl"][:, g0, 0:NS],
                                                     in_=eGbc[:, 0:NSAMP].rearrange("p (s w) -> p s w", w=TS)[:, :, TS - 1]),
                     reads=[eGbc], writes=[H["egl"]])

    def o_finish(a, hv, t0, nblk, L):
        H = heads[a]
        k.op("act", lambda e: e.activation(out=H["osq"][0:L, 0:nblk, :], in_=H["otok"][0:L, 0:nblk, :], func=AF.Square),
             reads=[H["otok"]], writes=[H["osq"]])
        k.op("dve", lambda e: e.reduce_sum(out=H["oss"][0:L, 0:nblk], in_=H["osq"][0:L, 0:nblk, :], axis=AX.X),
             reads=[H["osq"]], writes=[H["oss"]])
        k.op("dve", lambda e: e.tensor_scalar(out=H["oss"][0:L, 8:8 + nblk], in0=H["oss"][0:L, 0:nblk], scalar1=1.0 / 128,
                                              scalar2=RMS_EPS, op0=ALU.mult, op1=ALU.add), reads=[H["oss"]], writes=[H["oss"]])
        k.op("act", lambda e: e.activation(out=H["oss"][0:L, 0:nblk], in_=H["oss"][0:L, 8:8 + nblk], func=AF.Sqrt),
             reads=[H["oss"]], writes=[H["oss"]])
        k.op("dve", lambda e: e.reciprocal(out=H["oss"][0:L, 0:nblk], in_=H["oss"][0:L, 0:nblk]),
             reads=[H["oss"]], writes=[H["oss"]])
        k.op("dve", lambda e: e.tensor_tensor(out=H["on"][0:L, 0:nblk, :], in0=H["otok"][0:L, 0:nblk, :],
                                              in1=H["oss"][0:L, 0:nblk].unsqueeze(2).to_broadcast([L, nblk, 128]),
                                              op=ALU.mult), reads=[H["otok"], H["oss"]], writes=[H["on"]])
        pb = pbf.next()
        for bi in range(nblk):
            k.op("pe", lambda e: e.transpose(out=pb[:, bi * L:(bi + 1) * L], in_=H["on"][0:L, bi, :],
                                             identity=ident_b[0:L, 0:L]),
                 reads=[H["on"], ident_b], writes=[pb], inc=(bi == nblk - 1))
        n = nblk * L
        k.op("dve", lambda e: e.scalar_tensor_tensor(out=H["oT"][:, 0:n], in0=pb[:, 0:n], scalar=gnorm[:, 0:1],
                                                     in1=zsil[a][:, t0:t0 + n], op0=ALU.mult, op1=ALU.mult),
             reads=[pb, gnorm, zsil[a]], writes=[H["oT"]])
        k.dma("sp", S["oT"][hv * 128:(hv + 1) * 128, t0:t0 + n], H["oT"][:, 0:n], reads=[H["oT"]], writes=[dbuf["oT"]])

    kd_r = Ring([0, 1])

    def mk_kdec(H, hd, b, L):
        slot = kd_r.next()
        k.op("pool", lambda e: e.tensor_scalar(out=H["kdec"][0:L, slot, :], in0=k_tok[0:L, b, :],
                                               scalar1=hd[0:L, 3, b:b + 1], scalar2=None, op0=ALU.mult),
             reads=[k_tok, hd], writes=[H["kdec"]])
        return slot

    def scan(hv, hd):
        a = 0
        H = heads[0]
        k.op("pool", lambda e: e.memset(H["S"][:], 0.0), writes=[H["S"]])
        k.op("pool", lambda e: e.memset(H["Sb"][:], 0.0), writes=[H["Sb"]])
        for b in range(NBP):
            kslot = mk_kdec(H, hd, b, 64)
            pA = pf.next()
            k.op("pe", lambda e: e.matmul(pA[0:64, 0:128], lhsT=kqfm[:, b, 0:64], rhs=H["Sb"][:, :], start=True, stop=True),
                 reads=[kqfm, H["Sb"]], writes=[pA])
            k.op("dve", lambda e: e.scalar_tensor_tensor(out=H["r"][:, :], in0=pA[0:64, 0:128], scalar=hd[:, 2, b:b + 1],
                                                         in1=vb_tok[a][:, b, :], op0=ALU.mult, op1=ALU.add),
                 reads=[pA, hd, vb_tok[a]], writes=[H["r"]])
            pB_ = pf.next()
            k.op("pe", lambda e: e.matmul(pB_[0:64, 0:128], lhsT=H["TT"][:, b, :], rhs=H["r"][:, :], start=True, stop=True),
                 reads=[H["TT"], H["r"]], writes=[pB_])
            k.op("act", lambda e: e.copy(out=H["u"][:, :], in_=pB_[0:64, 0:128]), reads=[pB_], writes=[H["u"]])
            pO = pf.next()
            k.op("pe", lambda e: e.matmul(pO[0:64, 0:128], lhsT=H["qdec"][:, b, :], rhs=H["Sb"][:, :], start=True, stop=False),
                 reads=[H["qdec"], H["Sb"]], writes=[pO], inc=False)
            k.op("pe", lambda e: e.matmul(pO[0:64, 0:128], lhsT=H["AQT"][:, b, :], rhs=H["u"][:, :], start=False, stop=True),
                 reads=[H["AQT"], H["u"]], writes=[pO])
            pS = pf.next()
            k.op("pe", lambda e: e.matmul(pS[:, 0:128], lhsT=H["kdec"][:, kslot, :], rhs=H["u"][:, :], start=True, stop=True),
                 reads=[H["kdec"], H["u"]], writes=[pS])
            k.op("dve", lambda e: e.scalar_tensor_tensor(out=H["S"][:, :], in0=H["S"][:, :], scalar=H["egl"][:, b, 0:1],
                                                         in1=pS[:, 0:128], op0=ALU.mult, op1=ALU.add),
                 reads=[H["S"], H["egl"], pS], writes=[H["S"]])
            k.op("act", lambda e: e.copy(out=H["Sb"][:, :], in_=H["S"][:, :]), reads=[H["S"]], writes=[H["Sb"]])
            k.op("act", lambda e: e.copy(out=H["otok"][:, b % 8, :], in_=pO[0:64, 0:128]), reads=[pO], writes=[H["otok"]])
            if b % 8 == 7:
                o_finish(a, hv, (b - 7) * 64, 8, 64)
        k.dma("sp", O["o_gdn_p"][hv], H["S"][:, :], reads=[H["S"]], writes=[dbuf["o_gdn_p"]])
        L = NSAMP
        b = NBP
        kslot = mk_kdec(H, hd, b, L)
        k.op("pool", lambda e: e.memset(H["kzp"][:], 0.0), writes=[H["kzp"]])
        k.op("pool", lambda e: e.memset(H["qzp"][:], 0.0), writes=[H["qzp"]])
        for s in range(NS):
            k.dma("sp", H["Ss"][s][:, :], I["st_gdn"][s, hv], writes=[H["Ss"][s]])
            k.op("act", lambda e: e.copy(out=H["Ssb"][s][:, :], in_=H["Ss"][s][:, :]), reads=[H["Ss"][s]], writes=[H["Ssb"][s]])
            k.op("pool", lambda e: e.tensor_copy(out=H["kzp"][:, s, s * TS:(s + 1) * TS], in_=kqfm[:, b, s * TS:(s + 1) * TS]),
                 reads=[kqfm], writes=[H["kzp"]])
            k.op("pool", lambda e: e.tensor_copy(out=H["qzp"][:, s, s * TS:(s + 1) * TS], in_=H["qdec"][:, b, s * TS:(s + 1) * TS]),
                 reads=[H["qdec"]], writes=[H["qzp"]])
            k.op("dve", lambda e: e.tensor_scalar(out=H["kds"][0:L, s, :], in0=H["kdec"][0:L, kslot, :],
                                                  scalar1=rowmask_s[0:L, s:s + 1], scalar2=None, op0=ALU.mult),
                 reads=[H["kdec"], rowmask_s], writes=[H["kds"]])
        pA = pf.next()
        for s in range(NS):
            k.op("pe", lambda e: e.matmul(pA[0:L, 0:128], lhsT=H["kzp"][:, s, :], rhs=H["Ssb"][s][:, :],
                                          start=(s == 0), stop=(s == NS - 1)),
                 reads=[H["kzp"], H["Ssb"][s]], writes=[pA], inc=(s == NS - 1))
        k.op("dve", lambda e: e.scalar_tensor_tensor(out=H["r"][0:L, :], in0=pA[0:L, 0:128], scalar=hd[0:L, 2, b:b + 1],
                                                     in1=vb_tok[a][0:L, b, :], op0=ALU.mult, op1=ALU.add),
             reads=[pA, hd, vb_tok[a]], writes=[H["r"]])
        pB_ = pf.next()
        k.op("pe", lambda e: e.matmul(pB_[0:L, 0:128], lhsT=H["TT"][0:L, b, 0:L], rhs=H["r"][0:L, :], start=True, stop=True),
             reads=[H["TT"], H["r"]], writes=[pB_])
        k.op("act", lambda e: e.copy(out=H["u"][0:L, :], in_=pB_[0:L, 0:128]), reads=[pB_], writes=[H["u"]])
        pO = pf.next()
        for s in range(NS):
            k.op("pe", lambda e: e.matmul(pO[0:L, 0:128], lhsT=H["qzp"][:, s, :], rhs=H["Ssb"][s][:, :],
                                          start=(s == 0), stop=False),
                 reads=[H["qzp"], H["Ssb"][s]], writes=[pO], inc=False)
        k.op("pe", lambda e: e.matmul(pO[0:L, 0:128], lhsT=H["AQT"][0:L, b, 0:L], rhs=H["u"][0:L, :], start=False, stop=True),
             reads=[H["AQT"], H["u"]], writes=[pO])
        k.op("act", lambda e: e.copy(out=H["otok"][0:L, 0, :], in_=pO[0:L, 0:128]), reads=[pO], writes=[H["otok"]])
        for s in range(NS):
            pS = pf.next()
            k.op("pe", lambda e: e.matmul(pS[:, 0:128], lhsT=H["kds"][0:L, s, :], rhs=H["u"][0:L, :], start=True, stop=True),
                 reads=[H["kds"], H["u"]], writes=[pS])
            k.op("dve", lambda e: e.scalar_tensor_tensor(out=H["Ss"][s][:, :], in0=H["Ss"][s][:, :], scalar=H["egl"][:, b, s:s + 1],
                                                         in1=pS[:, 0:128], op0=ALU.mult, op1=ALU.add),
                 reads=[H["Ss"][s], H["egl"], pS], writes=[H["Ss"][s]])
            k.dma("sp", O["o_gdn_s"][s, hv], H["Ss"][s][:, :], reads=[H["Ss"][s]], writes=[dbuf["o_gdn_s"]])
        o_finish(a, hv, TP, 1, L)

    hd_next = load_hd(0)
    for j in range(NKH):
        proj_conv_silu(j * 128, cq)
        l2norm_to(cq, 1, 128 ** -0.5)
        proj_conv_silu(KD + j * 128, cq)
        l2norm_to(cq, 0, 1.0)

        def ksrc(bidx, t0, L):
            return kqfm[:, bidx, 0:L]
        ksrc.reads = [kqfm]

        def kevac(g0, gb, pb):
            nfull = sum(1 for (_, L) in gb if L == 64)
            if nfull:
                k.op("act", lambda e: e.copy(out=k_tok[:, g0:g0 + nfull, :],
                                             in_=pb[0:64, 0:nfull * 128].rearrange("p (b c) -> p b c", c=128)),
                     reads=[pb], writes=[k_tok])
            if nfull < len(gb):
                k.op("act", lambda e: e.copy(out=k_tok[0:NSAMP, g0 + nfull, :], in_=pb[0:NSAMP, nfull * 128:(nfull + 1) * 128]),
                     reads=[pb], writes=[k_tok])
        to_tok(ksrc, kevac)
        for a in range(2):
            hv = 2 * j + a
            hd = hd_next
            if hv + 1 < NVH:
                hd_next = load_hd(hv + 1)
            proj_conv_silu(2 * KD + hv * 128, vs)

            def vsrc(bidx, t0, L):
                return vs[:, t0:t0 + L]
            vsrc.reads = [vs]

            def vevac(g0, gb, pb, hd=hd):
                nfull = sum(1 for (_, L) in gb if L == 64)
                if nfull:
                    k.op("dve", lambda e: e.tensor_tensor(
                        out=vb_tok[0][:, g0:g0 + nfull, :],
                        in0=pb[0:64, 0:nfull * 128].rearrange("p (b c) -> p b c", c=128),
                        in1=hd[:, 0, g0:g0 + nfull].unsqueeze(2).to_broadcast([64, nfull, 128]), op=ALU.mult),
                        reads=[pb, hd], writes=[vb_tok[0]])
                if nfull < len(gb):
                    k.op("dve", lambda e: e.tensor_scalar(
                        out=vb_tok[0][0:NSAMP, g0 + nfull, :], in0=pb[0:NSAMP, nfull * 128:(nfull + 1) * 128],
                        scalar1=hd[0:NSAMP, 0, g0 + nfull:g0 + nfull + 1], scalar2=None, op0=ALU.mult),
                        reads=[pb, hd], writes=[vb_tok[0]])
            to_tok(vsrc, vevac)
            wt = load_w(CONVD + hv * 128)

            def zcons(ci, t0, n, ps):
                k.op("act", lambda e: e.activation(out=zsil[0][:, t0:t0 + n], in_=ps[:, 0:n], func=AF.Silu),
                     reads=[ps], writes=[zsil[0]])
            proj(wt, zcons)
            precompute(hv, hd)
            scan(hv, hd)

    nrow = (1 + NS) * 3
    cst_r = Ring([k.sb(f"g_cst{i}", [16, 512], F32) for i in range(2)])
    for g0 in range(0, NCT, 4):
        ps = pf.next()
        cst = cst_r.next()
        for q in range(4):
            ct = g0 + q
            k.op("pe", lambda e: e.transpose(out=ps[0:16, q * 128:(q + 1) * 128], in_=convout[:, ct, :],
                                             identity=ident_f[:, :]),
                 reads=[convout, ident_f], writes=[ps], inc=(q == 3))
        k.op("dve", lambda e: e.tensor_copy(out=cst[:, :], in_=ps[0:16, 0:512]), reads=[ps], writes=[cst])
        k.dma("sp", O["o_gdnc"][:, g0 * 128:(g0 + 4) * 128], cst[0:nrow, :], reads=[cst], writes=[dbuf["o_gdnc"]])
    if cfg.dbg:
        k.dma("sp", O["dbg_o"], S["oT"], reads=[dbuf["oT"]], writes=[dbuf["dbg_o"]])


_NC_CACHE = {}


def kernel(x_prompt, x_sample, state_gdn, state_gdn_conv, cache_nsa_kv, state_nsa_win, state_ffn_conv,
           page_table, norm_mix, norm_ffn, gdn_w_in, gdn_conv_w, gdn_A_log, gdn_dt_bias, gdn_norm, gdn_w_out,
           nsa_w_in, nsa_q_norm, nsa_k_norm, nsa_cmp_pe, nsa_cmp_w1, nsa_cmp_w2, rel_bias, nsa_w_out,
           ffn_w_up, ffn_conv_w, ffn_conv_b, ffn_w_down):
    cfg = Cfg()
    f = lambda a: np.ascontiguousarray(np.asarray(a))
    x_prompt, x_sample = f(x_prompt), f(x_sample)
    consts = host_consts(cfg)
    nconsts = {k_: v for k_, v in nsa_host_consts(cfg, cfg.PAST).items() if not k_.startswith("dims")}
    cache_r = f(cache_nsa_kv)[0].reshape(cfg.NPOOL * 128, 2048)
    n = 8
    NS = cfg.NS
    in_maps = []
    for c in range(n):
        b = c // 2
        sl = slice(NS * c, NS * (c + 1))
        m = dict(
            xin=np.ascontiguousarray(np.concatenate([x_prompt[b], x_sample[sl].reshape(-1, cfg.D)], 0)),
            norm_mix=f(norm_mix), norm_ffn=f(norm_ffn),
            gdn_w_in=f(gdn_w_in)[0], gdn_conv_w=f(gdn_conv_w)[0], gdn_A_log=f(gdn_A_log), gdn_dt_bias=f(gdn_dt_bias),
            gdn_norm=f(gdn_norm), gdn_w_out=f(gdn_w_out)[0],
            st_gdn=np.ascontiguousarray(f(state_gdn)[0, sl]),
            st_gdn_conv=np.ascontiguousarray(f(state_gdn_conv)[0, sl].reshape(NS * 3, cfg.CONVD)),
            ffn_w_up=f(ffn_w_up), ffn_conv_w=f(ffn_conv_w), ffn_conv_b=f(ffn_conv_b), ffn_w_down=f(ffn_w_down),
            st_ffn_conv=np.ascontiguousarray(f(state_ffn_conv)[:, sl].reshape(2, NS * 2, cfg.DFF)),
            nsa_w_in=f(nsa_w_in)[0], nsa_q_norm=f(nsa_q_norm), nsa_k_norm=f(nsa_k_norm),
            st_win=np.ascontiguousarray(f(state_nsa_win)[0, sl].reshape(NS, 512, 1024)),
            nsa_cmp_pe=f(nsa_cmp_pe)[0], nsa_cmp_w1=f(nsa_cmp_w1)[0], nsa_cmp_w2=f(nsa_cmp_w2)[0],
            rel_bias=f(rel_bias), nsa_w_out=f(nsa_w_out)[0],
            cache=cache_r, page_table=np.ascontiguousarray(f(page_table)[sl]).astype(np.int32),
        )
        m.update(nconsts)
        m.update(consts)
        in_maps.append(m)
    if "nc" not in _NC_CACHE:
        _NC_CACHE["nc"] = build(cfg)
    nc = _NC_CACHE["nc"]
    res = run_bass_kernel_spmd(nc, in_maps, core_ids=list(range(n)))
    R = res.results
    f32 = np.float32
    B, SEQ, DB, DS = 4, 2048, 32, 8
    p_gdn = np.stack([R[2 * b]["o_gdn_p"] for b in range(B)])[None].astype(f32)
    p_gdn_conv = np.stack([R[2 * b]["o_gdnc"][0:3] for b in range(B)])[None].astype(f32)
    s_gdn = np.concatenate([R[c]["o_gdn_s"] for c in range(n)], 0)[None].astype(f32)
    s_gdn_conv = np.concatenate([R[c]["o_gdnc"][3:3 + 3 * NS].reshape(NS, 3, -1) for c in range(n)], 0)[None].astype(f32)
    TP = cfg.TP
    y_prompt = np.stack([R[2 * b]["y"][:TP] for b in range(B)]).astype(f32)
    y_sample = np.concatenate([R[c]["y"][TP:].reshape(NS, DS, cfg.D) for c in range(n)], 0).astype(f32)
    p_nsa_kv = np.stack([R[2 * b]["o_kv"][:TP].reshape(SEQ, 4, 4, 128) for b in range(B)])[None].astype(f32)
    p_nsa_win = np.stack([R[2 * b]["o_pwin"].reshape(512, 2, 4, 128) for b in range(B)])[None].astype(f32)
    p_ffn_conv = np.stack([np.stack([R[2 * b]["o_ffnc"][li][0:2] for b in range(B)]) for li in range(2)]).astype(f32)
    s_nsa_kv = np.concatenate([R[c]["o_kv"][TP:].reshape(NS, DS, 4, 4, 128) for c in range(n)], 0)[None].astype(f32)
    s_nsa_win = np.concatenate([R[c]["o_swin"].reshape(NS, 512, 2, 4, 128) for c in range(n)], 0)[None].astype(f32)
    s_ffn_conv = np.stack([np.concatenate([R[c]["o_ffnc"][li][2:2 + 2 * NS].reshape(NS, 2, -1) for c in range(n)], 0)
                           for li in range(2)]).astype(f32)
    return (y_prompt, y_sample, p_gdn, p_gdn_conv, p_nsa_kv, p_nsa_win, p_ffn_conv,
            s_gdn, s_gdn_conv, s_nsa_kv, s_nsa_win, s_ffn_conv)


class Unpager:
    def __init__(self, k, cfg, I, S, dbuf):
        self.k, self.cfg, self.I, self.S, self.dbuf = k, cfg, I, S, dbuf
        self.NPG = cfg.PAST // 128
        self.total = cfg.NS * self.NPG
        self.done = 0

    def setup(self):
        k, cfg, I = self.k, self.cfg, self.I
        n = self.total
        self.idx = k.sb("u_idx", [128, n], I32)
        ptb = k.sb("u_ptb", [128, n], I32)
        k.dma("sp", ptb[:], I["page_table"].rearrange("s g -> (s g)").partition_broadcast(128), writes=[ptb])
        ptf = k.sb("u_ptf", [128, n], F32)
        iota_f = k.sb("u_iota", [128, 1], F32)
        k.dma("sp", iota_f[:], I["n_iota"], writes=[iota_f])
        k.op("dve", lambda e: e.tensor_copy(out=ptf[:], in_=ptb[:]), reads=[ptb], writes=[ptf])
        k.op("dve", lambda e: e.tensor_scalar(out=ptf[:], in0=ptf[:], scalar1=128.0, scalar2=iota_f[:, 0:1], op0=ALU.mult, op1=ALU.add),
             reads=[ptf, iota_f], writes=[ptf])
        k.op("dve", lambda e: e.tensor_copy(out=self.idx[:], in_=ptf[:]), reads=[ptf], writes=[self.idx])
        self.raw_r = Ring([k.sb(f"u_raw{i}", [128, 2048], F32) for i in range(2)])
        self.bf_r = Ring([k.sb(f"u_bf{i}", [128, 2048], BF16) for i in range(2)])

    def step(self, npages):
        k, I, S = self.k, self.I, self.S
        for _ in range(npages):
            if self.done >= self.total:
                return
            i = self.done
            self.done += 1
            raw = self.raw_r.next()
            k.op("pool", lambda e: e.indirect_dma_start(
                out=raw[:, :], out_offset=None, in_=I["cache"][:, :],
                in_offset=bass.IndirectOffsetOnAxis(ap=self.idx[:, i:i + 1], axis=0)),
                reads=[self.idx, self.dbuf["cache"]], writes=[raw], inc=False, dma=True)
            bf = self.bf_r.next()
            eng = k.ev_eng()
            if eng == "act":
                k.op("act", lambda e: e.copy(out=bf[:, :], in_=raw[:, :]), reads=[raw], writes=[bf])
            else:
                k.op("dve", lambda e: e.tensor_copy(out=bf[:, :], in_=raw[:, :]), reads=[raw], writes=[bf])
            k.dma("sp", S["past"][i * 128:(i + 1) * 128, :], bf[:, :], reads=[bf], writes=[self.dbuf["past"]])

    def finish(self):
        self.step(self.total)


def linear_tm(k, cfg, pf, dbuf, name, src_name, src_ap, w_name, w_ap, Kdim, resid_name, resid_ap, out_name, out_ap):
    D = cfg.D
    NKt = Kdim // 128
    with k.scope():
        wo_r = Ring([k.sb(f"{name}_w{i}", [128, NKt, 512], BF16) for i in range(2)])
        st_r = Ring([k.sb(f"{name}_s{i}", [128, NKt, 128], BF16) for i in range(3)])
        xr_r = Ring([k.sb(f"{name}_x{i}", [128, 512], F32) for i in range(3)])
        ho_r = Ring([k.sb(f"{name}_h{i}", [128, 512], F32) for i in range(3)])
        srcv = src_ap.rearrange("(kt p) t -> p kt t", p=128)
        def load_wo(c):
            wo = wo_r.next()
            k.dma("pool", wo[:], w_ap[:, c * 512:(c + 1) * 512].rearrange("(kt p) n -> p kt n", p=128),
                  reads=[dbuf[w_name]], writes=[wo])
            return wo
        wo_next = load_wo(0)
        for c in range(D // 512):
            wo = wo_next
            if c + 1 < D // 512:
                wo_next = load_wo(c + 1)
            for (r0, R) in cfg.ttiles():
                st = st_r.next()
                k.dma("sp", st[:, :, 0:R], srcv[:, :, r0:r0 + R], reads=[dbuf[src_name]], writes=[st])
                xr = xr_r.next()
                k.dma("sp", xr[0:R, :], resid_ap[r0:r0 + R, c * 512:(c + 1) * 512], reads=[dbuf[resid_name]], writes=[xr])
                ps = pf.next()
                for kt in range(NKt):
                    k.op("pe", lambda e: e.matmul(ps[0:R, :], lhsT=st[:, kt, 0:R], rhs=wo[:, kt, :],
                                                  start=(kt == 0), stop=(kt == NKt - 1)),
                         reads=[st, wo], writes=[ps], inc=(kt == NKt - 1))
                ho = ho_r.next()
                k.op("dve", lambda e: e.tensor_tensor(out=ho[0:R, :], in0=ps[0:R, :], in1=xr[0:R, :], op=ALU.add),
                     reads=[ps, xr], writes=[ho])
                k.dma("act", out_ap[r0:r0 + R, c * 512:(c + 1) * 512], ho[0:R, :], reads=[ho], writes=[dbuf[out_name]])


def ffn_layer(k, cfg, I, O, S, dbuf, pf, pbf, C, li, src_name, src_ap, out_name, out_ap, norm_fm, hook=None):
    D, KT, NT, TP, NS, TS, DFF = cfg.D, cfg.KT, cfg.NT, cfg.TP, cfg.NS, cfg.TS, cfg.DFF
    NSAMP = cfg.NSAMP
    ident_f = C["ident_f"]
    NM = DFF // 128
    W1 = 2
    PADW = W1 + TP + NS * (TS + W1)
    SOFF = W1 + TP
    SW = TS + W1
    chunks = cfg.chunks()
    with k.scope():
        xn = k.sb(f"f{li}_xn", [128, KT, NT], BF16)
        norm_fm(src_name, I["norm_ffn"][li:li + 1, :], xn)
        cw = k.sb(f"f{li}_cw", [128, NM, 3], F32)
        k.dma("sp", cw[:], I["ffn_conv_w"][li].rearrange("(t p) w -> p t w", p=128), writes=[cw])
        cb = k.sb(f"f{li}_cb", [128, NM], F32)
        k.dma("sp", cb[:], I["ffn_conv_b"][li].rearrange("(t p) -> p t", p=128), writes=[cb],
              allow_slow_non_contiguous=True)
        wt_r = Ring([k.sb(f"f{li}_wt{i}", [128, KT, 128], BF16) for i in range(4)])
        stc_r = Ring([k.sb(f"f{li}_stc{i}", [NS * W1, 128], F32) for i in range(2)])
        prep = k.sb(f"f{li}_prep", [128, PADW], F32)
        k.op("pool", lambda e: e.memset(prep[:], 0.0), writes=[prep])
        cq = k.sb(f"f{li}_cq", [128, NT], F32)
        act_r = Ring([k.sb(f"f{li}_act{i}", [128, NT], BF16) for i in range(2)])
        convout = k.sb(f"f{li}_convout", [128, NM, 16], F32)
        k.op("pool", lambda e: e.memset(convout[:], 0.0), writes=[convout])
        wup = I["ffn_w_up"][li]
        if hook is not None:
            hook.setup()

        def load_w(col0):
            wt = wt_r.next()
            k.dma("pool", wt[:], wup[:, col0:col0 + 128].rearrange("(kt p) c -> p kt c", p=128),
                  reads=[dbuf["ffn_w_up"]], writes=[wt])
            return wt

        def proj(wt, consumer):
            for ci, (t0, n) in enumerate(chunks):
                ps = pf.next()
                for kt in range(KT):
                    k.op("pe", lambda e: e.matmul(ps[:, 0:n], lhsT=wt[:, kt, :], rhs=xn[:, kt, t0:t0 + n],
                                                  start=(kt == 0), stop=(kt == KT - 1)),
                         reads=[wt, xn], writes=[ps], inc=(kt == KT - 1))
                consumer(ci, t0, n, ps)

        def sv(lo, hi):
            return prep[:, SOFF:SOFF + NS * SW].rearrange("p (s w) -> p s w", w=SW)[:, :, lo:hi]

        w_next = (load_w(0), load_w(DFF))
        for m in range(NM):
            wg, wv = w_next
            if m + 1 < NM:
                w_next = (load_w((m + 1) * 128), load_w(DFF + (m + 1) * 128))
            stc = stc_r.next()
            k.dma("sp", stc[:], I["st_ffn_conv"][li][:, m * 128:(m + 1) * 128], writes=[stc])
            pst = pf.next()
            k.op("pe", lambda e: e.transpose(out=pst[:, 0:NS * W1], in_=stc[0:NS * W1, 0:128],
                                             identity=ident_f[0:NS * W1, 0:NS * W1]),
                 reads=[stc, ident_f], writes=[pst])
            k.op("dve", lambda e: e.tensor_copy(out=sv(0, W1), in_=pst[:, 0:NS * W1].rearrange("p (s w) -> p s w", w=W1)),
                 reads=[pst], writes=[prep])

            def cons(ci, t0, n, ps):
                eng = k.ev_eng()
                if n == 512:
                    dv = prep[:, W1 + t0:W1 + t0 + n]
                    sv_ = ps[:, 0:n]
                else:
                    dv = sv(W1, SW)
                    sv_ = ps[:, 0:n].rearrange("p (s w) -> p s w", w=TS)
                if eng == "act":
                    k.op("act", lambda e: e.copy(out=dv, in_=sv_), reads=[ps], writes=[prep])
                else:
                    k.op("dve", lambda e: e.tensor_copy(out=dv, in_=sv_), reads=[ps], writes=[prep])
            proj(wg, cons)
            k.op("pool", lambda e: e.tensor_copy(out=convout[:, m, 0:W1], in_=prep[:, W1 + TP - W1:W1 + TP]),
                 reads=[prep], writes=[convout])
            k.op("pool", lambda e: e.tensor_copy(out=convout[:, m, W1:W1 + NS * W1].rearrange("p (s w) -> p s w", w=W1),
                                                 in_=sv(TS, SW)), reads=[prep], writes=[convout])
            for (ov, mkv) in ((cq[:, 0:TP], lambda i: prep[:, i:i + TP]),
                              (cq[:, TP:NT].rearrange("p (s w) -> p s w", w=TS), lambda i: sv(i, i + TS))):
                k.op("dve", lambda e: e.tensor_scalar(out=ov, in0=mkv(2), scalar1=cw[:, m, 2:3], scalar2=cb[:, m:m + 1],
                                                      op0=ALU.mult, op1=ALU.add), reads=[prep, cw, cb], writes=[cq])
                for i in range(2):
                    k.op("dve", lambda e: e.scalar_tensor_tensor(out=ov, in0=mkv(i), scalar=cw[:, m, i:i + 1], in1=ov,
                                                                 op0=ALU.mult, op1=ALU.add),
                         reads=[prep, cw, cq], writes=[cq])
            k.op("act", lambda e: e.activation(out=cq[:], in_=cq[:], func=AF.Silu), reads=[cq], writes=[cq])
            act = act_r.next()

            def vcons(ci, t0, n, ps):
                k.op("dve", lambda e: e.tensor_tensor(out=act[:, t0:t0 + n], in0=ps[:, 0:n], in1=cq[:, t0:t0 + n], op=ALU.mult),
                     reads=[ps, cq], writes=[act])
            proj(wv, vcons)
            k.dma("act", S["actT"][m * 128:(m + 1) * 128, :], act[:], reads=[act], writes=[dbuf["actT"]])
            if hook is not None:
                hook.step(-(-hook.total // NM))
        if hook is not None:
            hook.finish()
        nrow = (1 + NS) * W1
        cst_r = Ring([k.sb(f"f{li}_cst{i}", [16, 512], F32) for i in range(2)])
        for g0 in range(0, NM, 4):
            ps = pf.next()
            cst = cst_r.next()
            ng = min(4, NM - g0)
            for q in range(ng):
                k.op("pe", lambda e: e.transpose(out=ps[0:16, q * 128:(q + 1) * 128], in_=convout[:, g0 + q, :],
                                                 identity=ident_f[:, :]),
                     reads=[convout, ident_f], writes=[ps], inc=(q == ng - 1))
            k.op("dve", lambda e: e.tensor_copy(out=cst[:, 0:ng * 128], in_=ps[0:16, 0:ng * 128]), reads=[ps], writes=[cst])
            k.dma("sp", O["o_ffnc"][li][:, g0 * 128:(g0 + ng) * 128], cst[0:nrow, 0:ng * 128], reads=[cst],
                  writes=[dbuf["o_ffnc"]])
    linear_tm(k, cfg, pf, dbuf, f"fd{li}", "actT", S["actT"], "ffn_w_down", I["ffn_w_down"][li], DFF,
              src_name, src_ap, out_name, out_ap)


NEG = -30000.0
NSA_Z = 2063
NSA_GL = 5120
NSA_OFF = 384
NSA_X = 1920
NSA_XW = 1408


def t5_bucket_np(d):
    d = np.maximum(d, 0)
    f32 = np.float32
    scale = (32 - 16) / math.log(1024 / 16)
    large = 16 + (np.log(np.maximum(d, 1).astype(f32) / f32(16)) * f32(scale)).astype(np.int32)
    return np.where(d < 16, d, np.minimum(large, 31))


def nsa_host_consts(cfg, P):
    c = {}
    TP, TS, NS = cfg.TP, cfg.TS, cfg.NS
    f32 = np.float32
    dd = np.arange(NSA_GL) - NSA_Z
    oh = np.zeros((33, NSA_GL), f32)
    bk = t5_bucket_np(dd)
    for i in range(NSA_GL):
        if dd[i] >= 0:
            oh[bk[i], i] = 1.0
        else:
            oh[32, i] = 1.0
    c["n_oh"] = oh
    jm = np.zeros((128, 128), f32)
    jm[np.arange(128), 127 - np.arange(128)] = 1.0
    c["n_J"] = jm
    kk = np.arange(128)[:, None]
    xx = np.arange(NSA_XW)[None, :]
    c["n_wm"] = np.where((xx - kk - NSA_OFF) < 512, 0.0, NEG).astype(f32)

    def sel_consts(n_all, qpos, name):
        n_sb = -(-n_all // 64)
        ns = n_all // 16
        ncb = ns - 1
        c_end = np.arange(ncb) * 16 + 31
        c_start = c_end - 31
        sb_start = np.arange(n_sb) * 64
        ovl = np.maximum(np.minimum(c_end[:, None], sb_start[None, :] + 63) - np.maximum(c_start[:, None], sb_start[None, :]) + 1, 0).astype(f32) / 32
        ntile = -(-ncb // 128)
        ovl_t = np.zeros((ntile * 128, n_sb), f32)
        ovl_t[:ncb] = ovl
        c[name + "_ovl"] = ovl_t.reshape(ntile, 128, n_sb)
        cur = qpos // 64
        blk = np.arange(n_sb)
        ok = sb_start[None, :] <= qpos[:, None]
        forced = (blk[None, :] == 0) | (blk[None, :] == cur[:, None]) | (blk[None, :] == cur[:, None] - 1)
        c[name + "_add"] = np.where(ok, np.where(forced, 1000.0, 0.0), -1e30).astype(f32)
        c[name + "_valid"] = ok.astype(f32)
        nk = -(-n_all // 128) * 128
        ex = np.zeros((n_sb, nk), f32)
        keys = np.arange(n_all)
        ex[keys // 64, keys] = 1.0
        c[name + "_exp"] = ex.astype(ml_dtypes.bfloat16)
        return n_sb, ncb
    c["dims_p"] = sel_consts(TP, np.arange(TP), "np")
    c["dims_s"] = sel_consts(P + TS, P + np.arange(TS), "ns")
    c["n_iota"] = np.arange(128, dtype=np.float32).reshape(128, 1)
    return c


def nsa_proj(k, cfg, I, O, S, dbuf, pf, pbf, C, norm_fm, src_name):
    D, KT, NT, TP, NS, TS = cfg.D, cfg.KT, cfg.NT, cfg.TP, cfg.NS, cfg.TS
    ones_b = C["ones_b"]
    chunks = cfg.chunks()
    WIN = 512
    with k.scope():
        xn = k.sb("n_xn", [128, KT, NT], BF16)
        norm_fm(src_name, I["norm_mix"][1:2, :], xn)
        epsb = k.sb("n_epsb", [128, 1], F32)
        k.op("pool", lambda e: e.memset(epsb[:], RMS_EPS), writes=[epsb])
        qg = k.sb("n_qg", [128, 1], F32)
        k.dma("sp", qg[:], I["nsa_q_norm"].rearrange("o d -> d o"), writes=[qg])
        k.op("dve", lambda e: e.tensor_scalar(out=qg[:], in0=qg[:], scalar1=128 ** -0.5, scalar2=None, op0=ALU.mult),
             reads=[qg], writes=[qg])
        import os
        parts = os.environ.get("NSA_PARTS", "q,kv,win,g").split(",")
        with k.scope():
            wt_r = Ring([k.sb(f"n_wt{i}", [128, KT, 128], BF16) for i in range(2)])
            qraw = k.sb("n_qraw", [128, NT], F32)
            sqb = k.sb("n_sqb", [128, 512], BF16)
            rinv = k.sb("n_rinv", [128, 512], F32)
            qn_r = Ring([k.sb(f"n_qn{i}", [128, NT], BF16) for i in range(2)])
            for h in range(16 if "q" in parts else 0):
                wt = wt_r.next()
                k.dma("pool", wt[:], I["nsa_w_in"][:, h * 128:(h + 1) * 128].rearrange("(kt p) c -> p kt c", p=128),
                      reads=[dbuf["nsa_w_in"]], writes=[wt])
                qn = qn_r.next()
                for (t0, n) in chunks:
                    ps = pf.next()
                    for kt in range(KT):
                        k.op("pe", lambda e: e.matmul(ps[:, 0:n], lhsT=wt[:, kt, :], rhs=xn[:, kt, t0:t0 + n],
                                                      start=(kt == 0), stop=(kt == KT - 1)),
                             reads=[wt, xn], writes=[ps], inc=(kt == KT - 1))
                    qlvl = int(os.environ.get("NSA_QLVL", "9"))
                    k.op("dve", lambda e: e.tensor_copy(out=qraw[:, t0:t0 + n], in_=ps[:, 0:n]), reads=[ps], writes=[qraw])
                    if qlvl < 1:
                        continue
                    k.op("act", lambda e: e.activation(out=sqb[:, 0:n], in_=ps[:, 0:n], func=AF.Square), reads=[ps], writes=[sqb])
                    ps2 = pf.next()
                    k.op("pe", lambda e: e.matmul(ps2[:, 0:n], lhsT=ones_b[:, :], rhs=sqb[:, 0:n], start=True, stop=True),
                         reads=[ones_b, sqb], writes=[ps2])
                    if qlvl < 2:
                        continue
                    k.op("act", lambda e: e.activation(out=rinv[:, 0:n], in_=ps2[:, 0:n], func=AF.Sqrt, scale=1.0 / 128,
                                                       bias=epsb[:, 0:1]), reads=[ps2, epsb], writes=[rinv])
                    k.op("dve", lambda e: e.reciprocal(out=rinv[:, 0:n], in_=rinv[:, 0:n]), reads=[rinv], writes=[rinv])
                    if qlvl < 3:
                        continue
                    k.op("dve", lambda e: e.scalar_tensor_tensor(out=qn[:, t0:t0 + n], in0=qraw[:, t0:t0 + n], scalar=qg[:, 0:1],
                                                                 in1=rinv[:, 0:n], op0=ALU.mult, op1=ALU.mult),
                         reads=[qraw, qg, rinv], writes=[qn])
                if qlvl >= 4:
                    k.dma("sp", S["qT"][h * 128:(h + 1) * 128, :], qn[:], reads=[qn], writes=[dbuf["qT"]])
        with k.scope():
            wkv_r = Ring([k.sb(f"n_wkv{i}", [128, KT, 512], BF16) for i in range(2)])
            gain_k = k.sb("n_gaink", [128, 3, 128], F32)
            k.dma("sp", gain_k[:].rearrange("p a d -> p (a d)"),
                  I["nsa_k_norm"].rearrange("o a d -> o (a d)").partition_broadcast(128), writes=[gain_k])
            sqt = k.sb("n_sqt", [128, 512], F32)
            ss = k.sb("n_ss", [128, 8], F32)
            rows_r = Ring([k.sb(f"n_rows{i}", [128, 512], F32) for i in range(3)])
            for c in range(6 if "kv" in parts else 0):
                wc = wkv_r.next()
                k.dma("pool", wc[:], I["nsa_w_in"][:, 2048 + c * 512:2048 + (c + 1) * 512].rearrange("(kt p) c -> p kt c", p=128),
                      reads=[dbuf["nsa_w_in"]], writes=[wc])
                for (r0, R) in cfg.ttiles():
                    ps = pf.next()
                    for kt in range(KT):
                        k.op("pe", lambda e: e.matmul(ps[0:R, :], lhsT=xn[:, kt, r0:r0 + R], rhs=wc[:, kt, :],
                                                      start=(kt == 0), stop=(kt == KT - 1)),
                             reads=[xn, wc], writes=[ps], inc=(kt == KT - 1))
                    rows = rows_r.next()
                    if c in (2, 4):
                        gi = 1 if c == 2 else 2
                        k.op("act", lambda e: e.activation(out=sqt[0:R, :], in_=ps[0:R, :], func=AF.Square), reads=[ps], writes=[sqt])
                        k.op("dve", lambda e: e.reduce_sum(out=ss[0:R, 0:4], in_=sqt[0:R, :].rearrange("p (g d) -> p g d", d=128),
                                                           axis=AX.X), reads=[sqt], writes=[ss])
                        k.op("act", lambda e: e.activation(out=ss[0:R, 4:8], in_=ss[0:R, 0:4], func=AF.Sqrt, scale=1.0 / 128,
                                                           bias=epsb[0:R, 0:1]), reads=[ss, epsb], writes=[ss])
                        k.op("dve", lambda e: e.reciprocal(out=ss[0:R, 0:4], in_=ss[0:R, 4:8]), reads=[ss], writes=[ss])
                        k.op("dve", lambda e: e.tensor_tensor(out=sqt[0:R, :].rearrange("p (g d) -> p g d", d=128),
                                                              in0=ps[0:R, :].rearrange("p (g d) -> p g d", d=128),
                                                              in1=ss[0:R, 0:4].unsqueeze(2).to_broadcast([R, 4, 128]), op=ALU.mult),
                             reads=[ps, ss], writes=[sqt])
                        k.op("dve", lambda e: e.tensor_tensor(out=rows[0:R, :].rearrange("p (g d) -> p g d", d=128),
                                                              in0=sqt[0:R, :].rearrange("p (g d) -> p g d", d=128),
                                                              in1=gain_k[0:R, gi, :].unsqueeze(1).to_broadcast([R, 4, 128]), op=ALU.mult),
                             reads=[sqt, gain_k], writes=[rows])
                    else:
                        k.op("act", lambda e: e.copy(out=rows[0:R, :], in_=ps[0:R, :]), reads=[ps], writes=[rows])
                    if c < 4:
                        k.dma("sp", O["o_kv"][r0:r0 + R, c * 512:(c + 1) * 512], rows[0:R, :], reads=[rows], writes=[dbuf["o_kv"]])
                    else:
                        cc = (c - 4) * 512
                        k.dma("sp", S["win_new"][r0:r0 + R, cc:cc + 512], rows[0:R, :], reads=[rows], writes=[dbuf["win_new"]])
                        if r0 < TP and r0 >= TP - WIN:
                            k.dma("sp", O["o_pwin"][r0 - (TP - WIN):r0 - (TP - WIN) + R, cc:cc + 512], rows[0:R, :],
                                  reads=[rows], writes=[dbuf["o_pwin"]])
                        if r0 == TP:
                            for s in range(NS):
                                k.dma("sp", O["o_swin"][s, WIN - TS:WIN, cc:cc + 512], rows[s * TS:(s + 1) * TS, :],
                                      reads=[rows], writes=[dbuf["o_swin"]])
            swb_r = Ring([k.sb(f"n_swb{i}", [126, 4096], F32) for i in range(2)])
            for s in range(NS if "win" in parts else 0):
                swb = swb_r.next()
                k.dma("sp", swb[:, :], I["st_win"][s, TS:WIN, :].rearrange("(p a) c -> p (a c)", a=4),
                      reads=[dbuf["st_win"]], writes=[swb])
                k.dma("sp", O["o_swin"][s, 0:WIN - TS, :].rearrange("(p a) c -> p (a c)", a=4), swb[:, :],
                      reads=[swb], writes=[dbuf["o_swin"]])
            wg = k.sb("n_wg", [128, KT, 48], BF16)
            k.dma("pool", wg[:], I["nsa_w_in"][:, 5120:5168].rearrange("(kt p) c -> p kt c", p=128),
                  reads=[dbuf["nsa_w_in"]], writes=[wg])
            gt_r = Ring([k.sb(f"n_gt{i}", [128, 48], F32) for i in range(2)])
            for (r0, R) in (cfg.ttiles() if "g" in parts else []):
                ps = pf.next()
                for kt in range(KT):
                    k.op("pe", lambda e: e.matmul(ps[0:R, 0:48], lhsT=xn[:, kt, r0:r0 + R], rhs=wg[:, kt, :],
                                                  start=(kt == 0), stop=(kt == KT - 1)),
                         reads=[xn, wg], writes=[ps], inc=(kt == KT - 1))
                gt = gt_r.next()
                k.op("act", lambda e: e.activation(out=gt[0:R, :], in_=ps[0:R, 0:48], func=AF.Sigmoid), reads=[ps], writes=[gt])
                k.dma("sp", S["gates"][r0:r0 + R, :], gt[0:R, :], reads=[gt], writes=[dbuf["gates"]])


def nsa_attn(k, cfg, I, O, S, dbuf, pf, pbf, C):
    D, KT, NT, TP, NS, TS = cfg.D, cfg.KT, cfg.NT, cfg.TP, cfg.NS, cfg.TS
    P = cfg.PAST
    NQT = TP // 128
    NCH = TP // 512
    n_sb_p = TP // 64
    ncb_p = TP // 16 - 1
    Z, GL, OFF, X, XW = NSA_Z, NSA_GL, NSA_OFF, NSA_X, NSA_XW
    ident_f, ident_b, ones_b, ones_f = C["ident_f"], C["ident_b"], C["ones_b"], C["ones_f"]
    ringN = Ring(pf.bufs[0:4])
    ring2 = Ring(pf.bufs[0:2])
    accO = pf.bufs[2:6]
    GELU_C = 1.5957691216057308

    def evac(dst, src, rd, wr):
        eng = k.ev_eng()
        if eng == "act":
            k.op("act", lambda e: e.copy(out=dst, in_=src), reads=rd, writes=wr)
        else:
            k.op("dve", lambda e: e.tensor_copy(out=dst, in_=src), reads=rd, writes=wr)

    with k.scope():
        Jm = k.sb("a_J", [128, 128], F32)
        k.dma("sp", Jm[:], I["n_J"], writes=[Jm])
        relx = k.sb("a_relx", [64, 16], F32)
        k.op("pool", lambda e: e.memset(relx[32:64, :], NEG), writes=[relx])
        k.dma("sp", relx[0:32, :], I["rel_bias"], writes=[relx])
        epsb = k.sb("a_epsb", [128, 1], F32)
        k.op("pool", lambda e: e.memset(epsb[:], RMS_EPS), writes=[epsb])
        with k.scope():
            oh = k.sb("a_oh", [33, GL], F32)
            k.dma("sp", oh[:], I["n_oh"], writes=[oh])
            gsb_r = Ring([k.sb(f"a_gsb{i}", [16, 512], F32) for i in range(2)])
            for c0 in range(0, GL, 512):
                ps = ringN.next()
                k.op("pe", lambda e: e.matmul(ps[0:16, :], lhsT=relx[0:33, :], rhs=oh[:, c0:c0 + 512], start=True, stop=True),
                     reads=[relx, oh], writes=[ps])
                gsb = gsb_r.next()
                k.op("dve", lambda e: e.tensor_copy(out=gsb[:, :], in_=ps[0:16, :]), reads=[ps], writes=[gsb])
                k.dma("sp", S["gvec"][:, c0:c0 + 512], gsb[:, :], reads=[gsb], writes=[dbuf["gvec"]])
        wm = k.sb("a_wm", [128, XW], F32)
        k.dma("sp", wm[:], I["n_wm"], writes=[wm])
        ovl_p = k.sb("a_ovlp", [128, n_sb_p], BF16)
        k.dma("pool", ovl_p[:], I["np_ovl"][0], writes=[ovl_p])
        add_p = k.sb("a_addp", [128, NQT, n_sb_p], F32)
        k.dma("sp", add_p[:], I["np_add"].rearrange("(t p) m -> p t m", p=128), writes=[add_p])
        val_p = k.sb("a_valp", [128, NQT, n_sb_p], F32)
        k.dma("sp", val_p[:], I["np_valid"].rearrange("(t p) m -> p t m", p=128), writes=[val_p])
        exp_p = k.sb("a_expp", [n_sb_p, TP], BF16)
        k.dma("sp", exp_p[:], I["np_exp"], writes=[exp_p])
        gates = k.sb("a_gates", [128, NQT, 48], F32)
        k.dma("sp", gates[:], S["gates"][0:TP, :].rearrange("(t p) c -> p t c", p=128), reads=[dbuf["gates"]], writes=[gates])
        w1 = k.sb("a_w1", [128, 2, 32, 256], BF16)
        w2 = k.sb("a_w2", [128, 2, 2, 128], BF16)
        for wch in range(2):
            k.dma("pool", w1[:, wch], I["nsa_cmp_w1"][wch].rearrange("(s d) e -> d s e", d=128), writes=[w1])
            k.dma("pool", w2[:, wch], I["nsa_cmp_w2"][wch].rearrange("(h e) d -> e h d", e=128), writes=[w2])
        peT = k.sb("a_peT", [128, 2, 32], BF16)
        k.dma("pool", peT[:], I["nsa_cmp_pe"].rearrange("w s d -> d w s"), writes=[peT], allow_slow_non_contiguous=True)
        peb = k.sb("a_peb", [128, 2, 2], F32)
        for wch in range(2):
            for half in range(2):
                ps = ringN.next()
                for s in range(32):
                    k.op("pe", lambda e: e.matmul(ps[:, 0:1], lhsT=w1[:, wch, s, half * 128:(half + 1) * 128],
                                                  rhs=peT[:, wch, s:s + 1], start=(s == 0), stop=(s == 31)),
                         reads=[w1, peT], writes=[ps], inc=(s == 31))
                k.op("dve", lambda e: e.tensor_copy(out=peb[:, wch, half:half + 1], in_=ps[:, 0:1]), reads=[ps], writes=[peb])
        gk0 = k.sb("a_gk0", [128, 1], F32)
        k.dma("sp", gk0[:], I["nsa_k_norm"][0, 0:1, :].rearrange("o d -> d o"), writes=[gk0])

        t1_r = Ring([k.sb(f"a_t1{i}", [128, 2048], F32) for i in range(1)])

        def build_tab(dst_fn, h, base, W, pstride):
            t1 = t1_r.next()
            gv = S["gvec"]
            src = bass.AP(tensor=gv.tensor, offset=h * GL + base, ap=[[pstride, 128], [1, W]])
            k.dma("sp", t1[:, 0:W], src, reads=[dbuf["gvec"]], writes=[t1])
            for c0 in range(0, W, 512):
                n = min(512, W - c0)
                ps = ringN.next()
                k.op("pe", lambda e: e.matmul(ps[:, 0:n], lhsT=Jm[:, :], rhs=t1[:, c0:c0 + n], start=True, stop=True),
                     reads=[Jm, t1], writes=[ps])
                dst, wr = dst_fn(c0, n)
                evac(dst, ps[:, 0:n], [ps], wr)

        def compress(kx, wch, ncb, hg, tmpx, tmp2):
            kxv = kx[:, 0:(ncb + 1) * 16].rearrange("p (n r) -> p n r", r=16)
            for half in range(2):
                ps = ringN.next()
                for s in range(32):
                    rv = kxv[:, 0:ncb, s] if s < 16 else kxv[:, 1:ncb + 1, s - 16]
                    k.op("pe", lambda e: e.matmul(ps[:, 0:ncb], lhsT=w1[:, wch, s, half * 128:(half + 1) * 128], rhs=rv,
                                                  start=(s == 0), stop=(s == 31)),
                         reads=[w1, kx], writes=[ps], inc=(s == 31))
                k.op("dve", lambda e: e.tensor_scalar(out=tmpx[:, 0:ncb], in0=ps[:, 0:ncb], scalar1=peb[:, wch, half:half + 1],
                                                      scalar2=None, op0=ALU.add), reads=[ps, peb], writes=[tmpx])
                k.op("dve", lambda e: e.tensor_tensor(out=tmp2[:, 0:ncb], in0=tmpx[:, 0:ncb], in1=tmpx[:, 0:ncb], op=ALU.mult),
                     reads=[tmpx], writes=[tmp2])
                k.op("dve", lambda e: e.tensor_scalar(out=tmp2[:, 0:ncb], in0=tmp2[:, 0:ncb], scalar1=0.044715, scalar2=1.0,
                                                      op0=ALU.mult, op1=ALU.add), reads=[tmp2], writes=[tmp2])
                k.op("dve", lambda e: e.tensor_tensor(out=tmp2[:, 0:ncb], in0=tmp2[:, 0:ncb], in1=tmpx[:, 0:ncb], op=ALU.mult),
                     reads=[tmp2, tmpx], writes=[tmp2])
                k.op("act", lambda e: e.activation(out=tmp2[:, 0:ncb], in_=tmp2[:, 0:ncb], func=AF.Sigmoid, scale=GELU_C),
                     reads=[tmp2], writes=[tmp2])
                k.op("dve", lambda e: e.tensor_tensor(out=hg[:, half, 0:ncb], in0=tmp2[:, 0:ncb], in1=tmpx[:, 0:ncb], op=ALU.mult),
                     reads=[tmp2, tmpx], writes=[hg])

        def kc_finish(hg, ncb, kcT, tmpx, tmp2, sqb):
            ps = ringN.next()
            for half in range(2):
                k.op("pe", lambda e: e.matmul(ps[:, 0:ncb], lhsT=w2[:, 0, half, :], rhs=hg[:, half, 0:ncb],
                                              start=(half == 0), stop=(half == 1)), reads=[w2, hg], writes=[ps], inc=(half == 1))
            k.op("dve", lambda e: e.tensor_copy(out=tmpx[:, 0:ncb], in_=ps[:, 0:ncb]), reads=[ps], writes=[tmpx])
            k.op("act", lambda e: e.activation(out=sqb[:, 0:ncb], in_=tmpx[:, 0:ncb], func=AF.Square), reads=[tmpx], writes=[sqb])
            ps2 = ringN.next()
            k.op("pe", lambda e: e.matmul(ps2[:, 0:ncb], lhsT=ones_b[:, :], rhs=sqb[:, 0:ncb], start=True, stop=True),
                 reads=[ones_b, sqb], writes=[ps2])
            k.op("act", lambda e: e.activation(out=tmp2[:, 0:ncb], in_=ps2[:, 0:ncb], func=AF.Sqrt, scale=1.0 / 128, bias=epsb[:, 0:1]),
                 reads=[ps2, epsb], writes=[tmp2])
            k.op("dve", lambda e: e.reciprocal(out=tmp2[:, 0:ncb], in_=tmp2[:, 0:ncb]), reads=[tmp2], writes=[tmp2])
            k.op("dve", lambda e: e.scalar_tensor_tensor(out=kcT[:, 0:ncb], in0=tmpx[:, 0:ncb], scalar=gk0[:, 0:1], in1=tmp2[:, 0:ncb],
                                                         op0=ALU.mult, op1=ALU.mult), reads=[tmpx, gk0, tmp2], writes=[kcT])

        def transpose_tiles(tk, ntile, dstT):
            for g0 in range(0, ntile, 8):
                ng = min(8, ntile - g0)
                pb = pbf.next()
                for i in range(ng):
                    k.op("pe", lambda e: e.transpose(out=pb[:, i * 128:(i + 1) * 128], in_=tk[:, g0 + i, :], identity=ident_b[:, :]),
                         reads=[tk, ident_b], writes=[pb], inc=(i == ng - 1))
                evac(dstT[:, g0 * 128:(g0 + ng) * 128], pb[:, 0:ng * 128], [pb], [dstT])

        def fill_sample_bias(j, T, Tw, Bs, Bw):
            NPG_ = P // 128
            kcut = sum(1 for kt in range(NPG_) if P - 128 * kt >= 1024)
            if kcut:
                k.op("pool", lambda e: e.tensor_copy(out=Bs[:, 0:kcut, j, :],
                                                     in_=T[:, 1024 + OFF:1024 + OFF + 8].unsqueeze(1).to_broadcast([128, kcut, 8])),
                     reads=[T], writes=[Bs])
            for kt in range(kcut, NPG_):
                x0 = P - 128 * kt + OFF
                k.op("pool", lambda e: e.tensor_copy(out=Bs[:, kt, j, :], in_=T[:, x0:x0 + 8]), reads=[T], writes=[Bs])
            k.op("pool", lambda e: e.tensor_copy(out=Bs[:, NPG_, j, :], in_=T[:, OFF:OFF + 8]), reads=[T], writes=[Bs])
            for wt in range(4):
                x0 = 512 - 128 * wt + OFF
                k.op("pool", lambda e: e.tensor_copy(out=Bw[:, wt, j, :], in_=Tw[:, x0:x0 + 8]), reads=[Tw], writes=[Bw])
            k.op("pool", lambda e: e.tensor_copy(out=Bw[:, 4, j, :], in_=Tw[:, OFF:OFF + 8]), reads=[Tw], writes=[Bw])

        def prompt_group(g, qh, Bs, Bw):
            if True:
                kslcT = k.sb("a_kslcT", [128, TP], BF16)
                kwinT = k.sb("a_kwinT", [128, TP], BF16)
                vslc = k.sb("a_vslc", [128, NQT, 136], BF16)
                vwin = k.sb("a_vwin", [128, NQT, 136], BF16)
                k.op("pool", lambda e: e.memset(vslc[:, :, 128:129], 1.0), writes=[vslc])
                k.op("pool", lambda e: e.memset(vwin[:, :, 128:129], 1.0), writes=[vwin])
                kcT = k.sb("a_kcT", [128, 128], BF16)
                vc = k.sb("a_vc", [128, 128], BF16)
                prep_scope = k.scope()
                prep_scope.__enter__()
                tk_r = Ring([k.sb(f"a_tk{i}", [128, NQT, 128], BF16) for i in range(2)])
                kx = k.sb("a_kx", [128, TP], BF16)
                vx = k.sb("a_vx", [128, TP], BF16)
                for (nm, ap, dstT, dstV) in (
                        ("o_kv", O["o_kv"][0:TP, 0 * 512 + g * 128:0 * 512 + (g + 1) * 128], kx, None),
                        ("o_kv", O["o_kv"][0:TP, 1 * 512 + g * 128:1 * 512 + (g + 1) * 128], vx, None),
                        ("o_kv", O["o_kv"][0:TP, 2 * 512 + g * 128:2 * 512 + (g + 1) * 128], kslcT, None),
                        ("o_kv", O["o_kv"][0:TP, 3 * 512 + g * 128:3 * 512 + (g + 1) * 128], None, vslc),
                        ("win_new", S["win_new"][0:TP, g * 128:(g + 1) * 128], kwinT, None),
                        ("win_new", S["win_new"][0:TP, 512 + g * 128:512 + (g + 1) * 128], None, vwin)):
                    if dstV is not None:
                        k.dma("pool", dstV[:, :, 0:128], ap.rearrange("(t p) d -> p t d", p=128), reads=[dbuf[nm]], writes=[dstV])
                    else:
                        tk = tk_r.next()
                        k.dma("pool", tk[:], ap.rearrange("(t p) d -> p t d", p=128), reads=[dbuf[nm]], writes=[tk])
                        transpose_tiles(tk, NQT, dstT)
                hg = k.sb("a_hg", [128, 2, 512], BF16)
                tmpx = k.sb("a_tmpx", [128, 512], F32)
                tmp2 = k.sb("a_tmp2", [128, 512], F32)
                sqb = k.sb("a_sqb", [128, 512], BF16)
                compress(kx, 0, ncb_p, hg, tmpx, tmp2)
                kc_finish(hg, ncb_p, kcT, tmpx, tmp2, sqb)
                compress(vx, 1, ncb_p, hg, tmpx, tmp2)
                ps = ringN.next()
                for half in range(2):
                    k.op("pe", lambda e: e.matmul(ps[0:ncb_p, 0:128], lhsT=hg[:, half, 0:ncb_p], rhs=w2[:, 1, half, :],
                                                  start=(half == 0), stop=(half == 1)), reads=[hg, w2], writes=[ps], inc=(half == 1))
                k.op("act", lambda e: e.copy(out=vc[0:ncb_p, :], in_=ps[0:ncb_p, 0:128]), reads=[ps], writes=[vc])
                prep_scope.__exit__(None, None, None)
                o_acc = k.sb("a_oacc", [128, NQT, 4, 128], F32)
                import os
                BR = os.environ.get("NSA_BR", "c,s,w")
                k.op("pool", lambda e: e.memset(o_acc[:], 0.0), writes=[o_acc])
                selT = k.sb("a_selT", [n_sb_p, TP], BF16)
                ssb_r = Ring([k.sb(f"a_ssb{i}", [128, 512], F32) for i in range(2)])
                ebf_r = Ring([k.sb(f"a_ebf{i}", [128, 512], BF16) for i in range(2)])
                eb2_r = Ring([k.sb(f"a_eb2{i}", [128, 512], BF16) for i in range(2)])
                sm = k.sb("a_sm", [128, 8], F32)
                with k.scope():
                    Bc = k.sb("a_Bc", [128, TP], F32)
                    Pall = k.sb("a_Pall", [128, 4, TP], BF16)
                    rec = k.sb("a_rec", [128, 512], F32)
                    sc = k.sb("a_sc", [128, n_sb_p], F32)
                    wk = k.sb("a_wk", [128, n_sb_p], F32)
                    m8 = k.sb("a_m8", [128, 16], F32)
                    sel = k.sb("a_sel", [128, n_sb_p], F32)
                    for j in range(4):
                        build_tab(lambda c0, n: (Bc[:, c0:c0 + n], [Bc]), 4 * g + j, 0, TP, 16)
                        for c in range(NCH):
                            q0 = c * 512
                            ps = ringN.next()
                            k.op("pe", lambda e: e.matmul(ps[0:ncb_p, :], lhsT=kcT[:, 0:ncb_p], rhs=qh[:, j, q0:q0 + 512], start=True, stop=True),
                                 reads=[kcT, qh], writes=[ps])
                            ssb = ssb_r.next()
                            k.op("dve", lambda e: e.tensor_tensor(out=ssb[0:ncb_p, :], in0=ps[0:ncb_p, :], in1=Bc[0:ncb_p, q0:q0 + 512], op=ALU.add),
                                 reads=[ps, Bc], writes=[ssb])
                            ebf = ebf_r.next()
                            k.op("act", lambda e: e.activation(out=ebf[0:ncb_p, :], in_=ssb[0:ncb_p, :], func=AF.Exp), reads=[ssb], writes=[ebf])
                            psD = ringN.next()
                            k.op("pe", lambda e: e.matmul(psD[0:ncb_p, :], lhsT=ones_b[0:ncb_p, 0:ncb_p], rhs=ebf[0:ncb_p, :], start=True, stop=True),
                                 reads=[ones_b, ebf], writes=[psD])
                            k.op("dve", lambda e: e.tensor_scalar(out=rec[0:ncb_p, :], in0=psD[0:ncb_p, :], scalar1=1e-30, scalar2=None, op0=ALU.max),
                                 reads=[psD], writes=[rec])
                            k.op("dve", lambda e: e.reciprocal(out=rec[0:ncb_p, :], in_=rec[0:ncb_p, :]), reads=[rec], writes=[rec])
                            k.op("dve", lambda e: e.tensor_tensor(out=Pall[0:ncb_p, j, q0:q0 + 512], in0=ebf[0:ncb_p, :], in1=rec[0:ncb_p, :], op=ALU.mult),
                                 reads=[ebf, rec], writes=[Pall])
                            for qs in range(4):
                                qt = c * 4 + qs
                                pso = ringN.next()
                                k.op("pe", lambda e: e.matmul(pso[:, 0:128], lhsT=Pall[0:ncb_p, j, q0 + qs * 128:q0 + (qs + 1) * 128], rhs=vc[0:ncb_p, :],
                                                              start=True, stop=True), reads=[Pall, vc], writes=[pso])
                                if "c" in BR:
                                    k.op("act", lambda e: e.activation(out=o_acc[:, qt, j, :], in_=pso[:, 0:128], func=AF.Copy,
                                                                       scale=gates[:, qt, 4 * g + j:4 * g + j + 1]),
                                         reads=[pso, gates], writes=[o_acc])
                    for c in range(NCH):
                        q0 = c * 512
                        for qs in range(4):
                            qt = c * 4 + qs
                            psI = ringN.next()
                            for j in range(4):
                                k.op("pe", lambda e: e.matmul(psI[:, 0:n_sb_p], lhsT=Pall[0:ncb_p, j, q0 + qs * 128:q0 + (qs + 1) * 128], rhs=ovl_p[0:ncb_p, :],
                                                              start=(j == 0), stop=(j == 3)), reads=[Pall, ovl_p], writes=[psI], inc=(j == 3))
                            k.op("dve", lambda e: e.tensor_tensor(out=sc[:, :], in0=psI[:, 0:n_sb_p], in1=add_p[:, qt, :], op=ALU.add),
                                 reads=[psI, add_p], writes=[sc])
                            k.op("dve", lambda e: e.max(out=m8[:, 0:8], in_=sc[:, :]), reads=[sc], writes=[m8])
                            if n_sb_p > 8:
                                k.op("dve", lambda e: e.match_replace(out=wk[:, :], in_to_replace=m8[:, 0:8], in_values=sc[:, :], imm_value=-1e30),
                                     reads=[m8, sc], writes=[wk])
                                k.op("dve", lambda e: e.max(out=m8[:, 8:16], in_=wk[:, :]), reads=[wk], writes=[m8])
                                thr = m8[:, 15:16]
                            else:
                                thr = m8[:, 7:8]
                            k.op("dve", lambda e: e.tensor_scalar(out=sel[:, :], in0=sc[:, :], scalar1=thr, scalar2=None, op0=ALU.is_ge),
                                 reads=[sc, m8], writes=[sel])
                            k.op("dve", lambda e: e.tensor_tensor(out=sel[:, :], in0=sel[:, :], in1=val_p[:, qt, :], op=ALU.mult),
                                 reads=[sel, val_p], writes=[sel])
                            pst = ringN.next()
                            k.op("pe", lambda e: e.transpose(out=pst[0:n_sb_p, 0:128], in_=sel[:, :], identity=ident_f[:, :]),
                                 reads=[sel, ident_f], writes=[pst])
                            k.op("act", lambda e: e.copy(out=selT[:, qt * 128:(qt + 1) * 128], in_=pst[0:n_sb_p, 0:128]), reads=[pst], writes=[selT])
                with k.scope():
                    oT_r = Ring([k.sb(f"a_oTt{i}", [128, 512], BF16) for i in range(2)])
                    T = k.sb("a_T", [128, X], F32)
                    Tw = k.sb("a_Tw", [128, XW], F32)
                    for j in range(4):
                        build_tab(lambda c0, n: (T[:, c0:c0 + n], [T]), 4 * g + j, Z - OFF - 127, X, 1)
                        k.op("pool", lambda e: e.tensor_tensor(out=Tw[:], in0=T[:, 0:XW], in1=wm[:], op=ALU.add), reads=[T, wm], writes=[Tw])
                        fill_sample_bias(j, T, Tw, Bs, Bw)
                        for c in range(NCH):
                            q0 = c * 512
                            for (kT, vv, tab, gi, kts, use_mask) in (
                                    (kslcT, vslc, T, 1, list(range(0, 4 * c + 4)), True),
                                    (kwinT, vwin, Tw, 2, list(range(max(0, 4 * c - 4), 4 * c + 4)), False)):
                                if (use_mask and "s" not in BR) or (not use_mask and "w" not in BR):
                                    continue
                                for kt in kts:
                                    ps = ring2.next()
                                    k.op("pe", lambda e: e.matmul(ps[:, :], lhsT=kT[:, kt * 128:(kt + 1) * 128], rhs=qh[:, j, q0:q0 + 512], start=True, stop=True),
                                         reads=[kT, qh], writes=[ps])
                                    x0 = min(q0 - 128 * kt, 1024) + OFF
                                    ssb = ssb_r.next()
                                    k.op("dve", lambda e: e.tensor_tensor(out=ssb[:, :], in0=ps[:, :], in1=tab[:, x0:x0 + 512], op=ALU.add),
                                         reads=[ps, tab], writes=[ssb])
                                    ebf = ebf_r.next()
                                    k.op("act", lambda e: e.activation(out=ebf[:, :], in_=ssb[:, :], func=AF.Exp), reads=[ssb], writes=[ebf])
                                    if use_mask:
                                        psM = ring2.next()
                                        k.op("pe", lambda e: e.matmul(psM[:, :], lhsT=exp_p[:, kt * 128:(kt + 1) * 128], rhs=selT[:, q0:q0 + 512], start=True, stop=True),
                                             reads=[exp_p, selT], writes=[psM])
                                        eb2 = eb2_r.next()
                                        k.op("dve", lambda e: e.tensor_tensor(out=eb2[:, :], in0=ebf[:, :], in1=psM[:, :], op=ALU.mult),
                                             reads=[ebf, psM], writes=[eb2])
                                    else:
                                        eb2 = ebf
                                    for qs in range(4):
                                        acc = accO[qs]
                                        k.op("pe", lambda e: e.matmul(acc[:, 0:129], lhsT=eb2[:, qs * 128:(qs + 1) * 128],
                                                                      rhs=vv[:, kt, 0:129], start=(kt == kts[0]), stop=(kt == kts[-1])),
                                             reads=[eb2, vv], writes=[acc], inc=(kt == kts[-1]))
                                for qs in range(4):
                                    qt = c * 4 + qs
                                    acc = accO[qs]
                                    a0 = 0
                                    k.op("dve", lambda e: e.reciprocal(out=sm[:, 0:1], in_=acc[:, a0 + 128:a0 + 129]), reads=[acc], writes=[sm])
                                    k.op("dve", lambda e: e.tensor_tensor(out=sm[:, 1:2], in0=sm[:, 0:1],
                                                                          in1=gates[:, qt, gi * 16 + 4 * g + j:gi * 16 + 4 * g + j + 1], op=ALU.mult),
                                         reads=[sm, gates], writes=[sm])
                                    k.op("dve", lambda e: e.scalar_tensor_tensor(out=o_acc[:, qt, j, :], in0=acc[:, a0:a0 + 128], scalar=sm[:, 1:2],
                                                                                 in1=o_acc[:, qt, j, :], op0=ALU.mult, op1=ALU.add),
                                         reads=[acc, sm, o_acc], writes=[o_acc])
                            pst = ring2.next()
                            for qs in range(4):
                                k.op("pe", lambda e: e.transpose(out=pst[:, qs * 128:(qs + 1) * 128], in_=o_acc[:, c * 4 + qs, j, :], identity=ident_f[:, :]),
                                     reads=[o_acc, ident_f], writes=[pst], inc=(qs == 3))
                            oTt = oT_r.next()
                            evac(oTt[:, :], pst[:, :], [pst], [oTt])
                            k.dma("sp", S["oT2"][(4 * g + j) * 128:(4 * g + j + 1) * 128, q0:q0 + 512], oTt[:, :], reads=[oTt], writes=[dbuf["oT2"]])

        def sample_group(g, s, qh, Bs, Bw, Bcs):
            n_all_s = P + TS
            n_sb_s = -(-n_all_s // 64)
            ncb_s = P // 16 - 1
            NT4 = -(-ncb_s // 128)
            NPG = P // 128
            nA = min(128, n_sb_s)
            NTL = NPG + 1
            t0s = TP + s * TS
            pgc_r = Ring([k.sb(f"s_pgc{i}", [128, 16, 128], BF16) for i in range(2)])
            kslcT = k.sb("s_kslcT", [128, NTL * 128], BF16)
            vslc = k.sb("s_vslc", [128, NTL, 136], BF16)
            kwT = k.sb("s_kwT", [128, 5 * 128], BF16)
            vw = k.sb("s_vw", [128, 5, 136], BF16)
            k.op("pool", lambda e: e.memset(vslc[:, :, 128:129], 1.0), writes=[vslc])
            k.op("pool", lambda e: e.memset(vw[:, :, 128:129], 1.0), writes=[vw])
            kcT = k.sb("s_kcT", [128, 512], BF16)
            vc = k.sb("s_vc", [128, NT4, 128], BF16)
            k.op("pool", lambda e: e.memset(vc[:], 0.0), writes=[vc])
            qsm = k.sb("s_qsm", [128, 4, 8], BF16)
            k.op("pool", lambda e: e.tensor_copy(out=qsm[:], in_=qh[:, :, t0s:t0s + TS]), reads=[qh], writes=[qsm])
            qsv = qsm[:].rearrange("p j t -> p (j t)")

            def gather(c, dstT=None, dstV=None):
                col0 = c * 512 + g * 128
                src = S["past"][s * P:(s + 1) * P, col0:col0 + 128].rearrange("(a p) d -> p a d", p=128)
                if dstV is not None:
                    k.dma("sp", dstV[:, 0:NPG, 0:128], src, reads=[dbuf["past"]], writes=[dstV])
                    return
                for p0 in range(0, NPG, 16):
                    npg = min(16, NPG - p0)
                    pgc = pgc_r.next()
                    k.dma("sp", pgc[:, 0:npg, :], src[:, p0:p0 + npg, :], reads=[dbuf["past"]], writes=[pgc])
                    for q0 in range(0, npg, 8):
                        nq = min(8, npg - q0)
                        pb = pbf.next()
                        for i in range(nq):
                            k.op("pe", lambda e: e.transpose(out=pb[:, i * 128:(i + 1) * 128], in_=pgc[:, q0 + i, :], identity=ident_b[:, :]),
                                 reads=[pgc, ident_b], writes=[pb], inc=(i == nq - 1))
                        evac(dstT[:, (p0 + q0) * 128:(p0 + q0 + nq) * 128], pb[:, 0:nq * 128], [pb], [dstT])

            def small_T(src_ap, nm, rows, dstT_ap, dstT_buf):
                tkb = pgc_r.next()
                k.dma("pool", tkb[0:rows, 0, :], src_ap, reads=[dbuf[nm]], writes=[tkb])
                pb = pbf.next()
                k.op("pe", lambda e: e.transpose(out=pb[:, 0:rows], in_=tkb[0:rows, 0, :], identity=ident_b[0:rows, 0:rows]),
                     reads=[tkb, ident_b], writes=[pb])
                evac(dstT_ap, pb[:, 0:rows], [pb], [dstT_buf])

            cscope = k.scope()
            cscope.__enter__()
            kx = k.sb("s_kx", [128, P], BF16)
            hg = k.sb("s_hg", [128, 2, 512], BF16)
            tmpx = k.sb("s_tmpx", [128, 512], F32)
            tmp2 = k.sb("s_tmp2", [128, 512], F32)
            sqb = k.sb("s_sqb", [128, 512], BF16)
            gather(0, dstT=kx)
            compress(kx, 0, ncb_s, hg, tmpx, tmp2)
            kc_finish(hg, ncb_s, kcT, tmpx, tmp2, sqb)
            gather(1, dstT=kx)
            compress(kx, 1, ncb_s, hg, tmpx, tmp2)
            for nt in range(NT4):
                rows = min(128, ncb_s - nt * 128)
                ps = ringN.next()
                for half in range(2):
                    k.op("pe", lambda e: e.matmul(ps[0:rows, 0:128], lhsT=hg[:, half, nt * 128:nt * 128 + rows], rhs=w2[:, 1, half, :],
                                                  start=(half == 0), stop=(half == 1)), reads=[hg, w2], writes=[ps], inc=(half == 1))
                k.op("act", lambda e: e.copy(out=vc[0:rows, nt, :], in_=ps[0:rows, 0:128]), reads=[ps], writes=[vc])
            cscope.__exit__(None, None, None)
            exp_sA = k.sb("s_expsA", [128, (NPG + 1) * 128], BF16)
            k.dma("sp", exp_sA[0:nA, :], I["ns_exp"][0:nA, :], writes=[exp_sA])
            gather(2, dstT=kslcT)
            small_T(O["o_kv"][t0s:t0s + TS, 2 * 512 + g * 128:2 * 512 + (g + 1) * 128], "o_kv", TS, kslcT[:, NPG * 128:NPG * 128 + TS], kslcT)
            gather(3, dstV=vslc)
            k.dma("pool", vslc[0:TS, NPG, 0:128], O["o_kv"][t0s:t0s + TS, 3 * 512 + g * 128:3 * 512 + (g + 1) * 128],
                  reads=[dbuf["o_kv"]], writes=[vslc])
            tkw = k.sb("s_tkw", [128, 4, 128], BF16)
            k.dma("pool", tkw[:], I["st_win"][s, :, g * 128:(g + 1) * 128].rearrange("(t p) d -> p t d", p=128),
                  reads=[dbuf["st_win"]], writes=[tkw])
            transpose_tiles(tkw, 4, kwT)
            small_T(S["win_new"][t0s:t0s + TS, g * 128:(g + 1) * 128], "win_new", TS, kwT[:, 512:512 + TS], kwT)
            k.dma("pool", vw[:, 0:4, 0:128], I["st_win"][s, :, 512 + g * 128:512 + (g + 1) * 128].rearrange("(t p) d -> p t d", p=128),
                  reads=[dbuf["st_win"]], writes=[vw])
            k.dma("pool", vw[0:TS, 4, 0:128], S["win_new"][t0s:t0s + TS, 512 + g * 128:512 + (g + 1) * 128],
                  reads=[dbuf["win_new"]], writes=[vw])

            o_s = k.sb("s_os", [TS, 4, 128], F32)
            sm = k.sb("s_sm", [TS, 8], F32)
            ecs = k.sb("s_ecs", [128, NT4, 32], F32)
            ecb = k.sb("s_ecb", [128, NT4, 32], BF16)
            pcb = k.sb("s_pcb", [128, NT4, 32], BF16)
            recs = k.sb("s_recs", [128, 32], F32)
            ps = ringN.next()
            for nt in range(NT4):
                rows = min(128, ncb_s - nt * 128)
                k.op("pe", lambda e: e.matmul(ps[0:rows, nt * 32:(nt + 1) * 32], lhsT=kcT[:, nt * 128:nt * 128 + rows], rhs=qsv,
                                              start=True, stop=True), reads=[kcT, qsm], writes=[ps], inc=(nt == NT4 - 1))
            k.op("pool", lambda e: e.memset(ecs[:], NEG), writes=[ecs])
            for nt in range(NT4):
                rows = min(128, ncb_s - nt * 128)
                k.op("dve", lambda e: e.tensor_tensor(out=ecs[0:rows, nt, :], in0=ps[0:rows, nt * 32:(nt + 1) * 32],
                                                      in1=Bcs[0:rows, nt, :, :].rearrange("p j t -> p (j t)"), op=ALU.add),
                     reads=[ps, Bcs], writes=[ecs])
            k.op("act", lambda e: e.activation(out=ecb[:], in_=ecs[:], func=AF.Exp), reads=[ecs], writes=[ecb])
            psD = ringN.next()
            for nt in range(NT4):
                k.op("pe", lambda e: e.matmul(psD[:, 0:32], lhsT=ones_b[:, :], rhs=ecb[:, nt, :], start=(nt == 0), stop=(nt == NT4 - 1)),
                     reads=[ones_b, ecb], writes=[psD], inc=(nt == NT4 - 1))
            k.op("dve", lambda e: e.tensor_scalar(out=recs[:], in0=psD[:, 0:32], scalar1=1e-30, scalar2=None, op0=ALU.max), reads=[psD], writes=[recs])
            k.op("dve", lambda e: e.reciprocal(out=recs[:], in_=recs[:]), reads=[recs], writes=[recs])
            k.op("dve", lambda e: e.tensor_tensor(out=pcb[:], in0=ecb[:], in1=recs[:].unsqueeze(1).to_broadcast([128, NT4, 32]), op=ALU.mult),
                 reads=[ecb, recs], writes=[pcb])
            for j in range(4):
                pso = accO[j]
                for nt in range(NT4):
                    k.op("pe", lambda e: e.matmul(pso[0:TS, 0:128], lhsT=pcb[:, nt, j * 8:(j + 1) * 8], rhs=vc[:, nt, :],
                                                  start=(nt == 0), stop=(nt == NT4 - 1)), reads=[pcb, vc], writes=[pso], inc=(nt == NT4 - 1))
                k.op("act", lambda e: e.activation(out=o_s[:, j, :], in_=pso[0:TS, 0:128], func=AF.Copy,
                                                   scale=gates_s[:, s, 4 * g + j:4 * g + j + 1]), reads=[pso, gates_s], writes=[o_s])
            psI = ringN.next()
            first = True
            for j in range(4):
                for nt in range(NT4):
                    k.op("pe", lambda e: e.matmul(psI[0:TS, 0:n_sb_s], lhsT=pcb[:, nt, j * 8:(j + 1) * 8], rhs=ovl_s[:, nt, :],
                                                  start=first, stop=(j == 3 and nt == NT4 - 1)), reads=[pcb, ovl_s], writes=[psI],
                         inc=(j == 3 and nt == NT4 - 1))
                    first = False
            sc = k.sb("s_sc", [TS, n_sb_s], F32)
            wk = k.sb("s_wk", [TS, n_sb_s], F32)
            m8 = k.sb("s_m8", [TS, 16], F32)
            sel = k.sb("s_sel", [TS, n_sb_s], F32)
            k.op("dve", lambda e: e.tensor_tensor(out=sc[:, :], in0=psI[0:TS, 0:n_sb_s], in1=add_s[:, :], op=ALU.add), reads=[psI, add_s], writes=[sc])
            k.op("dve", lambda e: e.max(out=m8[:, 0:8], in_=sc[:, :]), reads=[sc], writes=[m8])
            k.op("dve", lambda e: e.match_replace(out=wk[:, :], in_to_replace=m8[:, 0:8], in_values=sc[:, :], imm_value=-1e30), reads=[m8, sc], writes=[wk])
            k.op("dve", lambda e: e.max(out=m8[:, 8:16], in_=wk[:, :]), reads=[wk], writes=[m8])
            k.op("dve", lambda e: e.tensor_scalar(out=sel[:, :], in0=sc[:, :], scalar1=m8[:, 15:16], scalar2=None, op0=ALU.is_ge), reads=[sc, m8], writes=[sel])
            k.op("dve", lambda e: e.tensor_tensor(out=sel[:, :], in0=sel[:, :], in1=val_s[:, :], op=ALU.mult), reads=[sel, val_s], writes=[sel])
            selTa = k.sb("s_selTa", [128, TS], BF16)
            selTb = k.sb("s_selTb", [1, TS], BF16)
            pst = ringN.next()
            k.op("pe", lambda e: e.transpose(out=pst[0:nA, 0:TS], in_=sel[:, 0:nA], identity=ident_f[0:TS, 0:TS]), reads=[sel, ident_f], writes=[pst])
            k.op("act", lambda e: e.copy(out=selTa[0:nA, :], in_=pst[0:nA, 0:TS]), reads=[pst], writes=[selTa])
            if n_sb_s > 128:
                pst2 = ringN.next()
                k.op("pe", lambda e: e.transpose(out=pst2[0:1, 0:TS], in_=sel[:, 128:129], identity=ident_f[0:TS, 0:TS]), reads=[sel, ident_f], writes=[pst2])
                k.op("act", lambda e: e.copy(out=selTb[0:1, :], in_=pst2[0:1, 0:TS]), reads=[pst2], writes=[selTb])

            def branch(kT, vv, ntile, last_rows, Btab, gi, use_mask):
                ess = k.sb("s_ess", [128, 16, 32], F32)
                esb = k.sb("s_esb", [128, ntile, 32], BF16)
                k.op("pool", lambda e: e.memset(esb[:], 0.0), writes=[esb])
                for t0 in range(0, ntile, 16):
                    nt_ = min(16, ntile - t0)
                    ps = ringN.next()
                    for i in range(nt_):
                        kt = t0 + i
                        rows = last_rows if kt == ntile - 1 else 128
                        k.op("pe", lambda e: e.matmul(ps[0:rows, i * 32:(i + 1) * 32], lhsT=kT[:, kt * 128:kt * 128 + rows], rhs=qsv,
                                                      start=True, stop=True), reads=[kT, qsm], writes=[ps], inc=(i == nt_ - 1))
                    full = nt_ if (t0 + nt_ < ntile) else nt_ - 1
                    if full:
                        k.op("dve", lambda e: e.tensor_tensor(out=ess[:, 0:full, :], in0=ps[:, 0:full * 32].rearrange("p (a b) -> p a b", b=32),
                                                              in1=Btab[:, t0:t0 + full, :, :].rearrange("p a j t -> p a (j t)"), op=ALU.add),
                             reads=[ps, Btab], writes=[ess])
                        k.op("act", lambda e: e.activation(out=esb[:, t0:t0 + full, :], in_=ess[:, 0:full, :], func=AF.Exp), reads=[ess], writes=[esb])
                    if full < nt_:
                        i = nt_ - 1
                        k.op("dve", lambda e: e.tensor_tensor(out=ess[0:last_rows, i, :], in0=ps[0:last_rows, i * 32:(i + 1) * 32],
                                                              in1=Btab[0:last_rows, ntile - 1, :, :].rearrange("p j t -> p (j t)"), op=ALU.add),
                             reads=[ps, Btab], writes=[ess])
                        k.op("act", lambda e: e.activation(out=esb[0:last_rows, ntile - 1, :], in_=ess[0:last_rows, i, :], func=AF.Exp),
                             reads=[ess], writes=[esb])
                if use_mask:
                    for t0 in range(0, ntile, 64):
                        nt_ = min(64, ntile - t0)
                        psM = ringN.next()
                        for i in range(nt_):
                            kt = t0 + i
                            rows = last_rows if kt == ntile - 1 else 128
                            two = n_sb_s > 128 and kt == ntile - 1
                            k.op("pe", lambda e: e.matmul(psM[0:rows, i * 8:(i + 1) * 8], lhsT=exp_sA[0:nA, kt * 128:kt * 128 + rows], rhs=selTa[0:nA, :],
                                                          start=True, stop=not two), reads=[exp_sA, selTa], writes=[psM], inc=(i == nt_ - 1 and not two))
                            if two:
                                k.op("pe", lambda e: e.matmul(psM[0:rows, i * 8:(i + 1) * 8], lhsT=exp_sB[0:1, 0:rows], rhs=selTb[0:1, :],
                                                              start=False, stop=True), reads=[exp_sB, selTb], writes=[psM], inc=(i == nt_ - 1))
                        full = nt_ if (t0 + nt_ < ntile) else nt_ - 1
                        if full:
                            k.op("dve", lambda e: e.tensor_tensor(
                                out=esb[:, t0:t0 + full, :].rearrange("p a (j t) -> p a j t", t=8),
                                in0=esb[:, t0:t0 + full, :].rearrange("p a (j t) -> p a j t", t=8),
                                in1=psM[:, 0:full * 8].rearrange("p (a t) -> p a t", t=8).unsqueeze(2).to_broadcast([128, full, 4, 8]), op=ALU.mult),
                                reads=[esb, psM], writes=[esb])
                        if full < nt_:
                            i = nt_ - 1
                            k.op("dve", lambda e: e.tensor_tensor(
                                out=esb[0:last_rows, ntile - 1, :].rearrange("p (j t) -> p j t", t=8),
                                in0=esb[0:last_rows, ntile - 1, :].rearrange("p (j t) -> p j t", t=8),
                                in1=psM[0:last_rows, i * 8:(i + 1) * 8].unsqueeze(1).to_broadcast([last_rows, 4, 8]), op=ALU.mult),
                                reads=[esb, psM], writes=[esb])
                for j in range(4):
                    acc = accO[j]
                    for kt in range(ntile):
                        rows = last_rows if kt == ntile - 1 else 128
                        k.op("pe", lambda e: e.matmul(acc[0:TS, 0:129], lhsT=esb[0:rows, kt, j * 8:(j + 1) * 8], rhs=vv[0:rows, kt, 0:129],
                                                      start=(kt == 0), stop=(kt == ntile - 1)), reads=[esb, vv], writes=[acc], inc=(kt == ntile - 1))
                    k.op("dve", lambda e: e.reciprocal(out=sm[:, 0:1], in_=acc[0:TS, 128:129]), reads=[acc], writes=[sm])
                    k.op("dve", lambda e: e.tensor_tensor(out=sm[:, 1:2], in0=sm[:, 0:1], in1=gates_s[:, s, gi * 16 + 4 * g + j:gi * 16 + 4 * g + j + 1], op=ALU.mult),
                         reads=[sm, gates_s], writes=[sm])
                    k.op("dve", lambda e: e.scalar_tensor_tensor(out=o_s[:, j, :], in0=acc[0:TS, 0:128], scalar=sm[:, 1:2], in1=o_s[:, j, :],
                                                                 op0=ALU.mult, op1=ALU.add), reads=[acc, sm, o_s], writes=[o_s])

            with k.scope():
                branch(kslcT, vslc, NTL, TS, Bs, 1, True)
            with k.scope():
                branch(kwT, vw, 5, TS, Bw, 2, False)
            pst = ringN.next()
            for j in range(4):
                k.op("pe", lambda e: e.transpose(out=pst[:, j * TS:(j + 1) * TS], in_=o_s[:, j, :], identity=ident_f[0:TS, 0:TS]),
                     reads=[o_s, ident_f], writes=[pst], inc=(j == 3))
            oTs = k.sb("s_oTs", [128, 4, TS], BF16)
            evac(oTs[:].rearrange("p j t -> p (j t)"), pst[:, 0:4 * TS], [pst], [oTs])
            k.dma("sp", S["oT2"][g * 512:(g + 1) * 512, t0s:t0s + TS].rearrange("(j p) t -> p j t", p=128), oTs[:],
                  reads=[oTs], writes=[dbuf["oT2"]])


        n_all_s = P + TS
        n_sb_s = -(-n_all_s // 64)
        ncb_s = P // 16 - 1
        NT4 = -(-ncb_s // 128)
        NPG = P // 128
        nA = min(128, n_sb_s)
        KTOT = (NPG + 1) * 128
        ovl_s = k.sb("a_ovls", [128, NT4, n_sb_s], BF16)
        k.dma("pool", ovl_s[:], I["ns_ovl"].rearrange("t p m -> p t m"), writes=[ovl_s])
        add_s = k.sb("a_adds", [TS, n_sb_s], F32)
        k.dma("sp", add_s[:], I["ns_add"], writes=[add_s])
        val_s = k.sb("a_vals", [TS, n_sb_s], F32)
        k.dma("sp", val_s[:], I["ns_valid"], writes=[val_s])
        exp_sB = k.sb("a_expsB", [1, 128], BF16)
        if n_sb_s > 128:
            k.dma("sp", exp_sB[:, :], I["ns_exp"][128:129, KTOT - 128:KTOT], writes=[exp_sB])
        iota_f = k.sb("a_iota", [128, 1], F32)
        k.dma("sp", iota_f[:], I["n_iota"], writes=[iota_f])
        gates_s = k.sb("a_gatess", [TS, NS, 48], F32)
        k.dma("sp", gates_s[:], S["gates"][TP:NT, :].rearrange("(s t) c -> t s c", t=TS), reads=[dbuf["gates"]], writes=[gates_s])
        for g in range(4):
          with k.scope():
            qh = k.sb("a_qh", [128, 4, NT], BF16)
            k.dma("sp", qh[:], S["qT"][g * 512:(g + 1) * 512, :].rearrange("(j p) t -> p j t", p=128),
                  reads=[dbuf["qT"]], writes=[qh])
            Bs = k.sb("a_Bs", [128, NPG + 1, 4, 8], F32)
            Bw = k.sb("a_Bw", [128, 5, 4, 8], F32)
            Bcs = k.sb("a_Bcs", [128, NT4, 4, 8], F32)
            for j in range(4):
                for nt in range(NT4):
                    off_d = P - 31 - 2048 * nt - 2032
                    base = Z + min(off_d, 790)
                    build_tab(lambda c0, n, j=j, nt=nt: (Bcs[:, nt, j, :], [Bcs]), 4 * g + j, base, 8, 16)
            with k.scope():
                prompt_group(g, qh, Bs, Bw)
            for s in range(NS):
                with k.scope():
                    sample_group(g, s, qh, Bs, Bw, Bcs)
```

```python
import math
import numpy as np
import ml_dtypes
import concourse.bass as bass
import concourse.mybir as mybir
from concourse.bass_utils import run_bass_kernel_spmd
from contextlib import ExitStack

F32 = mybir.dt.float32
BF16 = mybir.dt.bfloat16
I32 = mybir.dt.int32
AF = mybir.ActivationFunctionType
ALU = mybir.AluOpType
AX = mybir.AxisListType

ENGS = ("pe", "dve", "act", "pool", "sp")
DMA_RING = 8
RMS_EPS = 1e-6


class Buf:
    __slots__ = ("t", "w", "r", "name", "multi", "ws", "psum")

    def __init__(self, t, name="", multi=False):
        self.t = t
        self.w = None
        self.r = {}
        self.name = name
        self.multi = multi
        self.ws = {}
        self.psum = False

    def __getitem__(self, idx):
        return self.t[idx]


class K:
    def __init__(self, nc, es, same_engine_sync=True):
        self.nc = nc
        self.es = es
        self.eng = {"pe": nc.tensor, "dve": nc.vector, "act": nc.scalar,
                    "pool": nc.gpsimd, "sp": nc.sync}
        self.sems = {}
        self.cnt = {}
        for e in ("pe", "dve", "act", "pool"):
            self.sems[e] = es.enter_context(nc.semaphore("s_" + e))
            self.cnt[e] = 0
        self.ring = {}
        self.ring_n = {}
        for q in ("sp", "pool", "act"):
            self.ring[q] = [es.enter_context(nc.semaphore(f"d_{q}{i}")) for i in range(DMA_RING)]
            self.ring_n[q] = 0
        self.seen = {e: {} for e in ENGS}
        self.same_engine_sync = same_engine_sync
        self.n_inst = {e: 0 for e in ENGS}
        self.n_wait = {e: 0 for e in ENGS}
        self.rr = 0

    def sb(self, name, shape, dt=F32):
        self.uid = getattr(self, "uid", 0) + 1
        t = self.es.enter_context(self.nc.sbuf_tensor(f"sb{self.uid}_" + name, list(shape), dt))
        return Buf(t, name)

    def ps(self, name, shape, dt=F32):
        t = self.es.enter_context(self.nc.psum_tensor("ps_" + name, list(shape), dt))
        b = Buf(t, name)
        b.psum = True
        return b

    def _semobj(self, key):
        if isinstance(key, str):
            return self.sems[key]
        q, i = key
        return self.ring[q][i]

    def _wait(self, e, ev):
        if ev is None:
            return
        key, val = ev
        if key == e and (e == "pe" or not self.same_engine_sync):
            return
        if self.seen[e].get(key, 0) >= val:
            return
        self.eng[e].wait_ge(self._semobj(key), val)
        self.seen[e][key] = val
        self.n_wait[e] += 1

    def _deps(self, e, reads, writes):
        for b in reads:
            if b.multi:
                for key, val in b.ws.items():
                    self._wait(e, (key, val))
            else:
                self._wait(e, b.w)
            if b.psum:
                for key, val in b.r.items():
                    if key != e:
                        self._wait(e, (key, val))
        for b in writes:
            if not b.multi:
                self._wait(e, b.w)
            for key, val in b.r.items():
                self._wait(e, (key, val))

    def _commit(self, ev, reads, writes):
        k_, v = ev
        for b in reads:
            if b in writes:
                continue
            if b.r.get(k_, 0) < v:
                b.r[k_] = v
        for b in writes:
            if b.multi:
                if b.r:
                    b.ws = {}
                if b.ws.get(k_, 0) < v:
                    b.ws[k_] = v
            else:
                b.w = ev
            b.r = {}

    def op(self, e, fn, reads=(), writes=(), inc=True, dma=False):
        if dma:
            return self._dma_like(e, fn, reads, writes)
        self._deps(e, reads, writes)
        ins = fn(self.eng[e])
        self.n_inst[e] += 1
        if inc:
            ins.then_inc(self.sems[e], 1)
            self.cnt[e] += 1
            ev = (e, self.cnt[e])
        else:
            ev = (e, self.cnt[e] + 1)
        self._commit(ev, reads, writes)
        return ins

    def _dma_like(self, q, fn, reads, writes):
        self._deps(q, reads, writes)
        n = self.ring_n[q]
        slot = n % DMA_RING
        rnd = n // DMA_RING
        if rnd > 0:
            self._wait(q, ((q, slot), 16 * rnd))
        ins = fn(self.eng[q])
        ins.then_inc(self.ring[q][slot], 16)
        self.ring_n[q] = n + 1
        self.n_inst[q] += 1
        ev = ((q, slot), 16 * (rnd + 1))
        self._commit(ev, reads, writes)
        return ins

    def dma(self, q, out_ap, in_ap, reads=(), writes=(), **kw):
        self._deps(q, reads, writes)
        n = self.ring_n[q]
        slot = n % DMA_RING
        rnd = n // DMA_RING
        if rnd > 0:
            self._wait(q, ((q, slot), 16 * rnd))
        ins = self.eng[q].dma_start(out=out_ap, in_=in_ap, **kw)
        ins.then_inc(self.ring[q][slot], 16)
        self.ring_n[q] = n + 1
        self.n_inst[q] += 1
        ev = ((q, slot), 16 * (rnd + 1))
        self._commit(ev, reads, writes)
        return ins

    def finish(self):
        for q in ("sp", "pool", "act"):
            n = self.ring_n[q]
            for slot in range(DMA_RING):
                cnt = (n - slot + DMA_RING - 1) // DMA_RING if n > slot else 0
                if cnt > 0:
                    self._wait("sp", ((q, slot), 16 * cnt))

    def barrier(self):
        for e in ENGS:
            for o in ("pe", "dve", "act", "pool"):
                if o != e and self.cnt[o] > 0:
                    self._wait(e, (o, self.cnt[o]))
            for q in ("sp", "pool", "act"):
                n = self.ring_n[q]
                for slot in range(DMA_RING):
                    c = (n - slot + DMA_RING - 1) // DMA_RING if n > slot else 0
                    if c > 0:
                        self._wait(e, ((q, slot), 16 * c))

    def scope(self):
        kk = self

        class _S:
            def __enter__(s_):
                s_.old = kk.es
                s_.new = ExitStack()
                s_.new.__enter__()
                kk.es = s_.new
                return s_

            def __exit__(s_, *a):
                kk.barrier()
                kk.es = s_.old
                s_.new.__exit__(*a)
                return False
        return _S()

    def ev_eng(self):
        self.rr += 1
        return "act" if self.rr % 2 else "dve"


class Ring:
    def __init__(self, bufs):
        self.bufs = bufs
        self.i = 0

    def next(self):
        b = self.bufs[self.i % len(self.bufs)]
        self.i += 1
        return b


class Cfg:
    def __init__(self, TP=2048, NKH=16, D=2048, DFF=5504, NS=4, TS=8, stage=99, dbg=False, PAST=8192, NPOOL=2560):
        self.PAST = PAST
        self.NPOOL = NPOOL
        self.TP = TP
        self.NKH = NKH
        self.NVH = 2 * NKH
        self.D = D
        self.KT = D // 128
        self.DFF = DFF
        self.NS = NS
        self.TS = TS
        self.NSAMP = NS * TS
        self.NT = TP + self.NSAMP
        self.NBP = TP // 64
        self.NB = self.NBP + 1
        self.NCH = TP // 512
        self.KD = NKH * 128
        self.VD = self.NVH * 128
        self.CONVD = 2 * self.KD + self.VD
        self.PROJ = self.CONVD + self.VD + 2 * self.NVH
        self.stage = stage
        self.dbg = dbg

    def chunks(self):
        r = [(c * 512, 512) for c in range(self.NCH)]
        r.append((self.TP, self.NSAMP))
        return r

    def ttiles(self):
        r = [(t * 128, 128) for t in range(self.TP // 128)]
        r.append((self.TP, self.NSAMP))
        return r

    def blocks(self):
        r = [(b * 64, 64) for b in range(self.NBP)]
        r.append((self.TP, self.NSAMP))
        return r


def host_consts(cfg):
    c = {}
    c["ident_f"] = np.eye(128, dtype=np.float32)
    c["ident_b"] = np.eye(128).astype(ml_dtypes.bfloat16)
    c["ones_f"] = np.ones((128, 128), np.float32)
    c["ones_b"] = np.ones((128, 128)).astype(ml_dtypes.bfloat16)
    NEG = -30000.0
    j = np.arange(64)[:, None]
    i = np.arange(64)[None, :]
    def mk(blk, n):
        sj = (np.arange(n)[:, None] // blk)
        si = (np.arange(n)[None, :] // blk)
        same = (sj == si)
        jj = np.arange(n)[:, None]
        ii = np.arange(n)[None, :]
        ui = same & (ii >= jj)
        us = same & (ii > jj)
        ls = same & (ii < jj)
        out = np.zeros((5, 64, 64), np.float32)
        out[0, :n, :n] = ui
        out[1, :n, :n] = same
        out[2, :, :] = NEG
        out[2, :n, :n] = np.where(ui, 0.0, NEG)
        out[3, :, :] = NEG
        out[3, :n, :n] = np.where(us, 0.0, NEG)
        out[4, :, :] = -NEG
        out[4, :n, :n] = np.where(ls, 0.0, -NEG)
        return out
    c["masks_p"] = mk(64, 64)
    c["masks_s"] = mk(cfg.TS, cfg.NSAMP)
    rm = np.zeros((64, cfg.NS), np.float32)
    for s in range(cfg.NS):
        rm[s * cfg.TS:(s + 1) * cfg.TS, s] = 1.0
    c["rowmask_s"] = rm
    return c


def build(cfg):
    nc = bass.Bass("TRN2", target_bir_lowering=False)
    D, KT, NT, TP, NS, TS = cfg.D, cfg.KT, cfg.NT, cfg.TP, cfg.NS, cfg.TS
    NSAMP, NVH, NKH, NB = cfg.NSAMP, cfg.NVH, cfg.NKH, cfg.NB
    CONVD, PROJ, VD, KD = cfg.CONVD, cfg.PROJ, cfg.VD, cfg.KD

    def din(name, shape, dt=F32):
        return nc.dram_tensor(name, list(shape), dt, kind="ExternalInput").ap()

    def dout(name, shape, dt=F32):
        return nc.dram_tensor(name, list(shape), dt, kind="ExternalOutput").ap()

    def dscr(name, shape, dt=F32):
        return nc.dram_tensor(name, list(shape), dt, kind="Internal").ap()

    I = {}
    S = {}
    I["xin"] = din("xin", [NT, D])
    I["norm_mix"] = din("norm_mix", [2, D])
    I["norm_ffn"] = din("norm_ffn", [2, D])
    I["gdn_w_in"] = din("gdn_w_in", [D, PROJ])
    I["gdn_conv_w"] = din("gdn_conv_w", [CONVD, 4])
    I["gdn_A_log"] = din("gdn_A_log", [1, NVH])
    I["gdn_dt_bias"] = din("gdn_dt_bias", [1, NVH])
    I["gdn_norm"] = din("gdn_norm", [1, 128])
    I["gdn_w_out"] = din("gdn_w_out", [VD, D])
    I["st_gdn"] = din("st_gdn", [NS, NVH, 128, 128])
    I["st_gdn_conv"] = din("st_gdn_conv", [NS * 3, CONVD])
    DFF = cfg.DFF
    I["ffn_w_up"] = din("ffn_w_up", [2, D, 2 * DFF])
    I["ffn_conv_w"] = din("ffn_conv_w", [2, DFF, 3])
    I["ffn_conv_b"] = din("ffn_conv_b", [2, DFF])
    I["ffn_w_down"] = din("ffn_w_down", [2, DFF, D])
    I["st_ffn_conv"] = din("st_ffn_conv", [2, NS * 2, DFF])
    for nm, shp, dt in (("ident_f", [128, 128], F32), ("ident_b", [128, 128], BF16),
                        ("ones_f", [128, 128], F32), ("ones_b", [128, 128], BF16),
                        ("masks_p", [5, 64, 64], F32), ("masks_s", [5, 64, 64], F32),
                        ("rowmask_s", [64, NS], F32)):
        I[nm] = din(nm, shp, dt)
    O = {}
    O["o_gdn_p"] = dout("o_gdn_p", [NVH, 128, 128])
    O["o_gdn_s"] = dout("o_gdn_s", [NS, NVH, 128, 128])
    O["o_gdnc"] = dout("o_gdnc", [(1 + NS) * 3, CONVD])
    PAST, NPG, NPOOL = cfg.PAST, cfg.PAST // 128, cfg.NPOOL
    I["nsa_w_in"] = din("nsa_w_in", [D, 5168])
    I["nsa_q_norm"] = din("nsa_q_norm", [1, 128])
    I["nsa_k_norm"] = din("nsa_k_norm", [1, 3, 128])
    I["st_win"] = din("st_win", [NS, 512, 1024])
    I["nsa_cmp_pe"] = din("nsa_cmp_pe", [2, 32, 128])
    I["nsa_cmp_w1"] = din("nsa_cmp_w1", [2, 4096, 256])
    I["nsa_cmp_w2"] = din("nsa_cmp_w2", [2, 256, 128])
    I["rel_bias"] = din("rel_bias", [32, 16])
    I["nsa_w_out"] = din("nsa_w_out", [2048, 2048])
    I["cache"] = din("cache", [NPOOL * 128, 2048])
    S["past"] = dscr("past", [NS * PAST, 2048], BF16)
    I["page_table"] = din("page_table", [NS, NPG], I32)
    hc = nsa_host_consts(cfg, PAST)
    for nm, arr in hc.items():
        if nm.startswith("dims"):
            continue
        I[nm] = din(nm, list(arr.shape), {np.dtype(np.float32): F32, np.dtype(np.int32): I32}.get(arr.dtype, BF16))
    S["gvec"] = dscr("gvec", [16, NSA_GL])
    S["oT2"] = dscr("oT2", [2048, NT], BF16)
    O["o_kv"] = dout("o_kv", [NT, 2048])
    O["o_pwin"] = dout("o_pwin", [512, 1024])
    O["o_swin"] = dout("o_swin", [NS, 512, 1024])
    S["qT"] = dscr("qT", [2048, NT], BF16)
    S["win_new"] = dscr("win_new", [NT, 1024])
    S["gates"] = dscr("gates", [NT, 48])
    O["o_ffnc"] = dout("o_ffnc", [2, (1 + NS) * 2, DFF])
    O["y"] = dout("y", [NT, D])
    if cfg.dbg:
        O["dbg_h1"] = dout("dbg_h1", [NT, D])
        O["dbg_h2"] = dout("dbg_h2", [NT, D])
        O["dbg_h3"] = dout("dbg_h3", [NT, D])
        O["dbg_oT2"] = dout("dbg_oT2", [2048, NT], BF16)
        O["dbg_gvec"] = dout("dbg_gvec", [16, NSA_GL])
        O["dbg_qT"] = dout("dbg_qT", [2048, NT], BF16)
    if cfg.dbg:
        O["dbg_xn"] = dout("dbg_xn", [128, KT, NT], BF16)
        O["dbg_o"] = dout("dbg_o", [NVH * 128, NT], BF16)
    S["oT"] = dscr("oT", [NVH * 128, NT], BF16)
    S["tmS"] = dscr("tmS", [64, 5 * NVH * NB], F32)
    S["h1"] = dscr("h1", [NT, D])
    S["h2"] = dscr("h2", [NT, D])
    S["h3"] = dscr("h3", [NT, D])
    S["actT"] = dscr("actT", [DFF, NT], BF16)

    es = ExitStack()
    with es:
        k = K(nc, es)
        dbuf = {n: Buf(a, n, multi=True) for n, a in list(I.items()) + list(O.items()) + list(S.items())}

        ident_f = k.sb("ident_f", [128, 128], F32)
        ident_b = k.sb("ident_b", [128, 128], BF16)
        ones_f = k.sb("ones_f", [128, 128], F32)
        ones_b = k.sb("ones_b", [128, 128], BF16)
        masks_p = k.sb("masks_p", [64, 5, 64], F32)
        masks_s = k.sb("masks_s", [64, 5, 64], F32)
        rowmask_s = k.sb("rowmask_s", [64, NS], F32)
        for nm, t in (("ident_f", ident_f), ("ident_b", ident_b), ("ones_f", ones_f), ("ones_b", ones_b),
                      ("rowmask_s", rowmask_s)):
            k.dma("sp", t[:], I[nm], writes=[t])
        k.dma("sp", masks_p[:], I["masks_p"].rearrange("m j i -> j m i"), writes=[masks_p])
        k.dma("sp", masks_s[:], I["masks_s"].rearrange("m j i -> j m i"), writes=[masks_s])

        pf = Ring([k.ps(f"pf{i}", [128, 512], F32) for i in range(6)])
        pbf = Ring([k.ps(f"pb{i}", [128, 1024], BF16) for i in range(2)])

        def norm_fm(src_name, gain_ap, xn):
          with k.scope():
              gain_bc = k.sb(f"gain_{src_name}", [128, D], F32)
              k.dma("sp", gain_bc[:], gain_ap.partition_broadcast(128), writes=[gain_bc])
              xt_r = Ring([k.sb(f"xt{i}_{src_name}", [128, D], F32) for i in range(2)])
              sq = k.sb(f"sq_{src_name}", [128, D], F32)
              xs_r = Ring([k.sb(f"xs{i}_{src_name}", [128, D], BF16) for i in range(2)])
              ss_r = Ring([k.sb(f"ss{i}_{src_name}", [128, 2], F32) for i in range(2)])
              src = dbuf[src_name]
              for (r0, R) in cfg.ttiles():
                  xt = xt_r.next()
                  xs = xs_r.next()
                  ss = ss_r.next()
                  k.dma("sp", xt[0:R, :], src[r0:r0 + R, :], reads=[src], writes=[xt])
                  k.op("act", lambda e: e.activation(out=sq[0:R, :], in_=xt[0:R, :], func=AF.Square),
                       reads=[xt], writes=[sq])
                  k.op("dve", lambda e: e.reduce_sum(out=ss[0:R, 0:1], in_=sq[0:R, :], axis=AX.X),
                       reads=[sq], writes=[ss])
                  k.op("dve", lambda e: e.tensor_scalar(out=ss[0:R, 1:2], in0=ss[0:R, 0:1], scalar1=1.0 / D,
                                                        scalar2=RMS_EPS, op0=ALU.mult, op1=ALU.add),
                       reads=[ss], writes=[ss])
                  k.op("act", lambda e: e.activation(out=ss[0:R, 0:1], in_=ss[0:R, 1:2], func=AF.Sqrt),
                       reads=[ss], writes=[ss])
                  k.op("dve", lambda e: e.reciprocal(out=ss[0:R, 0:1], in_=ss[0:R, 0:1]),
                       reads=[ss], writes=[ss])
                  k.op("dve", lambda e: e.scalar_tensor_tensor(out=xs[0:R, :], in0=xt[0:R, :], scalar=ss[0:R, 0:1],
                                                               in1=gain_bc[0:R, :], op0=ALU.mult, op1=ALU.mult),
                       reads=[xt, ss, gain_bc], writes=[xs])
                  for g4 in range(KT // 4):
                      pb = pbf.next()
                      for q in range(4):
                          kt = g4 * 4 + q
                          k.op("pe", lambda e: e.transpose(out=pb[:, q * 128:q * 128 + R],
                                                           in_=xs[0:R, kt * 128:(kt + 1) * 128],
                                                           identity=ident_b[0:R, 0:R]),
                               reads=[xs, ident_b], writes=[pb], inc=(q == 3))
                      eng = k.ev_eng()
                      src_v = pb[:, 0:512].rearrange("p (q r) -> p q r", q=4)[:, :, 0:R]
                      dst_v = xn[:, g4 * 4:(g4 + 1) * 4, r0:r0 + R]
                      if eng == "act":
                          k.op("act", lambda e: e.copy(out=dst_v, in_=src_v), reads=[pb], writes=[xn])
                      else:
                          k.op("dve", lambda e: e.tensor_copy(out=dst_v, in_=src_v), reads=[pb], writes=[xn])

        CC = dict(ident_f=ident_f, ident_b=ident_b, ones_f=ones_f, ones_b=ones_b,
                  masks_p=masks_p, masks_s=masks_s, rowmask_s=rowmask_s)
        with k.scope():
            xn = k.sb("xn", [128, KT, NT], BF16)
            norm_fm("xin", I["norm_mix"][0:1, :], xn)
            if cfg.dbg:
                k.dma("sp", O["dbg_xn"], xn[:], reads=[xn], writes=[dbuf["dbg_xn"]])
            gdn_layer(k, cfg, I, O, S, dbuf, xn, pf, pbf, CC)
        if cfg.stage >= 2:
            linear_tm(k, cfg, pf, dbuf, "go", "oT", S["oT"], "gdn_w_out", I["gdn_w_out"], VD,
                      "xin", I["xin"], "h1", S["h1"])
            ffn_layer(k, cfg, I, O, S, dbuf, pf, pbf, CC, 0, "h1", S["h1"], "h2", S["h2"], norm_fm,
                      hook=(Unpager(k, cfg, I, S, dbuf) if cfg.stage >= 4 else None))
        if cfg.stage >= 3:
            nsa_proj(k, cfg, I, O, S, dbuf, pf, pbf, CC, norm_fm, "h2")
        if cfg.stage >= 4:
            nsa_attn(k, cfg, I, O, S, dbuf, pf, pbf, CC)
            linear_tm(k, cfg, pf, dbuf, "no", "oT2", S["oT2"], "nsa_w_out", I["nsa_w_out"], 2048,
                      "h2", S["h2"], "h3", S["h3"])
            ffn_layer(k, cfg, I, O, S, dbuf, pf, pbf, CC, 1, "h3", S["h3"], "y", O["y"], norm_fm)
            if cfg.dbg:
                k.dma("sp", O["dbg_h3"], S["h3"], reads=[dbuf["h3"]], writes=[dbuf["dbg_h3"]])
                k.dma("sp", O["dbg_oT2"], S["oT2"], reads=[dbuf["oT2"]], writes=[dbuf["dbg_oT2"]])
                k.dma("sp", O["dbg_gvec"], S["gvec"], reads=[dbuf["gvec"]], writes=[dbuf["dbg_gvec"]])
                k.dma("sp", O["dbg_qT"], S["qT"], reads=[dbuf["qT"]], writes=[dbuf["dbg_qT"]])
        if cfg.stage >= 2:
            if cfg.dbg:
                k.dma("sp", O["dbg_h1"], S["h1"], reads=[dbuf["h1"]], writes=[dbuf["dbg_h1"]])
                k.dma("sp", O["dbg_h2"], S["h2"], reads=[dbuf["h2"]], writes=[dbuf["dbg_h2"]])

        k.finish()
        print("inst", k.n_inst, "wait", k.n_wait)
    return nc


def gdn_layer(k, cfg, I, O, S, dbuf, xn, pf, pbf, C):
    D, KT, NT, TP, NS, TS = cfg.D, cfg.KT, cfg.NT, cfg.TP, cfg.NS, cfg.TS
    NSAMP, NVH, NKH, NB, NBP = cfg.NSAMP, cfg.NVH, cfg.NKH, cfg.NB, cfg.NBP
    CONVD, PROJ, VD, KD = cfg.CONVD, cfg.PROJ, cfg.VD, cfg.KD
    ident_f, ident_b, ones_f, ones_b = C["ident_f"], C["ident_b"], C["ones_f"], C["ones_b"]
    masks_p, masks_s, rowmask_s = C["masks_p"], C["masks_s"], C["rowmask_s"]
    NCT = CONVD // 128
    PADW = 3 + TP + NS * 11
    SOFF = 3 + TP
    blocks = cfg.blocks()
    chunks = cfg.chunks()

    cw = k.sb("g_cw", [128, NCT, 4], F32)
    k.dma("sp", cw[:], I["gdn_conv_w"].rearrange("(t p) w -> p t w", p=128), writes=[cw])
    alog = k.sb("g_alog", [64, NVH], F32)
    dtb = k.sb("g_dtb", [64, NVH], F32)
    k.dma("sp", alog[:], I["gdn_A_log"].partition_broadcast(64), writes=[alog])
    k.dma("sp", dtb[:], I["gdn_dt_bias"].partition_broadcast(64), writes=[dtb])
    gnorm = k.sb("g_norm", [128, 1], F32)
    k.dma("sp", gnorm[:], I["gdn_norm"].rearrange("o d -> d o"), writes=[gnorm])
    stc_r = Ring([k.sb(f"g_stc{i}", [NS * 3, 128], F32) for i in range(2)])
    negA = k.sb("g_negA", [64, NVH], F32)
    k.op("act", lambda e: e.activation(out=negA[:], in_=alog[:], func=AF.Exp), reads=[alog], writes=[negA])
    k.op("dve", lambda e: e.tensor_scalar(out=negA[:], in0=negA[:], scalar1=-1.0, scalar2=None, op0=ALU.mult),
         reads=[negA], writes=[negA])

    tm_scope = k.scope()
    tm_scope.__enter__()
    wbd = k.sb("g_wbd", [128, KT, 2 * NVH], BF16)
    k.dma("pool", wbd[:], I["gdn_w_in"][:, CONVD + VD:PROJ].rearrange("(kt p) c -> p kt c", p=128), writes=[wbd])
    NH2 = 2 * NVH
    bl_all = k.sb("g_bl", [64, NB, NH2], F32)
    k.op("pool", lambda e: e.memset(bl_all[:], 0.0), writes=[bl_all])
    per_bank = 512 // NH2
    for g0 in range(0, NB, per_bank):
        gb = blocks[g0:g0 + per_bank]
        ps = pf.next()
        for bi, (t0, L) in enumerate(gb):
            for kt in range(KT):
                k.op("pe", lambda e: e.matmul(ps[0:L, bi * NH2:(bi + 1) * NH2], lhsT=xn[:, kt, t0:t0 + L],
                                              rhs=wbd[:, kt, :], start=(kt == 0), stop=(kt == KT - 1)),
                     reads=[xn, wbd], writes=[ps], inc=(kt == KT - 1))
        nfull = sum(1 for (_, L) in gb if L == 64)
        if nfull:
            k.op("act", lambda e: e.copy(out=bl_all[:, g0:g0 + nfull, :],
                                         in_=ps[0:64, 0:nfull * NH2].rearrange("p (b c) -> p b c", c=NH2)),
                 reads=[ps], writes=[bl_all])
        if nfull < len(gb):
            bi = nfull
            k.op("act", lambda e: e.copy(out=bl_all[0:NSAMP, g0 + bi, :], in_=ps[0:NSAMP, bi * NH2:(bi + 1) * NH2]),
                 reads=[ps], writes=[bl_all])

    def tm(name):
        return k.sb(name, [64, NB, NVH], F32)
    beta_t, g_t, G_t, gl_t, c_t, kd_t, nb_t, tmp1, tmp2 = (tm("g_beta"), tm("g_g"), tm("g_G"), tm("g_gl"),
                                                          tm("g_c"), tm("g_kd"), tm("g_nb"), tm("g_t1"), tm("g_t2"))
    blv = bl_all[:, :, 0:NVH]
    alv = bl_all[:, :, NVH:NH2]
    dtb_b = dtb[:].unsqueeze(1).to_broadcast([64, NB, NVH])
    negA_b = negA[:].unsqueeze(1).to_broadcast([64, NB, NVH])
    k.op("act", lambda e: e.activation(out=beta_t[:], in_=blv, func=AF.Sigmoid), reads=[bl_all], writes=[beta_t])
    k.op("dve", lambda e: e.tensor_tensor(out=tmp1[:], in0=alv, in1=dtb_b, op=ALU.add), reads=[bl_all, dtb], writes=[tmp1])
    k.op("dve", lambda e: e.tensor_scalar(out=tmp2[:], in0=tmp1[:], scalar1=-1.0, scalar2=None, op0=ALU.mult),
         reads=[tmp1], writes=[tmp2])
    k.op("dve", lambda e: e.tensor_tensor(out=tmp2[:], in0=tmp2[:], in1=tmp1[:], op=ALU.min),
         reads=[tmp1, tmp2], writes=[tmp2])
    k.op("act", lambda e: e.activation(out=tmp2[:], in_=tmp2[:], func=AF.Exp), reads=[tmp2], writes=[tmp2])
    k.op("act", lambda e: e.activation(out=tmp2[:], in_=tmp2[:], func=AF.Ln, bias=1.0), reads=[tmp2], writes=[tmp2])
    k.op("dve", lambda e: e.scalar_tensor_tensor(out=tmp1[:], in0=tmp1[:], scalar=0.0, in1=tmp2[:],
                                                 op0=ALU.max, op1=ALU.add), reads=[tmp1, tmp2], writes=[tmp1])
    k.op("dve", lambda e: e.tensor_tensor(out=g_t[:], in0=tmp1[:], in1=negA_b, op=ALU.mult),
         reads=[tmp1, negA], writes=[g_t])
    per_bank = 512 // NVH
    for (dst, mi) in ((G_t, 0), (gl_t, 1)):
        for g0 in range(0, NB, per_bank):
            gb = blocks[g0:g0 + per_bank]
            ps = pf.next()
            for bi, (t0, L) in enumerate(gb):
                mk_ = masks_p if L == 64 else masks_s
                k.op("pe", lambda e: e.matmul(ps[0:L, bi * NVH:(bi + 1) * NVH], lhsT=mk_[0:L, mi, 0:L],
                                              rhs=g_t[0:L, g0 + bi, :], start=True, stop=True),
                     reads=[mk_, g_t], writes=[ps], inc=(bi == len(gb) - 1))
            k.op("dve", lambda e: e.tensor_copy(out=dst[:, g0:g0 + len(gb), :],
                                                in_=ps[0:64, 0:len(gb) * NVH].rearrange("p (b c) -> p b c", c=NVH)),
                 reads=[ps], writes=[dst])
    k.op("act", lambda e: e.activation(out=tmp1[:], in_=G_t[:], func=AF.Exp), reads=[G_t], writes=[tmp1])
    k.op("dve", lambda e: e.scalar_tensor_tensor(out=c_t[:], in0=tmp1[:], scalar=-1.0, in1=beta_t[:],
                                                 op0=ALU.mult, op1=ALU.mult), reads=[tmp1, beta_t], writes=[c_t])
    k.op("dve", lambda e: e.tensor_tensor(out=tmp2[:], in0=gl_t[:], in1=G_t[:], op=ALU.subtract),
         reads=[gl_t, G_t], writes=[tmp2])
    k.op("act", lambda e: e.activation(out=kd_t[:], in_=tmp2[:], func=AF.Exp), reads=[tmp2], writes=[kd_t])
    k.op("dve", lambda e: e.tensor_scalar(out=nb_t[:], in0=beta_t[:], scalar1=-1.0, scalar2=None, op0=ALU.mult),
         reads=[beta_t], writes=[nb_t])

    tmH = k.sb("g_tmH", [64, 5, NVH, NB], F32)
    for idx, src_t in enumerate((beta_t, G_t, c_t, kd_t, nb_t)):
        k.op("pool", lambda e: e.tensor_copy(out=tmH[:, idx].rearrange("p h b -> p b h"), in_=src_t[:]),
             reads=[src_t], writes=[tmH])
    k.dma("sp", S["tmS"], tmH[:].rearrange("p k h b -> p (k h b)"), reads=[tmH], writes=[dbuf["tmS"]])
    tm_scope.__exit__(None, None, None)
    hd_r = Ring([k.sb(f"g_hd{i}", [64, 5, NB], F32) for i in range(2)])

    def load_hd(hv):
        hd = hd_r.next()
        k.dma("sp", hd[:], S["tmS"].rearrange("p (k h b) -> p k h b", k=5, h=NVH)[:, :, hv, :],
              reads=[dbuf["tmS"]], writes=[hd])
        return hd

    wt_r = Ring([k.sb(f"g_wt{i}", [128, KT, 128], BF16) for i in range(2)])
    prep = k.sb("g_prep", [128, PADW], F32)
    k.op("pool", lambda e: e.memset(prep[:], 0.0), writes=[prep])
    cq = k.sb("g_cq", [128, NT], F32)
    sqb = k.sb("g_sqb", [128, 512], BF16)
    rinv = k.sb("g_rinv", [128, 512], F32)
    epsb = k.sb("g_epsb", [128, 1], F32)
    k.op("pool", lambda e: e.memset(epsb[:], RMS_EPS), writes=[epsb])
    kqfm = k.sb("g_kqfm", [128, NB, 128], BF16)
    k.op("pool", lambda e: e.memset(kqfm[:], 0.0), writes=[kqfm])
    k_tok = k.sb("g_ktok", [64, NB, 128], BF16)
    vs = k.sb("g_vs", [128, NT], BF16)
    zsil = [k.sb("g_zsil", [128, NT], BF16)] * 2
    vb_tok = [k.sb("g_vb", [64, NB, 128], BF16)] * 2
    convout = k.sb("g_convout", [128, NCT, 16], F32)
    k.op("pool", lambda e: e.memset(convout[:], 0.0), writes=[convout])

    def _load_w_now(col0):
        wt = wt_r.next()
        k.dma("pool", wt[:], I["gdn_w_in"][:, col0:col0 + 128].rearrange("(kt p) c -> p kt c", p=128),
              reads=[dbuf["gdn_w_in"]], writes=[wt])
        return wt

    col_order = []
    for j_ in range(NKH):
        col_order += [j_ * 128, KD + j_ * 128]
        for a_ in range(2):
            hv_ = 2 * j_ + a_
            col_order += [2 * KD + hv_ * 128, CONVD + hv_ * 128]
    pre = {"i": 0, "wt": None}

    def load_w(col0):
        i = pre["i"]
        assert col_order[i] == col0, (i, col0, col_order[i])
        wt = pre["wt"] if pre["wt"] is not None else _load_w_now(col0)
        pre["i"] = i + 1
        pre["wt"] = _load_w_now(col_order[i + 1]) if i + 1 < len(col_order) else None
        return wt

    def proj(wt, consumer):
        for ci, (t0, n) in enumerate(chunks):
            ps = pf.next()
            for kt in range(KT):
                k.op("pe", lambda e: e.matmul(ps[:, 0:n], lhsT=wt[:, kt, :], rhs=xn[:, kt, t0:t0 + n],
                                              start=(kt == 0), stop=(kt == KT - 1)),
                     reads=[wt, xn], writes=[ps], inc=(kt == KT - 1))
            consumer(ci, t0, n, ps)

    def samp_view(buf_ap_2d, w, lo, hi):
        return buf_ap_2d.rearrange("p (s w) -> p s w", w=w)[:, :, lo:hi]

    def proj_conv_silu(col0, dst, wt=None):
        ct = col0 // 128
        if wt is None:
            wt = load_w(col0)

        def cons(ci, t0, n, ps):
            eng = k.ev_eng()
            if n == 512:
                dv = prep[:, 3 + t0:3 + t0 + n]
                sv = ps[:, 0:n]
            else:
                dv = samp_view(prep[:, SOFF:SOFF + NS * 11], 11, 3, 11)
                sv = ps[:, 0:n].rearrange("p (s w) -> p s w", w=TS)
            if eng == "act":
                k.op("act", lambda e: e.copy(out=dv, in_=sv), reads=[ps], writes=[prep])
            else:
                k.op("dve", lambda e: e.tensor_copy(out=dv, in_=sv), reads=[ps], writes=[prep])
        pst = pf.next()
        stc = stc_r.next()
        k.dma("sp", stc[:], I["st_gdn_conv"][:, ct * 128:(ct + 1) * 128], writes=[stc])
        k.op("pe", lambda e: e.transpose(out=pst[:, 0:NS * 3], in_=stc[0:NS * 3, 0:128],
                                         identity=ident_f[0:NS * 3, 0:NS * 3]),
             reads=[stc, ident_f], writes=[pst])
        k.op("dve", lambda e: e.tensor_copy(out=samp_view(prep[:, SOFF:SOFF + NS * 11], 11, 0, 3),
                                            in_=pst[:, 0:NS * 3].rearrange("p (s w) -> p s w", w=3)),
             reads=[pst], writes=[prep])
        proj(wt, cons)
        k.op("pool", lambda e: e.tensor_copy(out=convout[:, ct, 0:3], in_=prep[:, 3 + TP - 3:3 + TP]),
             reads=[prep], writes=[convout])
        k.op("pool", lambda e: e.tensor_copy(out=convout[:, ct, 3:3 + NS * 3].rearrange("p (s w) -> p s w", w=3),
                                             in_=samp_view(prep[:, SOFF:SOFF + NS * 11], 11, 8, 11)),
             reads=[prep], writes=[convout])
        for (ov, mkv) in ((cq[:, 0:TP], lambda i: prep[:, i:i + TP]),
                          (cq[:, TP:NT].rearrange("p (s w) -> p s w", w=TS),
                           lambda i: samp_view(prep[:, SOFF:SOFF + NS * 11], 11, i, i + TS))):
            k.op("dve", lambda e: e.tensor_scalar(out=ov, in0=mkv(3), scalar1=cw[:, ct, 3:4], scalar2=None, op0=ALU.mult),
                 reads=[prep, cw], writes=[cq])
            for i in range(3):
                k.op("dve", lambda e: e.scalar_tensor_tensor(out=ov, in0=mkv(i), scalar=cw[:, ct, i:i + 1], in1=ov,
                                                             op0=ALU.mult, op1=ALU.add),
                     reads=[prep, cw, cq], writes=[cq])
        k.op("act", lambda e: e.activation(out=dst[:], in_=cq[:], func=AF.Silu), reads=[cq], writes=[dst])

    def l2norm_to(src, which, scale):
        for ci, (t0, n) in enumerate(chunks):
            k.op("act", lambda e: e.activation(out=sqb[:, 0:n], in_=src[:, t0:t0 + n], func=AF.Square), reads=[src], writes=[sqb])
            ps = pf.next()
            k.op("pe", lambda e: e.matmul(ps[:, 0:n], lhsT=ones_b[:, :], rhs=sqb[:, 0:n], start=True, stop=True),
                 reads=[ones_b, sqb], writes=[ps])
            k.op("act", lambda e: e.activation(out=rinv[:, 0:n], in_=ps[:, 0:n], func=AF.Sqrt, bias=epsb[:, 0:1]),
                 reads=[ps, epsb], writes=[rinv])
            k.op("dve", lambda e: e.reciprocal(out=rinv[:, 0:n], in_=rinv[:, 0:n]), reads=[rinv], writes=[rinv])
            if n == 512:
                b0 = t0 // 64
                dv = kqfm[:, b0:b0 + 8, which * 64:which * 64 + 64]
                sv = src[:, t0:t0 + n].rearrange("p (b i) -> p b i", i=64)
                rv = rinv[:, 0:n].rearrange("p (b i) -> p b i", i=64)
            else:
                dv = kqfm[:, NBP, which * 64:which * 64 + n]
                sv = src[:, t0:t0 + n]
                rv = rinv[:, 0:n]
            k.op("dve", lambda e: e.scalar_tensor_tensor(out=dv, in0=sv, scalar=scale, in1=rv, op0=ALU.mult, op1=ALU.mult),
                 reads=[src, rinv], writes=[kqfm])

    def to_tok(src_fn, evac):
        for g0 in range(0, NB, 8):
            gb = blocks[g0:g0 + 8]
            pb = pbf.next()
            for bi, (t0, L) in enumerate(gb):
                k.op("pe", lambda e: e.transpose(out=pb[0:L, bi * 128:(bi + 1) * 128], in_=src_fn(g0 + bi, t0, L),
                                                 identity=ident_b[:, :]),
                     reads=src_fn.reads + [ident_b], writes=[pb], inc=(bi == len(gb) - 1))
            evac(g0, gb, pb)

    NG = 4
    dgG = k.sb("g_dgG", [64, NG, 64], F32)
    dgB = k.sb("g_dgB", [64, NG, 64], F32)
    t1 = k.sb("g_t1b", [64, NG, 64], F32)
    tI = k.sb("g_tI", [64, NG, 64], F32)
    tS = k.sb("g_tS", [64, NG, 64], F32)
    tL = k.sb("g_tL", [64, NG, 64], F32)
    kkE = k.sb("g_kkE", [64, NG, 64], F32)
    kkL = k.sb("g_kkL", [64, NG, 64], F32)
    eGbc = k.sb("g_eGbc", [128, NG * 64], F32)
    PR = [k.sb(f"g_PR{i}", [64, NG, 128], BF16) for i in range(2)]
    PT = [k.sb(f"g_PT{i}", [64, NG, 64], BF16) for i in range(2)]
    slots = [(dgG, dgB, t1, tI, tS, tL, kkE, kkL, eGbc, PR, PT)]
    for si in range(1, 2):
        slots.append(tuple([k.sb(f"g{si}_{nm}", [64, NG, 64], F32) for nm in ("dgG", "dgB", "t1", "tI", "tS", "tL", "kkE", "kkL")]
                           + [k.sb(f"g{si}_eGbc", [128, NG * 64], F32),
                              [k.sb(f"g{si}_PR{i}", [64, NG, 128], BF16) for i in range(2)],
                              [k.sb(f"g{si}_PT{i}", [64, NG, 64], BF16) for i in range(2)]]))
    heads = []
    for a in range(1):
        h = dict(
            AQT=k.sb(f"g_AQT{a}", [64, NB, 64], BF16), TT=k.sb(f"g_TT{a}", [64, NB, 64], BF16),
            qdec=k.sb(f"g_qdec{a}", [128, NB, 64], BF16), kdec=k.sb(f"g_kdec{a}", [64, 2, 128], BF16),
            egl=k.sb(f"g_egl{a}", [128, NB, 4], F32),
            S=k.sb(f"g_S{a}", [128, 128], F32), Sb=k.sb(f"g_Sb{a}", [128, 128], BF16),
            Ss=[k.sb(f"g_Ss{a}_{s}", [128, 128], F32) for s in range(NS)],
            Ssb=[k.sb(f"g_Ssb{a}_{s}", [128, 128], BF16) for s in range(NS)],
            r=k.sb(f"g_r{a}", [64, 128], BF16), u=k.sb(f"g_u{a}", [64, 128], BF16),
            otok=k.sb(f"g_otok{a}", [64, 8, 128], F32), osq=k.sb(f"g_osq{a}", [64, 8, 128], F32),
            oss=k.sb(f"g_oss{a}", [64, 16], F32), on=k.sb(f"g_on{a}", [64, 8, 128], BF16),
            oT=k.sb(f"g_oT{a}", [128, 512], BF16),
            kzp=k.sb(f"g_kzp{a}", [128, NS, NSAMP], BF16), qzp=k.sb(f"g_qzp{a}", [128, NS, NSAMP], BF16),
            kds=k.sb(f"g_kds{a}", [64, NS, 128], BF16),
        )
        heads.append(h)
    heads.append(heads[0])
    HD = {}

    def grp_gen(hv, hd, g0, gb, slot):
        H = heads[0]
        dgG, dgB, t1, tI, tS, tL, kkE, kkL, eGbc, PR, PT = slot
        if True:
            nb = len(gb)
            L = gb[0][1]
            mk_ = masks_p if L == 64 else masks_s
            idb = ident_f[0:L, 0:64].unsqueeze(1).to_broadcast([L, nb, 64])
            k.op("dve", lambda e: e.tensor_tensor(out=dgG[0:L, 0:nb, :], in0=idb,
                                                  in1=hd[0:L, 1, g0:g0 + nb].unsqueeze(2).to_broadcast([L, nb, 64]),
                                                  op=ALU.mult), reads=[ident_f, hd], writes=[dgG])
            k.op("dve", lambda e: e.tensor_tensor(out=dgB[0:L, 0:nb, :], in0=idb,
                                                  in1=hd[0:L, 0, g0:g0 + nb].unsqueeze(2).to_broadcast([L, nb, 64]),
                                                  op=ALU.mult), reads=[ident_f, hd], writes=[dgB])
            pG = pf.next()
            k.op("pe", lambda e: e.matmul(pG[:, 0:nb * 64], lhsT=ones_f[0:L, :],
                                          rhs=dgG[0:L, 0:nb, :].rearrange("p b i -> p (b i)"), start=True, stop=True),
                 reads=[ones_f, dgG], writes=[pG])
            pB = pf.next()
            k.op("pe", lambda e: e.matmul(pB[0:64, 0:nb * 64], lhsT=ones_f[0:L, 0:64],
                                          rhs=dgB[0:L, 0:nb, :].rearrange("p b i -> p (b i)"), start=True, stop=True),
                 reads=[ones_f, dgB], writes=[pB])
            pGv = pG[0:L, 0:nb * 64].rearrange("p (b i) -> p b i", i=64)
            pBv = pB[0:L, 0:nb * 64].rearrange("p (b i) -> p b i", i=64)
            yield
            k.op("dve", lambda e: e.tensor_tensor(out=t1[0:L, 0:nb, :], in0=pGv,
                                                  in1=hd[0:L, 1, g0:g0 + nb].unsqueeze(2).to_broadcast([L, nb, 64]),
                                                  op=ALU.subtract), reads=[pG, hd], writes=[t1])
            for (dst, mi) in ((tI, 2), (tS, 3), (tL, 4)):
                k.op("pool", lambda e: e.tensor_tensor(out=dst[0:L, 0:nb, :], in0=t1[0:L, 0:nb, :],
                                                       in1=mk_[0:L, mi, :].unsqueeze(1).to_broadcast([L, nb, 64]),
                                                       op=ALU.add), reads=[t1, mk_], writes=[dst])
            yield
            k.op("act", lambda e: e.activation(out=tI[0:L, 0:nb, :], in_=tI[0:L, 0:nb, :], func=AF.Exp), reads=[tI], writes=[tI])
            k.op("act", lambda e: e.activation(out=tS[0:L, 0:nb, :], in_=tS[0:L, 0:nb, :], func=AF.Exp), reads=[tS], writes=[tS])
            k.op("act", lambda e: e.activation(out=tL[0:L, 0:nb, :], in_=tL[0:L, 0:nb, :], func=AF.Exp, scale=-1.0),
                 reads=[tL], writes=[tL])
            k.op("act", lambda e: e.activation(out=eGbc[:, 0:nb * 64], in_=pG[:, 0:nb * 64], func=AF.Exp),
                 reads=[pG], writes=[eGbc])
            yield
            pK = pf.next()
            for bi, (t0, Lb) in enumerate(gb):
                k.op("pe", lambda e: e.matmul(pK[0:L, bi * 128:(bi + 1) * 128], lhsT=kqfm[:, g0 + bi, 0:L],
                                              rhs=kqfm[:, g0 + bi, :], start=True, stop=True),
                     reads=[kqfm], writes=[pK], inc=(bi == nb - 1))
            pKv = pK[0:L, 0:nb * 128].rearrange("p (b c) -> p b c", c=128)
            k.op("dve", lambda e: e.tensor_tensor(out=H["AQT"][0:L, g0:g0 + nb, 0:L], in0=pKv[:, :, 64:64 + L],
                                                  in1=tI[0:L, 0:nb, 0:L], op=ALU.mult),
                 reads=[pK, tI], writes=[H["AQT"]])
            k.op("dve", lambda e: e.tensor_tensor(out=kkE[0:L, 0:nb, 0:L], in0=pKv[:, :, 0:L], in1=tS[0:L, 0:nb, 0:L],
                                                  op=ALU.mult), reads=[pK, tS], writes=[kkE])
            k.op("dve", lambda e: e.tensor_tensor(out=kkL[0:L, 0:nb, 0:L], in0=pKv[:, :, 0:L], in1=tL[0:L, 0:nb, 0:L],
                                                  op=ALU.mult), reads=[pK, tL], writes=[kkL])
            pr, pt = PR[0], PT[0]
            yield
            k.op("dve", lambda e: e.scalar_tensor_tensor(out=pr[0:L, 0:nb, 0:L], in0=kkE[0:L, 0:nb, 0:L], scalar=-1.0,
                                                         in1=pBv[:, :, 0:L], op0=ALU.mult, op1=ALU.mult),
                 reads=[kkE, pB], writes=[pr])
            k.op("pool", lambda e: e.tensor_copy(out=pr[0:L, 0:nb, 64:64 + L],
                                                 in_=ident_f[0:L, 0:L].unsqueeze(1).to_broadcast([L, nb, L])),
                 reads=[ident_f], writes=[pr])
            k.op("dve", lambda e: e.tensor_tensor(out=pt[0:L, 0:nb, 0:L], in0=kkL[0:L, 0:nb, 0:L],
                                                  in1=hd[0:L, 4, g0:g0 + nb].unsqueeze(2).to_broadcast([L, nb, L]),
                                                  op=ALU.mult), reads=[kkL, hd], writes=[pt])
            for s in range(6):
                yield
                pr, pt = PR[s % 2], PT[s % 2]
                prn, ptn = PR[(s + 1) % 2], PT[(s + 1) % 2]
                p1 = pf.next()
                last = (s == 5)
                for bi in range(nb):
                    k.op("pe", lambda e: e.matmul(p1[0:L, bi * 128:bi * 128 + 64 + L], lhsT=pt[0:L, bi, 0:L],
                                                  rhs=pr[0:L, bi, 0:64 + L], start=True, stop=True),
                         reads=[pt, pr], writes=[p1], inc=(bi == nb - 1))
                p1v = p1[0:L, 0:nb * 128].rearrange("p (b c) -> p b c", c=128)
                if not last:
                    p2 = pf.next()
                    for bi in range(nb):
                        k.op("pe", lambda e: e.matmul(p2[0:L, bi * 64:bi * 64 + L], lhsT=pr[0:L, bi, 0:L],
                                                      rhs=pt[0:L, bi, 0:L], start=True, stop=True),
                             reads=[pt, pr], writes=[p2], inc=(bi == nb - 1))
                    p2v = p2[0:L, 0:nb * 64].rearrange("p (b c) -> p b c", c=64)
                    k.op("act", lambda e: e.copy(out=prn[0:L, 0:nb, 0:L], in_=p1v[:, :, 0:L]), reads=[p1], writes=[prn])
                    k.op("act", lambda e: e.copy(out=ptn[0:L, 0:nb, 0:L], in_=p2v[:, :, 0:L]), reads=[p2], writes=[ptn])
                    k.op("dve", lambda e: e.tensor_tensor(out=prn[0:L, 0:nb, 64:64 + L], in0=p1v[:, :, 64:64 + L],
                                                          in1=pr[0:L, 0:nb, 64:64 + L], op=ALU.add),
                         reads=[p1, pr], writes=[prn])
                else:
                    k.op("dve", lambda e: e.tensor_tensor(out=H["TT"][0:L, g0:g0 + nb, 0:L], in0=p1v[:, :, 64:64 + L],
                                                          in1=pr[0:L, 0:nb, 64:64 + L], op=ALU.add),
                         reads=[p1, pr], writes=[H["TT"]])
            yield
            eGv = eGbc[:, 0:nb * 64].rearrange("p (b i) -> p b i", i=64)
            k.op("pool", lambda e: e.tensor_tensor(out=H["qdec"][:, g0:g0 + nb, 0:L], in0=kqfm[:, g0:g0 + nb, 64:64 + L],
                                                   in1=eGv[:, :, 0:L], op=ALU.mult),
                 reads=[kqfm, eGbc], writes=[H["qdec"]])
            if L == 64:
                k.op("pool", lambda e: e.tensor_copy(out=H["egl"][:, g0:g0 + nb, 0:1], in_=eGv[:, :, 63:64]),
                     reads=[eGbc], writes=[H["egl"]])
            else:
                k.op("pool", lambda e: e.tensor_copy(out=H["egl"][:, g0, 0:NS],
                                                     in_=eGbc[:, 0:NSAMP].rearrange("p (s w) -> p s w", w=TS)[:, :, TS - 1]),
                     reads=[eGbc], writes=[H["egl"]])

    def precompute(hv, hd):
        groups = [(g0, blocks[g0:g0 + NG]) for g0 in range(0, NBP, NG)] + [(NBP, [blocks[NBP]])]
        NSL = len(slots)
        for i0 in range(0, len(groups), NSL):
            alive = [grp_gen(hv, hd, g0, gb, slots[si]) for si, (g0, gb) in enumerate(groups[i0:i0 + NSL])]
            while alive:
                for gen in list(alive):
                    try:
                        next(gen)
                    except StopIteration:
                        alive.remove(gen)

    def o_finish(a, hv, t0, nblk, L):
        H = heads[a]
        k.op("act", lambda e: e.activation(out=H["osq"][0:L, 0:nblk, :], in_=H["otok"][0:L, 0:nblk, :], func=AF.Square),
             reads=[H["otok"]], writes=[H["osq"]])
        k.op("dve", lambda e: e.reduce_sum(out=H["oss"][0:L, 0:nblk], in_=H["osq"][0:L, 0:nblk, :], axis=AX.X),
             reads=[H["osq"]], writes=[H["oss"]])
        k.op("dve", lambda e: e.tensor_scalar(out=H["oss"][0:L, 8:8 + nblk], in0=H["oss"][0:L, 0:nblk], scalar1=1.0 / 128,
                                              scalar2=RMS_EPS, op0=ALU.mult, op1=ALU.add), reads=[H["oss"]], writes=[H["oss"]])
        k.op("act", lambda e: e.activation(out=H["oss"][0:L, 0:nblk], in_=H["oss"][0:L, 8:8 + nblk], func=AF.Sqrt),
             reads=[H["oss"]], writes=[H["oss"]])
        k.op("dve", lambda e: e.reciprocal(out=H["oss"][0:L, 0:nblk], in_=H["oss"][0:L, 0:nblk]),
             reads=[H["oss"]], writes=[H["oss"]])
        k.op("dve", lambda e: e.tensor_tensor(out=H["on"][0:L, 0:nblk, :], in0=H["otok"][0:L, 0:nblk, :],
                                              in1=H["oss"][0:L, 0:nblk].unsqueeze(2).to_broadcast([L, nblk, 128]),
                                              op=ALU.mult), reads=[H["otok"], H["oss"]], writes=[H["on"]])
        pb = pbf.next()
        for bi in range(nblk):
            k.op("pe", lambda e: e.transpose(out=pb[:, bi * L:(bi + 1) * L], in_=H["on"][0:L, bi, :],
                                             identity=ident_b[0:L, 0:L]),
                 reads=[H["on"], ident_b], writes=[pb], inc=(bi == nblk - 1))
        n = nblk * L
        k.op("dve", lambda e: e.scalar_tensor_tensor(out=H["oT"][:, 0:n], in0=pb[:, 0:n], scalar=gnorm[:, 0:1],
                                                     in1=zsil[a][:, t0:t0 + n], op0=ALU.mult, op1=ALU.mult),
             reads=[pb, gnorm, zsil[a]], writes=[H["oT"]])
        k.dma("sp", S["oT"][hv * 128:(hv + 1) * 128, t0:t0 + n], H["oT"][:, 0:n], reads=[H["oT"]], writes=[dbuf["oT"]])

    kd_r = Ring([0, 1])

    def mk_kdec(H, hd, b, L):
        slot = kd_r.next()
        k.op("pool", lambda e: e.tensor_scalar(out=H["kdec"][0:L, slot, :], in0=k_tok[0:L, b, :],
                                               scalar1=hd[0:L, 3, b:b + 1], scalar2=None, op0=ALU.mult),
             reads=[k_tok, hd], writes=[H["kdec"]])
        return slot

    def scan(hv, hd):
        a = 0
        H = heads[0]
        k.op("pool", lambda e: e.memset(H["S"][:], 0.0), writes=[H["S"]])
        k.op("pool", lambda e: e.memset(H["Sb"][:], 0.0), writes=[H["Sb"]])
        for b in range(NBP):
            kslot = mk_kdec(H, hd, b, 64)
            pA = pf.next()
            k.op("pe", lambda e: e.matmul(pA[0:64, 0:128], lhsT=kqfm[:, b, 0:64], rhs=H["Sb"][:, :], start=True, stop=True),
                 reads=[kqfm, H["Sb"]], writes=[pA])
            k.op("dve", lambda e: e.scalar_tensor_tensor(out=H["r"][:, :], in0=pA[0:64, 0:128], scalar=hd[:, 2, b:b + 1],
                                                         in1=vb_tok[a][:, b, :], op0=ALU.mult, op1=ALU.add),
                 reads=[pA, hd, vb_tok[a]], writes=[H["r"]])
            pB_ = pf.next()
            k.op("pe", lambda e: e.matmul(pB_[0:64, 0:128], lhsT=H["TT"][:, b, :], rhs=H["r"][:, :], start=True, stop=True),
                 reads=[H["TT"], H["r"]], writes=[pB_])
            k.op("act", lambda e: e.copy(out=H["u"][:, :], in_=pB_[0:64, 0:128]), reads=[pB_], writes=[H["u"]])
            pO = pf.next()
            k.op("pe", lambda e: e.matmul(pO[0:64, 0:128], lhsT=H["qdec"][:, b, :], rhs=H["Sb"][:, :], start=True, stop=False),
                 reads=[H["qdec"], H["Sb"]], writes=[pO], inc=False)
            k.op("pe", lambda e: e.matmul(pO[0:64, 0:128], lhsT=H["AQT"][:, b, :], rhs=H["u"][:, :], start=False, stop=True),
                 reads=[H["AQT"], H["u"]], writes=[pO])
            pS = pf.next()
            k.op("pe", lambda e: e.matmul(pS[:, 0:128], lhsT=H["kdec"][:, kslot, :], rhs=H["u"][:, :], start=True, stop=True),
                 reads=[H["kdec"], H["u"]], writes=[pS])
            k.op("dve", lambda e: e.scalar_tensor_tensor(out=H["S"][:, :], in0=H["S"][:, :], scalar=H["egl"][:, b, 0:1],
                                                         in1=pS[:, 0:128], op0=ALU.mult, op1=ALU.add),
                 reads=[H["S"], H["egl"], pS], writes=[H["S"]])
            k.op("act", lambda e: e.copy(out=H["Sb"][:, :], in_=H["S"][:, :]), reads=[H["S"]], writes=[H["Sb"]])
            k.op("act", lambda e: e.copy(out=H["otok"][:, b % 8, :], in_=pO[0:64, 0:128]), reads=[pO], writes=[H["otok"]])
            if b % 8 == 7:
                o_finish(a, hv, (b - 7) * 64, 8, 64)
        k.dma("sp", O["o_gdn_p"][hv], H["S"][:, :], reads=[H["S"]], writes=[dbuf["o_gdn_p"]])
        L = NSAMP
        b = NBP
        kslot = mk_kdec(H, hd, b, L)
        k.op("pool", lambda e: e.memset(H["kzp"][:], 0.0), writes=[H["kzp"]])
        k.op("pool", lambda e: e.memset(H["qzp"][:], 0.0), writes=[H["qzp"]])
        for s in range(NS):
            k.dma("sp", H["Ss"][s][:, :], I["st_gdn"][s, hv], writes=[H["Ss"][s]])
            k.op("act", lambda e: e.copy(out=H["Ssb"][s][:, :], in_=H["Ss"][s][:, :]), reads=[H["Ss"][s]], writes=[H["Ssb"][s]])
            k.op("pool", lambda e: e.tensor_copy(out=H["kzp"][:, s, s * TS:(s + 1) * TS], in_=kqfm[:, b, s * TS:(s + 1) * TS]),
                 reads=[kqfm], writes=[H["kzp"]])
            k.op("pool", lambda e: e.tensor_copy(out=H["qzp"][:, s, s * TS:(s + 1) * TS], in_=H["qdec"][:, b, s * TS:(s + 1) * TS]),
                 reads=[H["qdec"]], writes=[H["qzp"]])
            k.op("dve", lambda e: e.tensor_scalar(out=H["kds"][0:L, s, :], in0=H["kdec"][0:L, kslot, :],
                                                  scalar1=rowmask_s[0:L, s:s + 1], scalar2=None, op0=ALU.mult),
                 reads=[H["kdec"], rowmask_s], writes=[H["kds"]])
        pA = pf.next()
        for s in range(NS):
            k.op("pe", lambda e: e.matmul(pA[0:L, 0:128], lhsT=H["kzp"][:, s, :], rhs=H["Ssb"][s][:, :],
                                          start=(s == 0), stop=(s == NS - 1)),
                 reads=[H["kzp"], H["Ssb"][s]], writes=[pA], inc=(s == NS - 1))
        k.op("dve", lambda e: e.scalar_tensor_tensor(out=H["r"][0:L, :], in0=pA[0:L, 0:128], scalar=hd[0:L, 2, b:b + 1],
                                                     in1=vb_tok[a][0:L, b, :], op0=ALU.mult, op1=ALU.add),
             reads=[pA, hd, vb_tok[a]], writes=[H["r"]])
        pB_ = pf.next()
        k.op("pe", lambda e: e.matmul(pB_[0:L, 0:128], lhsT=H["TT"][0:L, b, 0:L], rhs=H["r"][0:L, :], start=True, stop=True),
             reads=[H["TT"], H["r"]], writes=[pB_])
        k.op("act", lambda e: e.copy(out=H["u"][0:L, :], in_=pB_[0:L, 0:128]), reads=[pB_], writes=[H["u"]])
        pO = pf.next()
        for s in range(NS):
            k.op("pe", lambda e: e.matmul(pO[0:L, 0:128], lhsT=H["qzp"][:, s, :], rhs=H["Ssb"][s][:, :],
                                          start=(s == 0), stop=False),
                 reads=[H["qzp"], H["Ssb"][s]], writes=[pO], inc=False)
        k.op("pe", lambda e: e.matmul(pO[0:L, 0:128], lhsT=H["AQT"][0:L, b, 0:L], rhs=H["u"][0:L, :], start=False, stop=True),
             reads=[H["AQT"], H["u"]], writes=[pO])
        k.op("act", lambda e: e.copy(out=H["otok"][0:L, 0, :], in_=pO[0:L, 0:128]), reads=[pO], writes=[H["otok"]])
        for s in range(NS):
            pS = pf.next()
            k.op("pe", lambda e: e.matmul(pS[:, 0:128], lhsT=H["kds"][0:L, s, :], rhs=H["u"][0:L, :], start=True, stop=True),
                 reads=[H["kds"], H["u"]], writes=[pS])
            k.op("dve", lambda e: e.scalar_tensor_tensor(out=H["Ss"][s][:, :], in0=H["Ss"][s][:, :], scalar=H["egl"][:, b, s:s + 1],
                                                         in1=pS[:, 0:128], op0=ALU.mult, op1=ALU.add),
                 reads=[H["Ss"][s], H["egl"], pS], writes=[H["Ss"][s]])
            k.dma("sp", O["o_gdn_s"][s, hv], H["Ss"][s][:, :], reads=[H["Ss"][s]], writes=[dbuf["o_gdn_s"]])
        o_finish(a, hv, TP, 1, L)

    hd_next = load_hd(0)
    for j in range(NKH):
        proj_conv_silu(j * 128, cq)
        l2norm_to(cq, 1, 128 ** -0.5)
        proj_conv_silu(KD + j * 128, cq)
        l2norm_to(cq, 0, 1.0)

        def ksrc(bidx, t0, L):
            return kqfm[:, bidx, 0:L]
        ksrc.reads = [kqfm]

        def kevac(g0, gb, pb):
            nfull = sum(1 for (_, L) in gb if L == 64)
            if nfull:
                k.op("act", lambda e: e.copy(out=k_tok[:, g0:g0 + nfull, :],
                                             in_=pb[0:64, 0:nfull * 128].rearrange("p (b c) -> p b c", c=128)),
                     reads=[pb], writes=[k_tok])
            if nfull < len(gb):
                k.op("act", lambda e: e.copy(out=k_tok[0:NSAMP, g0 + nfull, :], in_=pb[0:NSAMP, nfull * 128:(nfull + 1) * 128]),
                     reads=[pb], writes=[k_tok])
        to_tok(ksrc, kevac)
        for a in range(2):
            hv = 2 * j + a
            hd = hd_next
            if hv + 1 < NVH:
                hd_next = load_hd(hv + 1)
            proj_conv_silu(2 * KD + hv * 128, vs)

            def vsrc(bidx, t0, L):
                return vs[:, t0:t0 + L]
            vsrc.reads = [vs]

            def vevac(g0, gb, pb, hd=hd):
                nfull = sum(1 for (_, L) in gb if L == 64)
                if nfull:
                    k.op("dve", lambda e: e.tensor_tensor(
                        out=vb_tok[0][:, g0:g0 + nfull, :],
                        in0=pb[0:64, 0:nfull * 128].rearrange("p (b c) -> p b c", c=128),
                        in1=hd[:, 0, g0:g0 + nfull].unsqueeze(2).to_broadcast([64, nfull, 128]), op=ALU.mult),
                        reads=[pb, hd], writes=[vb_tok[0]])
                if nfull < len(gb):
                    k.op("dve", lambda e: e.tensor_scalar(
                        out=vb_tok[0][0:NSAMP, g0 + nfull, :], in0=pb[0:NSAMP, nfull * 128:(nfull + 1) * 128],
                        scalar1=hd[0:NSAMP, 0, g0 + nfull:g0 + nfull + 1], scalar2=None, op0=ALU.mult),
                        reads=[pb, hd], writes=[vb_tok[0]])
            to_tok(vsrc, vevac)
            wt = load_w(CONVD + hv * 128)

            def zcons(ci, t0, n, ps):
                k.op("act", lambda e: e.activation(out=zsil[0][:, t0:t0 + n], in_=ps[:, 0:n], func=AF.Silu),
                     reads=[ps], writes=[zsil[0]])
            proj(wt, zcons)
            precompute(hv, hd)
            scan(hv, hd)

    nrow = (1 + NS) * 3
    cst_r = Ring([k.sb(f"g_cst{i}", [16, 512], F32) for i in range(2)])
    for g0 in range(0, NCT, 4):
        ps = pf.next()
        cst = cst_r.next()
        for q in range(4):
            ct = g0 + q
            k.op("pe", lambda e: e.transpose(out=ps[0:16, q * 128:(q + 1) * 128], in_=convout[:, ct, :],
                                             identity=ident_f[:, :]),
                 reads=[convout, ident_f], writes=[ps], inc=(q == 3))
        k.op("dve", lambda e: e.tensor_copy(out=cst[:, :], in_=ps[0:16, 0:512]), reads=[ps], writes=[cst])
        k.dma("sp", O["o_gdnc"][:, g0 * 128:(g0 + 4) * 128], cst[0:nrow, :], reads=[cst], writes=[dbuf["o_gdnc"]])
    if cfg.dbg:
        k.dma("sp", O["dbg_o"], S["oT"], reads=[dbuf["oT"]], writes=[dbuf["dbg_o"]])


_NC_CACHE = {}


def kernel(x_prompt, x_sample, state_gdn, state_gdn_conv, cache_nsa_kv, state_nsa_win, state_ffn_conv,
           page_table, norm_mix, norm_ffn, gdn_w_in, gdn_conv_w, gdn_A_log, gdn_dt_bias, gdn_norm, gdn_w_out,
           nsa_w_in, nsa_q_norm, nsa_k_norm, nsa_cmp_pe, nsa_cmp_w1, nsa_cmp_w2, rel_bias, nsa_w_out,
           ffn_w_up, ffn_conv_w, ffn_conv_b, ffn_w_down):
    cfg = Cfg()
    f = lambda a: np.ascontiguousarray(np.asarray(a))
    x_prompt, x_sample = f(x_prompt), f(x_sample)
    consts = host_consts(cfg)
    nconsts = {k_: v for k_, v in nsa_host_consts(cfg, cfg.PAST).items() if not k_.startswith("dims")}
    cache_r = f(cache_nsa_kv)[0].reshape(cfg.NPOOL * 128, 2048)
    n = 8
    NS = cfg.NS
    in_maps = []
    for c in range(n):
        b = c // 2
        sl = slice(NS * c, NS * (c + 1))
        m = dict(
            xin=np.ascontiguousarray(np.concatenate([x_prompt[b], x_sample[sl].reshape(-1, cfg.D)], 0)),
            norm_mix=f(norm_mix), norm_ffn=f(norm_ffn),
            gdn_w_in=f(gdn_w_in)[0], gdn_conv_w=f(gdn_conv_w)[0], gdn_A_log=f(gdn_A_log), gdn_dt_bias=f(gdn_dt_bias),
            gdn_norm=f(gdn_norm), gdn_w_out=f(gdn_w_out)[0],
            st_gdn=np.ascontiguousarray(f(state_gdn)[0, sl]),
            st_gdn_conv=np.ascontiguousarray(f(state_gdn_conv)[0, sl].reshape(NS * 3, cfg.CONVD)),
            ffn_w_up=f(ffn_w_up), ffn_conv_w=f(ffn_conv_w), ffn_conv_b=f(ffn_conv_b), ffn_w_down=f(ffn_w_down),
            st_ffn_conv=np.ascontiguousarray(f(state_ffn_conv)[:, sl].reshape(2, NS * 2, cfg.DFF)),
            nsa_w_in=f(nsa_w_in)[0], nsa_q_norm=f(nsa_q_norm), nsa_k_norm=f(nsa_k_norm),
            st_win=np.ascontiguousarray(f(state_nsa_win)[0, sl].reshape(NS, 512, 1024)),
            nsa_cmp_pe=f(nsa_cmp_pe)[0], nsa_cmp_w1=f(nsa_cmp_w1)[0], nsa_cmp_w2=f(nsa_cmp_w2)[0],
            rel_bias=f(rel_bias), nsa_w_out=f(nsa_w_out)[0],
            cache=cache_r, page_table=np.ascontiguousarray(f(page_table)[sl]).astype(np.int32),
        )
        m.update(nconsts)
        m.update(consts)
        in_maps.append(m)
    if "nc" not in _NC_CACHE:
        _NC_CACHE["nc"] = build(cfg)
    nc = _NC_CACHE["nc"]
    res = run_bass_kernel_spmd(nc, in_maps, core_ids=list(range(n)))
    R = res.results
    f32 = np.float32
    B, SEQ, DB, DS = 4, 2048, 32, 8
    p_gdn = np.stack([R[2 * b]["o_gdn_p"] for b in range(B)])[None].astype(f32)
    p_gdn_conv = np.stack([R[2 * b]["o_gdnc"][0:3] for b in range(B)])[None].astype(f32)
    s_gdn = np.concatenate([R[c]["o_gdn_s"] for c in range(n)], 0)[None].astype(f32)
    s_gdn_conv = np.concatenate([R[c]["o_gdnc"][3:3 + 3 * NS].reshape(NS, 3, -1) for c in range(n)], 0)[None].astype(f32)
    TP = cfg.TP
    y_prompt = np.stack([R[2 * b]["y"][:TP] for b in range(B)]).astype(f32)
    y_sample = np.concatenate([R[c]["y"][TP:].reshape(NS, DS, cfg.D) for c in range(n)], 0).astype(f32)
    p_nsa_kv = np.stack([R[2 * b]["o_kv"][:TP].reshape(SEQ, 4, 4, 128) for b in range(B)])[None].astype(f32)
    p_nsa_win = np.stack([R[2 * b]["o_pwin"].reshape(512, 2, 4, 128) for b in range(B)])[None].astype(f32)
    p_ffn_conv = np.stack([np.stack([R[2 * b]["o_ffnc"][li][0:2] for b in range(B)]) for li in range(2)]).astype(f32)
    s_nsa_kv = np.concatenate([R[c]["o_kv"][TP:].reshape(NS, DS, 4, 4, 128) for c in range(n)], 0)[None].astype(f32)
    s_nsa_win = np.concatenate([R[c]["o_swin"].reshape(NS, 512, 2, 4, 128) for c in range(n)], 0)[None].astype(f32)
    s_ffn_conv = np.stack([np.concatenate([R[c]["o_ffnc"][li][2:2 + 2 * NS].reshape(NS, 2, -1) for c in range(n)], 0)
                           for li in range(2)]).astype(f32)
    return (y_prompt, y_sample, p_gdn, p_gdn_conv, p_nsa_kv, p_nsa_win, p_ffn_conv,
            s_gdn, s_gdn_conv, s_nsa_kv, s_nsa_win, s_ffn_conv)


class Unpager:
    def __init__(self, k, cfg, I, S, dbuf):
        self.k, self.cfg, self.I, self.S, self.dbuf = k, cfg, I, S, dbuf
        self.NPG = cfg.PAST // 128
        self.total = cfg.NS * self.NPG
        self.done = 0

    def setup(self):
        k, cfg, I = self.k, self.cfg, self.I
        n = self.total
        self.idx = k.sb("u_idx", [128, n], I32)
        ptb = k.sb("u_ptb", [128, n], I32)
        k.dma("sp", ptb[:], I["page_table"].rearrange("s g -> (s g)").partition_broadcast(128), writes=[ptb])
        ptf = k.sb("u_ptf", [128, n], F32)
        iota_f = k.sb("u_iota", [128, 1], F32)
        k.dma("sp", iota_f[:], I["n_iota"], writes=[iota_f])
        k.op("dve", lambda e: e.tensor_copy(out=ptf[:], in_=ptb[:]), reads=[ptb], writes=[ptf])
        k.op("dve", lambda e: e.tensor_scalar(out=ptf[:], in0=ptf[:], scalar1=128.0, scalar2=iota_f[:, 0:1], op0=ALU.mult, op1=ALU.add),
             reads=[ptf, iota_f], writes=[ptf])
        k.op("dve", lambda e: e.tensor_copy(out=self.idx[:], in_=ptf[:]), reads=[ptf], writes=[self.idx])
        self.raw_r = Ring([k.sb(f"u_raw{i}", [128, 2048], F32) for i in range(2)])
        self.bf_r = Ring([k.sb(f"u_bf{i}", [128, 2048], BF16) for i in range(2)])

    def step(self, npages):
        k, I, S = self.k, self.I, self.S
        for _ in range(npages):
            if self.done >= self.total:
                return
            i = self.done
            self.done += 1
            raw = self.raw_r.next()
            k.op("pool", lambda e: e.indirect_dma_start(
                out=raw[:, :], out_offset=None, in_=I["cache"][:, :],
                in_offset=bass.IndirectOffsetOnAxis(ap=self.idx[:, i:i + 1], axis=0)),
                reads=[self.idx, self.dbuf["cache"]], writes=[raw], inc=False, dma=True)
            bf = self.bf_r.next()
            eng = k.ev_eng()
            if eng == "act":
                k.op("act", lambda e: e.copy(out=bf[:, :], in_=raw[:, :]), reads=[raw], writes=[bf])
            else:
                k.op("dve", lambda e: e.tensor_copy(out=bf[:, :], in_=raw[:, :]), reads=[raw], writes=[bf])
            k.dma("sp", S["past"][i * 128:(i + 1) * 128, :], bf[:, :], reads=[bf], writes=[self.dbuf["past"]])

    def finish(self):
        self.step(self.total)


def linear_tm(k, cfg, pf, dbuf, name, src_name, src_ap, w_name, w_ap, Kdim, resid_name, resid_ap, out_name, out_ap):
    D = cfg.D
    NKt = Kdim // 128
    with k.scope():
        wo_r = Ring([k.sb(f"{name}_w{i}", [128, NKt, 512], BF16) for i in range(2)])
        st_r = Ring([k.sb(f"{name}_s{i}", [128, NKt, 128], BF16) for i in range(3)])
        xr_r = Ring([k.sb(f"{name}_x{i}", [128, 512], F32) for i in range(3)])
        ho_r = Ring([k.sb(f"{name}_h{i}", [128, 512], F32) for i in range(3)])
        srcv = src_ap.rearrange("(kt p) t -> p kt t", p=128)
        def load_wo(c):
            wo = wo_r.next()
            k.dma("pool", wo[:], w_ap[:, c * 512:(c + 1) * 512].rearrange("(kt p) n -> p kt n", p=128),
                  reads=[dbuf[w_name]], writes=[wo])
            return wo
        wo_next = load_wo(0)
        for c in range(D // 512):
            wo = wo_next
            if c + 1 < D // 512:
                wo_next = load_wo(c + 1)
            for (r0, R) in cfg.ttiles():
                st = st_r.next()
                k.dma("sp", st[:, :, 0:R], srcv[:, :, r0:r0 + R], reads=[dbuf[src_name]], writes=[st])
                xr = xr_r.next()
                k.dma("sp", xr[0:R, :], resid_ap[r0:r0 + R, c * 512:(c + 1) * 512], reads=[dbuf[resid_name]], writes=[xr])
                ps = pf.next()
                for kt in range(NKt):
                    k.op("pe", lambda e: e.matmul(ps[0:R, :], lhsT=st[:, kt, 0:R], rhs=wo[:, kt, :],
                                                  start=(kt == 0), stop=(kt == NKt - 1)),
                         reads=[st, wo], writes=[ps], inc=(kt == NKt - 1))
                ho = ho_r.next()
                k.op("dve", lambda e: e.tensor_tensor(out=ho[0:R, :], in0=ps[0:R, :], in1=xr[0:R, :], op=ALU.add),
                     reads=[ps, xr], writes=[ho])
                k.dma("act", out_ap[r0:r0 + R, c * 512:(c + 1) * 512], ho[0:R, :], reads=[ho], writes=[dbuf[out_name]])


def ffn_layer(k, cfg, I, O, S, dbuf, pf, pbf, C, li, src_name, src_ap, out_name, out_ap, norm_fm, hook=None):
    D, KT, NT, TP, NS, TS, DFF = cfg.D, cfg.KT, cfg.NT, cfg.TP, cfg.NS, cfg.TS, cfg.DFF
    NSAMP = cfg.NSAMP
    ident_f = C["ident_f"]
    NM = DFF // 128
    W1 = 2
    PADW = W1 + TP + NS * (TS + W1)
    SOFF = W1 + TP
    SW = TS + W1
    chunks = cfg.chunks()
    with k.scope():
        xn = k.sb(f"f{li}_xn", [128, KT, NT], BF16)
        norm_fm(src_name, I["norm_ffn"][li:li + 1, :], xn)
        cw = k.sb(f"f{li}_cw", [128, NM, 3], F32)
        k.dma("sp", cw[:], I["ffn_conv_w"][li].rearrange("(t p) w -> p t w", p=128), writes=[cw])
        cb = k.sb(f"f{li}_cb", [128, NM], F32)
        k.dma("sp", cb[:], I["ffn_conv_b"][li].rearrange("(t p) -> p t", p=128), writes=[cb],
              allow_slow_non_contiguous=True)
        wt_r = Ring([k.sb(f"f{li}_wt{i}", [128, KT, 128], BF16) for i in range(4)])
        stc_r = Ring([k.sb(f"f{li}_stc{i}", [NS * W1, 128], F32) for i in range(2)])
        prep = k.sb(f"f{li}_prep", [128, PADW], F32)
        k.op("pool", lambda e: e.memset(prep[:], 0.0), writes=[prep])
        cq = k.sb(f"f{li}_cq", [128, NT], F32)
        act_r = Ring([k.sb(f"f{li}_act{i}", [128, NT], BF16) for i in range(2)])
        convout = k.sb(f"f{li}_convout", [128, NM, 16], F32)
        k.op("pool", lambda e: e.memset(convout[:], 0.0), writes=[convout])
        wup = I["ffn_w_up"][li]
        if hook is not None:
            hook.setup()

        def load_w(col0):
            wt = wt_r.next()
            k.dma("pool", wt[:], wup[:, col0:col0 + 128].rearrange("(kt p) c -> p kt c", p=128),
                  reads=[dbuf["ffn_w_up"]], writes=[wt])
            return wt

        def proj(wt, consumer):
            for ci, (t0, n) in enumerate(chunks):
                ps = pf.next()
                for kt in range(KT):
                    k.op("pe", lambda e: e.matmul(ps[:, 0:n], lhsT=wt[:, kt, :], rhs=xn[:, kt, t0:t0 + n],
                                                  start=(kt == 0), stop=(kt == KT - 1)),
                         reads=[wt, xn], writes=[ps], inc=(kt == KT - 1))
                consumer(ci, t0, n, ps)

        def sv(lo, hi):
            return prep[:, SOFF:SOFF + NS * SW].rearrange("p (s w) -> p s w", w=SW)[:, :, lo:hi]

        w_next = (load_w(0), load_w(DFF))
        for m in range(NM):
            wg, wv = w_next
            if m + 1 < NM:
                w_next = (load_w((m + 1) * 128), load_w(DFF + (m + 1) * 128))
            stc = stc_r.next()
            k.dma("sp", stc[:], I["st_ffn_conv"][li][:, m * 128:(m + 1) * 128], writes=[stc])
            pst = pf.next()
            k.op("pe", lambda e: e.transpose(out=pst[:, 0:NS * W1], in_=stc[0:NS * W1, 0:128],
                                             identity=ident_f[0:NS * W1, 0:NS * W1]),
                 reads=[stc, ident_f], writes=[pst])
            k.op("dve", lambda e: e.tensor_copy(out=sv(0, W1), in_=pst[:, 0:NS * W1].rearrange("p (s w) -> p s w", w=W1)),
                 reads=[pst], writes=[prep])

            def cons(ci, t0, n, ps):
                eng = k.ev_eng()
                if n == 512:
                    dv = prep[:, W1 + t0:W1 + t0 + n]
                    sv_ = ps[:, 0:n]
                else:
                    dv = sv(W1, SW)
                    sv_ = ps[:, 0:n].rearrange("p (s w) -> p s w", w=TS)
                if eng == "act":
                    k.op("act", lambda e: e.copy(out=dv, in_=sv_), reads=[ps], writes=[prep])
                else:
                    k.op("dve", lambda e: e.tensor_copy(out=dv, in_=sv_), reads=[ps], writes=[prep])
            proj(wg, cons)
            k.op("pool", lambda e: e.tensor_copy(out=convout[:, m, 0:W1], in_=prep[:, W1 + TP - W1:W1 + TP]),
                 reads=[prep], writes=[convout])
            k.op("pool", lambda e: e.tensor_copy(out=convout[:, m, W1:W1 + NS * W1].rearrange("p (s w) -> p s w", w=W1),
                                                 in_=sv(TS, SW)), reads=[prep], writes=[convout])
            for (ov, mkv) in ((cq[:, 0:TP], lambda i: prep[:, i:i + TP]),
                              (cq[:, TP:NT].rearrange("p (s w) -> p s w", w=TS), lambda i: sv(i, i + TS))):
                k.op("dve", lambda e: e.tensor_scalar(out=ov, in0=mkv(2), scalar1=cw[:, m, 2:3], scalar2=cb[:, m:m + 1],
                                                      op0=ALU.mult, op1=ALU.add), reads=[prep, cw, cb], writes=[cq])
                for i in range(2):
                    k.op("dve", lambda e: e.scalar_tensor_tensor(out=ov, in0=mkv(i), scalar=cw[:, m, i:i + 1], in1=ov,
                                                                 op0=ALU.mult, op1=ALU.add),
                         reads=[prep, cw, cq], writes=[cq])
            k.op("act", lambda e: e.activation(out=cq[:], in_=cq[:], func=AF.Silu), reads=[cq], writes=[cq])
            act = act_r.next()

            def vcons(ci, t0, n, ps):
                k.op("dve", lambda e: e.tensor_tensor(out=act[:, t0:t0 + n], in0=ps[:, 0:n], in1=cq[:, t0:t0 + n], op=ALU.mult),
                     reads=[ps, cq], writes=[act])
            proj(wv, vcons)
            k.dma("act", S["actT"][m * 128:(m + 1) * 128, :], act[:], reads=[act], writes=[dbuf["actT"]])
            if hook is not None:
                hook.step(-(-hook.total // NM))
        if hook is not None:
            hook.finish()
        nrow = (1 + NS) * W1
        cst_r = Ring([k.sb(f"f{li}_cst{i}", [16, 512], F32) for i in range(2)])
        for g0 in range(0, NM, 4):
            ps = pf.next()
            cst = cst_r.next()
            ng = min(4, NM - g0)
            for q in range(ng):
                k.op("pe", lambda e: e.transpose(out=ps[0:16, q * 128:(q + 1) * 128], in_=convout[:, g0 + q, :],
                                                 identity=ident_f[:, :]),
                     reads=[convout, ident_f], writes=[ps], inc=(q == ng - 1))
            k.op("dve", lambda e: e.tensor_copy(out=cst[:, 0:ng * 128], in_=ps[0:16, 0:ng * 128]), reads=[ps], writes=[cst])
            k.dma("sp", O["o_ffnc"][li][:, g0 * 128:(g0 + ng) * 128], cst[0:nrow, 0:ng * 128], reads=[cst],
                  writes=[dbuf["o_ffnc"]])
    linear_tm(k, cfg, pf, dbuf, f"fd{li}", "actT", S["actT"], "ffn_w_down", I["ffn_w_down"][li], DFF,
              src_name, src_ap, out_name, out_ap)


NEG = -30000.0
NSA_Z = 2063
NSA_GL = 5120
NSA_OFF = 384
NSA_X = 1920
NSA_XW = 1408


def t5_bucket_np(d):
    d = np.maximum(d, 0)
    f32 = np.float32
    scale = (32 - 16) / math.log(1024 / 16)
    large = 16 + (np.log(np.maximum(d, 1).astype(f32) / f32(16)) * f32(scale)).astype(np.int32)
    return np.where(d < 16, d, np.minimum(large, 31))


def nsa_host_consts(cfg, P):
    c = {}
    TP, TS, NS = cfg.TP, cfg.TS, cfg.NS
    f32 = np.float32
    dd = np.arange(NSA_GL) - NSA_Z
    oh = np.zeros((33, NSA_GL), f32)
    bk = t5_bucket_np(dd)
    for i in range(NSA_GL):
        if dd[i] >= 0:
            oh[bk[i], i] = 1.0
        else:
            oh[32, i] = 1.0
    c["n_oh"] = oh
    jm = np.zeros((128, 128), f32)
    jm[np.arange(128), 127 - np.arange(128)] = 1.0
    c["n_J"] = jm
    kk = np.arange(128)[:, None]
    xx = np.arange(NSA_XW)[None, :]
    c["n_wm"] = np.where((xx - kk - NSA_OFF) < 512, 0.0, NEG).astype(f32)

    def sel_consts(n_all, qpos, name):
        n_sb = -(-n_all // 64)
        ns = n_all // 16
        ncb = ns - 1
        c_end = np.arange(ncb) * 16 + 31
        c_start = c_end - 31
        sb_start = np.arange(n_sb) * 64
        ovl = np.maximum(np.minimum(c_end[:, None], sb_start[None, :] + 63) - np.maximum(c_start[:, None], sb_start[None, :]) + 1, 0).astype(f32) / 32
        ntile = -(-ncb // 128)
        ovl_t = np.zeros((ntile * 128, n_sb), f32)
        ovl_t[:ncb] = ovl
        c[name + "_ovl"] = ovl_t.reshape(ntile, 128, n_sb)
        cur = qpos // 64
        blk = np.arange(n_sb)
        ok = sb_start[None, :] <= qpos[:, None]
        forced = (blk[None, :] == 0) | (blk[None, :] == cur[:, None]) | (blk[None, :] == cur[:, None] - 1)
        c[name + "_add"] = np.where(ok, np.where(forced, 1000.0, 0.0), -1e30).astype(f32)
        c[name + "_valid"] = ok.astype(f32)
        nk = -(-n_all // 128) * 128
        ex = np.zeros((n_sb, nk), f32)
        keys = np.arange(n_all)
        ex[keys // 64, keys] = 1.0
        c[name + "_exp"] = ex.astype(ml_dtypes.bfloat16)
        return n_sb, ncb
    c["dims_p"] = sel_consts(TP, np.arange(TP), "np")
    c["dims_s"] = sel_consts(P + TS, P + np.arange(TS), "ns")
    c["n_iota"] = np.arange(128, dtype=np.float32).reshape(128, 1)
    return c


def nsa_proj(k, cfg, I, O, S, dbuf, pf, pbf, C, norm_fm, src_name):
    D, KT, NT, TP, NS, TS = cfg.D, cfg.KT, cfg.NT, cfg.TP, cfg.NS, cfg.TS
    ones_b = C["ones_b"]
    chunks = cfg.chunks()
    WIN = 512
    with k.scope():
        xn = k.sb("n_xn", [128, KT, NT], BF16)
        norm_fm(src_name, I["norm_mix"][1:2, :], xn)
        epsb = k.sb("n_epsb", [128, 1], F32)
        k.op("pool", lambda e: e.memset(epsb[:], RMS_EPS), writes=[epsb])
        qg = k.sb("n_qg", [128, 1], F32)
        k.dma("sp", qg[:], I["nsa_q_norm"].rearrange("o d -> d o"), writes=[qg])
        k.op("dve", lambda e: e.tensor_scalar(out=qg[:], in0=qg[:], scalar1=128 ** -0.5, scalar2=None, op0=ALU.mult),
             reads=[qg], writes=[qg])
        import os
        parts = os.environ.get("NSA_PARTS", "q,kv,win,g").split(",")
        with k.scope():
            wt_r = Ring([k.sb(f"n_wt{i}", [128, KT, 128], BF16) for i in range(2)])
            qraw = k.sb("n_qraw", [128, NT], F32)
            sqb = k.sb("n_sqb", [128, 512], BF16)
            rinv = k.sb("n_rinv", [128, 512], F32)
            qn_r = Ring([k.sb(f"n_qn{i}", [128, NT], BF16) for i in range(2)])
            for h in range(16 if "q" in parts else 0):
                wt = wt_r.next()
                k.dma("pool", wt[:], I["nsa_w_in"][:, h * 128:(h + 1) * 128].rearrange("(kt p) c -> p kt c", p=128),
                      reads=[dbuf["nsa_w_in"]], writes=[wt])
                qn = qn_r.next()
                for (t0, n) in chunks:
                    ps = pf.next()
                    for kt in range(KT):
                        k.op("pe", lambda e: e.matmul(ps[:, 0:n], lhsT=wt[:, kt, :], rhs=xn[:, kt, t0:t0 + n],
                                                      start=(kt == 0), stop=(kt == KT - 1)),
                             reads=[wt, xn], writes=[ps], inc=(kt == KT - 1))
                    qlvl = int(os.environ.get("NSA_QLVL", "9"))
                    k.op("dve", lambda e: e.tensor_copy(out=qraw[:, t0:t0 + n], in_=ps[:, 0:n]), reads=[ps], writes=[qraw])
                    if qlvl < 1:
                        continue
                    k.op("act", lambda e: e.activation(out=sqb[:, 0:n], in_=ps[:, 0:n], func=AF.Square), reads=[ps], writes=[sqb])
                    ps2 = pf.next()
                    k.op("pe", lambda e: e.matmul(ps2[:, 0:n], lhsT=ones_b[:, :], rhs=sqb[:, 0:n], start=True, stop=True),
                         reads=[ones_b, sqb], writes=[ps2])
                    if qlvl < 2:
                        continue
                    k.op("act", lambda e: e.activation(out=rinv[:, 0:n], in_=ps2[:, 0:n], func=AF.Sqrt, scale=1.0 / 128,
                                                       bias=epsb[:, 0:1]), reads=[ps2, epsb], writes=[rinv])
                    k.op("dve", lambda e: e.reciprocal(out=rinv[:, 0:n], in_=rinv[:, 0:n]), reads=[rinv], writes=[rinv])
                    if qlvl < 3:
                        continue
                    k.op("dve", lambda e: e.scalar_tensor_tensor(out=qn[:, t0:t0 + n], in0=qraw[:, t0:t0 + n], scalar=qg[:, 0:1],
                                                                 in1=rinv[:, 0:n], op0=ALU.mult, op1=ALU.mult),
                         reads=[qraw, qg, rinv], writes=[qn])
                if qlvl >= 4:
                    k.dma("sp", S["qT"][h * 128:(h + 1) * 128, :], qn[:], reads=[qn], writes=[dbuf["qT"]])
        with k.scope():
            wkv_r = Ring([k.sb(f"n_wkv{i}", [128, KT, 512], BF16) for i in range(2)])
            gain_k = k.sb("n_gaink", [128, 3, 128], F32)
            k.dma("sp", gain_k[:].rearrange("p a d -> p (a d)"),
                  I["nsa_k_norm"].rearrange("o a d -> o (a d)").partition_broadcast(128), writes=[gain_k])
            sqt = k.sb("n_sqt", [128, 512], F32)
            ss = k.sb("n_ss", [128, 8], F32)
            rows_r = Ring([k.sb(f"n_rows{i}", [128, 512], F32) for i in range(3)])
            for c in range(6 if "kv" in parts else 0):
                wc = wkv_r.next()
                k.dma("pool", wc[:], I["nsa_w_in"][:, 2048 + c * 512:2048 + (c + 1) * 512].rearrange("(kt p) c -> p kt c", p=128),
                      reads=[dbuf["nsa_w_in"]], writes=[wc])
                for (r0, R) in cfg.ttiles():
                    ps = pf.next()
                    for kt in range(KT):
                        k.op("pe", lambda e: e.matmul(ps[0:R, :], lhsT=xn[:, kt, r0:r0 + R], rhs=wc[:, kt, :],
                                                      start=(kt == 0), stop=(kt == KT - 1)),
                             reads=[xn, wc], writes=[ps], inc=(kt == KT - 1))
                    rows = rows_r.next()
                    if c in (2, 4):
                        gi = 1 if c == 2 else 2
                        k.op("act", lambda e: e.activation(out=sqt[0:R, :], in_=ps[0:R, :], func=AF.Square), reads=[ps], writes=[sqt])
                        k.op("dve", lambda e: e.reduce_sum(out=ss[0:R, 0:4], in_=sqt[0:R, :].rearrange("p (g d) -> p g d", d=128),
                                                           axis=AX.X), reads=[sqt], writes=[ss])
                        k.op("act", lambda e: e.activation(out=ss[0:R, 4:8], in_=ss[0:R, 0:4], func=AF.Sqrt, scale=1.0 / 128,
                                                           bias=epsb[0:R, 0:1]), reads=[ss, epsb], writes=[ss])
                        k.op("dve", lambda e: e.reciprocal(out=ss[0:R, 0:4], in_=ss[0:R, 4:8]), reads=[ss], writes=[ss])
                        k.op("dve", lambda e: e.tensor_tensor(out=sqt[0:R, :].rearrange("p (g d) -> p g d", d=128),
                                                              in0=ps[0:R, :].rearrange("p (g d) -> p g d", d=128),
                                                              in1=ss[0:R, 0:4].unsqueeze(2).to_broadcast([R, 4, 128]), op=ALU.mult),
                             reads=[ps, ss], writes=[sqt])
                        k.op("dve", lambda e: e.tensor_tensor(out=rows[0:R, :].rearrange("p (g d) -> p g d", d=128),
                                                              in0=sqt[0:R, :].rearrange("p (g d) -> p g d", d=128),
                                                              in1=gain_k[0:R, gi, :].unsqueeze(1).to_broadcast([R, 4, 128]), op=ALU.mult),
                             reads=[sqt, gain_k], writes=[rows])
                    else:
                        k.op("act", lambda e: e.copy(out=rows[0:R, :], in_=ps[0:R, :]), reads=[ps], writes=[rows])
                    if c < 4:
                        k.dma("sp", O["o_kv"][r0:r0 + R, c * 512:(c + 1) * 512], rows[0:R, :], reads=[rows], writes=[dbuf["o_kv"]])
                    else:
                        cc = (c - 4) * 512
                        k.dma("sp", S["win_new"][r0:r0 + R, cc:cc + 512], rows[0:R, :], reads=[rows], writes=[dbuf["win_new"]])
                        if r0 < TP and r0 >= TP - WIN:
                            k.dma("sp", O["o_pwin"][r0 - (TP - WIN):r0 - (TP - WIN) + R, cc:cc + 512], rows[0:R, :],
                                  reads=[rows], writes=[dbuf["o_pwin"]])
                        if r0 == TP:
                            for s in range(NS):
                                k.dma("sp", O["o_swin"][s, WIN - TS:WIN, cc:cc + 512], rows[s * TS:(s + 1) * TS, :],
                                      reads=[rows], writes=[dbuf["o_swin"]])
            swb_r = Ring([k.sb(f"n_swb{i}", [126, 4096], F32) for i in range(2)])
            for s in range(NS if "win" in parts else 0):
                swb = swb_r.next()
                k.dma("sp", swb[:, :], I["st_win"][s, TS:WIN, :].rearrange("(p a) c -> p (a c)", a=4),
                      reads=[dbuf["st_win"]], writes=[swb])
                k.dma("sp", O["o_swin"][s, 0:WIN - TS, :].rearrange("(p a) c -> p (a c)", a=4), swb[:, :],
                      reads=[swb], writes=[dbuf["o_swin"]])
            wg = k.sb("n_wg", [128, KT, 48], BF16)
            k.dma("pool", wg[:], I["nsa_w_in"][:, 5120:5168].rearrange("(kt p) c -> p kt c", p=128),
                  reads=[dbuf["nsa_w_in"]], writes=[wg])
            gt_r = Ring([k.sb(f"n_gt{i}", [128, 48], F32) for i in range(2)])
            for (r0, R) in (cfg.ttiles() if "g" in parts else []):
                ps = pf.next()
                for kt in range(KT):
                    k.op("pe", lambda e: e.matmul(ps[0:R, 0:48], lhsT=xn[:, kt, r0:r0 + R], rhs=wg[:, kt, :],
                                                  start=(kt == 0), stop=(kt == KT - 1)),
                         reads=[xn, wg], writes=[ps], inc=(kt == KT - 1))
                gt = gt_r.next()
                k.op("act", lambda e: e.activation(out=gt[0:R, :], in_=ps[0:R, 0:48], func=AF.Sigmoid), reads=[ps], writes=[gt])
                k.dma("sp", S["gates"][r0:r0 + R, :], gt[0:R, :], reads=[gt], writes=[dbuf["gates"]])


def nsa_attn(k, cfg, I, O, S, dbuf, pf, pbf, C):
    D, KT, NT, TP, NS, TS = cfg.D, cfg.KT, cfg.NT, cfg.TP, cfg.NS, cfg.TS
    P = cfg.PAST
    NQT = TP // 128
    NCH = TP // 512
    n_sb_p = TP // 64
    ncb_p = TP // 16 - 1
    Z, GL, OFF, X, XW = NSA_Z, NSA_GL, NSA_OFF, NSA_X, NSA_XW
    ident_f, ident_b, ones_b, ones_f = C["ident_f"], C["ident_b"], C["ones_b"], C["ones_f"]
    ringN = Ring(pf.bufs[0:4])
    ring2 = Ring(pf.bufs[0:2])
    accO = pf.bufs[2:6]
    GELU_C = 1.5957691216057308

    def evac(dst, src, rd, wr):
        eng = k.ev_eng()
        if eng == "act":
            k.op("act", lambda e: e.copy(out=dst, in_=src), reads=rd, writes=wr)
        else:
            k.op("dve", lambda e: e.tensor_copy(out=dst, in_=src), reads=rd, writes=wr)

    with k.scope():
        Jm = k.sb("a_J", [128, 128], F32)
        k.dma("sp", Jm[:], I["n_J"], writes=[Jm])
        relx = k.sb("a_relx", [64, 16], F32)
        k.op("pool", lambda e: e.memset(relx[32:64, :], NEG), writes=[relx])
        k.dma("sp", relx[0:32, :], I["rel_bias"], writes=[relx])
        epsb = k.sb("a_epsb", [128, 1], F32)
        k.op("pool", lambda e: e.memset(epsb[:], RMS_EPS), writes=[epsb])
        with k.scope():
            oh = k.sb("a_oh", [33, GL], F32)
            k.dma("sp", oh[:], I["n_oh"], writes=[oh])
            gsb_r = Ring([k.sb(f"a_gsb{i}", [16, 512], F32) for i in range(2)])
            for c0 in range(0, GL, 512):
                ps = ringN.next()
                k.op("pe", lambda e: e.matmul(ps[0:16, :], lhsT=relx[0:33, :], rhs=oh[:, c0:c0 + 512], start=True, stop=True),
                     reads=[relx, oh], writes=[ps])
                gsb = gsb_r.next()
                k.op("dve", lambda e: e.tensor_copy(out=gsb[:, :], in_=ps[0:16, :]), reads=[ps], writes=[gsb])
                k.dma("sp", S["gvec"][:, c0:c0 + 512], gsb[:, :], reads=[gsb], writes=[dbuf["gvec"]])
        wm = k.sb("a_wm", [128, XW], F32)
        k.dma("sp", wm[:], I["n_wm"], writes=[wm])
        ovl_p = k.sb("a_ovlp", [128, n_sb_p], BF16)
        k.dma("pool", ovl_p[:], I["np_ovl"][0], writes=[ovl_p])
        add_p = k.sb("a_addp", [128, NQT, n_sb_p], F32)
        k.dma("sp", add_p[:], I["np_add"].rearrange("(t p) m -> p t m", p=128), writes=[add_p])
        val_p = k.sb("a_valp", [128, NQT, n_sb_p], F32)
        k.dma("sp", val_p[:], I["np_valid"].rearrange("(t p) m -> p t m", p=128), writes=[val_p])
        exp_p = k.sb("a_expp", [n_sb_p, TP], BF16)
        k.dma("sp", exp_p[:], I["np_exp"], writes=[exp_p])
        gates = k.sb("a_gates", [128, NQT, 48], F32)
        k.dma("sp", gates[:], S["gates"][0:TP, :].rearrange("(t p) c -> p t c", p=128), reads=[dbuf["gates"]], writes=[gates])
        w1 = k.sb("a_w1", [128, 2, 32, 256], BF16)
        w2 = k.sb("a_w2", [128, 2, 2, 128], BF16)
        for wch in range(2):
            k.dma("pool", w1[:, wch], I["nsa_cmp_w1"][wch].rearrange("(s d) e -> d s e", d=128), writes=[w1])
            k.dma("pool", w2[:, wch], I["nsa_cmp_w2"][wch].rearrange("(h e) d -> e h d", e=128), writes=[w2])
        peT = k.sb("a_peT", [128, 2, 32], BF16)
        k.dma("pool", peT[:], I["nsa_cmp_pe"].rearrange("w s d -> d w s"), writes=[peT], allow_slow_non_contiguous=True)
        peb = k.sb("a_peb", [128, 2, 2], F32)
        for wch in range(2):
            for half in range(2):
                ps = ringN.next()
                for s in range(32):
                    k.op("pe", lambda e: e.matmul(ps[:, 0:1], lhsT=w1[:, wch, s, half * 128:(half + 1) * 128],
                                                  rhs=peT[:, wch, s:s + 1], start=(s == 0), stop=(s == 31)),
                         reads=[w1, peT], writes=[ps], inc=(s == 31))
                k.op("dve", lambda e: e.tensor_copy(out=peb[:, wch, half:half + 1], in_=ps[:, 0:1]), reads=[ps], writes=[peb])
        gk0 = k.sb("a_gk0", [128, 1], F32)
        k.dma("sp", gk0[:], I["nsa_k_norm"][0, 0:1, :].rearrange("o d -> d o"), writes=[gk0])

        t1_r = Ring([k.sb(f"a_t1{i}", [128, 2048], F32) for i in range(1)])

        def build_tab(dst_fn, h, base, W, pstride):
            t1 = t1_r.next()
            gv = S["gvec"]
            src = bass.AP(tensor=gv.tensor, offset=h * GL + base, ap=[[pstride, 128], [1, W]])
            k.dma("sp", t1[:, 0:W], src, reads=[dbuf["gvec"]], writes=[t1])
            for c0 in range(0, W, 512):
                n = min(512, W - c0)
                ps = ringN.next()
                k.op("pe", lambda e: e.matmul(ps[:, 0:n], lhsT=Jm[:, :], rhs=t1[:, c0:c0 + n], start=True, stop=True),
                     reads=[Jm, t1], writes=[ps])
                dst, wr = dst_fn(c0, n)
                evac(dst, ps[:, 0:n], [ps], wr)

        def compress(kx, wch, ncb, hg, tmpx, tmp2):
            kxv = kx[:, 0:(ncb + 1) * 16].rearrange("p (n r) -> p n r", r=16)
            for half in range(2):
                ps = ringN.next()
                for s in range(32):
                    rv = kxv[:, 0:ncb, s] if s < 16 else kxv[:, 1:ncb + 1, s - 16]
                    k.op("pe", lambda e: e.matmul(ps[:, 0:ncb], lhsT=w1[:, wch, s, half * 128:(half + 1) * 128], rhs=rv,
                                                  start=(s == 0), stop=(s == 31)),
                         reads=[w1, kx], writes=[ps], inc=(s == 31))
                k.op("dve", lambda e: e.tensor_scalar(out=tmpx[:, 0:ncb], in0=ps[:, 0:ncb], scalar1=peb[:, wch, half:half + 1],
                                                      scalar2=None, op0=ALU.add), reads=[ps, peb], writes=[tmpx])
                k.op("dve", lambda e: e.tensor_tensor(out=tmp2[:, 0:ncb], in0=tmpx[:, 0:ncb], in1=tmpx[:, 0:ncb], op=ALU.mult),
                     reads=[tmpx], writes=[tmp2])
                k.op("dve", lambda e: e.tensor_scalar(out=tmp2[:, 0:ncb], in0=tmp2[:, 0:ncb], scalar1=0.044715, scalar2=1.0,
                                                      op0=ALU.mult, op1=ALU.add), reads=[tmp2], writes=[tmp2])
                k.op("dve", lambda e: e.tensor_tensor(out=tmp2[:, 0:ncb], in0=tmp2[:, 0:ncb], in1=tmpx[:, 0:ncb], op=ALU.mult),
                     reads=[tmp2, tmpx], writes=[tmp2])
                k.op("act", lambda e: e.activation(out=tmp2[:, 0:ncb], in_=tmp2[:, 0:ncb], func=AF.Sigmoid, scale=GELU_C),
                     reads=[tmp2], writes=[tmp2])
                k.op("dve", lambda e: e.tensor_tensor(out=hg[:, half, 0:ncb], in0=tmp2[:, 0:ncb], in1=tmpx[:, 0:ncb], op=ALU.mult),
                     reads=[tmp2, tmpx], writes=[hg])

        def kc_finish(hg, ncb, kcT, tmpx, tmp2, sqb):
            ps = ringN.next()
            for half in range(2):
                k.op("pe", lambda e: e.matmul(ps[:, 0:ncb], lhsT=w2[:, 0, half, :], rhs=hg[:, half, 0:ncb],
                                              start=(half == 0), stop=(half == 1)), reads=[w2, hg], writes=[ps], inc=(half == 1))
            k.op("dve", lambda e: e.tensor_copy(out=tmpx[:, 0:ncb], in_=ps[:, 0:ncb]), reads=[ps], writes=[tmpx])
            k.op("act", lambda e: e.activation(out=sqb[:, 0:ncb], in_=tmpx[:, 0:ncb], func=AF.Square), reads=[tmpx], writes=[sqb])
            ps2 = ringN.next()
            k.op("pe", lambda e: e.matmul(ps2[:, 0:ncb], lhsT=ones_b[:, :], rhs=sqb[:, 0:ncb], start=True, stop=True),
                 reads=[ones_b, sqb], writes=[ps2])
            k.op("act", lambda e: e.activation(out=tmp2[:, 0:ncb], in_=ps2[:, 0:ncb], func=AF.Sqrt, scale=1.0 / 128, bias=epsb[:, 0:1]),
                 reads=[ps2, epsb], writes=[tmp2])
            k.op("dve", lambda e: e.reciprocal(out=tmp2[:, 0:ncb], in_=tmp2[:, 0:ncb]), reads=[tmp2], writes=[tmp2])
            k.op("dve", lambda e: e.scalar_tensor_tensor(out=kcT[:, 0:ncb], in0=tmpx[:, 0:ncb], scalar=gk0[:, 0:1], in1=tmp2[:, 0:ncb],
                                                         op0=ALU.mult, op1=ALU.mult), reads=[tmpx, gk0, tmp2], writes=[kcT])

        def transpose_tiles(tk, ntile, dstT):
            for g0 in range(0, ntile, 8):
                ng = min(8, ntile - g0)
                pb = pbf.next()
                for i in range(ng):
                    k.op("pe", lambda e: e.transpose(out=pb[:, i * 128:(i + 1) * 128], in_=tk[:, g0 + i, :], identity=ident_b[:, :]),
                         reads=[tk, ident_b], writes=[pb], inc=(i == ng - 1))
                evac(dstT[:, g0 * 128:(g0 + ng) * 128], pb[:, 0:ng * 128], [pb], [dstT])

        def fill_sample_bias(j, T, Tw, Bs, Bw):
            NPG_ = P // 128
            kcut = sum(1 for kt in range(NPG_) if P - 128 * kt >= 1024)
            if kcut:
                k.op("pool", lambda e: e.tensor_copy(out=Bs[:, 0:kcut, j, :],
                                                     in_=T[:, 1024 + OFF:1024 + OFF + 8].unsqueeze(1).to_broadcast([128, kcut, 8])),
                     reads=[T], writes=[Bs])
            for kt in range(kcut, NPG_):
                x0 = P - 128 * kt + OFF
                k.op("pool", lambda e: e.tensor_copy(out=Bs[:, kt, j, :], in_=T[:, x0:x0 + 8]), reads=[T], writes=[Bs])
            k.op("pool", lambda e: e.tensor_copy(out=Bs[:, NPG_, j, :], in_=T[:, OFF:OFF + 8]), reads=[T], writes=[Bs])
            for wt in range(4):
                x0 = 512 - 128 * wt + OFF
                k.op("pool", lambda e: e.tensor_copy(out=Bw[:, wt, j, :], in_=Tw[:, x0:x0 + 8]), reads=[Tw], writes=[Bw])
            k.op("pool", lambda e: e.tensor_copy(out=Bw[:, 4, j, :], in_=Tw[:, OFF:OFF + 8]), reads=[Tw], writes=[Bw])

        def prompt_group(g, qh, Bs, Bw):
            if True:
                kslcT = k.sb("a_kslcT", [128, TP], BF16)
                kwinT = k.sb("a_kwinT", [128, TP], BF16)
                vslc = k.sb("a_vslc", [128, NQT, 136], BF16)
                vwin = k.sb("a_vwin", [128, NQT, 136], BF16)
                k.op("pool", lambda e: e.memset(vslc[:, :, 128:129], 1.0), writes=[vslc])
                k.op("pool", lambda e: e.memset(vwin[:, :, 128:129], 1.0), writes=[vwin])
                kcT = k.sb("a_kcT", [128, 128], BF16)
                vc = k.sb("a_vc", [128, 128], BF16)
                prep_scope = k.scope()
                prep_scope.__enter__()
                tk_r = Ring([k.sb(f"a_tk{i}", [128, NQT, 128], BF16) for i in range(2)])
                kx = k.sb("a_kx", [128, TP], BF16)
                vx = k.sb("a_vx", [128, TP], BF16)
                for (nm, ap, dstT, dstV) in (
                        ("o_kv", O["o_kv"][0:TP, 0 * 512 + g * 128:0 * 512 + (g + 1) * 128], kx, None),
                        ("o_kv", O["o_kv"][0:TP, 1 * 512 + g * 128:1 * 512 + (g + 1) * 128], vx, None),
                        ("o_kv", O["o_kv"][0:TP, 2 * 512 + g * 128:2 * 512 + (g + 1) * 128], kslcT, None),
                        ("o_kv", O["o_kv"][0:TP, 3 * 512 + g * 128:3 * 512 + (g + 1) * 128], None, vslc),
                        ("win_new", S["win_new"][0:TP, g * 128:(g + 1) * 128], kwinT, None),
                        ("win_new", S["win_new"][0:TP, 512 + g * 128:512 + (g + 1) * 128], None, vwin)):
                    if dstV is not None:
                        k.dma("pool", dstV[:, :, 0:128], ap.rearrange("(t p) d -> p t d", p=128), reads=[dbuf[nm]], writes=[dstV])
                    else:
                        tk = tk_r.next()
                        k.dma("pool", tk[:], ap.rearrange("(t p) d -> p t d", p=128), reads=[dbuf[nm]], writes=[tk])
                        transpose_tiles(tk, NQT, dstT)
                hg = k.sb("a_hg", [128, 2, 512], BF16)
                tmpx = k.sb("a_tmpx", [128, 512], F32)
                tmp2 = k.sb("a_tmp2", [128, 512], F32)
                sqb = k.sb("a_sqb", [128, 512], BF16)
                compress(kx, 0, ncb_p, hg, tmpx, tmp2)
                kc_finish(hg, ncb_p, kcT, tmpx, tmp2, sqb)
                compress(vx, 1, ncb_p, hg, tmpx, tmp2)
                ps = ringN.next()
                for half in range(2):
                    k.op("pe", lambda e: e.matmul(ps[0:ncb_p, 0:128], lhsT=hg[:, half, 0:ncb_p], rhs=w2[:, 1, half, :],
                                                  start=(half == 0), stop=(half == 1)), reads=[hg, w2], writes=[ps], inc=(half == 1))
                k.op("act", lambda e: e.copy(out=vc[0:ncb_p, :], in_=ps[0:ncb_p, 0:128]), reads=[ps], writes=[vc])
                prep_scope.__exit__(None, None, None)
                o_acc = k.sb("a_oacc", [128, NQT, 4, 128], F32)
                import os
                BR = os.environ.get("NSA_BR", "c,s,w")
                k.op("pool", lambda e: e.memset(o_acc[:], 0.0), writes=[o_acc])
                selT = k.sb("a_selT", [n_sb_p, TP], BF16)
                ssb_r = Ring([k.sb(f"a_ssb{i}", [128, 512], F32) for i in range(2)])
                ebf_r = Ring([k.sb(f"a_ebf{i}", [128, 512], BF16) for i in range(2)])
                eb2_r = Ring([k.sb(f"a_eb2{i}", [128, 512], BF16) for i in range(2)])
                sm = k.sb("a_sm", [128, 8], F32)
                with k.scope():
                    Bc = k.sb("a_Bc", [128, TP], F32)
                    Pall = k.sb("a_Pall", [128, 4, TP], BF16)
                    rec = k.sb("a_rec", [128, 512], F32)
                    sc = k.sb("a_sc", [128, n_sb_p], F32)
                    wk = k.sb("a_wk", [128, n_sb_p], F32)
                    m8 = k.sb("a_m8", [128, 16], F32)
                    sel = k.sb("a_sel", [128, n_sb_p], F32)
                    for j in range(4):
                        build_tab(lambda c0, n: (Bc[:, c0:c0 + n], [Bc]), 4 * g + j, 0, TP, 16)
                        for c in range(NCH):
                            q0 = c * 512
                            ps = ringN.next()
                            k.op("pe", lambda e: e.matmul(ps[0:ncb_p, :], lhsT=kcT[:, 0:ncb_p], rhs=qh[:, j, q0:q0 + 512], start=True, stop=True),
                                 reads=[kcT, qh], writes=[ps])
                            ssb = ssb_r.next()
                            k.op("dve", lambda e: e.tensor_tensor(out=ssb[0:ncb_p, :], in0=ps[0:ncb_p, :], in1=Bc[0:ncb_p, q0:q0 + 512], op=ALU.add),
                                 reads=[ps, Bc], writes=[ssb])
                            ebf = ebf_r.next()
                            k.op("act", lambda e: e.activation(out=ebf[0:ncb_p, :], in_=ssb[0:ncb_p, :], func=AF.Exp), reads=[ssb], writes=[ebf])
                            psD = ringN.next()
                            k.op("pe", lambda e: e.matmul(psD[0:ncb_p, :], lhsT=ones_b[0:ncb_p, 0:ncb_p], rhs=ebf[0:ncb_p, :], start=True, stop=True),
                                 reads=[ones_b, ebf], writes=[psD])
                            k.op("dve", lambda e: e.tensor_scalar(out=rec[0:ncb_p, :], in0=psD[0:ncb_p, :], scalar1=1e-30, scalar2=None, op0=ALU.max),
                                 reads=[psD], writes=[rec])
                            k.op("dve", lambda e: e.reciprocal(out=rec[0:ncb_p, :], in_=rec[0:ncb_p, :]), reads=[rec], writes=[rec])
                            k.op("dve", lambda e: e.tensor_tensor(out=Pall[0:ncb_p, j, q0:q0 + 512], in0=ebf[0:ncb_p, :], in1=rec[0:ncb_p, :], op=ALU.mult),
                                 reads=[ebf, rec], writes=[Pall])
                            for qs in range(4):
                                qt = c * 4 + qs
                                pso = ringN.next()
                                k.op("pe", lambda e: e.matmul(pso[:, 0:128], lhsT=Pall[0:ncb_p, j, q0 + qs * 128:q0 + (qs + 1) * 128], rhs=vc[0:ncb_p, :],
                                                              start=True, stop=True), reads=[Pall, vc], writes=[pso])
                                if "c" in BR:
                                    k.op("act", lambda e: e.activation(out=o_acc[:, qt, j, :], in_=pso[:, 0:128], func=AF.Copy,
                                                                       scale=gates[:, qt, 4 * g + j:4 * g + j + 1]),
                                         reads=[pso, gates], writes=[o_acc])
                    for c in range(NCH):
                        q0 = c * 512
                        for qs in range(4):
                            qt = c * 4 + qs
                            psI = ringN.next()
                            for j in range(4):
                                k.op("pe", lambda e: e.matmul(psI[:, 0:n_sb_p], lhsT=Pall[0:ncb_p, j, q0 + qs * 128:q0 + (qs + 1) * 128], rhs=ovl_p[0:ncb_p, :],
                                                              start=(j == 0), stop=(j == 3)), reads=[Pall, ovl_p], writes=[psI], inc=(j == 3))
                            k.op("dve", lambda e: e.tensor_tensor(out=sc[:, :], in0=psI[:, 0:n_sb_p], in1=add_p[:, qt, :], op=ALU.add),
                                 reads=[psI, add_p], writes=[sc])
                            k.op("dve", lambda e: e.max(out=m8[:, 0:8], in_=sc[:, :]), reads=[sc], writes=[m8])
                            if n_sb_p > 8:
                                k.op("dve", lambda e: e.match_replace(out=wk[:, :], in_to_replace=m8[:, 0:8], in_values=sc[:, :], imm_value=-1e30),
                                     reads=[m8, sc], writes=[wk])
                                k.op("dve", lambda e: e.max(out=m8[:, 8:16], in_=wk[:, :]), reads=[wk], writes=[m8])
                                thr = m8[:, 15:16]
                            else:
                                thr = m8[:, 7:8]
                            k.op("dve", lambda e: e.tensor_scalar(out=sel[:, :], in0=sc[:, :], scalar1=thr, scalar2=None, op0=ALU.is_ge),
                                 reads=[sc, m8], writes=[sel])
                            k.op("dve", lambda e: e.tensor_tensor(out=sel[:, :], in0=sel[:, :], in1=val_p[:, qt, :], op=ALU.mult),
                                 reads=[sel, val_p], writes=[sel])
                            pst = ringN.next()
                            k.op("pe", lambda e: e.transpose(out=pst[0:n_sb_p, 0:128], in_=sel[:, :], identity=ident_f[:, :]),
                                 reads=[sel, ident_f], writes=[pst])
                            k.op("act", lambda e: e.copy(out=selT[:, qt * 128:(qt + 1) * 128], in_=pst[0:n_sb_p, 0:128]), reads=[pst], writes=[selT])
                with k.scope():
                    oT_r = Ring([k.sb(f"a_oTt{i}", [128, 512], BF16) for i in range(2)])
                    T = k.sb("a_T", [128, X], F32)
                    Tw = k.sb("a_Tw", [128, XW], F32)
                    for j in range(4):
                        build_tab(lambda c0, n: (T[:, c0:c0 + n], [T]), 4 * g + j, Z - OFF - 127, X, 1)
                        k.op("pool", lambda e: e.tensor_tensor(out=Tw[:], in0=T[:, 0:XW], in1=wm[:], op=ALU.add), reads=[T, wm], writes=[Tw])
                        fill_sample_bias(j, T, Tw, Bs, Bw)
                        for c in range(NCH):
                            q0 = c * 512
                            for (kT, vv, tab, gi, kts, use_mask) in (
                                    (kslcT, vslc, T, 1, list(range(0, 4 * c + 4)), True),
                                    (kwinT, vwin, Tw, 2, list(range(max(0, 4 * c - 4), 4 * c + 4)), False)):
                                if (use_mask and "s" not in BR) or (not use_mask and "w" not in BR):
                                    continue
                                for kt in kts:
                                    ps = ring2.next()
                                    k.op("pe", lambda e: e.matmul(ps[:, :], lhsT=kT[:, kt * 128:(kt + 1) * 128], rhs=qh[:, j, q0:q0 + 512], start=True, stop=True),
                                         reads=[kT, qh], writes=[ps])
                                    x0 = min(q0 - 128 * kt, 1024) + OFF
                                    ssb = ssb_r.next()
                                    k.op("dve", lambda e: e.tensor_tensor(out=ssb[:, :], in0=ps[:, :], in1=tab[:, x0:x0 + 512], op=ALU.add),
                                         reads=[ps, tab], writes=[ssb])
                                    ebf = ebf_r.next()
                                    k.op("act", lambda e: e.activation(out=ebf[:, :], in_=ssb[:, :], func=AF.Exp), reads=[ssb], writes=[ebf])
                                    if use_mask:
                                        psM = ring2.next()
                                        k.op("pe", lambda e: e.matmul(psM[:, :], lhsT=exp_p[:, kt * 128:(kt + 1) * 128], rhs=selT[:, q0:q0 + 512], start=True, stop=True),
                                             reads=[exp_p, selT], writes=[psM])
                                        eb2 = eb2_r.next()
                                        k.op("dve", lambda e: e.tensor_tensor(out=eb2[:, :], in0=ebf[:, :], in1=psM[:, :], op=ALU.mult),
                                             reads=[ebf, psM], writes=[eb2])
                                    else:
                                        eb2 = ebf
                                    for qs in range(4):
                                        acc = accO[qs]
                                        k.op("pe", lambda e: e.matmul(acc[:, 0:129], lhsT=eb2[:, qs * 128:(qs + 1) * 128],
                                                                      rhs=vv[:, kt, 0:129], start=(kt == kts[0]), stop=(kt == kts[-1])),
                                             reads=[eb2, vv], writes=[acc], inc=(kt == kts[-1]))
                                for qs in range(4):
                                    qt = c * 4 + qs
                                    acc = accO[qs]
                                    a0 = 0
                                    k.op("dve", lambda e: e.reciprocal(out=sm[:, 0:1], in_=acc[:, a0 + 128:a0 + 129]), reads=[acc], writes=[sm])
                                    k.op("dve", lambda e: e.tensor_tensor(out=sm[:, 1:2], in0=sm[:, 0:1],
                                                                          in1=gates[:, qt, gi * 16 + 4 * g + j:gi * 16 + 4 * g + j + 1], op=ALU.mult),
                                         reads=[sm, gates], writes=[sm])
                                    k.op("dve", lambda e: e.scalar_tensor_tensor(out=o_acc[:, qt, j, :], in0=acc[:, a0:a0 + 128], scalar=sm[:, 1:2],
                                                                                 in1=o_acc[:, qt, j, :], op0=ALU.mult, op1=ALU.add),
                                         reads=[acc, sm, o_acc], writes=[o_acc])
                            pst = ring2.next()
                            for qs in range(4):
                                k.op("pe", lambda e: e.transpose(out=pst[:, qs * 128:(qs + 1) * 128], in_=o_acc[:, c * 4 + qs, j, :], identity=ident_f[:, :]),
                                     reads=[o_acc, ident_f], writes=[pst], inc=(qs == 3))
                            oTt = oT_r.next()
                            evac(oTt[:, :], pst[:, :], [pst], [oTt])
                            k.dma("sp", S["oT2"][(4 * g + j) * 128:(4 * g + j + 1) * 128, q0:q0 + 512], oTt[:, :], reads=[oTt], writes=[dbuf["oT2"]])

        def sample_group(g, s, qh, Bs, Bw, Bcs):
            n_all_s = P + TS
            n_sb_s = -(-n_all_s // 64)
            ncb_s = P // 16 - 1
            NT4 = -(-ncb_s // 128)
            NPG = P // 128
            nA = min(128, n_sb_s)
            NTL = NPG + 1
            t0s = TP + s * TS
            pgc_r = Ring([k.sb(f"s_pgc{i}", [128, 16, 128], BF16) for i in range(2)])
            kslcT = k.sb("s_kslcT", [128, NTL * 128], BF16)
            vslc = k.sb("s_vslc", [128, NTL, 136], BF16)
            kwT = k.sb("s_kwT", [128, 5 * 128], BF16)
            vw = k.sb("s_vw", [128, 5, 136], BF16)
            k.op("pool", lambda e: e.memset(vslc[:, :, 128:129], 1.0), writes=[vslc])
            k.op("pool", lambda e: e.memset(vw[:, :, 128:129], 1.0), writes=[vw])
            kcT = k.sb("s_kcT", [128, 512], BF16)
            vc = k.sb("s_vc", [128, NT4, 128], BF16)
            k.op("pool", lambda e: e.memset(vc[:], 0.0), writes=[vc])
            qsm = k.sb("s_qsm", [128, 4, 8], BF16)
            k.op("pool", lambda e: e.tensor_copy(out=qsm[:], in_=qh[:, :, t0s:t0s + TS]), reads=[qh], writes=[qsm])
            qsv = qsm[:].rearrange("p j t -> p (j t)")

            def gather(c, dstT=None, dstV=None):
                col0 = c * 512 + g * 128
                src = S["past"][s * P:(s + 1) * P, col0:col0 + 128].rearrange("(a p) d -> p a d", p=128)
                if dstV is not None:
                    k.dma("sp", dstV[:, 0:NPG, 0:128], src, reads=[dbuf["past"]], writes=[dstV])
                    return
                for p0 in range(0, NPG, 16):
                    npg = min(16, NPG - p0)
                    pgc = pgc_r.next()
                    k.dma("sp", pgc[:, 0:npg, :], src[:, p0:p0 + npg, :], reads=[dbuf["past"]], writes=[pgc])
                    for q0 in range(0, npg, 8):
                        nq = min(8, npg - q0)
                        pb = pbf.next()
                        for i in range(nq):
                            k.op("pe", lambda e: e.transpose(out=pb[:, i * 128:(i + 1) * 128], in_=pgc[:, q0 + i, :], identity=ident_b[:, :]),
                                 reads=[pgc, ident_b], writes=[pb], inc=(i == nq - 1))
                        evac(dstT[:, (p0 + q0) * 128:(p0 + q0 + nq) * 128], pb[:, 0:nq * 128], [pb], [dstT])

            def small_T(src_ap, nm, rows, dstT_ap, dstT_buf):
                tkb = pgc_r.next()
                k.dma("pool", tkb[0:rows, 0, :], src_ap, reads=[dbuf[nm]], writes=[tkb])
                pb = pbf.next()
                k.op("pe", lambda e: e.transpose(out=pb[:, 0:rows], in_=tkb[0:rows, 0, :], identity=ident_b[0:rows, 0:rows]),
                     reads=[tkb, ident_b], writes=[pb])
                evac(dstT_ap, pb[:, 0:rows], [pb], [dstT_buf])

            cscope = k.scope()
            cscope.__enter__()
            kx = k.sb("s_kx", [128, P], BF16)
            hg = k.sb("s_hg", [128, 2, 512], BF16)
            tmpx = k.sb("s_tmpx", [128, 512], F32)
            tmp2 = k.sb("s_tmp2", [128, 512], F32)
            sqb = k.sb("s_sqb", [128, 512], BF16)
            gather(0, dstT=kx)
            compress(kx, 0, ncb_s, hg, tmpx, tmp2)
            kc_finish(hg, ncb_s, kcT, tmpx, tmp2, sqb)
            gather(1, dstT=kx)
            compress(kx, 1, ncb_s, hg, tmpx, tmp2)
            for nt in range(NT4):
                rows = min(128, ncb_s - nt * 128)
                ps = ringN.next()
                for half in range(2):
                    k.op("pe", lambda e: e.matmul(ps[0:rows, 0:128], lhsT=hg[:, half, nt * 128:nt * 128 + rows], rhs=w2[:, 1, half, :],
                                                  start=(half == 0), stop=(half == 1)), reads=[hg, w2], writes=[ps], inc=(half == 1))
                k.op("act", lambda e: e.copy(out=vc[0:rows, nt, :], in_=ps[0:rows, 0:128]), reads=[ps], writes=[vc])
            cscope.__exit__(None, None, None)
            exp_sA = k.sb("s_expsA", [128, (NPG + 1) * 128], BF16)
            k.dma("sp", exp_sA[0:nA, :], I["ns_exp"][0:nA, :], writes=[exp_sA])
            gather(2, dstT=kslcT)
            small_T(O["o_kv"][t0s:t0s + TS, 2 * 512 + g * 128:2 * 512 + (g + 1) * 128], "o_kv", TS, kslcT[:, NPG * 128:NPG * 128 + TS], kslcT)
            gather(3, dstV=vslc)
            k.dma("pool", vslc[0:TS, NPG, 0:128], O["o_kv"][t0s:t0s + TS, 3 * 512 + g * 128:3 * 512 + (g + 1) * 128],
                  reads=[dbuf["o_kv"]], writes=[vslc])
            tkw = k.sb("s_tkw", [128, 4, 128], BF16)
            k.dma("pool", tkw[:], I["st_win"][s, :, g * 128:(g + 1) * 128].rearrange("(t p) d -> p t d", p=128),
                  reads=[dbuf["st_win"]], writes=[tkw])
            transpose_tiles(tkw, 4, kwT)
            small_T(S["win_new"][t0s:t0s + TS, g * 128:(g + 1) * 128], "win_new", TS, kwT[:, 512:512 + TS], kwT)
            k.dma("pool", vw[:, 0:4, 0:128], I["st_win"][s, :, 512 + g * 128:512 + (g + 1) * 128].rearrange("(t p) d -> p t d", p=128),
                  reads=[dbuf["st_win"]], writes=[vw])
            k.dma("pool", vw[0:TS, 4, 0:128], S["win_new"][t0s:t0s + TS, 512 + g * 128:512 + (g + 1) * 128],
                  reads=[dbuf["win_new"]], writes=[vw])

            o_s = k.sb("s_os", [TS, 4, 128], F32)
            sm = k.sb("s_sm", [TS, 8], F32)
            ecs = k.sb("s_ecs", [128, NT4, 32], F32)
            ecb = k.sb("s_ecb", [128, NT4, 32], BF16)
            pcb = k.sb("s_pcb", [128, NT4, 32], BF16)
            recs = k.sb("s_recs", [128, 32], F32)
            ps = ringN.next()
            for nt in range(NT4):
                rows = min(128, ncb_s - nt * 128)
                k.op("pe", lambda e: e.matmul(ps[0:rows, nt * 32:(nt + 1) * 32], lhsT=kcT[:, nt * 128:nt * 128 + rows], rhs=qsv,
                                              start=True, stop=True), reads=[kcT, qsm], writes=[ps], inc=(nt == NT4 - 1))
            k.op("pool", lambda e: e.memset(ecs[:], NEG), writes=[ecs])
            for nt in range(NT4):
                rows = min(128, ncb_s - nt * 128)
                k.op("dve", lambda e: e.tensor_tensor(out=ecs[0:rows, nt, :], in0=ps[0:rows, nt * 32:(nt + 1) * 32],
                                                      in1=Bcs[0:rows, nt, :, :].rearrange("p j t -> p (j t)"), op=ALU.add),
                     reads=[ps, Bcs], writes=[ecs])
            k.op("act", lambda e: e.activation(out=ecb[:], in_=ecs[:], func=AF.Exp), reads=[ecs], writes=[ecb])
            psD = ringN.next()
            for nt in range(NT4):
                k.op("pe", lambda e: e.matmul(psD[:, 0:32], lhsT=ones_b[:, :], rhs=ecb[:, nt, :], start=(nt == 0), stop=(nt == NT4 - 1)),
                     reads=[ones_b, ecb], writes=[psD], inc=(nt == NT4 - 1))
            k.op("dve", lambda e: e.tensor_scalar(out=recs[:], in0=psD[:, 0:32], scalar1=1e-30, scalar2=None, op0=ALU.max), reads=[psD], writes=[recs])
            k.op("dve", lambda e: e.reciprocal(out=recs[:], in_=recs[:]), reads=[recs], writes=[recs])
            k.op("dve", lambda e: e.tensor_tensor(out=pcb[:], in0=ecb[:], in1=recs[:].unsqueeze(1).to_broadcast([128, NT4, 32]), op=ALU.mult),
                 reads=[ecb, recs], writes=[pcb])
            for j in range(4):
                pso = accO[j]
                for nt in range(NT4):
                    k.op("pe", lambda e: e.matmul(pso[0:TS, 0:128], lhsT=pcb[:, nt, j * 8:(j + 1) * 8], rhs=vc[:, nt, :],
                                                  start=(nt == 0), stop=(nt == NT4 - 1)), reads=[pcb, vc], writes=[pso], inc=(nt == NT4 - 1))
                k.op("act", lambda e: e.activation(out=o_s[:, j, :], in_=pso[0:TS, 0:128], func=AF.Copy,
                                                   scale=gates_s[:, s, 4 * g + j:4 * g + j + 1]), reads=[pso, gates_s], writes=[o_s])
            psI = ringN.next()
            first = True
            for j in range(4):
                for nt in range(NT4):
                    k.op("pe", lambda e: e.matmul(psI[0:TS, 0:n_sb_s], lhsT=pcb[:, nt, j * 8:(j + 1) * 8], rhs=ovl_s[:, nt, :],
                                                  start=first, stop=(j == 3 and nt == NT4 - 1)), reads=[pcb, ovl_s], writes=[psI],
                         inc=(j == 3 and nt == NT4 - 1))
                    first = False
            sc = k.sb("s_sc", [TS, n_sb_s], F32)
            wk = k.sb("s_wk", [TS, n_sb_s], F32)
            m8 = k.sb("s_m8", [TS, 16], F32)
            sel = k.sb("s_sel", [TS, n_sb_s], F32)
            k.op("dve", lambda e: e.tensor_tensor(out=sc[:, :], in0=psI[0:TS, 0:n_sb_s], in1=add_s[:, :], op=ALU.add), reads=[psI, add_s], writes=[sc])
            k.op("dve", lambda e: e.max(out=m8[:, 0:8], in_=sc[:, :]), reads=[sc], writes=[m8])
            k.op("dve", lambda e: e.match_replace(out=wk[:, :], in_to_replace=m8[:, 0:8], in_values=sc[:, :], imm_value=-1e30), reads=[m8, sc], writes=[wk])
            k.op("dve", lambda e: e.max(out=m8[:, 8:16], in_=wk[:, :]), reads=[wk], writes=[m8])
            k.op("dve", lambda e: e.tensor_scalar(out=sel[:, :], in0=sc[:, :], scalar1=m8[:, 15:16], scalar2=None, op0=ALU.is_ge), reads=[sc, m8], writes=[sel])
            k.op("dve", lambda e: e.tensor_tensor(out=sel[:, :], in0=sel[:, :], in1=val_s[:, :], op=ALU.mult), reads=[sel, val_s], writes=[sel])
            selTa = k.sb("s_selTa", [128, TS], BF16)
            selTb = k.sb("s_selTb", [1, TS], BF16)
            pst = ringN.next()
            k.op("pe", lambda e: e.transpose(out=pst[0:nA, 0:TS], in_=sel[:, 0:nA], identity=ident_f[0:TS, 0:TS]), reads=[sel, ident_f], writes=[pst])
            k.op("act", lambda e: e.copy(out=selTa[0:nA, :], in_=pst[0:nA, 0:TS]), reads=[pst], writes=[selTa])
            if n_sb_s > 128:
                pst2 = ringN.next()
                k.op("pe", lambda e: e.transpose(out=pst2[0:1, 0:TS], in_=sel[:, 128:129], identity=ident_f[0:TS, 0:TS]), reads=[sel, ident_f], writes=[pst2])
                k.op("act", lambda e: e.copy(out=selTb[0:1, :], in_=pst2[0:1, 0:TS]), reads=[pst2], writes=[selTb])

            def branch(kT, vv, ntile, last_rows, Btab, gi, use_mask):
                ess = k.sb("s_ess", [128, 16, 32], F32)
                esb = k.sb("s_esb", [128, ntile, 32], BF16)
                k.op("pool", lambda e: e.memset(esb[:], 0.0), writes=[esb])
                for t0 in range(0, ntile, 16):
                    nt_ = min(16, ntile - t0)
                    ps = ringN.next()
                    for i in range(nt_):
                        kt = t0 + i
                        rows = last_rows if kt == ntile - 1 else 128
                        k.op("pe", lambda e: e.matmul(ps[0:rows, i * 32:(i + 1) * 32], lhsT=kT[:, kt * 128:kt * 128 + rows], rhs=qsv,
                                                      start=True, stop=True), reads=[kT, qsm], writes=[ps], inc=(i == nt_ - 1))
                    full = nt_ if (t0 + nt_ < ntile) else nt_ - 1
                    if full:
                        k.op("dve", lambda e: e.tensor_tensor(out=ess[:, 0:full, :], in0=ps[:, 0:full * 32].rearrange("p (a b) -> p a b", b=32),
                                                              in1=Btab[:, t0:t0 + full, :, :].rearrange("p a j t -> p a (j t)"), op=ALU.add),
                             reads=[ps, Btab], writes=[ess])
                        k.op("act", lambda e: e.activation(out=esb[:, t0:t0 + full, :], in_=ess[:, 0:full, :], func=AF.Exp), reads=[ess], writes=[esb])
                    if full < nt_:
                        i = nt_ - 1
                        k.op("dve", lambda e: e.tensor_tensor(out=ess[0:last_rows, i, :], in0=ps[0:last_rows, i * 32:(i + 1) * 32],
                                                              in1=Btab[0:last_rows, ntile - 1, :, :].rearrange("p j t -> p (j t)"), op=ALU.add),
                             reads=[ps, Btab], writes=[ess])
                        k.op("act", lambda e: e.activation(out=esb[0:last_rows, ntile - 1, :], in_=ess[0:last_rows, i, :], func=AF.Exp),
                             reads=[ess], writes=[esb])
                if use_mask:
                    for t0 in range(0, ntile, 64):
                        nt_ = min(64, ntile - t0)
                        psM = ringN.next()
                        for i in range(nt_):
                            kt = t0 + i
                            rows = last_rows if kt == ntile - 1 else 128
                            two = n_sb_s > 128 and kt == ntile - 1
                            k.op("pe", lambda e: e.matmul(psM[0:rows, i * 8:(i + 1) * 8], lhsT=exp_sA[0:nA, kt * 128:kt * 128 + rows], rhs=selTa[0:nA, :],
                                                          start=True, stop=not two), reads=[exp_sA, selTa], writes=[psM], inc=(i == nt_ - 1 and not two))
                            if two:
                                k.op("pe", lambda e: e.matmul(psM[0:rows, i * 8:(i + 1) * 8], lhsT=exp_sB[0:1, 0:rows], rhs=selTb[0:1, :],
                                                              start=False, stop=True), reads=[exp_sB, selTb], writes=[psM], inc=(i == nt_ - 1))
                        full = nt_ if (t0 + nt_ < ntile) else nt_ - 1
                        if full:
                            k.op("dve", lambda e: e.tensor_tensor(
                                out=esb[:, t0:t0 + full, :].rearrange("p a (j t) -> p a j t", t=8),
                                in0=esb[:, t0:t0 + full, :].rearrange("p a (j t) -> p a j t", t=8),
                                in1=psM[:, 0:full * 8].rearrange("p (a t) -> p a t", t=8).unsqueeze(2).to_broadcast([128, full, 4, 8]), op=ALU.mult),
                                reads=[esb, psM], writes=[esb])
                        if full < nt_:
                            i = nt_ - 1
                            k.op("dve", lambda e: e.tensor_tensor(
                                out=esb[0:last_rows, ntile - 1, :].rearrange("p (j t) -> p j t", t=8),
                                in0=esb[0:last_rows, ntile - 1, :].rearrange("p (j t) -> p j t", t=8),
                                in1=psM[0:last_rows, i * 8:(i + 1) * 8].unsqueeze(1).to_broadcast([last_rows, 4, 8]), op=ALU.mult),
                                reads=[esb, psM], writes=[esb])
                for j in range(4):
                    acc = accO[j]
                    for kt in range(ntile):
                        rows = last_rows if kt == ntile - 1 else 128
                        k.op("pe", lambda e: e.matmul(acc[0:TS, 0:129], lhsT=esb[0:rows, kt, j * 8:(j + 1) * 8], rhs=vv[0:rows, kt, 0:129],
                                                      start=(kt == 0), stop=(kt == ntile - 1)), reads=[esb, vv], writes=[acc], inc=(kt == ntile - 1))
                    k.op("dve", lambda e: e.reciprocal(out=sm[:, 0:1], in_=acc[0:TS, 128:129]), reads=[acc], writes=[sm])
                    k.op("dve", lambda e: e.tensor_tensor(out=sm[:, 1:2], in0=sm[:, 0:1], in1=gates_s[:, s, gi * 16 + 4 * g + j:gi * 16 + 4 * g + j + 1], op=ALU.mult),
                         reads=[sm, gates_s], writes=[sm])
                    k.op("dve", lambda e: e.scalar_tensor_tensor(out=o_s[:, j, :], in0=acc[0:TS, 0:128], scalar=sm[:, 1:2], in1=o_s[:, j, :],
                                                                 op0=ALU.mult, op1=ALU.add), reads=[acc, sm, o_s], writes=[o_s])

            with k.scope():
                branch(kslcT, vslc, NTL, TS, Bs, 1, True)
            with k.scope():
                branch(kwT, vw, 5, TS, Bw, 2, False)
            pst = ringN.next()
            for j in range(4):
                k.op("pe", lambda e: e.transpose(out=pst[:, j * TS:(j + 1) * TS], in_=o_s[:, j, :], identity=ident_f[0:TS, 0:TS]),
                     reads=[o_s, ident_f], writes=[pst], inc=(j == 3))
            oTs = k.sb("s_oTs", [128, 4, TS], BF16)
            evac(oTs[:].rearrange("p j t -> p (j t)"), pst[:, 0:4 * TS], [pst], [oTs])
            k.dma("sp", S["oT2"][g * 512:(g + 1) * 512, t0s:t0s + TS].rearrange("(j p) t -> p j t", p=128), oTs[:],
                  reads=[oTs], writes=[dbuf["oT2"]])


        n_all_s = P + TS
        n_sb_s = -(-n_all_s // 64)
        ncb_s = P // 16 - 1
        NT4 = -(-ncb_s // 128)
        NPG = P // 128
        nA = min(128, n_sb_s)
        KTOT = (NPG + 1) * 128
        ovl_s = k.sb("a_ovls", [128, NT4, n_sb_s], BF16)
        k.dma("pool", ovl_s[:], I["ns_ovl"].rearrange("t p m -> p t m"), writes=[ovl_s])
        add_s = k.sb("a_adds", [TS, n_sb_s], F32)
        k.dma("sp", add_s[:], I["ns_add"], writes=[add_s])
        val_s = k.sb("a_vals", [TS, n_sb_s], F32)
        k.dma("sp", val_s[:], I["ns_valid"], writes=[val_s])
        exp_sB = k.sb("a_expsB", [1, 128], BF16)
        if n_sb_s > 128:
            k.dma("sp", exp_sB[:, :], I["ns_exp"][128:129, KTOT - 128:KTOT], writes=[exp_sB])
        iota_f = k.sb("a_iota", [128, 1], F32)
        k.dma("sp", iota_f[:], I["n_iota"], writes=[iota_f])
        gates_s = k.sb("a_gatess", [TS, NS, 48], F32)
        k.dma("sp", gates_s[:], S["gates"][TP:NT, :].rearrange("(s t) c -> t s c", t=TS), reads=[dbuf["gates"]], writes=[gates_s])
        for g in range(4):
          with k.scope():
            qh = k.sb("a_qh", [128, 4, NT], BF16)
            k.dma("sp", qh[:], S["qT"][g * 512:(g + 1) * 512, :].rearrange("(j p) t -> p j t", p=128),
                  reads=[dbuf["qT"]], writes=[qh])
            Bs = k.sb("a_Bs", [128, NPG + 1, 4, 8], F32)
            Bw = k.sb("a_Bw", [128, 5, 4, 8], F32)
            Bcs = k.sb("a_Bcs", [128, NT4, 4, 8], F32)
            for j in range(4):
                for nt in range(NT4):
                    off_d = P - 31 - 2048 * nt - 2032
                    base = Z + min(off_d, 790)
                    build_tab(lambda c0, n, j=j, nt=nt: (Bcs[:, nt, j, :], [Bcs]), 4 * g + j, base, 8, 16)
            with k.scope():
                prompt_group(g, qh, Bs, Bw)
            for s in range(NS):
                with k.scope():
                    sample_group(g, s, qh, Bs, Bw, Bcs)
```

```python
import math
import numpy as np
import ml_dtypes
import concourse.bass as bass
import concourse.mybir as mybir
from concourse.bass_utils import run_bass_kernel_spmd
from contextlib import ExitStack

F32 = mybir.dt.float32
BF16 = mybir.dt.bfloat16
I32 = mybir.dt.int32
AF = mybir.ActivationFunctionType
ALU = mybir.AluOpType
AX = mybir.AxisListType

ENGS = ("pe", "dve", "act", "pool", "sp")
DMA_RING = 8
RMS_EPS = 1e-6


class Buf:
    __slots__ = ("t", "w", "r", "name", "multi", "ws", "psum")

    def __init__(self, t, name="", multi=False):
        self.t = t
        self.w = None
        self.r = {}
        self.name = name
        self.multi = multi
        self.ws = {}
        self.psum = False

    def __getitem__(self, idx):
        return self.t[idx]


class K:
    def __init__(self, nc, es, same_engine_sync=True):
        self.nc = nc
        self.es = es
        self.eng = {"pe": nc.tensor, "dve": nc.vector, "act": nc.scalar,
                    "pool": nc.gpsimd, "sp": nc.sync}
        self.sems = {}
        self.cnt = {}
        for e in ("pe", "dve", "act", "pool"):
            self.sems[e] = es.enter_context(nc.semaphore("s_" + e))
            self.cnt[e] = 0
        self.ring = {}
        self.ring_n = {}
        for q in ("sp", "pool", "act"):
            self.ring[q] = [es.enter_context(nc.semaphore(f"d_{q}{i}")) for i in range(DMA_RING)]
            self.ring_n[q] = 0
        self.seen = {e: {} for e in ENGS}
        self.same_engine_sync = same_engine_sync
        self.n_inst = {e: 0 for e in ENGS}
        self.n_wait = {e: 0 for e in ENGS}
        self.rr = 0

    def sb(self, name, shape, dt=F32):
        self.uid = getattr(self, "uid", 0) + 1
        t = self.es.enter_context(self.nc.sbuf_tensor(f"sb{self.uid}_" + name, list(shape), dt))
        return Buf(t, name)

    def ps(self, name, shape, dt=F32):
        t = self.es.enter_context(self.nc.psum_tensor("ps_" + name, list(shape), dt))
        b = Buf(t, name)
        b.psum = True
        return b

    def _semobj(self, key):
        if isinstance(key, str):
            return self.sems[key]
        q, i = key
        return self.ring[q][i]

    def _wait(self, e, ev):
        if ev is None:
            return
        key, val = ev
        if key == e and (e == "pe" or not self.same_engine_sync):
            return
        if self.seen[e].get(key, 0) >= val:
            return
        self.eng[e].wait_ge(self._semobj(key), val)
        self.seen[e][key] = val
        self.n_wait[e] += 1

    def _deps(self, e, reads, writes):
        for b in reads:
            if b.multi:
                for key, val in b.ws.items():
                    self._wait(e, (key, val))
            else:
                self._wait(e, b.w)
            if b.psum:
                for key, val in b.r.items():
                    if key != e:
                        self._wait(e, (key, val))
        for b in writes:
            if not b.multi:
                self._wait(e, b.w)
            for key, val in b.r.items():
                self._wait(e, (key, val))

    def _commit(self, ev, reads, writes):
        k_, v = ev
        for b in reads:
            if b in writes:
                continue
            if b.r.get(k_, 0) < v:
                b.r[k_] = v
        for b in writes:
            if b.multi:
                if b.r:
                    b.ws = {}
                if b.ws.get(k_, 0) < v:
                    b.ws[k_] = v
            else:
                b.w = ev
            b.r = {}

    def op(self, e, fn, reads=(), writes=(), inc=True, dma=False):
        if dma:
            return self._dma_like(e, fn, reads, writes)
        self._deps(e, reads, writes)
        ins = fn(self.eng[e])
        self.n_inst[e] += 1
        if inc:
            ins.then_inc(self.sems[e], 1)
            self.cnt[e] += 1
            ev = (e, self.cnt[e])
        else:
            ev = (e, self.cnt[e] + 1)
        self._commit(ev, reads, writes)
        return ins

    def _dma_like(self, q, fn, reads, writes):
        self._deps(q, reads, writes)
        n = self.ring_n[q]
        slot = n % DMA_RING
        rnd = n // DMA_RING
        if rnd > 0:
            self._wait(q, ((q, slot), 16 * rnd))
        ins = fn(self.eng[q])
        ins.then_inc(self.ring[q][slot], 16)
        self.ring_n[q] = n + 1
        self.n_inst[q] += 1
        ev = ((q, slot), 16 * (rnd + 1))
        self._commit(ev, reads, writes)
        return ins

    def dma(self, q, out_ap, in_ap, reads=(), writes=(), **kw):
        self._deps(q, reads, writes)
        n = self.ring_n[q]
        slot = n % DMA_RING
        rnd = n // DMA_RING
        if rnd > 0:
            self._wait(q, ((q, slot), 16 * rnd))
        ins = self.eng[q].dma_start(out=out_ap, in_=in_ap, **kw)
        ins.then_inc(self.ring[q][slot], 16)
        self.ring_n[q] = n + 1
        self.n_inst[q] += 1
        ev = ((q, slot), 16 * (rnd + 1))
        self._commit(ev, reads, writes)
        return ins

    def finish(self):
        for q in ("sp", "pool", "act"):
            n = self.ring_n[q]
            for slot in range(DMA_RING):
                cnt = (n - slot + DMA_RING - 1) // DMA_RING if n > slot else 0
                if cnt > 0:
                    self._wait("sp", ((q, slot), 16 * cnt))

    def barrier(self):
        for e in ENGS:
            for o in ("pe", "dve", "act", "pool"):
                if o != e and self.cnt[o] > 0:
                    self._wait(e, (o, self.cnt[o]))
            for q in ("sp", "pool", "act"):
                n = self.ring_n[q]
                for slot in range(DMA_RING):
                    c = (n - slot + DMA_RING - 1) // DMA_RING if n > slot else 0
                    if c > 0:
                        self._wait(e, ((q, slot), 16 * c))

    def scope(self):
        kk = self

        class _S:
            def __enter__(s_):
                s_.old = kk.es
                s_.new = ExitStack()
                s_.new.__enter__()
                kk.es = s_.new
                return s_

            def __exit__(s_, *a):
                kk.barrier()
                kk.es = s_.old
                s_.new.__exit__(*a)
                return False
        return _S()

    def ev_eng(self):
        self.rr += 1
        return "act" if self.rr % 2 else "dve"


class Ring:
    def __init__(self, bufs):
        self.bufs = bufs
        self.i = 0

    def next(self):
        b = self.bufs[self.i % len(self.bufs)]
        self.i += 1
        return b


class Cfg:
    def __init__(self, TP=2048, NKH=16, D=2048, DFF=5504, NS=4, TS=8, stage=99, dbg=False, PAST=8192, NPOOL=2560):
        self.PAST = PAST
        self.NPOOL = NPOOL
        self.TP = TP
        self.NKH = NKH
        self.NVH = 2 * NKH
        self.D = D
        self.KT = D // 128
        self.DFF = DFF
        self.NS = NS
        self.TS = TS
        self.NSAMP = NS * TS
        self.NT = TP + self.NSAMP
        self.NBP = TP // 64
        self.NB = self.NBP + 1
        self.NCH = TP // 512
        self.KD = NKH * 128
        self.VD = self.NVH * 128
        self.CONVD = 2 * self.KD + self.VD
        self.PROJ = self.CONVD + self.VD + 2 * self.NVH
        self.stage = stage
        self.dbg = dbg

    def chunks(self):
        r = [(c * 512, 512) for c in range(self.NCH)]
        r.append((self.TP, self.NSAMP))
        return r

    def ttiles(self):
        r = [(t * 128, 128) for t in range(self.TP // 128)]
        r.append((self.TP, self.NSAMP))
        return r

    def blocks(self):
        r = [(b * 64, 64) for b in range(self.NBP)]
        r.append((self.TP, self.NSAMP))
        return r


def host_consts(cfg):
    c = {}
    c["ident_f"] = np.eye(128, dtype=np.float32)
    c["ident_b"] = np.eye(128).astype(ml_dtypes.bfloat16)
    c["ones_f"] = np.ones((128, 128), np.float32)
    c["ones_b"] = np.ones((128, 128)).astype(ml_dtypes.bfloat16)
    NEG = -30000.0
    j = np.arange(64)[:, None]
    i = np.arange(64)[None, :]
    def mk(blk, n):
        sj = (np.arange(n)[:, None] // blk)
        si = (np.arange(n)[None, :] // blk)
        same = (sj == si)
        jj = np.arange(n)[:, None]
        ii = np.arange(n)[None, :]
        ui = same & (ii >= jj)
        us = same & (ii > jj)
        ls = same & (ii < jj)
        out = np.zeros((5, 64, 64), np.float32)
        out[0, :n, :n] = ui
        out[1, :n, :n] = same
        out[2, :, :] = NEG
        out[2, :n, :n] = np.where(ui, 0.0, NEG)
        out[3, :, :] = NEG
        out[3, :n, :n] = np.where(us, 0.0, NEG)
        out[4, :, :] = -NEG
        out[4, :n, :n] = np.where(ls, 0.0, -NEG)
        return out
    c["masks_p"] = mk(64, 64)
    c["masks_s"] = mk(cfg.TS, cfg.NSAMP)
    rm = np.zeros((64, cfg.NS), np.float32)
    for s in range(cfg.NS):
        rm[s * cfg.TS:(s + 1) * cfg.TS, s] = 1.0
    c["rowmask_s"] = rm
    return c


def build(cfg):
    nc = bass.Bass("TRN2", target_bir_lowering=False)
    D, KT, NT, TP, NS, TS = cfg.D, cfg.KT, cfg.NT, cfg.TP, cfg.NS, cfg.TS
    NSAMP, NVH, NKH, NB = cfg.NSAMP, cfg.NVH, cfg.NKH, cfg.NB
    CONVD, PROJ, VD, KD = cfg.CONVD, cfg.PROJ, cfg.VD, cfg.KD

    def din(name, shape, dt=F32):
        return nc.dram_tensor(name, list(shape), dt, kind="ExternalInput").ap()

    def dout(name, shape, dt=F32):
        return nc.dram_tensor(name, list(shape), dt, kind="ExternalOutput").ap()

    def dscr(name, shape, dt=F32):
        return nc.dram_tensor(name, list(shape), dt, kind="Internal").ap()

    I = {}
    S = {}
    I["xin"] = din("xin", [NT, D])
    I["norm_mix"] = din("norm_mix", [2, D])
    I["norm_ffn"] = din("norm_ffn", [2, D])
    I["gdn_w_in"] = din("gdn_w_in", [D, PROJ])
    I["gdn_conv_w"] = din("gdn_conv_w", [CONVD, 4])
    I["gdn_A_log"] = din("gdn_A_log", [1, NVH])
    I["gdn_dt_bias"] = din("gdn_dt_bias", [1, NVH])
    I["gdn_norm"] = din("gdn_norm", [1, 128])
    I["gdn_w_out"] = din("gdn_w_out", [VD, D])
    I["st_gdn"] = din("st_gdn", [NS, NVH, 128, 128])
    I["st_gdn_conv"] = din("st_gdn_conv", [NS * 3, CONVD])
    DFF = cfg.DFF
    I["ffn_w_up"] = din("ffn_w_up", [2, D, 2 * DFF])
    I["ffn_conv_w"] = din("ffn_conv_w", [2, DFF, 3])
    I["ffn_conv_b"] = din("ffn_conv_b", [2, DFF])
    I["ffn_w_down"] = din("ffn_w_down", [2, DFF, D])
    I["st_ffn_conv"] = din("st_ffn_conv", [2, NS * 2, DFF])
    for nm, shp, dt in (("ident_f", [128, 128], F32), ("ident_b", [128, 128], BF16),
                        ("ones_f", [128, 128], F32), ("ones_b", [128, 128], BF16),
                        ("masks_p", [5, 64, 64], F32), ("masks_s", [5, 64, 64], F32),
                        ("rowmask_s", [64, NS], F32)):
        I[nm] = din(nm, shp, dt)
    O = {}
    O["o_gdn_p"] = dout("o_gdn_p", [NVH, 128, 128])
    O["o_gdn_s"] = dout("o_gdn_s", [NS, NVH, 128, 128])
    O["o_gdnc"] = dout("o_gdnc", [(1 + NS) * 3, CONVD])
    PAST, NPG, NPOOL = cfg.PAST, cfg.PAST // 128, cfg.NPOOL
    I["nsa_w_in"] = din("nsa_w_in", [D, 5168])
    I["nsa_q_norm"] = din("nsa_q_norm", [1, 128])
    I["nsa_k_norm"] = din("nsa_k_norm", [1, 3, 128])
    I["st_win"] = din("st_win", [NS, 512, 1024])
    I["nsa_cmp_pe"] = din("nsa_cmp_pe", [2, 32, 128])
    I["nsa_cmp_w1"] = din("nsa_cmp_w1", [2, 4096, 256])
    I["nsa_cmp_w2"] = din("nsa_cmp_w2", [2, 256, 128])
    I["rel_bias"] = din("rel_bias", [32, 16])
    I["nsa_w_out"] = din("nsa_w_out", [2048, 2048])
    I["cache"] = din("cache", [NPOOL * 128, 2048])
    S["past"] = dscr("past", [NS * PAST, 2048], BF16)
    I["page_table"] = din("page_table", [NS, NPG], I32)
    hc = nsa_host_consts(cfg, PAST)
    for nm, arr in hc.items():
        if nm.startswith("dims"):
            continue
        I[nm] = din(nm, list(arr.shape), {np.dtype(np.float32): F32, np.dtype(np.int32): I32}.get(arr.dtype, BF16))
    S["gvec"] = dscr("gvec", [16, NSA_GL])
    S["oT2"] = dscr("oT2", [2048, NT], BF16)
    O["o_kv"] = dout("o_kv", [NT, 2048])
    O["o_pwin"] = dout("o_pwin", [512, 1024])
    O["o_swin"] = dout("o_swin", [NS, 512, 1024])
    S["qT"] = dscr("qT", [2048, NT], BF16)
    S["win_new"] = dscr("win_new", [NT, 1024])
    S["gates"] = dscr("gates", [NT, 48])
    O["o_ffnc"] = dout("o_ffnc", [2, (1 + NS) * 2, DFF])
    O["y"] = dout("y", [NT, D])
    if cfg.dbg:
        O["dbg_h1"] = dout("dbg_h1", [NT, D])
        O["dbg_h2"] = dout("dbg_h2", [NT, D])
        O["dbg_h3"] = dout("dbg_h3", [NT, D])
        O["dbg_oT2"] = dout("dbg_oT2", [2048, NT], BF16)
        O["dbg_gvec"] = dout("dbg_gvec", [16, NSA_GL])
        O["dbg_qT"] = dout("dbg_qT", [2048, NT], BF16)
    if cfg.dbg:
        O["dbg_xn"] = dout("dbg_xn", [128, KT, NT], BF16)
        O["dbg_o"] = dout("dbg_o", [NVH * 128, NT], BF16)
    S["oT"] = dscr("oT", [NVH * 128, NT], BF16)
    S["tmS"] = dscr("tmS", [64, 5 * NVH * NB], F32)
    S["h1"] = dscr("h1", [NT, D])
    S["h2"] = dscr("h2", [NT, D])
    S["h3"] = dscr("h3", [NT, D])
    S["actT"] = dscr("actT", [DFF, NT], BF16)

    es = ExitStack()
    with es:
        k = K(nc, es)
        dbuf = {n: Buf(a, n, multi=True) for n, a in list(I.items()) + list(O.items()) + list(S.items())}

        ident_f = k.sb("ident_f", [128, 128], F32)
        ident_b = k.sb("ident_b", [128, 128], BF16)
        ones_f = k.sb("ones_f", [128, 128], F32)
        ones_b = k.sb("ones_b", [128, 128], BF16)
        masks_p = k.sb("masks_p", [64, 5, 64], F32)
        masks_s = k.sb("masks_s", [64, 5, 64], F32)
        rowmask_s = k.sb("rowmask_s", [64, NS], F32)
        for nm, t in (("ident_f", ident_f), ("ident_b", ident_b), ("ones_f", ones_f), ("ones_b", ones_b),
                      ("rowmask_s", rowmask_s)):
            k.dma("sp", t[:], I[nm], writes=[t])
        k.dma("sp", masks_p[:], I["masks_p"].rearrange("m j i -> j m i"), writes=[masks_p])
        k.dma("sp", masks_s[:], I["masks_s"].rearrange("m j i -> j m i"), writes=[masks_s])

        pf = Ring([k.ps(f"pf{i}", [128, 512], F32) for i in range(6)])
        pbf = Ring([k.ps(f"pb{i}", [128, 1024], BF16) for i in range(2)])

        def norm_fm(src_name, gain_ap, xn):
          with k.scope():
              gain_bc = k.sb(f"gain_{src_name}", [128, D], F32)
              k.dma("sp", gain_bc[:], gain_ap.partition_broadcast(128), writes=[gain_bc])
              xt_r = Ring([k.sb(f"xt{i}_{src_name}", [128, D], F32) for i in range(2)])
              sq = k.sb(f"sq_{src_name}", [128, D], F32)
              xs_r = Ring([k.sb(f"xs{i}_{src_name}", [128, D], BF16) for i in range(2)])
              ss_r = Ring([k.sb(f"ss{i}_{src_name}", [128, 2], F32) for i in range(2)])
              src = dbuf[src_name]
              for (r0, R) in cfg.ttiles():
                  xt = xt_r.next()
                  xs = xs_r.next()
                  ss = ss_r.next()
                  k.dma("sp", xt[0:R, :], src[r0:r0 + R, :], reads=[src], writes=[xt])
                  k.op("act", lambda e: e.activation(out=sq[0:R, :], in_=xt[0:R, :], func=AF.Square),
                       reads=[xt], writes=[sq])
                  k.op("dve", lambda e: e.reduce_sum(out=ss[0:R, 0:1], in_=sq[0:R, :], axis=AX.X),
                       reads=[sq], writes=[ss])
                  k.op("dve", lambda e: e.tensor_scalar(out=ss[0:R, 1:2], in0=ss[0:R, 0:1], scalar1=1.0 / D,
                                                        scalar2=RMS_EPS, op0=ALU.mult, op1=ALU.add),
                       reads=[ss], writes=[ss])
                  k.op("act", lambda e: e.activation(out=ss[0:R, 0:1], in_=ss[0:R, 1:2], func=AF.Sqrt),
                       reads=[ss], writes=[ss])
                  k.op("dve", lambda e: e.reciprocal(out=ss[0:R, 0:1], in_=ss[0:R, 0:1]),
                       reads=[ss], writes=[ss])
                  k.op("dve", lambda e: e.scalar_tensor_tensor(out=xs[0:R, :], in0=xt[0:R, :], scalar=ss[0:R, 0:1],
                                                               in1=gain_bc[0:R, :], op0=ALU.mult, op1=ALU.mult),
                       reads=[xt, ss, gain_bc], writes=[xs])
                  for g4 in range(KT // 4):
                      pb = pbf.next()
                      for q in range(4):
                          kt = g4 * 4 + q
                          k.op("pe", lambda e: e.transpose(out=pb[:, q * 128:q * 128 + R],
                                                           in_=xs[0:R, kt * 128:(kt + 1) * 128],
                                                           identity=ident_b[0:R, 0:R]),
                               reads=[xs, ident_b], writes=[pb], inc=(q == 3))
                      eng = k.ev_eng()
                      src_v = pb[:, 0:512].rearrange("p (q r) -> p q r", q=4)[:, :, 0:R]
                      dst_v = xn[:, g4 * 4:(g4 + 1) * 4, r0:r0 + R]
                      if eng == "act":
                          k.op("act", lambda e: e.copy(out=dst_v, in_=src_v), reads=[pb], writes=[xn])
                      else:
                          k.op("dve", lambda e: e.tensor_copy(out=dst_v, in_=src_v), reads=[pb], writes=[xn])

        CC = dict(ident_f=ident_f, ident_b=ident_b, ones_f=ones_f, ones_b=ones_b,
                  masks_p=masks_p, masks_s=masks_s, rowmask_s=rowmask_s)
        with k.scope():
            xn = k.sb("xn", [128, KT, NT], BF16)
            norm_fm("xin", I["norm_mix"][0:1, :], xn)
            if cfg.dbg:
                k.dma("sp", O["dbg_xn"], xn[:], reads=[xn], writes=[dbuf["dbg_xn"]])
            gdn_layer(k, cfg, I, O, S, dbuf, xn, pf, pbf, CC)
        if cfg.stage >= 2:
            linear_tm(k, cfg, pf, dbuf, "go", "oT", S["oT"], "gdn_w_out", I["gdn_w_out"], VD,
                      "xin", I["xin"], "h1", S["h1"])
            ffn_layer(k, cfg, I, O, S, dbuf, pf, pbf, CC, 0, "h1", S["h1"], "h2", S["h2"], norm_fm,
                      hook=(Unpager(k, cfg, I, S, dbuf) if cfg.stage >= 4 else None))
        if cfg.stage >= 3:
            nsa_proj(k, cfg, I, O, S, dbuf, pf, pbf, CC, norm_fm, "h2")
        if cfg.stage >= 4:
            nsa_attn(k, cfg, I, O, S, dbuf, pf, pbf, CC)
            linear_tm(k, cfg, pf, dbuf, "no", "oT2", S["oT2"], "nsa_w_out", I["nsa_w_out"], 2048,
                      "h2", S["h2"], "h3", S["h3"])
            ffn_layer(k, cfg, I, O, S, dbuf, pf, pbf, CC, 1, "h3", S["h3"], "y", O["y"], norm_fm)
            if cfg.dbg:
                k.dma("sp", O["dbg_h3"], S["h3"], reads=[dbuf["h3"]], writes=[dbuf["dbg_h3"]])
                k.dma("sp", O["dbg_oT2"], S["oT2"], reads=[dbuf["oT2"]], writes=[dbuf["dbg_oT2"]])
                k.dma("sp", O["dbg_gvec"], S["gvec"], reads=[dbuf["gvec"]], writes=[dbuf["dbg_gvec"]])
                k.dma("sp", O["dbg_qT"], S["qT"], reads=[dbuf["qT"]], writes=[dbuf["dbg_qT"]])
        if cfg.stage >= 2:
            if cfg.dbg:
                k.dma("sp", O["dbg_h1"], S["h1"], reads=[dbuf["h1"]], writes=[dbuf["dbg_h1"]])
                k.dma("sp", O["dbg_h2"], S["h2"], reads=[dbuf["h2"]], writes=[dbuf["dbg_h2"]])

        k.finish()
        print("inst", k.n_inst, "wait", k.n_wait)
    return nc


def gdn_layer(k, cfg, I, O, S, dbuf, xn, pf, pbf, C):
    D, KT, NT, TP, NS, TS = cfg.D, cfg.KT, cfg.NT, cfg.TP, cfg.NS, cfg.TS
    NSAMP, NVH, NKH, NB, NBP = cfg.NSAMP, cfg.NVH, cfg.NKH, cfg.NB, cfg.NBP
    CONVD, PROJ, VD, KD = cfg.CONVD, cfg.PROJ, cfg.VD, cfg.KD
    ident_f, ident_b, ones_f, ones_b = C["ident_f"], C["ident_b"], C["ones_f"], C["ones_b"]
    masks_p, masks_s, rowmask_s = C["masks_p"], C["masks_s"], C["rowmask_s"]
    NCT = CONVD // 128
    PADW = 3 + TP + NS * 11
    SOFF = 3 + TP
    blocks = cfg.blocks()
    chunks = cfg.chunks()

    cw = k.sb("g_cw", [128, NCT, 4], F32)
    k.dma("sp", cw[:], I["gdn_conv_w"].rearrange("(t p) w -> p t w", p=128), writes=[cw])
    alog = k.sb("g_alog", [64, NVH], F32)
    dtb = k.sb("g_dtb", [64, NVH], F32)
    k.dma("sp", alog[:], I["gdn_A_log"].partition_broadcast(64), writes=[alog])
    k.dma("sp", dtb[:], I["gdn_dt_bias"].partition_broadcast(64), writes=[dtb])
    gnorm = k.sb("g_norm", [128, 1], F32)
    k.dma("sp", gnorm[:], I["gdn_norm"].rearrange("o d -> d o"), writes=[gnorm])
    stc_r = Ring([k.sb(f"g_stc{i}", [NS * 3, 128], F32) for i in range(2)])
    negA = k.sb("g_negA", [64, NVH], F32)
    k.op("act", lambda e: e.activation(out=negA[:], in_=alog[:], func=AF.Exp), reads=[alog], writes=[negA])
    k.op("dve", lambda e: e.tensor_scalar(out=negA[:], in0=negA[:], scalar1=-1.0, scalar2=None, op0=ALU.mult),
         reads=[negA], writes=[negA])

    tm_scope = k.scope()
    tm_scope.__enter__()
    wbd = k.sb("g_wbd", [128, KT, 2 * NVH], BF16)
    k.dma("pool", wbd[:], I["gdn_w_in"][:, CONVD + VD:PROJ].rearrange("(kt p) c -> p kt c", p=128), writes=[wbd])
    NH2 = 2 * NVH
    bl_all = k.sb("g_bl", [64, NB, NH2], F32)
    k.op("pool", lambda e: e.memset(bl_all[:], 0.0), writes=[bl_all])
    per_bank = 512 // NH2
    for g0 in range(0, NB, per_bank):
        gb = blocks[g0:g0 + per_bank]
        ps = pf.next()
        for bi, (t0, L) in enumerate(gb):
            for kt in range(KT):
                k.op("pe", lambda e: e.matmul(ps[0:L, bi * NH2:(bi + 1) * NH2], lhsT=xn[:, kt, t0:t0 + L],
                                              rhs=wbd[:, kt, :], start=(kt == 0), stop=(kt == KT - 1)),
                     reads=[xn, wbd], writes=[ps], inc=(kt == KT - 1))
        nfull = sum(1 for (_, L) in gb if L == 64)
        if nfull:
            k.op("act", lambda e: e.copy(out=bl_all[:, g0:g0 + nfull, :],
                                         in_=ps[0:64, 0:nfull * NH2].rearrange("p (b c) -> p b c", c=NH2)),
                 reads=[ps], writes=[bl_all])
        if nfull < len(gb):
            bi = nfull
            k.op("act", lambda e: e.copy(out=bl_all[0:NSAMP, g0 + bi, :], in_=ps[0:NSAMP, bi * NH2:(bi + 1) * NH2]),
                 reads=[ps], writes=[bl_all])

    def tm(name):
        return k.sb(name, [64, NB, NVH], F32)
    beta_t, g_t, G_t, gl_t, c_t, kd_t, nb_t, tmp1, tmp2 = (tm("g_beta"), tm("g_g"), tm("g_G"), tm("g_gl"),
                                                          tm("g_c"), tm("g_kd"), tm("g_nb"), tm("g_t1"), tm("g_t2"))
    blv = bl_all[:, :, 0:NVH]
    alv = bl_all[:, :, NVH:NH2]
    dtb_b = dtb[:].unsqueeze(1).to_broadcast([64, NB, NVH])
    negA_b = negA[:].unsqueeze(1).to_broadcast([64, NB, NVH])
    k.op("act", lambda e: e.activation(out=beta_t[:], in_=blv, func=AF.Sigmoid), reads=[bl_all], writes=[beta_t])
    k.op("dve", lambda e: e.tensor_tensor(out=tmp1[:], in0=alv, in1=dtb_b, op=ALU.add), reads=[bl_all, dtb], writes=[tmp1])
    k.op("dve", lambda e: e.tensor_scalar(out=tmp2[:], in0=tmp1[:], scalar1=-1.0, scalar2=None, op0=ALU.mult),
         reads=[tmp1], writes=[tmp2])
    k.op("dve", lambda e: e.tensor_tensor(out=tmp2[:], in0=tmp2[:], in1=tmp1[:], op=ALU.min),
         reads=[tmp1, tmp2], writes=[tmp2])
    k.op("act", lambda e: e.activation(out=tmp2[:], in_=tmp2[:], func=AF.Exp), reads=[tmp2], writes=[tmp2])
    k.op("act", lambda e: e.activation(out=tmp2[:], in_=tmp2[:], func=AF.Ln, bias=1.0), reads=[tmp2], writes=[tmp2])
    k.op("dve", lambda e: e.scalar_tensor_tensor(out=tmp1[:], in0=tmp1[:], scalar=0.0, in1=tmp2[:],
                                                 op0=ALU.max, op1=ALU.add), reads=[tmp1, tmp2], writes=[tmp1])
    k.op("dve", lambda e: e.tensor_tensor(out=g_t[:], in0=tmp1[:], in1=negA_b, op=ALU.mult),
         reads=[tmp1, negA], writes=[g_t])
    per_bank = 512 // NVH
    for (dst, mi) in ((G_t, 0), (gl_t, 1)):
        for g0 in range(0, NB, per_bank):
            gb = blocks[g0:g0 + per_bank]
            ps = pf.next()
            for bi, (t0, L) in enumerate(gb):
                mk_ = masks_p if L == 64 else masks_s
                k.op("pe", lambda e: e.matmul(ps[0:L, bi * NVH:(bi + 1) * NVH], lhsT=mk_[0:L, mi, 0:L],
                                              rhs=g_t[0:L, g0 + bi, :], start=True, stop=True),
                     reads=[mk_, g_t], writes=[ps], inc=(bi == len(gb) - 1))
            k.op("dve", lambda e: e.tensor_copy(out=dst[:, g0:g0 + len(gb), :],
                                                in_=ps[0:64, 0:len(gb) * NVH].rearrange("p (b c) -> p b c", c=NVH)),
                 reads=[ps], writes=[dst])
    k.op("act", lambda e: e.activation(out=tmp1[:], in_=G_t[:], func=AF.Exp), reads=[G_t], writes=[tmp1])
    k.op("dve", lambda e: e.scalar_tensor_tensor(out=c_t[:], in0=tmp1[:], scalar=-1.0, in1=beta_t[:],
                                                 op0=ALU.mult, op1=ALU.mult), reads=[tmp1, beta_t], writes=[c_t])
    k.op("dve", lambda e: e.tensor_tensor(out=tmp2[:], in0=gl_t[:], in1=G_t[:], op=ALU.subtract),
         reads=[gl_t, G_t], writes=[tmp2])
    k.op("act", lambda e: e.activation(out=kd_t[:], in_=tmp2[:], func=AF.Exp), reads=[tmp2], writes=[kd_t])
    k.op("dve", lambda e: e.tensor_scalar(out=nb_t[:], in0=beta_t[:], scalar1=-1.0, scalar2=None, op0=ALU.mult),
         reads=[beta_t], writes=[nb_t])

    tmH = k.sb("g_tmH", [64, 5, NVH, NB], F32)
    for idx, src_t in enumerate((beta_t, G_t, c_t, kd_t, nb_t)):
        k.op("pool", lambda e: e.tensor_copy(out=tmH[:, idx].rearrange("p h b -> p b h"), in_=src_t[:]),
             reads=[src_t], writes=[tmH])
    k.dma("sp", S["tmS"], tmH[:].rearrange("p k h b -> p (k h b)"), reads=[tmH], writes=[dbuf["tmS"]])
    tm_scope.__exit__(None, None, None)
    hd_r = Ring([k.sb(f"g_hd{i}", [64, 5, NB], F32) for i in range(2)])

    def load_hd(hv):
        hd = hd_r.next()
        k.dma("sp", hd[:], S["tmS"].rearrange("p (k h b) -> p k h b", k=5, h=NVH)[:, :, hv, :],
              reads=[dbuf["tmS"]], writes=[hd])
        return hd

    wt_r = Ring([k.sb(f"g_wt{i}", [128, KT, 128], BF16) for i in range(2)])
    prep = k.sb("g_prep", [128, PADW], F32)
    k.op("pool", lambda e: e.memset(prep[:], 0.0), writes=[prep])
    cq = k.sb("g_cq", [128, NT], F32)
    sqb = k.sb("g_sqb", [128, 512], BF16)
    rinv = k.sb("g_rinv", [128, 512], F32)
    epsb = k.sb("g_epsb", [128, 1], F32)
    k.op("pool", lambda e: e.memset(epsb[:], RMS_EPS), writes=[epsb])
    kqfm = k.sb("g_kqfm", [128, NB, 128], BF16)
    k.op("pool", lambda e: e.memset(kqfm[:], 0.0), writes=[kqfm])
    k_tok = k.sb("g_ktok", [64, NB, 128], BF16)
    vs = k.sb("g_vs", [128, NT], BF16)
    zsil = [k.sb("g_zsil", [128, NT], BF16)] * 2
    vb_tok = [k.sb("g_vb", [64, NB, 128], BF16)] * 2
    convout = k.sb("g_convout", [128, NCT, 16], F32)
    k.op("pool", lambda e: e.memset(convout[:], 0.0), writes=[convout])

    def _load_w_now(col0):
        wt = wt_r.next()
        k.dma("pool", wt[:], I["gdn_w_in"][:, col0:col0 + 128].rearrange("(kt p) c -> p kt c", p=128),
              reads=[dbuf["gdn_w_in"]], writes=[wt])
        return wt

    col_order = []
    for j_ in range(NKH):
        col_order += [j_ * 128, KD + j_ * 128]
        for a_ in range(2):
            hv_ = 2 * j_ + a_
            col_order += [2 * KD + hv_ * 128, CONVD + hv_ * 128]
    pre = {"i": 0, "wt": None}

    def load_w(col0):
        i = pre["i"]
        assert col_order[i] == col0, (i, col0, col_order[i])
        wt = pre["wt"] if pre["wt"] is not None else _load_w_now(col0)
        pre["i"] = i + 1
        pre["wt"] = _load_w_now(col_order[i + 1]) if i + 1 < len(col_order) else None
        return wt

    def proj(wt, consumer):
        for ci, (t0, n) in enumerate(chunks):
            ps = pf.next()
            for kt in range(KT):
                k.op("pe", lambda e: e.matmul(ps[:, 0:n], lhsT=wt[:, kt, :], rhs=xn[:, kt, t0:t0 + n],
                                              start=(kt == 0), stop=(kt == KT - 1)),
                     reads=[wt, xn], writes=[ps], inc=(kt == KT - 1))
            consumer(ci, t0, n, ps)

    def samp_view(buf_ap_2d, w, lo, hi):
        return buf_ap_2d.rearrange("p (s w) -> p s w", w=w)[:, :, lo:hi]

    def proj_conv_silu(col0, dst, wt=None):
        ct = col0 // 128
        if wt is None:
            wt = load_w(col0)

        def cons(ci, t0, n, ps):
            eng = k.ev_eng()
            if n == 512:
                dv = prep[:, 3 + t0:3 + t0 + n]
                sv = ps[:, 0:n]
            else:
                dv = samp_view(prep[:, SOFF:SOFF + NS * 11], 11, 3, 11)
                sv = ps[:, 0:n].rearrange("p (s w) -> p s w", w=TS)
            if eng == "act":
                k.op("act", lambda e: e.copy(out=dv, in_=sv), reads=[ps], writes=[prep])
            else:
                k.op("dve", lambda e: e.tensor_copy(out=dv, in_=sv), reads=[ps], writes=[prep])
        pst = pf.next()
        stc = stc_r.next()
        k.dma("sp", stc[:], I["st_gdn_conv"][:, ct * 128:(ct + 1) * 128], writes=[stc])
        k.op("pe", lambda e: e.transpose(out=pst[:, 0:NS * 3], in_=stc[0:NS * 3, 0:128],
                                         identity=ident_f[0:NS * 3, 0:NS * 3]),
             reads=[stc, ident_f], writes=[pst])
        k.op("dve", lambda e: e.tensor_copy(out=samp_view(prep[:, SOFF:SOFF + NS * 11], 11, 0, 3),
                                            in_=pst[:, 0:NS * 3].rearrange("p (s w) -> p s w", w=3)),
             reads=[pst], writes=[prep])
        proj(wt, cons)
        k.op("pool", lambda e: e.tensor_copy(out=convout[:, ct, 0:3], in_=prep[:, 3 + TP - 3:3 + TP]),
             reads=[prep], writes=[convout])
        k.op("pool", lambda e: e.tensor_copy(out=convout[:, ct, 3:3 + NS * 3].rearrange("p (s w) -> p s w", w=3),
                                             in_=samp_view(prep[:, SOFF:SOFF + NS * 11], 11, 8, 11)),
             reads=[prep], writes=[convout])
        for (ov, mkv) in ((cq[:, 0:TP], lambda i: prep[:, i:i + TP]),
                          (cq[:, TP:NT].rearrange("p (s w) -> p s w", w=TS),
                           lambda i: samp_view(prep[:, SOFF:SOFF + NS * 11], 11, i, i + TS))):
            k.op("dve", lambda e: e.tensor_scalar(out=ov, in0=mkv(3), scalar1=cw[:, ct, 3:4], scalar2=None, op0=ALU.mult),
                 reads=[prep, cw], writes=[cq])
            for i in range(3):
                k.op("dve", lambda e: e.scalar_tensor_tensor(out=ov, in0=mkv(i), scalar=cw[:, ct, i:i + 1], in1=ov,
                                                             op0=ALU.mult, op1=ALU.add),
                     reads=[prep, cw, cq], writes=[cq])
        k.op("act", lambda e: e.activation(out=dst[:], in_=cq[:], func=AF.Silu), reads=[cq], writes=[dst])

    def l2norm_to(src, which, scale):
        for ci, (t0, n) in enumerate(chunks):
            k.op("act", lambda e: e.activation(out=sqb[:, 0:n], in_=src[:, t0:t0 + n], func=AF.Square), reads=[src], writes=[sqb])
            ps = pf.next()
            k.op("pe", lambda e: e.matmul(ps[:, 0:n], lhsT=ones_b[:, :], rhs=sqb[:, 0:n], start=True, stop=True),
                 reads=[ones_b, sqb], writes=[ps])
            k.op("act", lambda e: e.activation(out=rinv[:, 0:n], in_=ps[:, 0:n], func=AF.Sqrt, bias=epsb[:, 0:1]),
                 reads=[ps, epsb], writes=[rinv])
            k.op("dve", lambda e: e.reciprocal(out=rinv[:, 0:n], in_=rinv[:, 0:n]), reads=[rinv], writes=[rinv])
            if n == 512:
                b0 = t0 // 64
                dv = kqfm[:, b0:b0 + 8, which * 64:which * 64 + 64]
                sv = src[:, t0:t0 + n].rearrange("p (b i) -> p b i", i=64)
                rv = rinv[:, 0:n].rearrange("p (b i) -> p b i", i=64)
            else:
                dv = kqfm[:, NBP, which * 64:which * 64 + n]
                sv = src[:, t0:t0 + n]
                rv = rinv[:, 0:n]
            k.op("dve", lambda e: e.scalar_tensor_tensor(out=dv, in0=sv, scalar=scale, in1=rv, op0=ALU.mult, op1=ALU.mult),
                 reads=[src, rinv], writes=[kqfm])

    def to_tok(src_fn, evac):
        for g0 in range(0, NB, 8):
            gb = blocks[g0:g0 + 8]
            pb = pbf.next()
            for bi, (t0, L) in enumerate(gb):
                k.op("pe", lambda e: e.transpose(out=pb[0:L, bi * 128:(bi + 1) * 128], in_=src_fn(g0 + bi, t0, L),
                                                 identity=ident_b[:, :]),
                     reads=src_fn.reads + [ident_b], writes=[pb], inc=(bi == len(gb) - 1))
            evac(g0, gb, pb)

    NG = 4
    dgG = k.sb("g_dgG", [64, NG, 64], F32)
    dgB = k.sb("g_dgB", [64, NG, 64], F32)
    t1 = k.sb("g_t1b", [64, NG, 64], F32)
    tI = k.sb("g_tI", [64, NG, 64], F32)
    tS = k.sb("g_tS", [64, NG, 64], F32)
    tL = k.sb("g_tL", [64, NG, 64], F32)
    kkE = k.sb("g_kkE", [64, NG, 64], F32)
    kkL = k.sb("g_kkL", [64, NG, 64], F32)
    eGbc = k.sb("g_eGbc", [128, NG * 64], F32)
    PR = [k.sb(f"g_PR{i}", [64, NG, 128], BF16) for i in range(2)]
    PT = [k.sb(f"g_PT{i}", [64, NG, 64], BF16) for i in range(2)]
    slots = [(dgG, dgB, t1, tI, tS, tL, kkE, kkL, eGbc, PR, PT)]
    pBs_slots = [k.sb(f"g_pBs{i}", [64, NG, 64], F32) for i in range(3)]
    for si in range(1, 3):
        slots.append(tuple([k.sb(f"g{si}_{nm}", [64, NG, 64], F32) for nm in ("dgG", "dgB", "t1", "tI", "tS", "tL", "kkE", "kkL")]
                           + [k.sb(f"g{si}_eGbc", [128, NG * 64], F32),
                              [k.sb(f"g{si}_PR{i}", [64, NG, 128], BF16) for i in range(2)],
                              [k.sb(f"g{si}_PT{i}", [64, NG, 64], BF16) for i in range(2)]]))
    heads = []
    for a in range(1):
        h = dict(
            AQT=k.sb(f"g_AQT{a}", [64, NB, 64], BF16), TT=k.sb(f"g_TT{a}", [64, NB, 64], BF16),
            qdec=k.sb(f"g_qdec{a}", [128, NB, 64], BF16), kdec=k.sb(f"g_kdec{a}", [64, 2, 128], BF16),
            egl=k.sb(f"g_egl{a}", [128, NB, 4], F32),
            S=k.sb(f"g_S{a}", [128, 128], F32), Sb=k.sb(f"g_Sb{a}", [128, 128], BF16),
            Ss=[k.sb(f"g_Ss{a}_{s}", [128, 128], F32) for s in range(NS)],
            Ssb=[k.sb(f"g_Ssb{a}_{s}", [128, 128], BF16) for s in range(NS)],
            r=k.sb(f"g_r{a}", [64, 128], BF16), u=k.sb(f"g_u{a}", [64, 128], BF16),
            otok=k.sb(f"g_otok{a}", [64, 8, 128], F32), osq=k.sb(f"g_osq{a}", [64, 8, 128], F32),
            oss=k.sb(f"g_oss{a}", [64, 16], F32), on=k.sb(f"g_on{a}", [64, 8, 128], BF16),
            oT=k.sb(f"g_oT{a}", [128, 512], BF16),
            kzp=k.sb(f"g_kzp{a}", [128, NS, NSAMP], BF16), qzp=k.sb(f"g_qzp{a}", [128, NS, NSAMP], BF16),
            kds=k.sb(f"g_kds{a}", [64, NS, 128], BF16),
        )
        heads.append(h)
    heads.append(heads[0])
    HD = {}

    def grp_gen(hv, hd, g0, gb, slot):
        H = heads[0]
        dgG, dgB, t1, tI, tS, tL, kkE, kkL, eGbc, PR, PT = slot
        pBs = pBs_slots[slots.index(slot)]
        if True:
            nb = len(gb)
            L = gb[0][1]
            mk_ = masks_p if L == 64 else masks_s
            idb = ident_f[0:L, 0:64].unsqueeze(1).to_broadcast([L, nb, 64])
            k.op("dve", lambda e: e.tensor_tensor(out=dgG[0:L, 0:nb, :], in0=idb,
                                                  in1=hd[0:L, 1, g0:g0 + nb].unsqueeze(2).to_broadcast([L, nb, 64]),
                                                  op=ALU.mult), reads=[ident_f, hd], writes=[dgG])
            k.op("dve", lambda e: e.tensor_tensor(out=dgB[0:L, 0:nb, :], in0=idb,
                                                  in1=hd[0:L, 0, g0:g0 + nb].unsqueeze(2).to_broadcast([L, nb, 64]),
                                                  op=ALU.mult), reads=[ident_f, hd], writes=[dgB])
            pG = pf.next()
            k.op("pe", lambda e: e.matmul(pG[:, 0:nb * 64], lhsT=ones_f[0:L, :],
                                          rhs=dgG[0:L, 0:nb, :].rearrange("p b i -> p (b i)"), start=True, stop=True),
                 reads=[ones_f, dgG], writes=[pG])
            pB = pf.next()
            k.op("pe", lambda e: e.matmul(pB[0:64, 0:nb * 64], lhsT=ones_f[0:L, 0:64],
                                          rhs=dgB[0:L, 0:nb, :].rearrange("p b i -> p (b i)"), start=True, stop=True),
                 reads=[ones_f, dgB], writes=[pB])
            pGv = pG[0:L, 0:nb * 64].rearrange("p (b i) -> p b i", i=64)
            pBv = pB[0:L, 0:nb * 64].rearrange("p (b i) -> p b i", i=64)
            yield
            k.op("dve", lambda e: e.tensor_tensor(out=t1[0:L, 0:nb, :], in0=pGv,
                                                  in1=hd[0:L, 1, g0:g0 + nb].unsqueeze(2).to_broadcast([L, nb, 64]),
                                                  op=ALU.subtract), reads=[pG, hd], writes=[t1])
            k.op("dve", lambda e: e.tensor_copy(out=pBs[0:L, 0:nb, :], in_=pBv), reads=[pB], writes=[pBs])
            k.op("act", lambda e: e.activation(out=eGbc[:, 0:nb * 64], in_=pG[:, 0:nb * 64], func=AF.Exp),
                 reads=[pG], writes=[eGbc])
            for (dst, mi) in ((tI, 2), (tS, 3), (tL, 4)):
                k.op("pool", lambda e: e.tensor_tensor(out=dst[0:L, 0:nb, :], in0=t1[0:L, 0:nb, :],
                                                       in1=mk_[0:L, mi, :].unsqueeze(1).to_broadcast([L, nb, 64]),
                                                       op=ALU.add), reads=[t1, mk_], writes=[dst])
            yield
            k.op("act", lambda e: e.activation(out=tI[0:L, 0:nb, :], in_=tI[0:L, 0:nb, :], func=AF.Exp), reads=[tI], writes=[tI])
            k.op("act", lambda e: e.activation(out=tS[0:L, 0:nb, :], in_=tS[0:L, 0:nb, :], func=AF.Exp), reads=[tS], writes=[tS])
            k.op("act", lambda e: e.activation(out=tL[0:L, 0:nb, :], in_=tL[0:L, 0:nb, :], func=AF.Exp, scale=-1.0),
                 reads=[tL], writes=[tL])
            yield
            pK = pf.next()
            for bi, (t0, Lb) in enumerate(gb):
                k.op("pe", lambda e: e.matmul(pK[0:L, bi * 128:(bi + 1) * 128], lhsT=kqfm[:, g0 + bi, 0:L],
                                              rhs=kqfm[:, g0 + bi, :], start=True, stop=True),
                     reads=[kqfm], writes=[pK], inc=(bi == nb - 1))
            pKv = pK[0:L, 0:nb * 128].rearrange("p (b c) -> p b c", c=128)
            k.op("dve", lambda e: e.tensor_tensor(out=H["AQT"][0:L, g0:g0 + nb, 0:L], in0=pKv[:, :, 64:64 + L],
                                                  in1=tI[0:L, 0:nb, 0:L], op=ALU.mult),
                 reads=[pK, tI], writes=[H["AQT"]])
            k.op("dve", lambda e: e.tensor_tensor(out=kkE[0:L, 0:nb, 0:L], in0=pKv[:, :, 0:L], in1=tS[0:L, 0:nb, 0:L],
                                                  op=ALU.mult), reads=[pK, tS], writes=[kkE])
            k.op("dve", lambda e: e.tensor_tensor(out=kkL[0:L, 0:nb, 0:L], in0=pKv[:, :, 0:L], in1=tL[0:L, 0:nb, 0:L],
                                                  op=ALU.mult), reads=[pK, tL], writes=[kkL])
            pr, pt = PR[0], PT[0]
            yield
            k.op("dve", lambda e: e.scalar_tensor_tensor(out=pr[0:L, 0:nb, 0:L], in0=kkE[0:L, 0:nb, 0:L], scalar=-1.0,
                                                         in1=pBs[0:L, 0:nb, 0:L], op0=ALU.mult, op1=ALU.mult),
                 reads=[kkE, pBs], writes=[pr])
            k.op("pool", lambda e: e.tensor_copy(out=pr[0:L, 0:nb, 64:64 + L],
                                                 in_=ident_f[0:L, 0:L].unsqueeze(1).to_broadcast([L, nb, L])),
                 reads=[ident_f], writes=[pr])
            k.op("dve", lambda e: e.tensor_tensor(out=pt[0:L, 0:nb, 0:L], in0=kkL[0:L, 0:nb, 0:L],
                                                  in1=hd[0:L, 4, g0:g0 + nb].unsqueeze(2).to_broadcast([L, nb, L]),
                                                  op=ALU.mult), reads=[kkL, hd], writes=[pt])
            for s in range(6):
                yield
                pr, pt = PR[s % 2], PT[s % 2]
                prn, ptn = PR[(s + 1) % 2], PT[(s + 1) % 2]
                p1 = pf.next()
                last = (s == 5)
                for bi in range(nb):
                    k.op("pe", lambda e: e.matmul(p1[0:L, bi * 128:bi * 128 + 64 + L], lhsT=pt[0:L, bi, 0:L],
                                                  rhs=pr[0:L, bi, 0:64 + L], start=True, stop=True),
                         reads=[pt, pr], writes=[p1], inc=(bi == nb - 1))
                p1v = p1[0:L, 0:nb * 128].rearrange("p (b c) -> p b c", c=128)
                if not last:
                    p2 = pf.next()
                    for bi in range(nb):
                        k.op("pe", lambda e: e.matmul(p2[0:L, bi * 64:bi * 64 + L], lhsT=pr[0:L, bi, 0:L],
                                                      rhs=pt[0:L, bi, 0:L], start=True, stop=True),
                             reads=[pt, pr], writes=[p2], inc=(bi == nb - 1))
                    p2v = p2[0:L, 0:nb * 64].rearrange("p (b c) -> p b c", c=64)
                    k.op("act", lambda e: e.copy(out=prn[0:L, 0:nb, 0:L], in_=p1v[:, :, 0:L]), reads=[p1], writes=[prn])
                    k.op("act", lambda e: e.copy(out=ptn[0:L, 0:nb, 0:L], in_=p2v[:, :, 0:L]), reads=[p2], writes=[ptn])
                    k.op("dve", lambda e: e.tensor_tensor(out=prn[0:L, 0:nb, 64:64 + L], in0=p1v[:, :, 64:64 + L],
                                                          in1=pr[0:L, 0:nb, 64:64 + L], op=ALU.add),
                         reads=[p1, pr], writes=[prn])
                else:
                    k.op("dve", lambda e: e.tensor_tensor(out=H["TT"][0:L, g0:g0 + nb, 0:L], in0=p1v[:, :, 64:64 + L],
                                                          in1=pr[0:L, 0:nb, 64:64 + L], op=ALU.add),
                         reads=[p1, pr], writes=[H["TT"]])
            yield
            eGv = eGbc[:, 0:nb * 64].rearrange("p (b i) -> p b i", i=64)
            k.op("pool", lambda e: e.tensor_tensor(out=H["qdec"][:, g0:g0 + nb, 0:L], in0=kqfm[:, g0:g0 + nb, 64:64 + L],
                                                   in1=eGv[:, :, 0:L], op=ALU.mult),
                 reads=[kqfm, eGbc], writes=[H["qdec"]])
            if L == 64:
                k.op("pool", lambda e: e.tensor_copy(out=H["egl"][:, g0:g0 + nb, 0:1], in_=eGv[:, :, 63:64]),
                     reads=[eGbc], writes=[H["egl"]])
            else:
                k.op("pool", lambda e: e.tensor_copy(out=H["egl"][:, g0, 0:NS],
                                                     in_=eGbc[:, 0:NSAMP].rearrange("p (s w) -> p s w", w=TS)[:, :, TS - 1]),
                     reads=[eGbc], writes=[H["egl"]])

    def precompute(hv, hd):
        groups = [(g0, blocks[g0:g0 + NG]) for g0 in range(0, NBP, NG)] + [(NBP, [blocks[NBP]])]
        NSL = len(slots)
        for i0 in range(0, len(groups), NSL):
            alive = [grp_gen(hv, hd, g0, gb, slots[si]) for si, (g0, gb) in enumerate(groups[i0:i0 + NSL])]
            while alive:
                for gen in list(alive):
                    try:
                        next(gen)
                    except StopIteration:
                        alive.remove(gen)

    def o_finish(a, hv, t0, nblk, L):
        H = heads[a]
        k.op("act", lambda e: e.activation(out=H["osq"][0:L, 0:nblk, :], in_=H["otok"][0:L, 0:nblk, :], func=AF.Square),
             reads=[H["otok"]], writes=[H["osq"]])
        k.op("dve", lambda e: e.reduce_sum(out=H["oss"][0:L, 0:nblk], in_=H["osq"][0:L, 0:nblk, :], axis=AX.X),
             reads=[H["osq"]], writes=[H["oss"]])
        k.op("dve", lambda e: e.tensor_scalar(out=H["oss"][0:L, 8:8 + nblk], in0=H["oss"][0:L, 0:nblk], scalar1=1.0 / 128,
                                              scalar2=RMS_EPS, op0=ALU.mult, op1=ALU.add), reads=[H["oss"]], writes=[H["oss"]])
        k.op("act", lambda e: e.activation(out=H["oss"][0:L, 0:nblk], in_=H["oss"][0:L, 8:8 + nblk], func=AF.Sqrt),
             reads=[H["oss"]], writes=[H["oss"]])
        k.op("dve", lambda e: e.reciprocal(out=H["oss"][0:L, 0:nblk], in_=H["oss"][0:L, 0:nblk]),
             reads=[H["oss"]], writes=[H["oss"]])
        k.op("dve", lambda e: e.tensor_tensor(out=H["on"][0:L, 0:nblk, :], in0=H["otok"][0:L, 0:nblk, :],
                                              in1=H["oss"][0:L, 0:nblk].unsqueeze(2).to_broadcast([L, nblk, 128]),
                                              op=ALU.mult), reads=[H["otok"], H["oss"]], writes=[H["on"]])
        pb = pbf.next()
        for bi in range(nblk):
            k.op("pe", lambda e: e.transpose(out=pb[:, bi * L:(bi + 1) * L], in_=H["on"][0:L, bi, :],
                                             identity=ident_b[0:L, 0:L]),
                 reads=[H["on"], ident_b], writes=[pb], inc=(bi == nblk - 1))
        n = nblk * L
        k.op("dve", lambda e: e.scalar_tensor_tensor(out=H["oT"][:, 0:n], in0=pb[:, 0:n], scalar=gnorm[:, 0:1],
                                                     in1=zsil[a][:, t0:t0 + n], op0=ALU.mult, op1=ALU.mult),
             reads=[pb, gnorm, zsil[a]], writes=[H["oT"]])
        k.dma("sp", S["oT"][hv * 128:(hv + 1) * 128, t0:t0 + n], H["oT"][:, 0:n], reads=[H["oT"]], writes=[dbuf["oT"]])

    kd_r = Ring([0, 1])

    def mk_kdec(H, hd, b, L):
        slot = kd_r.next()
        k.op("pool", lambda e: e.tensor_scalar(out=H["kdec"][0:L, slot, :], in0=k_tok[0:L, b, :],
                                               scalar1=hd[0:L, 3, b:b + 1], scalar2=None, op0=ALU.mult),
             reads=[k_tok, hd], writes=[H["kdec"]])
        return slot

    def scan(hv, hd):
        a = 0
        H = heads[0]
        k.op("pool", lambda e: e.memset(H["S"][:], 0.0), writes=[H["S"]])
        k.op("pool", lambda e: e.memset(H["Sb"][:], 0.0), writes=[H["Sb"]])
        for b in range(NBP):
            kslot = mk_kdec(H, hd, b, 64)
            pA = pf.next()
            k.op("pe", lambda e: e.matmul(pA[0:64, 0:128], lhsT=kqfm[:, b, 0:64], rhs=H["Sb"][:, :], start=True, stop=True),
                 reads=[kqfm, H["Sb"]], writes=[pA])
            k.op("dve", lambda e: e.scalar_tensor_tensor(out=H["r"][:, :], in0=pA[0:64, 0:128], scalar=hd[:, 2, b:b + 1],
                                                         in1=vb_tok[a][:, b, :], op0=ALU.mult, op1=ALU.add),
                 reads=[pA, hd, vb_tok[a]], writes=[H["r"]])
            pB_ = pf.next()
            k.op("pe", lambda e: e.matmul(pB_[0:64, 0:128], lhsT=H["TT"][:, b, :], rhs=H["r"][:, :], start=True, stop=True),
                 reads=[H["TT"], H["r"]], writes=[pB_])
            k.op("act", lambda e: e.copy(out=H["u"][:, :], in_=pB_[0:64, 0:128]), reads=[pB_], writes=[H["u"]])
            pO = pf.next()
            k.op("pe", lambda e: e.matmul(pO[0:64, 0:128], lhsT=H["qdec"][:, b, :], rhs=H["Sb"][:, :], start=True, stop=False),
                 reads=[H["qdec"], H["Sb"]], writes=[pO], inc=False)
            k.op("pe", lambda e: e.matmul(pO[0:64, 0:128], lhsT=H["AQT"][:, b, :], rhs=H["u"][:, :], start=False, stop=True),
                 reads=[H["AQT"], H["u"]], writes=[pO])
            pS = pf.next()
            k.op("pe", lambda e: e.matmul(pS[:, 0:128], lhsT=H["kdec"][:, kslot, :], rhs=H["u"][:, :], start=True, stop=True),
                 reads=[H["kdec"], H["u"]], writes=[pS])
            k.op("dve", lambda e: e.scalar_tensor_tensor(out=H["S"][:, :], in0=H["S"][:, :], scalar=H["egl"][:, b, 0:1],
                                                         in1=pS[:, 0:128], op0=ALU.mult, op1=ALU.add),
                 reads=[H["S"], H["egl"], pS], writes=[H["S"]])
            k.op("act", lambda e: e.copy(out=H["Sb"][:, :], in_=H["S"][:, :]), reads=[H["S"]], writes=[H["Sb"]])
            k.op("act", lambda e: e.copy(out=H["otok"][:, b % 8, :], in_=pO[0:64, 0:128]), reads=[pO], writes=[H["otok"]])
            if b % 8 == 7:
                o_finish(a, hv, (b - 7) * 64, 8, 64)
        k.dma("sp", O["o_gdn_p"][hv], H["S"][:, :], reads=[H["S"]], writes=[dbuf["o_gdn_p"]])
        L = NSAMP
        b = NBP
        kslot = mk_kdec(H, hd, b, L)
        k.op("pool", lambda e: e.memset(H["kzp"][:], 0.0), writes=[H["kzp"]])
        k.op("pool", lambda e: e.memset(H["qzp"][:], 0.0), writes=[H["qzp"]])
        for s in range(NS):
            k.dma("sp", H["Ss"][s][:, :], I["st_gdn"][s, hv], writes=[H["Ss"][s]])
            k.op("act", lambda e: e.copy(out=H["Ssb"][s][:, :], in_=H["Ss"][s][:, :]), reads=[H["Ss"][s]], writes=[H["Ssb"][s]])
            k.op("pool", lambda e: e.tensor_copy(out=H["kzp"][:, s, s * TS:(s + 1) * TS], in_=kqfm[:, b, s * TS:(s + 1) * TS]),
                 reads=[kqfm], writes=[H["kzp"]])
            k.op("pool", lambda e: e.tensor_copy(out=H["qzp"][:, s, s * TS:(s + 1) * TS], in_=H["qdec"][:, b, s * TS:(s + 1) * TS]),
                 reads=[H["qdec"]], writes=[H["qzp"]])
            k.op("dve", lambda e: e.tensor_scalar(out=H["kds"][0:L, s, :], in0=H["kdec"][0:L, kslot, :],
                                                  scalar1=rowmask_s[0:L, s:s + 1], scalar2=None, op0=ALU.mult),
                 reads=[H["kdec"], rowmask_s], writes=[H["kds"]])
        pA = pf.next()
        for s in range(NS):
            k.op("pe", lambda e: e.matmul(pA[0:L, 0:128], lhsT=H["kzp"][:, s, :], rhs=H["Ssb"][s][:, :],
                                          start=(s == 0), stop=(s == NS - 1)),
                 reads=[H["kzp"], H["Ssb"][s]], writes=[pA], inc=(s == NS - 1))
        k.op("dve", lambda e: e.scalar_tensor_tensor(out=H["r"][0:L, :], in0=pA[0:L, 0:128], scalar=hd[0:L, 2, b:b + 1],
                                                     in1=vb_tok[a][0:L, b, :], op0=ALU.mult, op1=ALU.add),
             reads=[pA, hd, vb_tok[a]], writes=[H["r"]])
        pB_ = pf.next()
        k.op("pe", lambda e: e.matmul(pB_[0:L, 0:128], lhsT=H["TT"][0:L, b, 0:L], rhs=H["r"][0:L, :], start=True, stop=True),
             reads=[H["TT"], H["r"]], writes=[pB_])
        k.op("act", lambda e: e.copy(out=H["u"][0:L, :], in_=pB_[0:L, 0:128]), reads=[pB_], writes=[H["u"]])
        pO = pf.next()
        for s in range(NS):
            k.op("pe", lambda e: e.matmul(pO[0:L, 0:128], lhsT=H["qzp"][:, s, :], rhs=H["Ssb"][s][:, :],
                                          start=(s == 0), stop=False),
                 reads=[H["qzp"], H["Ssb"][s]], writes=[pO], inc=False)
        k.op("pe", lambda e: e.matmul(pO[0:L, 0:128], lhsT=H["AQT"][0:L, b, 0:L], rhs=H["u"][0:L, :], start=False, stop=True),
             reads=[H["AQT"], H["u"]], writes=[pO])
        k.op("act", lambda e: e.copy(out=H["otok"][0:L, 0, :], in_=pO[0:L, 0:128]), reads=[pO], writes=[H["otok"]])
        for s in range(NS):
            pS = pf.next()
            k.op("pe", lambda e: e.matmul(pS[:, 0:128], lhsT=H["kds"][0:L, s, :], rhs=H["u"][0:L, :], start=True, stop=True),
                 reads=[H["kds"], H["u"]], writes=[pS])
            k.op("dve", lambda e: e.scalar_tensor_tensor(out=H["Ss"][s][:, :], in0=H["Ss"][s][:, :], scalar=H["egl"][:, b, s:s + 1],
                                                         in1=pS[:, 0:128], op0=ALU.mult, op1=ALU.add),
                 reads=[H["Ss"][s], H["egl"], pS], writes=[H["Ss"][s]])
            k.dma("sp", O["o_gdn_s"][s, hv], H["Ss"][s][:, :], reads=[H["Ss"][s]], writes=[dbuf["o_gdn_s"]])
        o_finish(a, hv, TP, 1, L)

    hd_next = load_hd(0)
    for j in range(NKH):
        proj_conv_silu(j * 128, cq)
        l2norm_to(cq, 1, 128 ** -0.5)
        proj_conv_silu(KD + j * 128, cq)
        l2norm_to(cq, 0, 1.0)

        def ksrc(bidx, t0, L):
            return kqfm[:, bidx, 0:L]
        ksrc.reads = [kqfm]

        def kevac(g0, gb, pb):
            nfull = sum(1 for (_, L) in gb if L == 64)
            if nfull:
                k.op("act", lambda e: e.copy(out=k_tok[:, g0:g0 + nfull, :],
                                             in_=pb[0:64, 0:nfull * 128].rearrange("p (b c) -> p b c", c=128)),
                     reads=[pb], writes=[k_tok])
            if nfull < len(gb):
                k.op("act", lambda e: e.copy(out=k_tok[0:NSAMP, g0 + nfull, :], in_=pb[0:NSAMP, nfull * 128:(nfull + 1) * 128]),
                     reads=[pb], writes=[k_tok])
        to_tok(ksrc, kevac)
        for a in range(2):
            hv = 2 * j + a
            hd = hd_next
            if hv + 1 < NVH:
                hd_next = load_hd(hv + 1)
            proj_conv_silu(2 * KD + hv * 128, vs)

            def vsrc(bidx, t0, L):
                return vs[:, t0:t0 + L]
            vsrc.reads = [vs]

            def vevac(g0, gb, pb, hd=hd):
                nfull = sum(1 for (_, L) in gb if L == 64)
                if nfull:
                    k.op("dve", lambda e: e.tensor_tensor(
                        out=vb_tok[0][:, g0:g0 + nfull, :],
                        in0=pb[0:64, 0:nfull * 128].rearrange("p (b c) -> p b c", c=128),
                        in1=hd[:, 0, g0:g0 + nfull].unsqueeze(2).to_broadcast([64, nfull, 128]), op=ALU.mult),
                        reads=[pb, hd], writes=[vb_tok[0]])
                if nfull < len(gb):
                    k.op("dve", lambda e: e.tensor_scalar(
                        out=vb_tok[0][0:NSAMP, g0 + nfull, :], in0=pb[0:NSAMP, nfull * 128:(nfull + 1) * 128],
                        scalar1=hd[0:NSAMP, 0, g0 + nfull:g0 + nfull + 1], scalar2=None, op0=ALU.mult),
                        reads=[pb, hd], writes=[vb_tok[0]])
            to_tok(vsrc, vevac)
            wt = load_w(CONVD + hv * 128)

            def zcons(ci, t0, n, ps):
                k.op("act", lambda e: e.activation(out=zsil[0][:, t0:t0 + n], in_=ps[:, 0:n], func=AF.Silu),
                     reads=[ps], writes=[zsil[0]])
            proj(wt, zcons)
            precompute(hv, hd)
            scan(hv, hd)

    nrow = (1 + NS) * 3
    class _CstView:
        def __init__(self, off):
            self.off = off

        def __getitem__(self, idx):
            return prep[0:16, self.off:self.off + 512]
    for g0 in range(0, NCT, 4):
        ps = pf.next()
        coff = ((g0 // 4) % 2) * 512
        for q in range(4):
            ct = g0 + q
            k.op("pe", lambda e: e.transpose(out=ps[0:16, q * 128:(q + 1) * 128], in_=convout[:, ct, :],
                                             identity=ident_f[:, :]),
                 reads=[convout, ident_f], writes=[ps], inc=(q == 3))
        k.op("dve", lambda e: e.tensor_copy(out=prep[0:16, coff:coff + 512], in_=ps[0:16, 0:512]), reads=[ps], writes=[prep])
        k.dma("sp", O["o_gdnc"][:, g0 * 128:(g0 + 4) * 128], prep[0:nrow, coff:coff + 512], reads=[prep], writes=[dbuf["o_gdnc"]])
    if cfg.dbg:
        k.dma("sp", O["dbg_o"], S["oT"], reads=[dbuf["oT"]], writes=[dbuf["dbg_o"]])


_NC_CACHE = {}


def kernel(x_prompt, x_sample, state_gdn, state_gdn_conv, cache_nsa_kv, state_nsa_win, state_ffn_conv,
           page_table, norm_mix, norm_ffn, gdn_w_in, gdn_conv_w, gdn_A_log, gdn_dt_bias, gdn_norm, gdn_w_out,
           nsa_w_in, nsa_q_norm, nsa_k_norm, nsa_cmp_pe, nsa_cmp_w1, nsa_cmp_w2, rel_bias, nsa_w_out,
           ffn_w_up, ffn_conv_w, ffn_conv_b, ffn_w_down):
    cfg = Cfg()
    f = lambda a: np.ascontiguousarray(np.asarray(a))
    x_prompt, x_sample = f(x_prompt), f(x_sample)
    consts = host_consts(cfg)
    nconsts = {k_: v for k_, v in nsa_host_consts(cfg, cfg.PAST).items() if not k_.startswith("dims")}
    cache_r = f(cache_nsa_kv)[0].reshape(cfg.NPOOL * 128, 2048)
    n = 8
    NS = cfg.NS
    in_maps = []
    for c in range(n):
        b = c // 2
        sl = slice(NS * c, NS * (c + 1))
        m = dict(
            xin=np.ascontiguousarray(np.concatenate([x_prompt[b], x_sample[sl].reshape(-1, cfg.D)], 0)),
            norm_mix=f(norm_mix), norm_ffn=f(norm_ffn),
            gdn_w_in=f(gdn_w_in)[0], gdn_conv_w=f(gdn_conv_w)[0], gdn_A_log=f(gdn_A_log), gdn_dt_bias=f(gdn_dt_bias),
            gdn_norm=f(gdn_norm), gdn_w_out=f(gdn_w_out)[0],
            st_gdn=np.ascontiguousarray(f(state_gdn)[0, sl]),
            st_gdn_conv=np.ascontiguousarray(f(state_gdn_conv)[0, sl].reshape(NS * 3, cfg.CONVD)),
            ffn_w_up=f(ffn_w_up), ffn_conv_w=f(ffn_conv_w), ffn_conv_b=f(ffn_conv_b), ffn_w_down=f(ffn_w_down),
            st_ffn_conv=np.ascontiguousarray(f(state_ffn_conv)[:, sl].reshape(2, NS * 2, cfg.DFF)),
            nsa_w_in=f(nsa_w_in)[0], nsa_q_norm=f(nsa_q_norm), nsa_k_norm=f(nsa_k_norm),
            st_win=np.ascontiguousarray(f(state_nsa_win)[0, sl].reshape(NS, 512, 1024)),
            nsa_cmp_pe=f(nsa_cmp_pe)[0], nsa_cmp_w1=f(nsa_cmp_w1)[0], nsa_cmp_w2=f(nsa_cmp_w2)[0],
            rel_bias=f(rel_bias), nsa_w_out=f(nsa_w_out)[0],
            cache=cache_r, page_table=np.ascontiguousarray(f(page_table)[sl]).astype(np.int32),
        )
        m.update(nconsts)
        m.update(consts)
        in_maps.append(m)
    if "nc" not in _NC_CACHE:
        _NC_CACHE["nc"] = build(cfg)
    nc = _NC_CACHE["nc"]
    res = run_bass_kernel_spmd(nc, in_maps, core_ids=list(range(n)))
    R = res.results
    f32 = np.float32
    B, SEQ, DB, DS = 4, 2048, 32, 8
    p_gdn = np.stack([R[2 * b]["o_gdn_p"] for b in range(B)])[None].astype(f32)
    p_gdn_conv = np.stack([R[2 * b]["o_gdnc"][0:3] for b in range(B)])[None].astype(f32)
    s_gdn = np.concatenate([R[c]["o_gdn_s"] for c in range(n)], 0)[None].astype(f32)
    s_gdn_conv = np.concatenate([R[c]["o_gdnc"][3:3 + 3 * NS].reshape(NS, 3, -1) for c in range(n)], 0)[None].astype(f32)
    TP = cfg.TP
    y_prompt = np.stack([R[2 * b]["y"][:TP] for b in range(B)]).astype(f32)
    y_sample = np.concatenate([R[c]["y"][TP:].reshape(NS, DS, cfg.D) for c in range(n)], 0).astype(f32)
    p_nsa_kv = np.stack([R[2 * b]["o_kv"][:TP].reshape(SEQ, 4, 4, 128) for b in range(B)])[None].astype(f32)
    p_nsa_win = np.stack([R[2 * b]["o_pwin"].reshape(512, 2, 4, 128) for b in range(B)])[None].astype(f32)
    p_ffn_conv = np.stack([np.stack([R[2 * b]["o_ffnc"][li][0:2] for b in range(B)]) for li in range(2)]).astype(f32)
    s_nsa_kv = np.concatenate([R[c]["o_kv"][TP:].reshape(NS, DS, 4, 4, 128) for c in range(n)], 0)[None].astype(f32)
    s_nsa_win = np.concatenate([R[c]["o_swin"].reshape(NS, 512, 2, 4, 128) for c in range(n)], 0)[None].astype(f32)
    s_ffn_conv = np.stack([np.concatenate([R[c]["o_ffnc"][li][2:2 + 2 * NS].reshape(NS, 2, -1) for c in range(n)], 0)
                           for li in range(2)]).astype(f32)
    return (y_prompt, y_sample, p_gdn, p_gdn_conv, p_nsa_kv, p_nsa_win, p_ffn_conv,
            s_gdn, s_gdn_conv, s_nsa_kv, s_nsa_win, s_ffn_conv)


class Unpager:
    def __init__(self, k, cfg, I, S, dbuf):
        self.k, self.cfg, self.I, self.S, self.dbuf = k, cfg, I, S, dbuf
        self.NPG = cfg.PAST // 128
        self.total = cfg.NS * self.NPG
        self.done = 0

    def setup(self):
        k, cfg, I = self.k, self.cfg, self.I
        n = self.total
        self.idx = k.sb("u_idx", [128, n], I32)
        ptb = k.sb("u_ptb", [128, n], I32)
        k.dma("sp", ptb[:], I["page_table"].rearrange("s g -> (s g)").partition_broadcast(128), writes=[ptb])
        ptf = k.sb("u_ptf", [128, n], F32)
        iota_f = k.sb("u_iota", [128, 1], F32)
        k.dma("sp", iota_f[:], I["n_iota"], writes=[iota_f])
        k.op("dve", lambda e: e.tensor_copy(out=ptf[:], in_=ptb[:]), reads=[ptb], writes=[ptf])
        k.op("dve", lambda e: e.tensor_scalar(out=ptf[:], in0=ptf[:], scalar1=128.0, scalar2=iota_f[:, 0:1], op0=ALU.mult, op1=ALU.add),
             reads=[ptf, iota_f], writes=[ptf])
        k.op("dve", lambda e: e.tensor_copy(out=self.idx[:], in_=ptf[:]), reads=[ptf], writes=[self.idx])
        self.raw_r = Ring([k.sb(f"u_raw{i}", [128, 2048], F32) for i in range(2)])
        self.bf_r = Ring([k.sb(f"u_bf{i}", [128, 2048], BF16) for i in range(2)])

    def step(self, npages):
        k, I, S = self.k, self.I, self.S
        for _ in range(npages):
            if self.done >= self.total:
                return
            i = self.done
            self.done += 1
            raw = self.raw_r.next()
            k.op("pool", lambda e: e.indirect_dma_start(
                out=raw[:, :], out_offset=None, in_=I["cache"][:, :],
                in_offset=bass.IndirectOffsetOnAxis(ap=self.idx[:, i:i + 1], axis=0)),
                reads=[self.idx, self.dbuf["cache"]], writes=[raw], inc=False, dma=True)
            bf = self.bf_r.next()
            eng = k.ev_eng()
            if eng == "act":
                k.op("act", lambda e: e.copy(out=bf[:, :], in_=raw[:, :]), reads=[raw], writes=[bf])
            else:
                k.op("dve", lambda e: e.tensor_copy(out=bf[:, :], in_=raw[:, :]), reads=[raw], writes=[bf])
            k.dma("sp", S["past"][i * 128:(i + 1) * 128, :], bf[:, :], reads=[bf], writes=[self.dbuf["past"]])

    def finish(self):
        self.step(self.total)


def linear_tm(k, cfg, pf, dbuf, name, src_name, src_ap, w_name, w_ap, Kdim, resid_name, resid_ap, out_name, out_ap):
    D = cfg.D
    NKt = Kdim // 128
    with k.scope():
        wo_r = Ring([k.sb(f"{name}_w{i}", [128, NKt, 512], BF16) for i in range(2)])
        st_r = Ring([k.sb(f"{name}_s{i}", [128, NKt, 128], BF16) for i in range(3)])
        xr_r = Ring([k.sb(f"{name}_x{i}", [128, 512], F32) for i in range(3)])
        ho_r = Ring([k.sb(f"{name}_h{i}", [128, 512], F32) for i in range(3)])
        srcv = src_ap.rearrange("(kt p) t -> p kt t", p=128)
        def load_wo(c):
            wo = wo_r.next()
            k.dma("pool", wo[:], w_ap[:, c * 512:(c + 1) * 512].rearrange("(kt p) n -> p kt n", p=128),
                  reads=[dbuf[w_name]], writes=[wo])
            return wo
        wo_next = load_wo(0)
        for c in range(D // 512):
            wo = wo_next
            if c + 1 < D // 512:
                wo_next = load_wo(c + 1)
            for (r0, R) in cfg.ttiles():
                st = st_r.next()
                k.dma("sp", st[:, :, 0:R], srcv[:, :, r0:r0 + R], reads=[dbuf[src_name]], writes=[st])
                xr = xr_r.next()
                k.dma("sp", xr[0:R, :], resid_ap[r0:r0 + R, c * 512:(c + 1) * 512], reads=[dbuf[resid_name]], writes=[xr])
                ps = pf.next()
                for kt in range(NKt):
                    k.op("pe", lambda e: e.matmul(ps[0:R, :], lhsT=st[:, kt, 0:R], rhs=wo[:, kt, :],
                                                  start=(kt == 0), stop=(kt == NKt - 1)),
                         reads=[st, wo], writes=[ps], inc=(kt == NKt - 1))
                ho = ho_r.next()
                k.op("dve", lambda e: e.tensor_tensor(out=ho[0:R, :], in0=ps[0:R, :], in1=xr[0:R, :], op=ALU.add),
                     reads=[ps, xr], writes=[ho])
                k.dma("act", out_ap[r0:r0 + R, c * 512:(c + 1) * 512], ho[0:R, :], reads=[ho], writes=[dbuf[out_name]])


def ffn_layer(k, cfg, I, O, S, dbuf, pf, pbf, C, li, src_name, src_ap, out_name, out_ap, norm_fm, hook=None):
    D, KT, NT, TP, NS, TS, DFF = cfg.D, cfg.KT, cfg.NT, cfg.TP, cfg.NS, cfg.TS, cfg.DFF
    NSAMP = cfg.NSAMP
    ident_f = C["ident_f"]
    NM = DFF // 128
    W1 = 2
    PADW = W1 + TP + NS * (TS + W1)
    SOFF = W1 + TP
    SW = TS + W1
    chunks = cfg.chunks()
    with k.scope():
        xn = k.sb(f"f{li}_xn", [128, KT, NT], BF16)
        norm_fm(src_name, I["norm_ffn"][li:li + 1, :], xn)
        cw = k.sb(f"f{li}_cw", [128, NM, 3], F32)
        k.dma("sp", cw[:], I["ffn_conv_w"][li].rearrange("(t p) w -> p t w", p=128), writes=[cw])
        cb = k.sb(f"f{li}_cb", [128, NM], F32)
        k.dma("sp", cb[:], I["ffn_conv_b"][li].rearrange("(t p) -> p t", p=128), writes=[cb],
              allow_slow_non_contiguous=True)
        wt_r = Ring([k.sb(f"f{li}_wt{i}", [128, KT, 128], BF16) for i in range(4)])
        stc_r = Ring([k.sb(f"f{li}_stc{i}", [NS * W1, 128], F32) for i in range(2)])
        prep = k.sb(f"f{li}_prep", [128, PADW], F32)
        k.op("pool", lambda e: e.memset(prep[:], 0.0), writes=[prep])
        cq = k.sb(f"f{li}_cq", [128, NT], F32)
        act_r = Ring([k.sb(f"f{li}_act{i}", [128, NT], BF16) for i in range(2)])
        convout = k.sb(f"f{li}_convout", [128, NM, 16], F32)
        k.op("pool", lambda e: e.memset(convout[:], 0.0), writes=[convout])
        wup = I["ffn_w_up"][li]
        if hook is not None:
            hook.setup()

        def load_w(col0):
            wt = wt_r.next()
            k.dma("pool", wt[:], wup[:, col0:col0 + 128].rearrange("(kt p) c -> p kt c", p=128),
                  reads=[dbuf["ffn_w_up"]], writes=[wt])
            return wt

        def proj(wt, consumer):
            for ci, (t0, n) in enumerate(chunks):
                ps = pf.next()
                for kt in range(KT):
                    k.op("pe", lambda e: e.matmul(ps[:, 0:n], lhsT=wt[:, kt, :], rhs=xn[:, kt, t0:t0 + n],
                                                  start=(kt == 0), stop=(kt == KT - 1)),
                         reads=[wt, xn], writes=[ps], inc=(kt == KT - 1))
                consumer(ci, t0, n, ps)

        def sv(lo, hi):
            return prep[:, SOFF:SOFF + NS * SW].rearrange("p (s w) -> p s w", w=SW)[:, :, lo:hi]

        w_next = (load_w(0), load_w(DFF))
        for m in range(NM):
            wg, wv = w_next
            if m + 1 < NM:
                w_next = (load_w((m + 1) * 128), load_w(DFF + (m + 1) * 128))
            stc = stc_r.next()
            k.dma("sp", stc[:], I["st_ffn_conv"][li][:, m * 128:(m + 1) * 128], writes=[stc])
            pst = pf.next()
            k.op("pe", lambda e: e.transpose(out=pst[:, 0:NS * W1], in_=stc[0:NS * W1, 0:128],
                                             identity=ident_f[0:NS * W1, 0:NS * W1]),
                 reads=[stc, ident_f], writes=[pst])
            k.op("dve", lambda e: e.tensor_copy(out=sv(0, W1), in_=pst[:, 0:NS * W1].rearrange("p (s w) -> p s w", w=W1)),
                 reads=[pst], writes=[prep])

            def cons(ci, t0, n, ps):
                eng = k.ev_eng()
                if n == 512:
                    dv = prep[:, W1 + t0:W1 + t0 + n]
                    sv_ = ps[:, 0:n]
                else:
                    dv = sv(W1, SW)
                    sv_ = ps[:, 0:n].rearrange("p (s w) -> p s w", w=TS)
                if eng == "act":
                    k.op("act", lambda e: e.copy(out=dv, in_=sv_), reads=[ps], writes=[prep])
                else:
                    k.op("dve", lambda e: e.tensor_copy(out=dv, in_=sv_), reads=[ps], writes=[prep])
            proj(wg, cons)
            k.op("pool", lambda e: e.tensor_copy(out=convout[:, m, 0:W1], in_=prep[:, W1 + TP - W1:W1 + TP]),
                 reads=[prep], writes=[convout])
            k.op("pool", lambda e: e.tensor_copy(out=convout[:, m, W1:W1 + NS * W1].rearrange("p (s w) -> p s w", w=W1),
                                                 in_=sv(TS, SW)), reads=[prep], writes=[convout])
            for (ov, mkv) in ((cq[:, 0:TP], lambda i: prep[:, i:i + TP]),
                              (cq[:, TP:NT].rearrange("p (s w) -> p s w", w=TS), lambda i: sv(i, i + TS))):
                k.op("dve", lambda e: e.tensor_scalar(out=ov, in0=mkv(2), scalar1=cw[:, m, 2:3], scalar2=cb[:, m:m + 1],
                                                      op0=ALU.mult, op1=ALU.add), reads=[prep, cw, cb], writes=[cq])
                for i in range(2):
                    k.op("dve", lambda e: e.scalar_tensor_tensor(out=ov, in0=mkv(i), scalar=cw[:, m, i:i + 1], in1=ov,
                                                                 op0=ALU.mult, op1=ALU.add),
                         reads=[prep, cw, cq], writes=[cq])
            k.op("act", lambda e: e.activation(out=cq[:], in_=cq[:], func=AF.Silu), reads=[cq], writes=[cq])
            act = act_r.next()

            def vcons(ci, t0, n, ps):
                k.op("dve", lambda e: e.tensor_tensor(out=act[:, t0:t0 + n], in0=ps[:, 0:n], in1=cq[:, t0:t0 + n], op=ALU.mult),
                     reads=[ps, cq], writes=[act])
            proj(wv, vcons)
            k.dma("act", S["actT"][m * 128:(m + 1) * 128, :], act[:], reads=[act], writes=[dbuf["actT"]])
            if hook is not None:
                hook.step(-(-hook.total // NM))
        if hook is not None:
            hook.finish()
        nrow = (1 + NS) * W1
        cst_r = Ring([k.sb(f"f{li}_cst{i}", [16, 512], F32) for i in range(2)])
        for g0 in range(0, NM, 4):
            ps = pf.next()
            cst = cst_r.next()
            ng = min(4, NM - g0)
            for q in range(ng):
                k.op("pe", lambda e: e.transpose(out=ps[0:16, q * 128:(q + 1) * 128], in_=convout[:, g0 + q, :],
                                                 identity=ident_f[:, :]),
                     reads=[convout, ident_f], writes=[ps], inc=(q == ng - 1))
            k.op("dve", lambda e: e.tensor_copy(out=cst[:, 0:ng * 128], in_=ps[0:16, 0:ng * 128]), reads=[ps], writes=[cst])
            k.dma("sp", O["o_ffnc"][li][:, g0 * 128:(g0 + ng) * 128], cst[0:nrow, 0:ng * 128], reads=[cst],
                  writes=[dbuf["o_ffnc"]])
    linear_tm(k, cfg, pf, dbuf, f"fd{li}", "actT", S["actT"], "ffn_w_down", I["ffn_w_down"][li], DFF,
              src_name, src_ap, out_name, out_ap)


NEG = -30000.0
NSA_Z = 2063
NSA_GL = 5120
NSA_OFF = 384
NSA_X = 1920
NSA_XW = 1408


def t5_bucket_np(d):
    d = np.maximum(d, 0)
    f32 = np.float32
    scale = (32 - 16) / math.log(1024 / 16)
    large = 16 + (np.log(np.maximum(d, 1).astype(f32) / f32(16)) * f32(scale)).astype(np.int32)
    return np.where(d < 16, d, np.minimum(large, 31))


def nsa_host_consts(cfg, P):
    c = {}
    TP, TS, NS = cfg.TP, cfg.TS, cfg.NS
    f32 = np.float32
    dd = np.arange(NSA_GL) - NSA_Z
    oh = np.zeros((33, NSA_GL), f32)
    bk = t5_bucket_np(dd)
    for i in range(NSA_GL):
        if dd[i] >= 0:
            oh[bk[i], i] = 1.0
        else:
            oh[32, i] = 1.0
    c["n_oh"] = oh
    jm = np.zeros((128, 128), f32)
    jm[np.arange(128), 127 - np.arange(128)] = 1.0
    c["n_J"] = jm
    kk = np.arange(128)[:, None]
    xx = np.arange(NSA_XW)[None, :]
    c["n_wm"] = np.where((xx - kk - NSA_OFF) < 512, 0.0, NEG).astype(f32)

    def sel_consts(n_all, qpos, name):
        n_sb = -(-n_all // 64)
        ns = n_all // 16
        ncb = ns - 1
        c_end = np.arange(ncb) * 16 + 31
        c_start = c_end - 31
        sb_start = np.arange(n_sb) * 64
        ovl = np.maximum(np.minimum(c_end[:, None], sb_start[None, :] + 63) - np.maximum(c_start[:, None], sb_start[None, :]) + 1, 0).astype(f32) / 32
        ntile = -(-ncb // 128)
        ovl_t = np.zeros((ntile * 128, n_sb), f32)
        ovl_t[:ncb] = ovl
        c[name + "_ovl"] = ovl_t.reshape(ntile, 128, n_sb)
        cur = qpos // 64
        blk = np.arange(n_sb)
        ok = sb_start[None, :] <= qpos[:, None]
        forced = (blk[None, :] == 0) | (blk[None, :] == cur[:, None]) | (blk[None, :] == cur[:, None] - 1)
        c[name + "_add"] = np.where(ok, np.where(forced, 1000.0, 0.0), -1e30).astype(f32)
        c[name + "_valid"] = ok.astype(f32)
        nk = -(-n_all // 128) * 128
        ex = np.zeros((n_sb, nk), f32)
        keys = np.arange(n_all)
        ex[keys // 64, keys] = 1.0
        c[name + "_exp"] = ex.astype(ml_dtypes.bfloat16)
        return n_sb, ncb
    c["dims_p"] = sel_consts(TP, np.arange(TP), "np")
    c["dims_s"] = sel_consts(P + TS, P + np.arange(TS), "ns")
    c["n_iota"] = np.arange(128, dtype=np.float32).reshape(128, 1)
    return c


def nsa_proj(k, cfg, I, O, S, dbuf, pf, pbf, C, norm_fm, src_name):
    D, KT, NT, TP, NS, TS = cfg.D, cfg.KT, cfg.NT, cfg.TP, cfg.NS, cfg.TS
    ones_b = C["ones_b"]
    chunks = cfg.chunks()
    WIN = 512
    with k.scope():
        xn = k.sb("n_xn", [128, KT, NT], BF16)
        norm_fm(src_name, I["norm_mix"][1:2, :], xn)
        epsb = k.sb("n_epsb", [128, 1], F32)
        k.op("pool", lambda e: e.memset(epsb[:], RMS_EPS), writes=[epsb])
        qg = k.sb("n_qg", [128, 1], F32)
        k.dma("sp", qg[:], I["nsa_q_norm"].rearrange("o d -> d o"), writes=[qg])
        k.op("dve", lambda e: e.tensor_scalar(out=qg[:], in0=qg[:], scalar1=128 ** -0.5, scalar2=None, op0=ALU.mult),
             reads=[qg], writes=[qg])
        import os
        parts = os.environ.get("NSA_PARTS", "q,kv,win,g").split(",")
        with k.scope():
            wt_r = Ring([k.sb(f"n_wt{i}", [128, KT, 128], BF16) for i in range(2)])
            qraw = k.sb("n_qraw", [128, NT], F32)
            sqb = k.sb("n_sqb", [128, 512], BF16)
            rinv = k.sb("n_rinv", [128, 512], F32)
            qn_r = Ring([k.sb(f"n_qn{i}", [128, NT], BF16) for i in range(2)])
            for h in range(16 if "q" in parts else 0):
                wt = wt_r.next()
                k.dma("pool", wt[:], I["nsa_w_in"][:, h * 128:(h + 1) * 128].rearrange("(kt p) c -> p kt c", p=128),
                      reads=[dbuf["nsa_w_in"]], writes=[wt])
                qn = qn_r.next()
                for (t0, n) in chunks:
                    ps = pf.next()
                    for kt in range(KT):
                        k.op("pe", lambda e: e.matmul(ps[:, 0:n], lhsT=wt[:, kt, :], rhs=xn[:, kt, t0:t0 + n],
                                                      start=(kt == 0), stop=(kt == KT - 1)),
                             reads=[wt, xn], writes=[ps], inc=(kt == KT - 1))
                    qlvl = int(os.environ.get("NSA_QLVL", "9"))
                    k.op("dve", lambda e: e.tensor_copy(out=qraw[:, t0:t0 + n], in_=ps[:, 0:n]), reads=[ps], writes=[qraw])
                    if qlvl < 1:
                        continue
                    k.op("act", lambda e: e.activation(out=sqb[:, 0:n], in_=ps[:, 0:n], func=AF.Square), reads=[ps], writes=[sqb])
                    ps2 = pf.next()
                    k.op("pe", lambda e: e.matmul(ps2[:, 0:n], lhsT=ones_b[:, :], rhs=sqb[:, 0:n], start=True, stop=True),
                         reads=[ones_b, sqb], writes=[ps2])
                    if qlvl < 2:
                        continue
                    k.op("act", lambda e: e.activation(out=rinv[:, 0:n], in_=ps2[:, 0:n], func=AF.Sqrt, scale=1.0 / 128,
                                                       bias=epsb[:, 0:1]), reads=[ps2, epsb], writes=[rinv])
                    k.op("dve", lambda e: e.reciprocal(out=rinv[:, 0:n], in_=rinv[:, 0:n]), reads=[rinv], writes=[rinv])
                    if qlvl < 3:
                        continue
                    k.op("dve", lambda e: e.scalar_tensor_tensor(out=qn[:, t0:t0 + n], in0=qraw[:, t0:t0 + n], scalar=qg[:, 0:1],
                                                                 in1=rinv[:, 0:n], op0=ALU.mult, op1=ALU.mult),
                         reads=[qraw, qg, rinv], writes=[qn])
                if qlvl >= 4:
                    k.dma("sp", S["qT"][h * 128:(h + 1) * 128, :], qn[:], reads=[qn], writes=[dbuf["qT"]])
        with k.scope():
            wkv_r = Ring([k.sb(f"n_wkv{i}", [128, KT, 512], BF16) for i in range(2)])
            gain_k = k.sb("n_gaink", [128, 3, 128], F32)
            k.dma("sp", gain_k[:].rearrange("p a d -> p (a d)"),
                  I["nsa_k_norm"].rearrange("o a d -> o (a d)").partition_broadcast(128), writes=[gain_k])
            sqt = k.sb("n_sqt", [128, 512], F32)
            ss = k.sb("n_ss", [128, 8], F32)
            rows_r = Ring([k.sb(f"n_rows{i}", [128, 512], F32) for i in range(3)])
            for c in range(6 if "kv" in parts else 0):
                wc = wkv_r.next()
                k.dma("pool", wc[:], I["nsa_w_in"][:, 2048 + c * 512:2048 + (c + 1) * 512].rearrange("(kt p) c -> p kt c", p=128),
                      reads=[dbuf["nsa_w_in"]], writes=[wc])
                for (r0, R) in cfg.ttiles():
                    ps = pf.next()
                    for kt in range(KT):
                        k.op("pe", lambda e: e.matmul(ps[0:R, :], lhsT=xn[:, kt, r0:r0 + R], rhs=wc[:, kt, :],
                                                      start=(kt == 0), stop=(kt == KT - 1)),
                             reads=[xn, wc], writes=[ps], inc=(kt == KT - 1))
                    rows = rows_r.next()
                    if c in (2, 4):
                        gi = 1 if c == 2 else 2
                        k.op("act", lambda e: e.activation(out=sqt[0:R, :], in_=ps[0:R, :], func=AF.Square), reads=[ps], writes=[sqt])
                        k.op("dve", lambda e: e.reduce_sum(out=ss[0:R, 0:4], in_=sqt[0:R, :].rearrange("p (g d) -> p g d", d=128),
                                                           axis=AX.X), reads=[sqt], writes=[ss])
                        k.op("act", lambda e: e.activation(out=ss[0:R, 4:8], in_=ss[0:R, 0:4], func=AF.Sqrt, scale=1.0 / 128,
                                                           bias=epsb[0:R, 0:1]), reads=[ss, epsb], writes=[ss])
                        k.op("dve", lambda e: e.reciprocal(out=ss[0:R, 0:4], in_=ss[0:R, 4:8]), reads=[ss], writes=[ss])
                        k.op("dve", lambda e: e.tensor_tensor(out=sqt[0:R, :].rearrange("p (g d) -> p g d", d=128),
                                                              in0=ps[0:R, :].rearrange("p (g d) -> p g d", d=128),
                                                              in1=ss[0:R, 0:4].unsqueeze(2).to_broadcast([R, 4, 128]), op=ALU.mult),
                             reads=[ps, ss], writes=[sqt])
                        k.op("dve", lambda e: e.tensor_tensor(out=rows[0:R, :].rearrange("p (g d) -> p g d", d=128),
                                                              in0=sqt[0:R, :].rearrange("p (g d) -> p g d", d=128),
                                                              in1=gain_k[0:R, gi, :].unsqueeze(1).to_broadcast([R, 4, 128]), op=ALU.mult),
                             reads=[sqt, gain_k], writes=[rows])
                    else:
                        k.op("act", lambda e: e.copy(out=rows[0:R, :], in_=ps[0:R, :]), reads=[ps], writes=[rows])
                    if c < 4:
                        k.dma("sp", O["o_kv"][r0:r0 + R, c * 512:(c + 1) * 512], rows[0:R, :], reads=[rows], writes=[dbuf["o_kv"]])
                    else:
                        cc = (c - 4) * 512
                        k.dma("sp", S["win_new"][r0:r0 + R, cc:cc + 512], rows[0:R, :], reads=[rows], writes=[dbuf["win_new"]])
                        if r0 < TP and r0 >= TP - WIN:
                            k.dma("sp", O["o_pwin"][r0 - (TP - WIN):r0 - (TP - WIN) + R, cc:cc + 512], rows[0:R, :],
                                  reads=[rows], writes=[dbuf["o_pwin"]])
                        if r0 == TP:
                            for s in range(NS):
                                k.dma("sp", O["o_swin"][s, WIN - TS:WIN, cc:cc + 512], rows[s * TS:(s + 1) * TS, :],
                                      reads=[rows], writes=[dbuf["o_swin"]])
            swb_r = Ring([k.sb(f"n_swb{i}", [126, 4096], F32) for i in range(2)])
            for s in range(NS if "win" in parts else 0):
                swb = swb_r.next()
                k.dma("sp", swb[:, :], I["st_win"][s, TS:WIN, :].rearrange("(p a) c -> p (a c)", a=4),
                      reads=[dbuf["st_win"]], writes=[swb])
                k.dma("sp", O["o_swin"][s, 0:WIN - TS, :].rearrange("(p a) c -> p (a c)", a=4), swb[:, :],
                      reads=[swb], writes=[dbuf["o_swin"]])
            wg = k.sb("n_wg", [128, KT, 48], BF16)
            k.dma("pool", wg[:], I["nsa_w_in"][:, 5120:5168].rearrange("(kt p) c -> p kt c", p=128),
                  reads=[dbuf["nsa_w_in"]], writes=[wg])
            gt_r = Ring([k.sb(f"n_gt{i}", [128, 48], F32) for i in range(2)])
            for (r0, R) in (cfg.ttiles() if "g" in parts else []):
                ps = pf.next()
                for kt in range(KT):
                    k.op("pe", lambda e: e.matmul(ps[0:R, 0:48], lhsT=xn[:, kt, r0:r0 + R], rhs=wg[:, kt, :],
                                                  start=(kt == 0), stop=(kt == KT - 1)),
                         reads=[xn, wg], writes=[ps], inc=(kt == KT - 1))
                gt = gt_r.next()
                k.op("act", lambda e: e.activation(out=gt[0:R, :], in_=ps[0:R, 0:48], func=AF.Sigmoid), reads=[ps], writes=[gt])
                k.dma("sp", S["gates"][r0:r0 + R, :], gt[0:R, :], reads=[gt], writes=[dbuf["gates"]])


def nsa_attn(k, cfg, I, O, S, dbuf, pf, pbf, C):
    D, KT, NT, TP, NS, TS = cfg.D, cfg.KT, cfg.NT, cfg.TP, cfg.NS, cfg.TS
    P = cfg.PAST
    NQT = TP // 128
    NCH = TP // 512
    n_sb_p = TP // 64
    ncb_p = TP // 16 - 1
    Z, GL, OFF, X, XW = NSA_Z, NSA_GL, NSA_OFF, NSA_X, NSA_XW
    ident_f, ident_b, ones_b, ones_f = C["ident_f"], C["ident_b"], C["ones_b"], C["ones_f"]
    ringN = Ring(pf.bufs[0:4])
    ring2 = Ring(pf.bufs[0:2])
    accO = pf.bufs[2:6]
    GELU_C = 1.5957691216057308

    def evac(dst, src, rd, wr):
        eng = k.ev_eng()
        if eng == "act":
            k.op("act", lambda e: e.copy(out=dst, in_=src), reads=rd, writes=wr)
        else:
            k.op("dve", lambda e: e.tensor_copy(out=dst, in_=src), reads=rd, writes=wr)

    with k.scope():
        Jm = k.sb("a_J", [128, 128], F32)
        k.dma("sp", Jm[:], I["n_J"], writes=[Jm])
        relx = k.sb("a_relx", [64, 16], F32)
        k.op("pool", lambda e: e.memset(relx[32:64, :], NEG), writes=[relx])
        k.dma("sp", relx[0:32, :], I["rel_bias"], writes=[relx])
        epsb = k.sb("a_epsb", [128, 1], F32)
        k.op("pool", lambda e: e.memset(epsb[:], RMS_EPS), writes=[epsb])
        with k.scope():
            oh = k.sb("a_oh", [33, GL], F32)
            k.dma("sp", oh[:], I["n_oh"], writes=[oh])
            gsb_r = Ring([k.sb(f"a_gsb{i}", [16, 512], F32) for i in range(2)])
            for c0 in range(0, GL, 512):
                ps = ringN.next()
                k.op("pe", lambda e: e.matmul(ps[0:16, :], lhsT=relx[0:33, :], rhs=oh[:, c0:c0 + 512], start=True, stop=True),
                     reads=[relx, oh], writes=[ps])
                gsb = gsb_r.next()
                k.op("dve", lambda e: e.tensor_copy(out=gsb[:, :], in_=ps[0:16, :]), reads=[ps], writes=[gsb])
                k.dma("sp", S["gvec"][:, c0:c0 + 512], gsb[:, :], reads=[gsb], writes=[dbuf["gvec"]])
        wm = k.sb("a_wm", [128, XW], F32)
        k.dma("sp", wm[:], I["n_wm"], writes=[wm])
        ovl_p = k.sb("a_ovlp", [128, n_sb_p], BF16)
        k.dma("pool", ovl_p[:], I["np_ovl"][0], writes=[ovl_p])
        add_p = k.sb("a_addp", [128, NQT, n_sb_p], F32)
        k.dma("sp", add_p[:], I["np_add"].rearrange("(t p) m -> p t m", p=128), writes=[add_p])
        val_p = k.sb("a_valp", [128, NQT, n_sb_p], F32)
        k.dma("sp", val_p[:], I["np_valid"].rearrange("(t p) m -> p t m", p=128), writes=[val_p])
        exp_p = k.sb("a_expp", [n_sb_p, TP], BF16)
        k.dma("sp", exp_p[:], I["np_exp"], writes=[exp_p])
        gates = k.sb("a_gates", [128, NQT, 48], F32)
        k.dma("sp", gates[:], S["gates"][0:TP, :].rearrange("(t p) c -> p t c", p=128), reads=[dbuf["gates"]], writes=[gates])
        w1 = k.sb("a_w1", [128, 2, 32, 256], BF16)
        w2 = k.sb("a_w2", [128, 2, 2, 128], BF16)
        for wch in range(2):
            k.dma("pool", w1[:, wch], I["nsa_cmp_w1"][wch].rearrange("(s d) e -> d s e", d=128), writes=[w1])
            k.dma("pool", w2[:, wch], I["nsa_cmp_w2"][wch].rearrange("(h e) d -> e h d", e=128), writes=[w2])
        peT = k.sb("a_peT", [128, 2, 32], BF16)
        k.dma("pool", peT[:], I["nsa_cmp_pe"].rearrange("w s d -> d w s"), writes=[peT], allow_slow_non_contiguous=True)
        peb = k.sb("a_peb", [128, 2, 2], F32)
        for wch in range(2):
            for half in range(2):
                ps = ringN.next()
                for s in range(32):
                    k.op("pe", lambda e: e.matmul(ps[:, 0:1], lhsT=w1[:, wch, s, half * 128:(half + 1) * 128],
                                                  rhs=peT[:, wch, s:s + 1], start=(s == 0), stop=(s == 31)),
                         reads=[w1, peT], writes=[ps], inc=(s == 31))
                k.op("dve", lambda e: e.tensor_copy(out=peb[:, wch, half:half + 1], in_=ps[:, 0:1]), reads=[ps], writes=[peb])
        gk0 = k.sb("a_gk0", [128, 1], F32)
        k.dma("sp", gk0[:], I["nsa_k_norm"][0, 0:1, :].rearrange("o d -> d o"), writes=[gk0])

        t1_r = Ring([k.sb(f"a_t1{i}", [128, 2048], F32) for i in range(1)])

        def build_tab(dst_fn, h, base, W, pstride):
            t1 = t1_r.next()
            gv = S["gvec"]
            src = bass.AP(tensor=gv.tensor, offset=h * GL + base, ap=[[pstride, 128], [1, W]])
            k.dma("sp", t1[:, 0:W], src, reads=[dbuf["gvec"]], writes=[t1])
            for c0 in range(0, W, 512):
                n = min(512, W - c0)
                ps = ringN.next()
                k.op("pe", lambda e: e.matmul(ps[:, 0:n], lhsT=Jm[:, :], rhs=t1[:, c0:c0 + n], start=True, stop=True),
                     reads=[Jm, t1], writes=[ps])
                dst, wr = dst_fn(c0, n)
                evac(dst, ps[:, 0:n], [ps], wr)

        def compress(kx, wch, ncb, hg, tmpx, tmp2):
            kxv = kx[:, 0:(ncb + 1) * 16].rearrange("p (n r) -> p n r", r=16)
            for half in range(2):
                ps = ringN.next()
                for s in range(32):
                    rv = kxv[:, 0:ncb, s] if s < 16 else kxv[:, 1:ncb + 1, s - 16]
                    k.op("pe", lambda e: e.matmul(ps[:, 0:ncb], lhsT=w1[:, wch, s, half * 128:(half + 1) * 128], rhs=rv,
                                                  start=(s == 0), stop=(s == 31)),
                         reads=[w1, kx], writes=[ps], inc=(s == 31))
                k.op("dve", lambda e: e.tensor_scalar(out=tmpx[:, 0:ncb], in0=ps[:, 0:ncb], scalar1=peb[:, wch, half:half + 1],
                                                      scalar2=None, op0=ALU.add), reads=[ps, peb], writes=[tmpx])
                k.op("dve", lambda e: e.tensor_tensor(out=tmp2[:, 0:ncb], in0=tmpx[:, 0:ncb], in1=tmpx[:, 0:ncb], op=ALU.mult),
                     reads=[tmpx], writes=[tmp2])
                k.op("dve", lambda e: e.tensor_scalar(out=tmp2[:, 0:ncb], in0=tmp2[:, 0:ncb], scalar1=0.044715, scalar2=1.0,
                                                      op0=ALU.mult, op1=ALU.add), reads=[tmp2], writes=[tmp2])
                k.op("dve", lambda e: e.tensor_tensor(out=tmp2[:, 0:ncb], in0=tmp2[:, 0:ncb], in1=tmpx[:, 0:ncb], op=ALU.mult),
                     reads=[tmp2, tmpx], writes=[tmp2])
                k.op("act", lambda e: e.activation(out=tmp2[:, 0:ncb], in_=tmp2[:, 0:ncb], func=AF.Sigmoid, scale=GELU_C),
                     reads=[tmp2], writes=[tmp2])
                k.op("dve", lambda e: e.tensor_tensor(out=hg[:, half, 0:ncb], in0=tmp2[:, 0:ncb], in1=tmpx[:, 0:ncb], op=ALU.mult),
                     reads=[tmp2, tmpx], writes=[hg])

        def kc_finish(hg, ncb, kcT, tmpx, tmp2, sqb):
            ps = ringN.next()
            for half in range(2):
                k.op("pe", lambda e: e.matmul(ps[:, 0:ncb], lhsT=w2[:, 0, half, :], rhs=hg[:, half, 0:ncb],
                                              start=(half == 0), stop=(half == 1)), reads=[w2, hg], writes=[ps], inc=(half == 1))
            k.op("dve", lambda e: e.tensor_copy(out=tmpx[:, 0:ncb], in_=ps[:, 0:ncb]), reads=[ps], writes=[tmpx])
            k.op("act", lambda e: e.activation(out=sqb[:, 0:ncb], in_=tmpx[:, 0:ncb], func=AF.Square), reads=[tmpx], writes=[sqb])
            ps2 = ringN.next()
            k.op("pe", lambda e: e.matmul(ps2[:, 0:ncb], lhsT=ones_b[:, :], rhs=sqb[:, 0:ncb], start=True, stop=True),
                 reads=[ones_b, sqb], writes=[ps2])
            k.op("act", lambda e: e.activation(out=tmp2[:, 0:ncb], in_=ps2[:, 0:ncb], func=AF.Sqrt, scale=1.0 / 128, bias=epsb[:, 0:1]),
                 reads=[ps2, epsb], writes=[tmp2])
            k.op("dve", lambda e: e.reciprocal(out=tmp2[:, 0:ncb], in_=tmp2[:, 0:ncb]), reads=[tmp2], writes=[tmp2])
            k.op("dve", lambda e: e.scalar_tensor_tensor(out=kcT[:, 0:ncb], in0=tmpx[:, 0:ncb], scalar=gk0[:, 0:1], in1=tmp2[:, 0:ncb],
                                                         op0=ALU.mult, op1=ALU.mult), reads=[tmpx, gk0, tmp2], writes=[kcT])

        def transpose_tiles(tk, ntile, dstT):
            for g0 in range(0, ntile, 8):
                ng = min(8, ntile - g0)
                pb = pbf.next()
                for i in range(ng):
                    k.op("pe", lambda e: e.transpose(out=pb[:, i * 128:(i + 1) * 128], in_=tk[:, g0 + i, :], identity=ident_b[:, :]),
                         reads=[tk, ident_b], writes=[pb], inc=(i == ng - 1))
                evac(dstT[:, g0 * 128:(g0 + ng) * 128], pb[:, 0:ng * 128], [pb], [dstT])

        def fill_sample_bias(j, T, Tw, Bs, Bw):
            NPG_ = P // 128
            kcut = sum(1 for kt in range(NPG_) if P - 128 * kt >= 1024)
            if kcut:
                k.op("pool", lambda e: e.tensor_copy(out=Bs[:, 0:kcut, j, :],
                                                     in_=T[:, 1024 + OFF:1024 + OFF + 8].unsqueeze(1).to_broadcast([128, kcut, 8])),
                     reads=[T], writes=[Bs])
            for kt in range(kcut, NPG_):
                x0 = P - 128 * kt + OFF
                k.op("pool", lambda e: e.tensor_copy(out=Bs[:, kt, j, :], in_=T[:, x0:x0 + 8]), reads=[T], writes=[Bs])
            k.op("pool", lambda e: e.tensor_copy(out=Bs[:, NPG_, j, :], in_=T[:, OFF:OFF + 8]), reads=[T], writes=[Bs])
            for wt in range(4):
                x0 = 512 - 128 * wt + OFF
                k.op("pool", lambda e: e.tensor_copy(out=Bw[:, wt, j, :], in_=Tw[:, x0:x0 + 8]), reads=[Tw], writes=[Bw])
            k.op("pool", lambda e: e.tensor_copy(out=Bw[:, 4, j, :], in_=Tw[:, OFF:OFF + 8]), reads=[Tw], writes=[Bw])

        def prompt_group(g, qh, Bs, Bw):
            if True:
                kslcT = k.sb("a_kslcT", [128, TP], BF16)
                kwinT = k.sb("a_kwinT", [128, TP], BF16)
                vslc = k.sb("a_vslc", [128, NQT, 136], BF16)
                vwin = k.sb("a_vwin", [128, NQT, 136], BF16)
                k.op("pool", lambda e: e.memset(vslc[:, :, 128:129], 1.0), writes=[vslc])
                k.op("pool", lambda e: e.memset(vwin[:, :, 128:129], 1.0), writes=[vwin])
                kcT = k.sb("a_kcT", [128, 128], BF16)
                vc = k.sb("a_vc", [128, 128], BF16)
                prep_scope = k.scope()
                prep_scope.__enter__()
                tk_r = Ring([k.sb(f"a_tk{i}", [128, NQT, 128], BF16) for i in range(2)])
                kx = k.sb("a_kx", [128, TP], BF16)
                vx = k.sb("a_vx", [128, TP], BF16)
                for (nm, ap, dstT, dstV) in (
                        ("o_kv", O["o_kv"][0:TP, 0 * 512 + g * 128:0 * 512 + (g + 1) * 128], kx, None),
                        ("o_kv", O["o_kv"][0:TP, 1 * 512 + g * 128:1 * 512 + (g + 1) * 128], vx, None),
                        ("o_kv", O["o_kv"][0:TP, 2 * 512 + g * 128:2 * 512 + (g + 1) * 128], kslcT, None),
                        ("o_kv", O["o_kv"][0:TP, 3 * 512 + g * 128:3 * 512 + (g + 1) * 128], None, vslc),
                        ("win_new", S["win_new"][0:TP, g * 128:(g + 1) * 128], kwinT, None),
                        ("win_new", S["win_new"][0:TP, 512 + g * 128:512 + (g + 1) * 128], None, vwin)):
                    if dstV is not None:
                        k.dma("pool", dstV[:, :, 0:128], ap.rearrange("(t p) d -> p t d", p=128), reads=[dbuf[nm]], writes=[dstV])
                    else:
                        tk = tk_r.next()
                        k.dma("pool", tk[:], ap.rearrange("(t p) d -> p t d", p=128), reads=[dbuf[nm]], writes=[tk])
                        transpose_tiles(tk, NQT, dstT)
                hg = k.sb("a_hg", [128, 2, 512], BF16)
                tmpx = k.sb("a_tmpx", [128, 512], F32)
                tmp2 = k.sb("a_tmp2", [128, 512], F32)
                sqb = k.sb("a_sqb", [128, 512], BF16)
                compress(kx, 0, ncb_p, hg, tmpx, tmp2)
                kc_finish(hg, ncb_p, kcT, tmpx, tmp2, sqb)
                compress(vx, 1, ncb_p, hg, tmpx, tmp2)
                ps = ringN.next()
                for half in range(2):
                    k.op("pe", lambda e: e.matmul(ps[0:ncb_p, 0:128], lhsT=hg[:, half, 0:ncb_p], rhs=w2[:, 1, half, :],
                                                  start=(half == 0), stop=(half == 1)), reads=[hg, w2], writes=[ps], inc=(half == 1))
                k.op("act", lambda e: e.copy(out=vc[0:ncb_p, :], in_=ps[0:ncb_p, 0:128]), reads=[ps], writes=[vc])
                prep_scope.__exit__(None, None, None)
                o_acc = k.sb("a_oacc", [128, NQT, 4, 128], F32)
                import os
                BR = os.environ.get("NSA_BR", "c,s,w")
                k.op("pool", lambda e: e.memset(o_acc[:], 0.0), writes=[o_acc])
                selT = k.sb("a_selT", [n_sb_p, TP], BF16)
                ssb_r = Ring([k.sb(f"a_ssb{i}", [128, 512], F32) for i in range(2)])
                ebf_r = Ring([k.sb(f"a_ebf{i}", [128, 512], BF16) for i in range(2)])
                eb2_r = Ring([k.sb(f"a_eb2{i}", [128, 512], BF16) for i in range(2)])
                sm = k.sb("a_sm", [128, 8], F32)
                with k.scope():
                    Bc = k.sb("a_Bc", [128, TP], F32)
                    Pall = k.sb("a_Pall", [128, 4, TP], BF16)
                    rec = k.sb("a_rec", [128, 512], F32)
                    sc = k.sb("a_sc", [128, n_sb_p], F32)
                    wk = k.sb("a_wk", [128, n_sb_p], F32)
                    m8 = k.sb("a_m8", [128, 16], F32)
                    sel = k.sb("a_sel", [128, n_sb_p], F32)
                    for j in range(4):
                        build_tab(lambda c0, n: (Bc[:, c0:c0 + n], [Bc]), 4 * g + j, 0, TP, 16)
                        for c in range(NCH):
                            q0 = c * 512
                            ps = ringN.next()
                            k.op("pe", lambda e: e.matmul(ps[0:ncb_p, :], lhsT=kcT[:, 0:ncb_p], rhs=qh[:, j, q0:q0 + 512], start=True, stop=True),
                                 reads=[kcT, qh], writes=[ps])
                            ssb = ssb_r.next()
                            k.op("dve", lambda e: e.tensor_tensor(out=ssb[0:ncb_p, :], in0=ps[0:ncb_p, :], in1=Bc[0:ncb_p, q0:q0 + 512], op=ALU.add),
                                 reads=[ps, Bc], writes=[ssb])
                            ebf = ebf_r.next()
                            k.op("act", lambda e: e.activation(out=ebf[0:ncb_p, :], in_=ssb[0:ncb_p, :], func=AF.Exp), reads=[ssb], writes=[ebf])
                            psD = ringN.next()
                            k.op("pe", lambda e: e.matmul(psD[0:ncb_p, :], lhsT=ones_b[0:ncb_p, 0:ncb_p], rhs=ebf[0:ncb_p, :], start=True, stop=True),
                                 reads=[ones_b, ebf], writes=[psD])
                            k.op("dve", lambda e: e.tensor_scalar(out=rec[0:ncb_p, :], in0=psD[0:ncb_p, :], scalar1=1e-30, scalar2=None, op0=ALU.max),
                                 reads=[psD], writes=[rec])
                            k.op("dve", lambda e: e.reciprocal(out=rec[0:ncb_p, :], in_=rec[0:ncb_p, :]), reads=[rec], writes=[rec])
                            k.op("dve", lambda e: e.tensor_tensor(out=Pall[0:ncb_p, j, q0:q0 + 512], in0=ebf[0:ncb_p, :], in1=rec[0:ncb_p, :], op=ALU.mult),
                                 reads=[ebf, rec], writes=[Pall])
                            for qs in range(4):
                                qt = c * 4 + qs
                                pso = ringN.next()
                                k.op("pe", lambda e: e.matmul(pso[:, 0:128], lhsT=Pall[0:ncb_p, j, q0 + qs * 128:q0 + (qs + 1) * 128], rhs=vc[0:ncb_p, :],
                                                              start=True, stop=True), reads=[Pall, vc], writes=[pso])
                                if "c" in BR:
                                    k.op("act", lambda e: e.activation(out=o_acc[:, qt, j, :], in_=pso[:, 0:128], func=AF.Copy,
                                                                       scale=gates[:, qt, 4 * g + j:4 * g + j + 1]),
                                         reads=[pso, gates], writes=[o_acc])
                    for c in range(NCH):
                        q0 = c * 512
                        for qs in range(4):
                            qt = c * 4 + qs
                            psI = ringN.next()
                            for j in range(4):
                                k.op("pe", lambda e: e.matmul(psI[:, 0:n_sb_p], lhsT=Pall[0:ncb_p, j, q0 + qs * 128:q0 + (qs + 1) * 128], rhs=ovl_p[0:ncb_p, :],
                                                              start=(j == 0), stop=(j == 3)), reads=[Pall, ovl_p], writes=[psI], inc=(j == 3))
                            k.op("dve", lambda e: e.tensor_tensor(out=sc[:, :], in0=psI[:, 0:n_sb_p], in1=add_p[:, qt, :], op=ALU.add),
                                 reads=[psI, add_p], writes=[sc])
                            k.op("dve", lambda e: e.max(out=m8[:, 0:8], in_=sc[:, :]), reads=[sc], writes=[m8])
                            if n_sb_p > 8:
                                k.op("dve", lambda e: e.match_replace(out=wk[:, :], in_to_replace=m8[:, 0:8], in_values=sc[:, :], imm_value=-1e30),
                                     reads=[m8, sc], writes=[wk])
                                k.op("dve", lambda e: e.max(out=m8[:, 8:16], in_=wk[:, :]), reads=[wk], writes=[m8])
                                thr = m8[:, 15:16]
                            else:
                                thr = m8[:, 7:8]
                            k.op("dve", lambda e: e.tensor_scalar(out=sel[:, :], in0=sc[:, :], scalar1=thr, scalar2=None, op0=ALU.is_ge),
                                 reads=[sc, m8], writes=[sel])
                            k.op("dve", lambda e: e.tensor_tensor(out=sel[:, :], in0=sel[:, :], in1=val_p[:, qt, :], op=ALU.mult),
                                 reads=[sel, val_p], writes=[sel])
                            pst = ringN.next()
                            k.op("pe", lambda e: e.transpose(out=pst[0:n_sb_p, 0:128], in_=sel[:, :], identity=ident_f[:, :]),
                                 reads=[sel, ident_f], writes=[pst])
                            k.op("act", lambda e: e.copy(out=selT[:, qt * 128:(qt + 1) * 128], in_=pst[0:n_sb_p, 0:128]), reads=[pst], writes=[selT])
                with k.scope():
                    oT_r = Ring([k.sb(f"a_oTt{i}", [128, 512], BF16) for i in range(2)])
                    T = k.sb("a_T", [128, X], F32)
                    Tw = k.sb("a_Tw", [128, XW], F32)
                    for j in range(4):
                        build_tab(lambda c0, n: (T[:, c0:c0 + n], [T]), 4 * g + j, Z - OFF - 127, X, 1)
                        k.op("pool", lambda e: e.tensor_tensor(out=Tw[:], in0=T[:, 0:XW], in1=wm[:], op=ALU.add), reads=[T, wm], writes=[Tw])
                        fill_sample_bias(j, T, Tw, Bs, Bw)
                        for c in range(NCH):
                            q0 = c * 512
                            for (kT, vv, tab, gi, kts, use_mask) in (
                                    (kslcT, vslc, T, 1, list(range(0, 4 * c + 4)), True),
                                    (kwinT, vwin, Tw, 2, list(range(max(0, 4 * c - 4), 4 * c + 4)), False)):
                                if (use_mask and "s" not in BR) or (not use_mask and "w" not in BR):
                                    continue
                                for kt in kts:
                                    ps = ring2.next()
                                    k.op("pe", lambda e: e.matmul(ps[:, :], lhsT=kT[:, kt * 128:(kt + 1) * 128], rhs=qh[:, j, q0:q0 + 512], start=True, stop=True),
                                         reads=[kT, qh], writes=[ps])
                                    x0 = min(q0 - 128 * kt, 1024) + OFF
                                    ssb = ssb_r.next()
                                    k.op("dve", lambda e: e.tensor_tensor(out=ssb[:, :], in0=ps[:, :], in1=tab[:, x0:x0 + 512], op=ALU.add),
                                         reads=[ps, tab], writes=[ssb])
                                    ebf = ebf_r.next()
                                    k.op("act", lambda e: e.activation(out=ebf[:, :], in_=ssb[:, :], func=AF.Exp), reads=[ssb], writes=[ebf])
                                    if use_mask:
                                        psM = ring2.next()
                                        k.op("pe", lambda e: e.matmul(psM[:, :], lhsT=exp_p[:, kt * 128:(kt + 1) * 128], rhs=selT[:, q0:q0 + 512], start=True, stop=True),
                                             reads=[exp_p, selT], writes=[psM])
                                        eb2 = eb2_r.next()
                                        k.op("dve", lambda e: e.tensor_tensor(out=eb2[:, :], in0=ebf[:, :], in1=psM[:, :], op=ALU.mult),
                                             reads=[ebf, psM], writes=[eb2])
                                    else:
                                        eb2 = ebf
                                    for qs in range(4):
                                        acc = accO[qs]
                                        k.op("pe", lambda e: e.matmul(acc[:, 0:129], lhsT=eb2[:, qs * 128:(qs + 1) * 128],
                                                                      rhs=vv[:, kt, 0:129], start=(kt == kts[0]), stop=(kt == kts[-1])),
                                             reads=[eb2, vv], writes=[acc], inc=(kt == kts[-1]))
                                for qs in range(4):
                                    qt = c * 4 + qs
                                    acc = accO[qs]
                                    a0 = 0
                                    k.op("dve", lambda e: e.reciprocal(out=sm[:, 0:1], in_=acc[:, a0 + 128:a0 + 129]), reads=[acc], writes=[sm])
                                    k.op("dve", lambda e: e.tensor_tensor(out=sm[:, 1:2], in0=sm[:, 0:1],
                                                                          in1=gates[:, qt, gi * 16 + 4 * g + j:gi * 16 + 4 * g + j + 1], op=ALU.mult),
                                         reads=[sm, gates], writes=[sm])
                                    k.op("dve", lambda e: e.scalar_tensor_tensor(out=o_acc[:, qt, j, :], in0=acc[:, a0:a0 + 128], scalar=sm[:, 1:2],
                                                                                 in1=o_acc[:, qt, j, :], op0=ALU.mult, op1=ALU.add),
                                         reads=[acc, sm, o_acc], writes=[o_acc])
                            pst = ring2.next()
                            for qs in range(4):
                                k.op("pe", lambda e: e.transpose(out=pst[:, qs * 128:(qs + 1) * 128], in_=o_acc[:, c * 4 + qs, j, :], identity=ident_f[:, :]),
                                     reads=[o_acc, ident_f], writes=[pst], inc=(qs == 3))
                            oTt = oT_r.next()
                            evac(oTt[:, :], pst[:, :], [pst], [oTt])
                            k.dma("sp", S["oT2"][(4 * g + j) * 128:(4 * g + j + 1) * 128, q0:q0 + 512], oTt[:, :], reads=[oTt], writes=[dbuf["oT2"]])

        def sample_group(g, s, qh, Bs, Bw, Bcs):
            n_all_s = P + TS
            n_sb_s = -(-n_all_s // 64)
            ncb_s = P // 16 - 1
            NT4 = -(-ncb_s // 128)
            NPG = P // 128
            nA = min(128, n_sb_s)
            NTL = NPG + 1
            t0s = TP + s * TS
            pgc_r = Ring([k.sb(f"s_pgc{i}", [128, 16, 128], BF16) for i in range(2)])
            kslcT = k.sb("s_kslcT", [128, NTL * 128], BF16)
            vslc = k.sb("s_vslc", [128, NTL, 136], BF16)
            kwT = k.sb("s_kwT", [128, 5 * 128], BF16)
            vw = k.sb("s_vw", [128, 5, 136], BF16)
            k.op("pool", lambda e: e.memset(vslc[:, :, 128:129], 1.0), writes=[vslc])
            k.op("pool", lambda e: e.memset(vw[:, :, 128:129], 1.0), writes=[vw])
            kcT = k.sb("s_kcT", [128, 512], BF16)
            vc = k.sb("s_vc", [128, NT4, 128], BF16)
            k.op("pool", lambda e: e.memset(vc[:], 0.0), writes=[vc])
            qsm = k.sb("s_qsm", [128, 4, 8], BF16)
            k.op("pool", lambda e: e.tensor_copy(out=qsm[:], in_=qh[:, :, t0s:t0s + TS]), reads=[qh], writes=[qsm])
            qsv = qsm[:].rearrange("p j t -> p (j t)")

            def gather(c, dstT=None, dstV=None):
                col0 = c * 512 + g * 128
                src = S["past"][s * P:(s + 1) * P, col0:col0 + 128].rearrange("(a p) d -> p a d", p=128)
                if dstV is not None:
                    k.dma("sp", dstV[:, 0:NPG, 0:128], src, reads=[dbuf["past"]], writes=[dstV])
                    return
                for p0 in range(0, NPG, 16):
                    npg = min(16, NPG - p0)
                    pgc = pgc_r.next()
                    k.dma("sp", pgc[:, 0:npg, :], src[:, p0:p0 + npg, :], reads=[dbuf["past"]], writes=[pgc])
                    for q0 in range(0, npg, 8):
                        nq = min(8, npg - q0)
                        pb = pbf.next()
                        for i in range(nq):
                            k.op("pe", lambda e: e.transpose(out=pb[:, i * 128:(i + 1) * 128], in_=pgc[:, q0 + i, :], identity=ident_b[:, :]),
                                 reads=[pgc, ident_b], writes=[pb], inc=(i == nq - 1))
                        evac(dstT[:, (p0 + q0) * 128:(p0 + q0 + nq) * 128], pb[:, 0:nq * 128], [pb], [dstT])

            def small_T(src_ap, nm, rows, dstT_ap, dstT_buf):
                tkb = pgc_r.next()
                k.dma("pool", tkb[0:rows, 0, :], src_ap, reads=[dbuf[nm]], writes=[tkb])
                pb = pbf.next()
                k.op("pe", lambda e: e.transpose(out=pb[:, 0:rows], in_=tkb[0:rows, 0, :], identity=ident_b[0:rows, 0:rows]),
                     reads=[tkb, ident_b], writes=[pb])
                evac(dstT_ap, pb[:, 0:rows], [pb], [dstT_buf])

            cscope = k.scope()
            cscope.__enter__()
            kx = k.sb("s_kx", [128, P], BF16)
            hg = k.sb("s_hg", [128, 2, 512], BF16)
            tmpx = k.sb("s_tmpx", [128, 512], F32)
            tmp2 = k.sb("s_tmp2", [128, 512], F32)
            sqb = k.sb("s_sqb", [128, 512], BF16)
            gather(0, dstT=kx)
            compress(kx, 0, ncb_s, hg, tmpx, tmp2)
            kc_finish(hg, ncb_s, kcT, tmpx, tmp2, sqb)
            gather(1, dstT=kx)
            compress(kx, 1, ncb_s, hg, tmpx, tmp2)
            for nt in range(NT4):
                rows = min(128, ncb_s - nt * 128)
                ps = ringN.next()
                for half in range(2):
                    k.op("pe", lambda e: e.matmul(ps[0:rows, 0:128], lhsT=hg[:, half, nt * 128:nt * 128 + rows], rhs=w2[:, 1, half, :],
                                                  start=(half == 0), stop=(half == 1)), reads=[hg, w2], writes=[ps], inc=(half == 1))
                k.op("act", lambda e: e.copy(out=vc[0:rows, nt, :], in_=ps[0:rows, 0:128]), reads=[ps], writes=[vc])
            cscope.__exit__(None, None, None)
            exp_sA = k.sb("s_expsA", [128, (NPG + 1) * 128], BF16)
            k.dma("sp", exp_sA[0:nA, :], I["ns_exp"][0:nA, :], writes=[exp_sA])
            gather(2, dstT=kslcT)
            small_T(O["o_kv"][t0s:t0s + TS, 2 * 512 + g * 128:2 * 512 + (g + 1) * 128], "o_kv", TS, kslcT[:, NPG * 128:NPG * 128 + TS], kslcT)
            gather(3, dstV=vslc)
            k.dma("pool", vslc[0:TS, NPG, 0:128], O["o_kv"][t0s:t0s + TS, 3 * 512 + g * 128:3 * 512 + (g + 1) * 128],
                  reads=[dbuf["o_kv"]], writes=[vslc])
            tkw = k.sb("s_tkw", [128, 4, 128], BF16)
            k.dma("pool", tkw[:], I["st_win"][s, :, g * 128:(g + 1) * 128].rearrange("(t p) d -> p t d", p=128),
                  reads=[dbuf["st_win"]], writes=[tkw])
            transpose_tiles(tkw, 4, kwT)
            small_T(S["win_new"][t0s:t0s + TS, g * 128:(g + 1) * 128], "win_new", TS, kwT[:, 512:512 + TS], kwT)
            k.dma("pool", vw[:, 0:4, 0:128], I["st_win"][s, :, 512 + g * 128:512 + (g + 1) * 128].rearrange("(t p) d -> p t d", p=128),
                  reads=[dbuf["st_win"]], writes=[vw])
            k.dma("pool", vw[0:TS, 4, 0:128], S["win_new"][t0s:t0s + TS, 512 + g * 128:512 + (g + 1) * 128],
                  reads=[dbuf["win_new"]], writes=[vw])

            o_s = k.sb("s_os", [TS, 4, 128], F32)
            sm = k.sb("s_sm", [TS, 8], F32)
            ecs = k.sb("s_ecs", [128, NT4, 32], F32)
            ecb = k.sb("s_ecb", [128, NT4, 32], BF16)
            pcb = k.sb("s_pcb", [128, NT4, 32], BF16)
            recs = k.sb("s_recs", [128, 32], F32)
            ps = ringN.next()
            for nt in range(NT4):
                rows = min(128, ncb_s - nt * 128)
                k.op("pe", lambda e: e.matmul(ps[0:rows, nt * 32:(nt + 1) * 32], lhsT=kcT[:, nt * 128:nt * 128 + rows], rhs=qsv,
                                              start=True, stop=True), reads=[kcT, qsm], writes=[ps], inc=(nt == NT4 - 1))
            k.op("pool", lambda e: e.memset(ecs[:], NEG), writes=[ecs])
            for nt in range(NT4):
                rows = min(128, ncb_s - nt * 128)
                k.op("dve", lambda e: e.tensor_tensor(out=ecs[0:rows, nt, :], in0=ps[0:rows, nt * 32:(nt + 1) * 32],
                                                      in1=Bcs[0:rows, nt, :, :].rearrange("p j t -> p (j t)"), op=ALU.add),
                     reads=[ps, Bcs], writes=[ecs])
            k.op("act", lambda e: e.activation(out=ecb[:], in_=ecs[:], func=AF.Exp), reads=[ecs], writes=[ecb])
            psD = ringN.next()
            for nt in range(NT4):
                k.op("pe", lambda e: e.matmul(psD[:, 0:32], lhsT=ones_b[:, :], rhs=ecb[:, nt, :], start=(nt == 0), stop=(nt == NT4 - 1)),
                     reads=[ones_b, ecb], writes=[psD], inc=(nt == NT4 - 1))
            k.op("dve", lambda e: e.tensor_scalar(out=recs[:], in0=psD[:, 0:32], scalar1=1e-30, scalar2=None, op0=ALU.max), reads=[psD], writes=[recs])
            k.op("dve", lambda e: e.reciprocal(out=recs[:], in_=recs[:]), reads=[recs], writes=[recs])
            k.op("dve", lambda e: e.tensor_tensor(out=pcb[:], in0=ecb[:], in1=recs[:].unsqueeze(1).to_broadcast([128, NT4, 32]), op=ALU.mult),
                 reads=[ecb, recs], writes=[pcb])
            for j in range(4):
                pso = accO[j]
                for nt in range(NT4):
                    k.op("pe", lambda e: e.matmul(pso[0:TS, 0:128], lhsT=pcb[:, nt, j * 8:(j + 1) * 8], rhs=vc[:, nt, :],
                                                  start=(nt == 0), stop=(nt == NT4 - 1)), reads=[pcb, vc], writes=[pso], inc=(nt == NT4 - 1))
                k.op("act", lambda e: e.activation(out=o_s[:, j, :], in_=pso[0:TS, 0:128], func=AF.Copy,
                                                   scale=gates_s[:, s, 4 * g + j:4 * g + j + 1]), reads=[pso, gates_s], writes=[o_s])
            psI = ringN.next()
            first = True
            for j in range(4):
                for nt in range(NT4):
                    k.op("pe", lambda e: e.matmul(psI[0:TS, 0:n_sb_s], lhsT=pcb[:, nt, j * 8:(j + 1) * 8], rhs=ovl_s[:, nt, :],
                                                  start=first, stop=(j == 3 and nt == NT4 - 1)), reads=[pcb, ovl_s], writes=[psI],
                         inc=(j == 3 and nt == NT4 - 1))
                    first = False
            sc = k.sb("s_sc", [TS, n_sb_s], F32)
            wk = k.sb("s_wk", [TS, n_sb_s], F32)
            m8 = k.sb("s_m8", [TS, 16], F32)
            sel = k.sb("s_sel", [TS, n_sb_s], F32)
            k.op("dve", lambda e: e.tensor_tensor(out=sc[:, :], in0=psI[0:TS, 0:n_sb_s], in1=add_s[:, :], op=ALU.add), reads=[psI, add_s], writes=[sc])
            k.op("dve", lambda e: e.max(out=m8[:, 0:8], in_=sc[:, :]), reads=[sc], writes=[m8])
            k.op("dve", lambda e: e.match_replace(out=wk[:, :], in_to_replace=m8[:, 0:8], in_values=sc[:, :], imm_value=-1e30), reads=[m8, sc], writes=[wk])
            k.op("dve", lambda e: e.max(out=m8[:, 8:16], in_=wk[:, :]), reads=[wk], writes=[m8])
            k.op("dve", lambda e: e.tensor_scalar(out=sel[:, :], in0=sc[:, :], scalar1=m8[:, 15:16], scalar2=None, op0=ALU.is_ge), reads=[sc, m8], writes=[sel])
            k.op("dve", lambda e: e.tensor_tensor(out=sel[:, :], in0=sel[:, :], in1=val_s[:, :], op=ALU.mult), reads=[sel, val_s], writes=[sel])
            selTa = k.sb("s_selTa", [128, TS], BF16)
            selTb = k.sb("s_selTb", [1, TS], BF16)
            pst = ringN.next()
            k.op("pe", lambda e: e.transpose(out=pst[0:nA, 0:TS], in_=sel[:, 0:nA], identity=ident_f[0:TS, 0:TS]), reads=[sel, ident_f], writes=[pst])
            k.op("act", lambda e: e.copy(out=selTa[0:nA, :], in_=pst[0:nA, 0:TS]), reads=[pst], writes=[selTa])
            if n_sb_s > 128:
                pst2 = ringN.next()
                k.op("pe", lambda e: e.transpose(out=pst2[0:1, 0:TS], in_=sel[:, 128:129], identity=ident_f[0:TS, 0:TS]), reads=[sel, ident_f], writes=[pst2])
                k.op("act", lambda e: e.copy(out=selTb[0:1, :], in_=pst2[0:1, 0:TS]), reads=[pst2], writes=[selTb])

            def branch(kT, vv, ntile, last_rows, Btab, gi, use_mask):
                ess = k.sb("s_ess", [128, 16, 32], F32)
                esb = k.sb("s_esb", [128, ntile, 32], BF16)
                k.op("pool", lambda e: e.memset(esb[:], 0.0), writes=[esb])
                for t0 in range(0, ntile, 16):
                    nt_ = min(16, ntile - t0)
                    ps = ringN.next()
                    for i in range(nt_):
                        kt = t0 + i
                        rows = last_rows if kt == ntile - 1 else 128
                        k.op("pe", lambda e: e.matmul(ps[0:rows, i * 32:(i + 1) * 32], lhsT=kT[:, kt * 128:kt * 128 + rows], rhs=qsv,
                                                      start=True, stop=True), reads=[kT, qsm], writes=[ps], inc=(i == nt_ - 1))
                    full = nt_ if (t0 + nt_ < ntile) else nt_ - 1
                    if full:
                        k.op("dve", lambda e: e.tensor_tensor(out=ess[:, 0:full, :], in0=ps[:, 0:full * 32].rearrange("p (a b) -> p a b", b=32),
                                                              in1=Btab[:, t0:t0 + full, :, :].rearrange("p a j t -> p a (j t)"), op=ALU.add),
                             reads=[ps, Btab], writes=[ess])
                        k.op("act", lambda e: e.activation(out=esb[:, t0:t0 + full, :], in_=ess[:, 0:full, :], func=AF.Exp), reads=[ess], writes=[esb])
                    if full < nt_:
                        i = nt_ - 1
                        k.op("dve", lambda e: e.tensor_tensor(out=ess[0:last_rows, i, :], in0=ps[0:last_rows, i * 32:(i + 1) * 32],
                                                              in1=Btab[0:last_rows, ntile - 1, :, :].rearrange("p j t -> p (j t)"), op=ALU.add),
                             reads=[ps, Btab], writes=[ess])
                        k.op("act", lambda e: e.activation(out=esb[0:last_rows, ntile - 1, :], in_=ess[0:last_rows, i, :], func=AF.Exp),
                             reads=[ess], writes=[esb])
                if use_mask:
                    for t0 in range(0, ntile, 64):
                        nt_ = min(64, ntile - t0)
                        psM = ringN.next()
                        for i in range(nt_):
                            kt = t0 + i
                            rows = last_rows if kt == ntile - 1 else 128
                            two = n_sb_s > 128 and kt == ntile - 1
                            k.op("pe", lambda e: e.matmul(psM[0:rows, i * 8:(i + 1) * 8], lhsT=exp_sA[0:nA, kt * 128:kt * 128 + rows], rhs=selTa[0:nA, :],
                                                          start=True, stop=not two), reads=[exp_sA, selTa], writes=[psM], inc=(i == nt_ - 1 and not two))
                            if two:
                                k.op("pe", lambda e: e.matmul(psM[0:rows, i * 8:(i + 1) * 8], lhsT=exp_sB[0:1, 0:rows], rhs=selTb[0:1, :],
                                                              start=False, stop=True), reads=[exp_sB, selTb], writes=[psM], inc=(i == nt_ - 1))
                        full = nt_ if (t0 + nt_ < ntile) else nt_ - 1
                        if full:
                            k.op("dve", lambda e: e.tensor_tensor(
                                out=esb[:, t0:t0 + full, :].rearrange("p a (j t) -> p a j t", t=8),
                                in0=esb[:, t0:t0 + full, :].rearrange("p a (j t) -> p a j t", t=8),
                                in1=psM[:, 0:full * 8].rearrange("p (a t) -> p a t", t=8).unsqueeze(2).to_broadcast([128, full, 4, 8]), op=ALU.mult),
                                reads=[esb, psM], writes=[esb])
                        if full < nt_:
                            i = nt_ - 1
                            k.op("dve", lambda e: e.tensor_tensor(
                                out=esb[0:last_rows, ntile - 1, :].rearrange("p (j t) -> p j t", t=8),
                                in0=esb[0:last_rows, ntile - 1, :].rearrange("p (j t) -> p j t", t=8),
                                in1=psM[0:last_rows, i * 8:(i + 1) * 8].unsqueeze(1).to_broadcast([last_rows, 4, 8]), op=ALU.mult),
                                reads=[esb, psM], writes=[esb])
                for j in range(4):
                    acc = accO[j]
                    for kt in range(ntile):
                        rows = last_rows if kt == ntile - 1 else 128
                        k.op("pe", lambda e: e.matmul(acc[0:TS, 0:129], lhsT=esb[0:rows, kt, j * 8:(j + 1) * 8], rhs=vv[0:rows, kt, 0:129],
                                                      start=(kt == 0), stop=(kt == ntile - 1)), reads=[esb, vv], writes=[acc], inc=(kt == ntile - 1))
                    k.op("dve", lambda e: e.reciprocal(out=sm[:, 0:1], in_=acc[0:TS, 128:129]), reads=[acc], writes=[sm])
                    k.op("dve", lambda e: e.tensor_tensor(out=sm[:, 1:2], in0=sm[:, 0:1], in1=gates_s[:, s, gi * 16 + 4 * g + j:gi * 16 + 4 * g + j + 1], op=ALU.mult),
                         reads=[sm, gates_s], writes=[sm])
                    k.op("dve", lambda e: e.scalar_tensor_tensor(out=o_s[:, j, :], in0=acc[0:TS, 0:128], scalar=sm[:, 1:2], in1=o_s[:, j, :],
                                                                 op0=ALU.mult, op1=ALU.add), reads=[acc, sm, o_s], writes=[o_s])

            with k.scope():
                branch(kslcT, vslc, NTL, TS, Bs, 1, True)
            with k.scope():
                branch(kwT, vw, 5, TS, Bw, 2, False)
            pst = ringN.next()
            for j in range(4):
                k.op("pe", lambda e: e.transpose(out=pst[:, j * TS:(j + 1) * TS], in_=o_s[:, j, :], identity=ident_f[0:TS, 0:TS]),
                     reads=[o_s, ident_f], writes=[pst], inc=(j == 3))
            oTs = k.sb("s_oTs", [128, 4, TS], BF16)
            evac(oTs[:].rearrange("p j t -> p (j t)"), pst[:, 0:4 * TS], [pst], [oTs])
            k.dma("sp", S["oT2"][g * 512:(g + 1) * 512, t0s:t0s + TS].rearrange("(j p) t -> p j t", p=128), oTs[:],
                  reads=[oTs], writes=[dbuf["oT2"]])


        n_all_s = P + TS
        n_sb_s = -(-n_all_s // 64)
        ncb_s = P // 16 - 1
        NT4 = -(-ncb_s // 128)
        NPG = P // 128
        nA = min(128, n_sb_s)
        KTOT = (NPG + 1) * 128
        ovl_s = k.sb("a_ovls", [128, NT4, n_sb_s], BF16)
        k.dma("pool", ovl_s[:], I["ns_ovl"].rearrange("t p m -> p t m"), writes=[ovl_s])
        add_s = k.sb("a_adds", [TS, n_sb_s], F32)
        k.dma("sp", add_s[:], I["ns_add"], writes=[add_s])
        val_s = k.sb("a_vals", [TS, n_sb_s], F32)
        k.dma("sp", val_s[:], I["ns_valid"], writes=[val_s])
        exp_sB = k.sb("a_expsB", [1, 128], BF16)
        if n_sb_s > 128:
            k.dma("sp", exp_sB[:, :], I["ns_exp"][128:129, KTOT - 128:KTOT], writes=[exp_sB])
        iota_f = k.sb("a_iota", [128, 1], F32)
        k.dma("sp", iota_f[:], I["n_iota"], writes=[iota_f])
        gates_s = k.sb("a_gatess", [TS, NS, 48], F32)
        k.dma("sp", gates_s[:], S["gates"][TP:NT, :].rearrange("(s t) c -> t s c", t=TS), reads=[dbuf["gates"]], writes=[gates_s])
        for g in range(4):
          with k.scope():
            qh = k.sb("a_qh", [128, 4, NT], BF16)
            k.dma("sp", qh[:], S["qT"][g * 512:(g + 1) * 512, :].rearrange("(j p) t -> p j t", p=128),
                  reads=[dbuf["qT"]], writes=[qh])
            Bs = k.sb("a_Bs", [128, NPG + 1, 4, 8], F32)
            Bw = k.sb("a_Bw", [128, 5, 4, 8], F32)
            Bcs = k.sb("a_Bcs", [128, NT4, 4, 8], F32)
            for j in range(4):
                for nt in range(NT4):
                    off_d = P - 31 - 2048 * nt - 2032
                    base = Z + min(off_d, 790)
                    build_tab(lambda c0, n, j=j, nt=nt: (Bcs[:, nt, j, :], [Bcs]), 4 * g + j, base, 8, 16)
            with k.scope():
                prompt_group(g, qh, Bs, Bw)
            for s in range(NS):
                with k.scope():
                    sample_group(g, s, qh, Bs, Bw, Bcs)
```

```python
import math
import numpy as np
import ml_dtypes
import concourse.bass as bass
import concourse.mybir as mybir
from concourse.bass_utils import run_bass_kernel_spmd
from contextlib import ExitStack

F32 = mybir.dt.float32
BF16 = mybir.dt.bfloat16
I32 = mybir.dt.int32
AF = mybir.ActivationFunctionType
ALU = mybir.AluOpType
AX = mybir.AxisListType

ENGS = ("pe", "dve", "act", "pool", "sp")
DMA_RING = 16
RMS_EPS = 1e-6


class Buf:
    __slots__ = ("t", "w", "r", "name", "multi", "ws", "psum")

    def __init__(self, t, name="", multi=False):
        self.t = t
        self.w = None
        self.r = {}
        self.name = name
        self.multi = multi
        self.ws = {}
        self.psum = False

    def __getitem__(self, idx):
        return self.t[idx]


class K:
    def __init__(self, nc, es, same_engine_sync=True):
        self.nc = nc
        self.es = es
        self.eng = {"pe": nc.tensor, "dve": nc.vector, "act": nc.scalar,
                    "pool": nc.gpsimd, "sp": nc.sync}
        self.sems = {}
        self.cnt = {}
        for e in ("pe", "dve", "act", "pool"):
            self.sems[e] = es.enter_context(nc.semaphore("s_" + e))
            self.cnt[e] = 0
        self.ring = {}
        self.ring_n = {}
        for q in ("sp", "pool", "act"):
            self.ring[q] = [es.enter_context(nc.semaphore(f"d_{q}{i}")) for i in range(DMA_RING)]
            self.ring_n[q] = 0
        self.seen = {e: {} for e in ENGS}
        self.same_engine_sync = same_engine_sync
        self.n_inst = {e: 0 for e in ENGS}
        self.n_wait = {e: 0 for e in ENGS}
        self.rr = 0

    def sb(self, name, shape, dt=F32):
        self.uid = getattr(self, "uid", 0) + 1
        t = self.es.enter_context(self.nc.sbuf_tensor(f"sb{self.uid}_" + name, list(shape), dt))
        return Buf(t, name)

    def ps(self, name, shape, dt=F32):
        t = self.es.enter_context(self.nc.psum_tensor("ps_" + name, list(shape), dt))
        b = Buf(t, name)
        b.psum = True
        return b

    def _semobj(self, key):
        if isinstance(key, str):
            return self.sems[key]
        q, i = key
        return self.ring[q][i]

    def _wait(self, e, ev):
        if ev is None:
            return
        key, val = ev
        if key == e and (e == "pe" or not self.same_engine_sync):
            return
        if self.seen[e].get(key, 0) >= val:
            return
        self.eng[e].wait_ge(self._semobj(key), val)
        self.seen[e][key] = val
        self.n_wait[e] += 1

    def _deps(self, e, reads, writes):
        for b in reads:
            if b.multi:
                for key, val in b.ws.items():
                    self._wait(e, (key, val))
            else:
                self._wait(e, b.w)
            if b.psum:
                for key, val in b.r.items():
                    if key != e:
                        self._wait(e, (key, val))
        for b in writes:
            if not b.multi:
                self._wait(e, b.w)
            for key, val in b.r.items():
                self._wait(e, (key, val))

    def _commit(self, ev, reads, writes):
        k_, v = ev
        for b in reads:
            if b in writes:
                continue
            if b.r.get(k_, 0) < v:
                b.r[k_] = v
        for b in writes:
            if b.multi:
                if b.r:
                    b.ws = {}
                if b.ws.get(k_, 0) < v:
                    b.ws[k_] = v
            else:
                b.w = ev
            b.r = {}

    def op(self, e, fn, reads=(), writes=(), inc=True, dma=False):
        if dma:
            return self._dma_like(e, fn, reads, writes)
        self._deps(e, reads, writes)
        ins = fn(self.eng[e])
        self.n_inst[e] += 1
        if inc:
            ins.then_inc(self.sems[e], 1)
            self.cnt[e] += 1
            ev = (e, self.cnt[e])
        else:
            ev = (e, self.cnt[e] + 1)
        self._commit(ev, reads, writes)
        return ins

    def _dma_like(self, q, fn, reads, writes):
        self._deps(q, reads, writes)
        n = self.ring_n[q]
        slot = n % DMA_RING
        rnd = n // DMA_RING
        if rnd > 0:
            self._wait(q, ((q, slot), 16 * rnd))
        ins = fn(self.eng[q])
        ins.then_inc(self.ring[q][slot], 16)
        self.ring_n[q] = n + 1
        self.n_inst[q] += 1
        ev = ((q, slot), 16 * (rnd + 1))
        self._commit(ev, reads, writes)
        return ins

    def dma(self, q, out_ap, in_ap, reads=(), writes=(), **kw):
        self._deps(q, reads, writes)
        n = self.ring_n[q]
        slot = n % DMA_RING
        rnd = n // DMA_RING
        if rnd > 0:
            self._wait(q, ((q, slot), 16 * rnd))
        ins = self.eng[q].dma_start(out=out_ap, in_=in_ap, **kw)
        ins.then_inc(self.ring[q][slot], 16)
        self.ring_n[q] = n + 1
        self.n_inst[q] += 1
        ev = ((q, slot), 16 * (rnd + 1))
        self._commit(ev, reads, writes)
        return ins

    def finish(self):
        for q in ("sp", "pool", "act"):
            n = self.ring_n[q]
            for slot in range(DMA_RING):
                cnt = (n - slot + DMA_RING - 1) // DMA_RING if n > slot else 0
                if cnt > 0:
                    self._wait("sp", ((q, slot), 16 * cnt))

    def barrier(self):
        for e in ENGS:
            for o in ("pe", "dve", "act", "pool"):
                if o != e and self.cnt[o] > 0:
                    self._wait(e, (o, self.cnt[o]))
            for q in ("sp", "pool", "act"):
                n = self.ring_n[q]
                for slot in range(DMA_RING):
                    c = (n - slot + DMA_RING - 1) // DMA_RING if n > slot else 0
                    if c > 0:
                        self._wait(e, ((q, slot), 16 * c))

    def scope(self):
        kk = self

        class _S:
            def __enter__(s_):
                s_.old = kk.es
                s_.new = ExitStack()
                s_.new.__enter__()
                kk.es = s_.new
                return s_

            def __exit__(s_, *a):
                kk.barrier()
                kk.es = s_.old
                s_.new.__exit__(*a)
                return False
        return _S()

    def ev_eng(self):
        self.rr += 1
        return "act" if self.rr % 2 else "dve"


class Ring:
    def __init__(self, bufs):
        self.bufs = bufs
        self.i = 0

    def next(self):
        b = self.bufs[self.i % len(self.bufs)]
        self.i += 1
        return b


class Cfg:
    def __init__(self, TP=2048, NKH=16, D=2048, DFF=5504, NS=4, TS=8, stage=99, dbg=False, PAST=8192, NPOOL=2560):
        self.PAST = PAST
        self.NPOOL = NPOOL
        self.TP = TP
        self.NKH = NKH
        self.NVH = 2 * NKH
        self.D = D
        self.KT = D // 128
        self.DFF = DFF
        self.NS = NS
        self.TS = TS
        self.NSAMP = NS * TS
        self.NT = TP + self.NSAMP
        self.NBP = TP // 64
        self.NB = self.NBP + 1
        self.NCH = TP // 512
        self.KD = NKH * 128
        self.VD = self.NVH * 128
        self.CONVD = 2 * self.KD + self.VD
        self.PROJ = self.CONVD + self.VD + 2 * self.NVH
        self.stage = stage
        self.dbg = dbg

    def chunks(self):
        r = [(c * 512, 512) for c in range(self.NCH)]
        r.append((self.TP, self.NSAMP))
        return r

    def ttiles(self):
        r = [(t * 128, 128) for t in range(self.TP // 128)]
        r.append((self.TP, self.NSAMP))
        return r

    def blocks(self):
        r = [(b * 64, 64) for b in range(self.NBP)]
        r.append((self.TP, self.NSAMP))
        return r


def host_consts(cfg):
    c = {}
    c["ident_f"] = np.eye(128, dtype=np.float32)
    c["ident_b"] = np.eye(128).astype(ml_dtypes.bfloat16)
    c["ones_f"] = np.ones((128, 128), np.float32)
    c["ones_b"] = np.ones((128, 128)).astype(ml_dtypes.bfloat16)
    NEG = -30000.0
    j = np.arange(64)[:, None]
    i = np.arange(64)[None, :]
    def mk(blk, n):
        sj = (np.arange(n)[:, None] // blk)
        si = (np.arange(n)[None, :] // blk)
        same = (sj == si)
        jj = np.arange(n)[:, None]
        ii = np.arange(n)[None, :]
        ui = same & (ii >= jj)
        us = same & (ii > jj)
        ls = same & (ii < jj)
        out = np.zeros((5, 64, 64), np.float32)
        out[0, :n, :n] = ui
        out[1, :n, :n] = same
        out[2, :, :] = NEG
        out[2, :n, :n] = np.where(ui, 0.0, NEG)
        out[3, :, :] = NEG
        out[3, :n, :n] = np.where(us, 0.0, NEG)
        out[4, :, :] = -NEG
        out[4, :n, :n] = np.where(ls, 0.0, -NEG)
        return out
    c["masks_p"] = mk(64, 64)
    c["masks_s"] = mk(cfg.TS, cfg.NSAMP)
    rm = np.zeros((64, cfg.NS), np.float32)
    for s in range(cfg.NS):
        rm[s * cfg.TS:(s + 1) * cfg.TS, s] = 1.0
    c["rowmask_s"] = rm
    return c


def build(cfg):
    nc = bass.Bass("TRN2", target_bir_lowering=False)
    D, KT, NT, TP, NS, TS = cfg.D, cfg.KT, cfg.NT, cfg.TP, cfg.NS, cfg.TS
    NSAMP, NVH, NKH, NB = cfg.NSAMP, cfg.NVH, cfg.NKH, cfg.NB
    CONVD, PROJ, VD, KD = cfg.CONVD, cfg.PROJ, cfg.VD, cfg.KD

    def din(name, shape, dt=F32):
        return nc.dram_tensor(name, list(shape), dt, kind="ExternalInput").ap()

    def dout(name, shape, dt=F32):
        return nc.dram_tensor(name, list(shape), dt, kind="ExternalOutput").ap()

    def dscr(name, shape, dt=F32):
        return nc.dram_tensor(name, list(shape), dt, kind="Internal").ap()

    I = {}
    S = {}
    I["xin"] = din("xin", [NT, D])
    I["norm_mix"] = din("norm_mix", [2, D])
    I["norm_ffn"] = din("norm_ffn", [2, D])
    I["gdn_w_in"] = din("gdn_w_in", [D, PROJ])
    I["gdn_conv_w"] = din("gdn_conv_w", [CONVD, 4])
    I["gdn_A_log"] = din("gdn_A_log", [1, NVH])
    I["gdn_dt_bias"] = din("gdn_dt_bias", [1, NVH])
    I["gdn_norm"] = din("gdn_norm", [1, 128])
    I["gdn_w_out"] = din("gdn_w_out", [VD, D])
    I["st_gdn"] = din("st_gdn", [NS, NVH, 128, 128])
    I["st_gdn_conv"] = din("st_gdn_conv", [NS * 3, CONVD])
    DFF = cfg.DFF
    I["ffn_w_up"] = din("ffn_w_up", [2, D, 2 * DFF])
    I["ffn_conv_w"] = din("ffn_conv_w", [2, DFF, 3])
    I["ffn_conv_b"] = din("ffn_conv_b", [2, DFF])
    I["ffn_w_down"] = din("ffn_w_down", [2, DFF, D])
    I["st_ffn_conv"] = din("st_ffn_conv", [2, NS * 2, DFF])
    for nm, shp, dt in (("ident_f", [128, 128], F32), ("ident_b", [128, 128], BF16),
                        ("ones_f", [128, 128], F32), ("ones_b", [128, 128], BF16),
                        ("masks_p", [5, 64, 64], F32), ("masks_s", [5, 64, 64], F32),
                        ("rowmask_s", [64, NS], F32)):
        I[nm] = din(nm, shp, dt)
    O = {}
    O["o_gdn_p"] = dout("o_gdn_p", [NVH, 128, 128])
    O["o_gdn_s"] = dout("o_gdn_s", [NS, NVH, 128, 128])
    O["o_gdnc"] = dout("o_gdnc", [(1 + NS) * 3, CONVD])
    PAST, NPG, NPOOL = cfg.PAST, cfg.PAST // 128, cfg.NPOOL
    I["nsa_w_in"] = din("nsa_w_in", [D, 5168])
    I["nsa_q_norm"] = din("nsa_q_norm", [1, 128])
    I["nsa_k_norm"] = din("nsa_k_norm", [1, 3, 128])
    I["st_win"] = din("st_win", [NS, 512, 1024])
    I["nsa_cmp_pe"] = din("nsa_cmp_pe", [2, 32, 128])
    I["nsa_cmp_w1"] = din("nsa_cmp_w1", [2, 4096, 256])
    I["nsa_cmp_w2"] = din("nsa_cmp_w2", [2, 256, 128])
    I["rel_bias"] = din("rel_bias", [32, 16])
    I["nsa_w_out"] = din("nsa_w_out", [2048, 2048])
    I["cache"] = din("cache", [NPOOL * 128, 2048])
    S["past"] = dscr("past", [NS * PAST, 2048], BF16)
    I["page_table"] = din("page_table", [NS, NPG], I32)
    hc = nsa_host_consts(cfg, PAST)
    for nm, arr in hc.items():
        if nm.startswith("dims"):
            continue
        I[nm] = din(nm, list(arr.shape), {np.dtype(np.float32): F32, np.dtype(np.int32): I32}.get(arr.dtype, BF16))
    S["gvec"] = dscr("gvec", [16, NSA_GL])
    S["oT2"] = dscr("oT2", [2048, NT], BF16)
    O["o_kv"] = dout("o_kv", [NT, 2048])
    O["o_pwin"] = dout("o_pwin", [512, 1024])
    O["o_swin"] = dout("o_swin", [NS, 512, 1024])
    S["qT"] = dscr("qT", [2048, NT], BF16)
    S["win_new"] = dscr("win_new", [NT, 1024])
    S["gates"] = dscr("gates", [NT, 48])
    O["o_ffnc"] = dout("o_ffnc", [2, (1 + NS) * 2, DFF])
    O["y"] = dout("y", [NT, D])
    if cfg.dbg:
        O["dbg_h1"] = dout("dbg_h1", [NT, D])
        O["dbg_h2"] = dout("dbg_h2", [NT, D])
        O["dbg_h3"] = dout("dbg_h3", [NT, D])
        O["dbg_oT2"] = dout("dbg_oT2", [2048, NT], BF16)
        O["dbg_gvec"] = dout("dbg_gvec", [16, NSA_GL])
        O["dbg_qT"] = dout("dbg_qT", [2048, NT], BF16)
    if cfg.dbg:
        O["dbg_xn"] = dout("dbg_xn", [128, KT, NT], BF16)
        O["dbg_o"] = dout("dbg_o", [NVH * 128, NT], BF16)
    S["oT"] = dscr("oT", [NVH * 128, NT], BF16)
    S["tmS"] = dscr("tmS", [64, 5 * NVH * NB], F32)
    S["h1"] = dscr("h1", [NT, D])
    S["h2"] = dscr("h2", [NT, D])
    S["h3"] = dscr("h3", [NT, D])
    S["actT"] = dscr("actT", [DFF, NT], BF16)

    es = ExitStack()
    with es:
        k = K(nc, es)
        dbuf = {n: Buf(a, n, multi=True) for n, a in list(I.items()) + list(O.items()) + list(S.items())}

        ident_f = k.sb("ident_f", [128, 128], F32)
        ident_b = k.sb("ident_b", [128, 128], BF16)
        ones_f = k.sb("ones_f", [128, 128], F32)
        ones_b = k.sb("ones_b", [128, 128], BF16)
        masks_p = k.sb("masks_p", [64, 5, 64], F32)
        masks_s = k.sb("masks_s", [64, 5, 64], F32)
        rowmask_s = k.sb("rowmask_s", [64, NS], F32)
        for nm, t in (("ident_f", ident_f), ("ident_b", ident_b), ("ones_f", ones_f), ("ones_b", ones_b),
                      ("rowmask_s", rowmask_s)):
            k.dma("sp", t[:], I[nm], writes=[t])
        k.dma("sp", masks_p[:], I["masks_p"].rearrange("m j i -> j m i"), writes=[masks_p])
        k.dma("sp", masks_s[:], I["masks_s"].rearrange("m j i -> j m i"), writes=[masks_s])

        pf = Ring([k.ps(f"pf{i}", [128, 512], F32) for i in range(6)])
        pbf = Ring([k.ps(f"pb{i}", [128, 1024], BF16) for i in range(2)])

        def norm_fm(src_name, gain_ap, xn):
          with k.scope():
              gain_bc = k.sb(f"gain_{src_name}", [128, D], F32)
              k.dma("sp", gain_bc[:], gain_ap.partition_broadcast(128), writes=[gain_bc])
              xt_r = Ring([k.sb(f"xt{i}_{src_name}", [128, D], F32) for i in range(2)])
              sq = k.sb(f"sq_{src_name}", [128, D], F32)
              xs_r = Ring([k.sb(f"xs{i}_{src_name}", [128, D], BF16) for i in range(2)])
              ss_r = Ring([k.sb(f"ss{i}_{src_name}", [128, 2], F32) for i in range(2)])
              src = dbuf[src_name]
              for (r0, R) in cfg.ttiles():
                  xt = xt_r.next()
                  xs = xs_r.next()
                  ss = ss_r.next()
                  k.dma("sp", xt[0:R, :], src[r0:r0 + R, :], reads=[src], writes=[xt])
                  k.op("act", lambda e: e.activation(out=sq[0:R, :], in_=xt[0:R, :], func=AF.Square),
                       reads=[xt], writes=[sq])
                  k.op("dve", lambda e: e.reduce_sum(out=ss[0:R, 0:1], in_=sq[0:R, :], axis=AX.X),
                       reads=[sq], writes=[ss])
                  k.op("dve", lambda e: e.tensor_scalar(out=ss[0:R, 1:2], in0=ss[0:R, 0:1], scalar1=1.0 / D,
                                                        scalar2=RMS_EPS, op0=ALU.mult, op1=ALU.add),
                       reads=[ss], writes=[ss])
                  k.op("act", lambda e: e.activation(out=ss[0:R, 0:1], in_=ss[0:R, 1:2], func=AF.Sqrt),
                       reads=[ss], writes=[ss])
                  k.op("dve", lambda e: e.reciprocal(out=ss[0:R, 0:1], in_=ss[0:R, 0:1]),
                       reads=[ss], writes=[ss])
                  k.op("dve", lambda e: e.scalar_tensor_tensor(out=xs[0:R, :], in0=xt[0:R, :], scalar=ss[0:R, 0:1],
                                                               in1=gain_bc[0:R, :], op0=ALU.mult, op1=ALU.mult),
                       reads=[xt, ss, gain_bc], writes=[xs])
                  for g4 in range(KT // 4):
                      pb = pbf.next()
                      for q in range(4):
                          kt = g4 * 4 + q
                          k.op("pe", lambda e: e.transpose(out=pb[:, q * 128:q * 128 + R],
                                                           in_=xs[0:R, kt * 128:(kt + 1) * 128],
                                                           identity=ident_b[0:R, 0:R]),
                               reads=[xs, ident_b], writes=[pb], inc=(q == 3))
                      eng = k.ev_eng()
                      src_v = pb[:, 0:512].rearrange("p (q r) -> p q r", q=4)[:, :, 0:R]
                      dst_v = xn[:, g4 * 4:(g4 + 1) * 4, r0:r0 + R]
                      if eng == "act":
                          k.op("act", lambda e: e.copy(out=dst_v, in_=src_v), reads=[pb], writes=[xn])
                      else:
                          k.op("dve", lambda e: e.tensor_copy(out=dst_v, in_=src_v), reads=[pb], writes=[xn])

        CC = dict(ident_f=ident_f, ident_b=ident_b, ones_f=ones_f, ones_b=ones_b,
                  masks_p=masks_p, masks_s=masks_s, rowmask_s=rowmask_s)
        with k.scope():
            xn = k.sb("xn", [128, KT, NT], BF16)
            norm_fm("xin", I["norm_mix"][0:1, :], xn)
            if cfg.dbg:
                k.dma("sp", O["dbg_xn"], xn[:], reads=[xn], writes=[dbuf["dbg_xn"]])
            gdn_layer(k, cfg, I, O, S, dbuf, xn, pf, pbf, CC)
        if cfg.stage >= 2:
            linear_tm(k, cfg, pf, dbuf, "go", "oT", S["oT"], "gdn_w_out", I["gdn_w_out"], VD,
                      "xin", I["xin"], "h1", S["h1"])
            ffn_layer(k, cfg, I, O, S, dbuf, pf, pbf, CC, 0, "h1", S["h1"], "h2", S["h2"], norm_fm,
                      hook=(Unpager(k, cfg, I, S, dbuf) if cfg.stage >= 4 else None))
        if cfg.stage >= 3:
            nsa_proj(k, cfg, I, O, S, dbuf, pf, pbf, CC, norm_fm, "h2")
        if cfg.stage >= 4:
            nsa_attn(k, cfg, I, O, S, dbuf, pf, pbf, CC)
            linear_tm(k, cfg, pf, dbuf, "no", "oT2", S["oT2"], "nsa_w_out", I["nsa_w_out"], 2048,
                      "h2", S["h2"], "h3", S["h3"])
            ffn_layer(k, cfg, I, O, S, dbuf, pf, pbf, CC, 1, "h3", S["h3"], "y", O["y"], norm_fm)
            if cfg.dbg:
                k.dma("sp", O["dbg_h3"], S["h3"], reads=[dbuf["h3"]], writes=[dbuf["dbg_h3"]])
                k.dma("sp", O["dbg_oT2"], S["oT2"], reads=[dbuf["oT2"]], writes=[dbuf["dbg_oT2"]])
                k.dma("sp", O["dbg_gvec"], S["gvec"], reads=[dbuf["gvec"]], writes=[dbuf["dbg_gvec"]])
                k.dma("sp", O["dbg_qT"], S["qT"], reads=[dbuf["qT"]], writes=[dbuf["dbg_qT"]])
        if cfg.stage >= 2:
            if cfg.dbg:
                k.dma("sp", O["dbg_h1"], S["h1"], reads=[dbuf["h1"]], writes=[dbuf["dbg_h1"]])
                k.dma("sp", O["dbg_h2"], S["h2"], reads=[dbuf["h2"]], writes=[dbuf["dbg_h2"]])

        k.finish()
        print("inst", k.n_inst, "wait", k.n_wait)
    return nc


def gdn_layer(k, cfg, I, O, S, dbuf, xn, pf, pbf, C):
    D, KT, NT, TP, NS, TS = cfg.D, cfg.KT, cfg.NT, cfg.TP, cfg.NS, cfg.TS
    NSAMP, NVH, NKH, NB, NBP = cfg.NSAMP, cfg.NVH, cfg.NKH, cfg.NB, cfg.NBP
    CONVD, PROJ, VD, KD = cfg.CONVD, cfg.PROJ, cfg.VD, cfg.KD
    ident_f, ident_b, ones_f, ones_b = C["ident_f"], C["ident_b"], C["ones_f"], C["ones_b"]
    masks_p, masks_s, rowmask_s = C["masks_p"], C["masks_s"], C["rowmask_s"]
    NCT = CONVD // 128
    PADW = 3 + TP + NS * 11
    SOFF = 3 + TP
    blocks = cfg.blocks()
    chunks = cfg.chunks()

    cw = k.sb("g_cw", [128, NCT, 4], F32)
    k.dma("sp", cw[:], I["gdn_conv_w"].rearrange("(t p) w -> p t w", p=128), writes=[cw])
    alog = k.sb("g_alog", [64, NVH], F32)
    dtb = k.sb("g_dtb", [64, NVH], F32)
    k.dma("sp", alog[:], I["gdn_A_log"].partition_broadcast(64), writes=[alog])
    k.dma("sp", dtb[:], I["gdn_dt_bias"].partition_broadcast(64), writes=[dtb])
    gnorm = k.sb("g_norm", [128, 1], F32)
    k.dma("sp", gnorm[:], I["gdn_norm"].rearrange("o d -> d o"), writes=[gnorm])
    stc_r = Ring([k.sb(f"g_stc{i}", [NS * 3, 128], F32) for i in range(2)])
    negA = k.sb("g_negA", [64, NVH], F32)
    k.op("act", lambda e: e.activation(out=negA[:], in_=alog[:], func=AF.Exp), reads=[alog], writes=[negA])
    k.op("dve", lambda e: e.tensor_scalar(out=negA[:], in0=negA[:], scalar1=-1.0, scalar2=None, op0=ALU.mult),
         reads=[negA], writes=[negA])

    tm_scope = k.scope()
    tm_scope.__enter__()
    wbd = k.sb("g_wbd", [128, KT, 2 * NVH], BF16)
    k.dma("pool", wbd[:], I["gdn_w_in"][:, CONVD + VD:PROJ].rearrange("(kt p) c -> p kt c", p=128), writes=[wbd])
    NH2 = 2 * NVH
    bl_all = k.sb("g_bl", [64, NB, NH2], F32)
    k.op("pool", lambda e: e.memset(bl_all[:], 0.0), writes=[bl_all])
    per_bank = 512 // NH2
    for g0 in range(0, NB, per_bank):
        gb = blocks[g0:g0 + per_bank]
        ps = pf.next()
        for bi, (t0, L) in enumerate(gb):
            for kt in range(KT):
                k.op("pe", lambda e: e.matmul(ps[0:L, bi * NH2:(bi + 1) * NH2], lhsT=xn[:, kt, t0:t0 + L],
                                              rhs=wbd[:, kt, :], start=(kt == 0), stop=(kt == KT - 1)),
                     reads=[xn, wbd], writes=[ps], inc=(kt == KT - 1))
        nfull = sum(1 for (_, L) in gb if L == 64)
        if nfull:
            k.op("act", lambda e: e.copy(out=bl_all[:, g0:g0 + nfull, :],
                                         in_=ps[0:64, 0:nfull * NH2].rearrange("p (b c) -> p b c", c=NH2)),
                 reads=[ps], writes=[bl_all])
        if nfull < len(gb):
            bi = nfull
            k.op("act", lambda e: e.copy(out=bl_all[0:NSAMP, g0 + bi, :], in_=ps[0:NSAMP, bi * NH2:(bi + 1) * NH2]),
                 reads=[ps], writes=[bl_all])

    def tm(name):
        return k.sb(name, [64, NB, NVH], F32)
    beta_t, g_t, G_t, gl_t, c_t, kd_t, nb_t, tmp1, tmp2 = (tm("g_beta"), tm("g_g"), tm("g_G"), tm("g_gl"),
                                                          tm("g_c"), tm("g_kd"), tm("g_nb"), tm("g_t1"), tm("g_t2"))
    blv = bl_all[:, :, 0:NVH]
    alv = bl_all[:, :, NVH:NH2]
    dtb_b = dtb[:].unsqueeze(1).to_broadcast([64, NB, NVH])
    negA_b = negA[:].unsqueeze(1).to_broadcast([64, NB, NVH])
    k.op("act", lambda e: e.activation(out=beta_t[:], in_=blv, func=AF.Sigmoid), reads=[bl_all], writes=[beta_t])
    k.op("dve", lambda e: e.tensor_tensor(out=tmp1[:], in0=alv, in1=dtb_b, op=ALU.add), reads=[bl_all, dtb], writes=[tmp1])
    k.op("dve", lambda e: e.tensor_scalar(out=tmp2[:], in0=tmp1[:], scalar1=-1.0, scalar2=None, op0=ALU.mult),
         reads=[tmp1], writes=[tmp2])
    k.op("dve", lambda e: e.tensor_tensor(out=tmp2[:], in0=tmp2[:], in1=tmp1[:], op=ALU.min),
         reads=[tmp1, tmp2], writes=[tmp2])
    k.op("act", lambda e: e.activation(out=tmp2[:], in_=tmp2[:], func=AF.Exp), reads=[tmp2], writes=[tmp2])
    k.op("act", lambda e: e.activation(out=tmp2[:], in_=tmp2[:], func=AF.Ln, bias=1.0), reads=[tmp2], writes=[tmp2])
    k.op("dve", lambda e: e.scalar_tensor_tensor(out=tmp1[:], in0=tmp1[:], scalar=0.0, in1=tmp2[:],
                                                 op0=ALU.max, op1=ALU.add), reads=[tmp1, tmp2], writes=[tmp1])
    k.op("dve", lambda e: e.tensor_tensor(out=g_t[:], in0=tmp1[:], in1=negA_b, op=ALU.mult),
         reads=[tmp1, negA], writes=[g_t])
    per_bank = 512 // NVH
    for (dst, mi) in ((G_t, 0), (gl_t, 1)):
        for g0 in range(0, NB, per_bank):
            gb = blocks[g0:g0 + per_bank]
            ps = pf.next()
            for bi, (t0, L) in enumerate(gb):
                mk_ = masks_p if L == 64 else masks_s
                k.op("pe", lambda e: e.matmul(ps[0:L, bi * NVH:(bi + 1) * NVH], lhsT=mk_[0:L, mi, 0:L],
                                              rhs=g_t[0:L, g0 + bi, :], start=True, stop=True),
                     reads=[mk_, g_t], writes=[ps], inc=(bi == len(gb) - 1))
            k.op("dve", lambda e: e.tensor_copy(out=dst[:, g0:g0 + len(gb), :],
                                                in_=ps[0:64, 0:len(gb) * NVH].rearrange("p (b c) -> p b c", c=NVH)),
                 reads=[ps], writes=[dst])
    k.op("act", lambda e: e.activation(out=tmp1[:], in_=G_t[:], func=AF.Exp), reads=[G_t], writes=[tmp1])
    k.op("dve", lambda e: e.scalar_tensor_tensor(out=c_t[:], in0=tmp1[:], scalar=-1.0, in1=beta_t[:],
                                                 op0=ALU.mult, op1=ALU.mult), reads=[tmp1, beta_t], writes=[c_t])
    k.op("dve", lambda e: e.tensor_tensor(out=tmp2[:], in0=gl_t[:], in1=G_t[:], op=ALU.subtract),
         reads=[gl_t, G_t], writes=[tmp2])
    k.op("act", lambda e: e.activation(out=kd_t[:], in_=tmp2[:], func=AF.Exp), reads=[tmp2], writes=[kd_t])
    k.op("dve", lambda e: e.tensor_scalar(out=nb_t[:], in0=beta_t[:], scalar1=-1.0, scalar2=None, op0=ALU.mult),
         reads=[beta_t], writes=[nb_t])

    tmH = k.sb("g_tmH", [64, 5, NVH, NB], F32)
    for idx, src_t in enumerate((beta_t, G_t, c_t, kd_t, nb_t)):
        k.op("pool", lambda e: e.tensor_copy(out=tmH[:, idx].rearrange("p h b -> p b h"), in_=src_t[:]),
             reads=[src_t], writes=[tmH])
    k.dma("sp", S["tmS"], tmH[:].rearrange("p k h b -> p (k h b)"), reads=[tmH], writes=[dbuf["tmS"]])
    tm_scope.__exit__(None, None, None)
    hd_r = Ring([k.sb(f"g_hd{i}", [64, 5, NB], F32) for i in range(2)])

    def load_hd(hv):
        hd = hd_r.next()
        k.dma("sp", hd[:], S["tmS"].rearrange("p (k h b) -> p k h b", k=5, h=NVH)[:, :, hv, :],
              reads=[dbuf["tmS"]], writes=[hd])
        return hd

    wt_r = Ring([k.sb(f"g_wt{i}", [128, KT, 128], BF16) for i in range(2)])
    prep = k.sb("g_prep", [128, PADW], F32)
    k.op("pool", lambda e: e.memset(prep[:], 0.0), writes=[prep])
    cq = k.sb("g_cq", [128, NT], F32)
    sqb = k.sb("g_sqb", [128, 512], BF16)
    rinv = k.sb("g_rinv", [128, 512], F32)
    epsb = k.sb("g_epsb", [128, 1], F32)
    k.op("pool", lambda e: e.memset(epsb[:], RMS_EPS), writes=[epsb])
    kqfm = k.sb("g_kqfm", [128, NB, 128], BF16)
    k.op("pool", lambda e: e.memset(kqfm[:], 0.0), writes=[kqfm])
    k_tok = k.sb("g_ktok", [64, NB, 128], BF16)
    vs = k.sb("g_vs", [128, NT], BF16)
    zsil = [k.sb("g_zsil", [128, NT], BF16)] * 2
    vb_tok = [k.sb("g_vb", [64, NB, 128], BF16)] * 2
    convout = k.sb("g_convout", [128, NCT, 16], F32)
    k.op("pool", lambda e: e.memset(convout[:], 0.0), writes=[convout])

    def _load_w_now(col0):
        wt = wt_r.next()
        k.dma("pool", wt[:], I["gdn_w_in"][:, col0:col0 + 128].rearrange("(kt p) c -> p kt c", p=128),
              reads=[dbuf["gdn_w_in"]], writes=[wt])
        return wt

    col_order = []
    for j_ in range(NKH):
        col_order += [j_ * 128, KD + j_ * 128]
        for a_ in range(2):
            hv_ = 2 * j_ + a_
            col_order += [2 * KD + hv_ * 128, CONVD + hv_ * 128]
    pre = {"i": 0, "wt": None}

    def load_w(col0):
        i = pre["i"]
        assert col_order[i] == col0, (i, col0, col_order[i])
        wt = pre["wt"] if pre["wt"] is not None else _load_w_now(col0)
        pre["i"] = i + 1
        pre["wt"] = _load_w_now(col_order[i + 1]) if i + 1 < len(col_order) else None
        return wt

    def proj(wt, consumer):
        for ci, (t0, n) in enumerate(chunks):
            ps = pf.next()
            for kt in range(KT):
                k.op("pe", lambda e: e.matmul(ps[:, 0:n], lhsT=wt[:, kt, :], rhs=xn[:, kt, t0:t0 + n],
                                              start=(kt == 0), stop=(kt == KT - 1)),
                     reads=[wt, xn], writes=[ps], inc=(kt == KT - 1))
            consumer(ci, t0, n, ps)

    def samp_view(buf_ap_2d, w, lo, hi):
        return buf_ap_2d.rearrange("p (s w) -> p s w", w=w)[:, :, lo:hi]

    def proj_conv_silu(col0, dst, wt=None):
        ct = col0 // 128
        if wt is None:
            wt = load_w(col0)

        def cons(ci, t0, n, ps):
            eng = k.ev_eng()
            if n == 512:
                dv = prep[:, 3 + t0:3 + t0 + n]
                sv = ps[:, 0:n]
            else:
                dv = samp_view(prep[:, SOFF:SOFF + NS * 11], 11, 3, 11)
                sv = ps[:, 0:n].rearrange("p (s w) -> p s w", w=TS)
            if eng == "act":
                k.op("act", lambda e: e.copy(out=dv, in_=sv), reads=[ps], writes=[prep])
            else:
                k.op("dve", lambda e: e.tensor_copy(out=dv, in_=sv), reads=[ps], writes=[prep])
        pst = pf.next()
        stc = stc_r.next()
        k.dma("sp", stc[:], I["st_gdn_conv"][:, ct * 128:(ct + 1) * 128], writes=[stc])
        k.op("pe", lambda e: e.transpose(out=pst[:, 0:NS * 3], in_=stc[0:NS * 3, 0:128],
                                         identity=ident_f[0:NS * 3, 0:NS * 3]),
             reads=[stc, ident_f], writes=[pst])
        k.op("dve", lambda e: e.tensor_copy(out=samp_view(prep[:, SOFF:SOFF + NS * 11], 11, 0, 3),
                                            in_=pst[:, 0:NS * 3].rearrange("p (s w) -> p s w", w=3)),
             reads=[pst], writes=[prep])
        proj(wt, cons)
        k.op("pool", lambda e: e.tensor_copy(out=convout[:, ct, 0:3], in_=prep[:, 3 + TP - 3:3 + TP]),
             reads=[prep], writes=[convout])
        k.op("pool", lambda e: e.tensor_copy(out=convout[:, ct, 3:3 + NS * 3].rearrange("p (s w) -> p s w", w=3),
                                             in_=samp_view(prep[:, SOFF:SOFF + NS * 11], 11, 8, 11)),
             reads=[prep], writes=[convout])
        for (ov, mkv) in ((cq[:, 0:TP], lambda i: prep[:, i:i + TP]),
                          (cq[:, TP:NT].rearrange("p (s w) -> p s w", w=TS),
                           lambda i: samp_view(prep[:, SOFF:SOFF + NS * 11], 11, i, i + TS))):
            k.op("dve", lambda e: e.tensor_scalar(out=ov, in0=mkv(3), scalar1=cw[:, ct, 3:4], scalar2=None, op0=ALU.mult),
                 reads=[prep, cw], writes=[cq])
            for i in range(3):
                k.op("dve", lambda e: e.scalar_tensor_tensor(out=ov, in0=mkv(i), scalar=cw[:, ct, i:i + 1], in1=ov,
                                                             op0=ALU.mult, op1=ALU.add),
                     reads=[prep, cw, cq], writes=[cq])
        k.op("act", lambda e: e.activation(out=dst[:], in_=cq[:], func=AF.Silu), reads=[cq], writes=[dst])

    def l2norm_to(src, which, scale):
        for ci, (t0, n) in enumerate(chunks):
            k.op("act", lambda e: e.activation(out=sqb[:, 0:n], in_=src[:, t0:t0 + n], func=AF.Square), reads=[src], writes=[sqb])
            ps = pf.next()
            k.op("pe", lambda e: e.matmul(ps[:, 0:n], lhsT=ones_b[:, :], rhs=sqb[:, 0:n], start=True, stop=True),
                 reads=[ones_b, sqb], writes=[ps])
            k.op("act", lambda e: e.activation(out=rinv[:, 0:n], in_=ps[:, 0:n], func=AF.Sqrt, bias=epsb[:, 0:1]),
                 reads=[ps, epsb], writes=[rinv])
            k.op("dve", lambda e: e.reciprocal(out=rinv[:, 0:n], in_=rinv[:, 0:n]), reads=[rinv], writes=[rinv])
            if n == 512:
                b0 = t0 // 64
                dv = kqfm[:, b0:b0 + 8, which * 64:which * 64 + 64]
                sv = src[:, t0:t0 + n].rearrange("p (b i) -> p b i", i=64)
                rv = rinv[:, 0:n].rearrange("p (b i) -> p b i", i=64)
            else:
                dv = kqfm[:, NBP, which * 64:which * 64 + n]
                sv = src[:, t0:t0 + n]
                rv = rinv[:, 0:n]
            k.op("dve", lambda e: e.scalar_tensor_tensor(out=dv, in0=sv, scalar=scale, in1=rv, op0=ALU.mult, op1=ALU.mult),
                 reads=[src, rinv], writes=[kqfm])

    def to_tok(src_fn, evac):
        for g0 in range(0, NB, 8):
            gb = blocks[g0:g0 + 8]
            pb = pbf.next()
            for bi, (t0, L) in enumerate(gb):
                k.op("pe", lambda e: e.transpose(out=pb[0:L, bi * 128:(bi + 1) * 128], in_=src_fn(g0 + bi, t0, L),
                                                 identity=ident_b[:, :]),
                     reads=src_fn.reads + [ident_b], writes=[pb], inc=(bi == len(gb) - 1))
            evac(g0, gb, pb)

    NG = 4
    dgG = k.sb("g_dgG", [64, NG, 64], F32)
    dgB = k.sb("g_dgB", [64, NG, 64], F32)
    t1 = k.sb("g_t1b", [64, NG, 64], F32)
    tI = k.sb("g_tI", [64, NG, 64], F32)
    tS = k.sb("g_tS", [64, NG, 64], F32)
    tL = k.sb("g_tL", [64, NG, 64], F32)
    kkE = k.sb("g_kkE", [64, NG, 64], F32)
    kkL = k.sb("g_kkL", [64, NG, 64], F32)
    eGbc = k.sb("g_eGbc", [128, NG * 64], F32)
    PR = [k.sb(f"g_PR{i}", [64, NG, 128], BF16) for i in range(2)]
    PT = [k.sb(f"g_PT{i}", [64, NG, 64], BF16) for i in range(2)]
    slots = [(dgG, dgB, t1, tI, tS, tL, kkE, kkL, eGbc, PR, PT)]
    pBs_slots = [k.sb(f"g_pBs{i}", [64, NG, 64], F32) for i in range(3)]
    for si in range(1, 3):
        slots.append(tuple([k.sb(f"g{si}_{nm}", [64, NG, 64], F32) for nm in ("dgG", "dgB", "t1", "tI", "tS", "tL", "kkE", "kkL")]
                           + [k.sb(f"g{si}_eGbc", [128, NG * 64], F32),
                              [k.sb(f"g{si}_PR{i}", [64, NG, 128], BF16) for i in range(2)],
                              [k.sb(f"g{si}_PT{i}", [64, NG, 64], BF16) for i in range(2)]]))
    heads = []
    for a in range(1):
        h = dict(
            AQT=k.sb(f"g_AQT{a}", [64, NB, 64], BF16), TT=k.sb(f"g_TT{a}", [64, NB, 64], BF16),
            qdec=k.sb(f"g_qdec{a}", [128, NB, 64], BF16), kdec=k.sb(f"g_kdec{a}", [64, 2, 128], BF16),
            egl=k.sb(f"g_egl{a}", [128, NB, 4], F32),
            S=k.sb(f"g_S{a}", [128, 128], F32), Sb=k.sb(f"g_Sb{a}", [128, 128], BF16),
            Ss=[k.sb(f"g_Ss{a}_{s}", [128, 128], F32) for s in range(NS)],
            Ssb=[k.sb(f"g_Ssb{a}_{s}", [128, 128], BF16) for s in range(NS)],
            r=k.sb(f"g_r{a}", [64, 128], BF16), u=k.sb(f"g_u{a}", [64, 128], BF16),
            otok=k.sb(f"g_otok{a}", [64, 8, 128], F32), osq=k.sb(f"g_osq{a}", [64, 8, 128], F32),
            oss=k.sb(f"g_oss{a}", [64, 16], F32), on=k.sb(f"g_on{a}", [64, 8, 128], BF16),
            oT=k.sb(f"g_oT{a}", [128, 512], BF16),
            kzp=k.sb(f"g_kzp{a}", [128, NS, NSAMP], BF16), qzp=k.sb(f"g_qzp{a}", [128, NS, NSAMP], BF16),
            kds=k.sb(f"g_kds{a}", [64, NS, 128], BF16),
        )
        heads.append(h)
    heads.append(heads[0])
    HD = {}

    def grp_gen(hv, hd, g0, gb, slot):
        H = heads[0]
        dgG, dgB, t1, tI, tS, tL, kkE, kkL, eGbc, PR, PT = slot
        pBs = pBs_slots[slots.index(slot)]
        if True:
            nb = len(gb)
            L = gb[0][1]
            mk_ = masks_p if L == 64 else masks_s
            idb = ident_f[0:L, 0:64].unsqueeze(1).to_broadcast([L, nb, 64])
            k.op("dve", lambda e: e.tensor_tensor(out=dgG[0:L, 0:nb, :], in0=idb,
                                                  in1=hd[0:L, 1, g0:g0 + nb].unsqueeze(2).to_broadcast([L, nb, 64]),
                                                  op=ALU.mult), reads=[ident_f, hd], writes=[dgG])
            k.op("dve", lambda e: e.tensor_tensor(out=dgB[0:L, 0:nb, :], in0=idb,
                                                  in1=hd[0:L, 0, g0:g0 + nb].unsqueeze(2).to_broadcast([L, nb, 64]),
                                                  op=ALU.mult), reads=[ident_f, hd], writes=[dgB])
            pG = pf.next()
            k.op("pe", lambda e: e.matmul(pG[:, 0:nb * 64], lhsT=ones_f[0:L, :],
                                          rhs=dgG[0:L, 0:nb, :].rearrange("p b i -> p (b i)"), start=True, stop=True),
                 reads=[ones_f, dgG], writes=[pG])
            pB = pf.next()
            k.op("pe", lambda e: e.matmul(pB[0:64, 0:nb * 64], lhsT=ones_f[0:L, 0:64],
                                          rhs=dgB[0:L, 0:nb, :].rearrange("p b i -> p (b i)"), start=True, stop=True),
                 reads=[ones_f, dgB], writes=[pB])
            pGv = pG[0:L, 0:nb * 64].rearrange("p (b i) -> p b i", i=64)
            pBv = pB[0:L, 0:nb * 64].rearrange("p (b i) -> p b i", i=64)
            yield
            k.op("dve", lambda e: e.tensor_tensor(out=t1[0:L, 0:nb, :], in0=pGv,
                                                  in1=hd[0:L, 1, g0:g0 + nb].unsqueeze(2).to_broadcast([L, nb, 64]),
                                                  op=ALU.subtract), reads=[pG, hd], writes=[t1])
            k.op("dve", lambda e: e.tensor_copy(out=pBs[0:L, 0:nb, :], in_=pBv), reads=[pB], writes=[pBs])
            k.op("act", lambda e: e.activation(out=eGbc[:, 0:nb * 64], in_=pG[:, 0:nb * 64], func=AF.Exp),
                 reads=[pG], writes=[eGbc])
            for (dst, mi) in ((tI, 2), (tS, 3), (tL, 4)):
                k.op("pool", lambda e: e.tensor_tensor(out=dst[0:L, 0:nb, :], in0=t1[0:L, 0:nb, :],
                                                       in1=mk_[0:L, mi, :].unsqueeze(1).to_broadcast([L, nb, 64]),
                                                       op=ALU.add), reads=[t1, mk_], writes=[dst])
            yield
            k.op("act", lambda e: e.activation(out=tI[0:L, 0:nb, :], in_=tI[0:L, 0:nb, :], func=AF.Exp), reads=[tI], writes=[tI])
            k.op("act", lambda e: e.activation(out=tS[0:L, 0:nb, :], in_=tS[0:L, 0:nb, :], func=AF.Exp), reads=[tS], writes=[tS])
            k.op("act", lambda e: e.activation(out=tL[0:L, 0:nb, :], in_=tL[0:L, 0:nb, :], func=AF.Exp, scale=-1.0),
                 reads=[tL], writes=[tL])
            yield
            pK = pf.next()
            for bi, (t0, Lb) in enumerate(gb):
                k.op("pe", lambda e: e.matmul(pK[0:L, bi * 128:(bi + 1) * 128], lhsT=kqfm[:, g0 + bi, 0:L],
                                              rhs=kqfm[:, g0 + bi, :], start=True, stop=True),
                     reads=[kqfm], writes=[pK], inc=(bi == nb - 1))
            pKv = pK[0:L, 0:nb * 128].rearrange("p (b c) -> p b c", c=128)
            k.op("dve", lambda e: e.tensor_tensor(out=H["AQT"][0:L, g0:g0 + nb, 0:L], in0=pKv[:, :, 64:64 + L],
                                                  in1=tI[0:L, 0:nb, 0:L], op=ALU.mult),
                 reads=[pK, tI], writes=[H["AQT"]])
            k.op("dve", lambda e: e.tensor_tensor(out=kkE[0:L, 0:nb, 0:L], in0=pKv[:, :, 0:L], in1=tS[0:L, 0:nb, 0:L],
                                                  op=ALU.mult), reads=[pK, tS], writes=[kkE])
            k.op("dve", lambda e: e.tensor_tensor(out=kkL[0:L, 0:nb, 0:L], in0=pKv[:, :, 0:L], in1=tL[0:L, 0:nb, 0:L],
                                                  op=ALU.mult), reads=[pK, tL], writes=[kkL])
            pr, pt = PR[0], PT[0]
            yield
            k.op("dve", lambda e: e.scalar_tensor_tensor(out=pr[0:L, 0:nb, 0:L], in0=kkE[0:L, 0:nb, 0:L], scalar=-1.0,
                                                         in1=pBs[0:L, 0:nb, 0:L], op0=ALU.mult, op1=ALU.mult),
                 reads=[kkE, pBs], writes=[pr])
            k.op("pool", lambda e: e.tensor_copy(out=pr[0:L, 0:nb, 64:64 + L],
                                                 in_=ident_f[0:L, 0:L].unsqueeze(1).to_broadcast([L, nb, L])),
                 reads=[ident_f], writes=[pr])
            k.op("dve", lambda e: e.tensor_tensor(out=pt[0:L, 0:nb, 0:L], in0=kkL[0:L, 0:nb, 0:L],
                                                  in1=hd[0:L, 4, g0:g0 + nb].unsqueeze(2).to_broadcast([L, nb, L]),
                                                  op=ALU.mult), reads=[kkL, hd], writes=[pt])
            for s in range(6):
                yield
                pr, pt = PR[s % 2], PT[s % 2]
                prn, ptn = PR[(s + 1) % 2], PT[(s + 1) % 2]
                p1 = pf.next()
                last = (s == 5)
                for bi in range(nb):
                    k.op("pe", lambda e: e.matmul(p1[0:L, bi * 128:bi * 128 + 64 + L], lhsT=pt[0:L, bi, 0:L],
                                                  rhs=pr[0:L, bi, 0:64 + L], start=True, stop=True),
                         reads=[pt, pr], writes=[p1], inc=(bi == nb - 1))
                p1v = p1[0:L, 0:nb * 128].rearrange("p (b c) -> p b c", c=128)
                if not last:
                    p2 = pf.next()
                    for bi in range(nb):
                        k.op("pe", lambda e: e.matmul(p2[0:L, bi * 64:bi * 64 + L], lhsT=pr[0:L, bi, 0:L],
                                                      rhs=pt[0:L, bi, 0:L], start=True, stop=True),
                             reads=[pt, pr], writes=[p2], inc=(bi == nb - 1))
                    p2v = p2[0:L, 0:nb * 64].rearrange("p (b c) -> p b c", c=64)
                    k.op("act", lambda e: e.copy(out=prn[0:L, 0:nb, 0:L], in_=p1v[:, :, 0:L]), reads=[p1], writes=[prn])
                    k.op("act", lambda e: e.copy(out=ptn[0:L, 0:nb, 0:L], in_=p2v[:, :, 0:L]), reads=[p2], writes=[ptn])
                    k.op("dve", lambda e: e.tensor_tensor(out=prn[0:L, 0:nb, 64:64 + L], in0=p1v[:, :, 64:64 + L],
                                                          in1=pr[0:L, 0:nb, 64:64 + L], op=ALU.add),
                         reads=[p1, pr], writes=[prn])
                else:
                    k.op("dve", lambda e: e.tensor_tensor(out=H["TT"][0:L, g0:g0 + nb, 0:L], in0=p1v[:, :, 64:64 + L],
                                                          in1=pr[0:L, 0:nb, 64:64 + L], op=ALU.add),
                         reads=[p1, pr], writes=[H["TT"]])
            yield
            eGv = eGbc[:, 0:nb * 64].rearrange("p (b i) -> p b i", i=64)
            k.op("pool", lambda e: e.tensor_tensor(out=H["qdec"][:, g0:g0 + nb, 0:L], in0=kqfm[:, g0:g0 + nb, 64:64 + L],
                                                   in1=eGv[:, :, 0:L], op=ALU.mult),
                 reads=[kqfm, eGbc], writes=[H["qdec"]])
            if L == 64:
                k.op("pool", lambda e: e.tensor_copy(out=H["egl"][:, g0:g0 + nb, 0:1], in_=eGv[:, :, 63:64]),
                     reads=[eGbc], writes=[H["egl"]])
            else:
                k.op("pool", lambda e: e.tensor_copy(out=H["egl"][:, g0, 0:NS],
                                                     in_=eGbc[:, 0:NSAMP].rearrange("p (s w) -> p s w", w=TS)[:, :, TS - 1]),
                     reads=[eGbc], writes=[H["egl"]])

    def precompute(hv, hd):
        groups = [(g0, blocks[g0:g0 + NG]) for g0 in range(0, NBP, NG)] + [(NBP, [blocks[NBP]])]
        NSL = len(slots)
        for i0 in range(0, len(groups), NSL):
            alive = [grp_gen(hv, hd, g0, gb, slots[si]) for si, (g0, gb) in enumerate(groups[i0:i0 + NSL])]
            while alive:
                for gen in list(alive):
                    try:
                        next(gen)
                    except StopIteration:
                        alive.remove(gen)

    def o_finish(a, hv, t0, nblk, L):
        H = heads[a]
        k.op("act", lambda e: e.activation(out=H["osq"][0:L, 0:nblk, :], in_=H["otok"][0:L, 0:nblk, :], func=AF.Square),
             reads=[H["otok"]], writes=[H["osq"]])
        k.op("dve", lambda e: e.reduce_sum(out=H["oss"][0:L, 0:nblk], in_=H["osq"][0:L, 0:nblk, :], axis=AX.X),
             reads=[H["osq"]], writes=[H["oss"]])
        k.op("dve", lambda e: e.tensor_scalar(out=H["oss"][0:L, 8:8 + nblk], in0=H["oss"][0:L, 0:nblk], scalar1=1.0 / 128,
                                              scalar2=RMS_EPS, op0=ALU.mult, op1=ALU.add), reads=[H["oss"]], writes=[H["oss"]])
        k.op("act", lambda e: e.activation(out=H["oss"][0:L, 0:nblk], in_=H["oss"][0:L, 8:8 + nblk], func=AF.Sqrt),
             reads=[H["oss"]], writes=[H["oss"]])
        k.op("dve", lambda e: e.reciprocal(out=H["oss"][0:L, 0:nblk], in_=H["oss"][0:L, 0:nblk]),
             reads=[H["oss"]], writes=[H["oss"]])
        k.op("dve", lambda e: e.tensor_tensor(out=H["on"][0:L, 0:nblk, :], in0=H["otok"][0:L, 0:nblk, :],
                                              in1=H["oss"][0:L, 0:nblk].unsqueeze(2).to_broadcast([L, nblk, 128]),
                                              op=ALU.mult), reads=[H["otok"], H["oss"]], writes=[H["on"]])
        pb = pbf.next()
        for bi in range(nblk):
            k.op("pe", lambda e: e.transpose(out=pb[:, bi * L:(bi + 1) * L], in_=H["on"][0:L, bi, :],
                                             identity=ident_b[0:L, 0:L]),
                 reads=[H["on"], ident_b], writes=[pb], inc=(bi == nblk - 1))
        n = nblk * L
        k.op("dve", lambda e: e.scalar_tensor_tensor(out=H["oT"][:, 0:n], in0=pb[:, 0:n], scalar=gnorm[:, 0:1],
                                                     in1=zsil[a][:, t0:t0 + n], op0=ALU.mult, op1=ALU.mult),
             reads=[pb, gnorm, zsil[a]], writes=[H["oT"]])
        k.dma("sp", S["oT"][hv * 128:(hv + 1) * 128, t0:t0 + n], H["oT"][:, 0:n], reads=[H["oT"]], writes=[dbuf["oT"]])

    kd_r = Ring([0, 1])

    def mk_kdec(H, hd, b, L):
        slot = kd_r.next()
        k.op("pool", lambda e: e.tensor_scalar(out=H["kdec"][0:L, slot, :], in0=k_tok[0:L, b, :],
                                               scalar1=hd[0:L, 3, b:b + 1], scalar2=None, op0=ALU.mult),
             reads=[k_tok, hd], writes=[H["kdec"]])
        return slot

    def scan(hv, hd):
        a = 0
        H = heads[0]
        k.op("pool", lambda e: e.memset(H["S"][:], 0.0), writes=[H["S"]])
        k.op("pool", lambda e: e.memset(H["Sb"][:], 0.0), writes=[H["Sb"]])
        for b in range(NBP):
            kslot = mk_kdec(H, hd, b, 64)
            pA = pf.next()
            k.op("pe", lambda e: e.matmul(pA[0:64, 0:128], lhsT=kqfm[:, b, 0:64], rhs=H["Sb"][:, :], start=True, stop=True),
                 reads=[kqfm, H["Sb"]], writes=[pA])
            k.op("dve", lambda e: e.scalar_tensor_tensor(out=H["r"][:, :], in0=pA[0:64, 0:128], scalar=hd[:, 2, b:b + 1],
                                                         in1=vb_tok[a][:, b, :], op0=ALU.mult, op1=ALU.add),
                 reads=[pA, hd, vb_tok[a]], writes=[H["r"]])
            pB_ = pf.next()
            k.op("pe", lambda e: e.matmul(pB_[0:64, 0:128], lhsT=H["TT"][:, b, :], rhs=H["r"][:, :], start=True, stop=True),
                 reads=[H["TT"], H["r"]], writes=[pB_])
            k.op("act", lambda e: e.copy(out=H["u"][:, :], in_=pB_[0:64, 0:128]), reads=[pB_], writes=[H["u"]])
            pO = pf.next()
            k.op("pe", lambda e: e.matmul(pO[0:64, 0:128], lhsT=H["qdec"][:, b, :], rhs=H["Sb"][:, :], start=True, stop=False),
                 reads=[H["qdec"], H["Sb"]], writes=[pO], inc=False)
            k.op("pe", lambda e: e.matmul(pO[0:64, 0:128], lhsT=H["AQT"][:, b, :], rhs=H["u"][:, :], start=False, stop=True),
                 reads=[H["AQT"], H["u"]], writes=[pO])
            pS = pf.next()
            k.op("pe", lambda e: e.matmul(pS[:, 0:128], lhsT=H["kdec"][:, kslot, :], rhs=H["u"][:, :], start=True, stop=True),
                 reads=[H["kdec"], H["u"]], writes=[pS])
            k.op("dve", lambda e: e.scalar_tensor_tensor(out=H["Sb"][:, :], in0=H["S"][:, :], scalar=H["egl"][:, b, 0:1],
                                                         in1=pS[:, 0:128], op0=ALU.mult, op1=ALU.add),
                 reads=[H["S"], H["egl"], pS], writes=[H["Sb"]])
            k.op("dve", lambda e: e.scalar_tensor_tensor(out=H["S"][:, :], in0=H["S"][:, :], scalar=H["egl"][:, b, 0:1],
                                                         in1=pS[:, 0:128], op0=ALU.mult, op1=ALU.add),
                 reads=[H["S"], H["egl"], pS], writes=[H["S"]])
            k.op("act", lambda e: e.copy(out=H["otok"][:, b % 8, :], in_=pO[0:64, 0:128]), reads=[pO], writes=[H["otok"]])
            if b % 8 == 7:
                o_finish(a, hv, (b - 7) * 64, 8, 64)
        k.dma("sp", O["o_gdn_p"][hv], H["S"][:, :], reads=[H["S"]], writes=[dbuf["o_gdn_p"]])
        L = NSAMP
        b = NBP
        kslot = mk_kdec(H, hd, b, L)
        k.op("pool", lambda e: e.memset(H["kzp"][:], 0.0), writes=[H["kzp"]])
        k.op("pool", lambda e: e.memset(H["qzp"][:], 0.0), writes=[H["qzp"]])
        for s in range(NS):
            k.dma("sp", H["Ss"][s][:, :], I["st_gdn"][s, hv], writes=[H["Ss"][s]])
            k.op("act", lambda e: e.copy(out=H["Ssb"][s][:, :], in_=H["Ss"][s][:, :]), reads=[H["Ss"][s]], writes=[H["Ssb"][s]])
            k.op("pool", lambda e: e.tensor_copy(out=H["kzp"][:, s, s * TS:(s + 1) * TS], in_=kqfm[:, b, s * TS:(s + 1) * TS]),
                 reads=[kqfm], writes=[H["kzp"]])
            k.op("pool", lambda e: e.tensor_copy(out=H["qzp"][:, s, s * TS:(s + 1) * TS], in_=H["qdec"][:, b, s * TS:(s + 1) * TS]),
                 reads=[H["qdec"]], writes=[H["qzp"]])
            k.op("dve", lambda e: e.tensor_scalar(out=H["kds"][0:L, s, :], in0=H["kdec"][0:L, kslot, :],
                                                  scalar1=rowmask_s[0:L, s:s + 1], scalar2=None, op0=ALU.mult),
                 reads=[H["kdec"], rowmask_s], writes=[H["kds"]])
        pA = pf.next()
        for s in range(NS):
            k.op("pe", lambda e: e.matmul(pA[0:L, 0:128], lhsT=H["kzp"][:, s, :], rhs=H["Ssb"][s][:, :],
                                          start=(s == 0), stop=(s == NS - 1)),
                 reads=[H["kzp"], H["Ssb"][s]], writes=[pA], inc=(s == NS - 1))
        k.op("dve", lambda e: e.scalar_tensor_tensor(out=H["r"][0:L, :], in0=pA[0:L, 0:128], scalar=hd[0:L, 2, b:b + 1],
                                                     in1=vb_tok[a][0:L, b, :], op0=ALU.mult, op1=ALU.add),
             reads=[pA, hd, vb_tok[a]], writes=[H["r"]])
        pB_ = pf.next()
        k.op("pe", lambda e: e.matmul(pB_[0:L, 0:128], lhsT=H["TT"][0:L, b, 0:L], rhs=H["r"][0:L, :], start=True, stop=True),
             reads=[H["TT"], H["r"]], writes=[pB_])
        k.op("act", lambda e: e.copy(out=H["u"][0:L, :], in_=pB_[0:L, 0:128]), reads=[pB_], writes=[H["u"]])
        pO = pf.next()
        for s in range(NS):
            k.op("pe", lambda e: e.matmul(pO[0:L, 0:128], lhsT=H["qzp"][:, s, :], rhs=H["Ssb"][s][:, :],
                                          start=(s == 0), stop=False),
                 reads=[H["qzp"], H["Ssb"][s]], writes=[pO], inc=False)
        k.op("pe", lambda e: e.matmul(pO[0:L, 0:128], lhsT=H["AQT"][0:L, b, 0:L], rhs=H["u"][0:L, :], start=False, stop=True),
             reads=[H["AQT"], H["u"]], writes=[pO])
        k.op("act", lambda e: e.copy(out=H["otok"][0:L, 0, :], in_=pO[0:L, 0:128]), reads=[pO], writes=[H["otok"]])
        for s in range(NS):
            pS = pf.next()
            k.op("pe", lambda e: e.matmul(pS[:, 0:128], lhsT=H["kds"][0:L, s, :], rhs=H["u"][0:L, :], start=True, stop=True),
                 reads=[H["kds"], H["u"]], writes=[pS])
            k.op("dve", lambda e: e.scalar_tensor_tensor(out=H["Ss"][s][:, :], in0=H["Ss"][s][:, :], scalar=H["egl"][:, b, s:s + 1],
                                                         in1=pS[:, 0:128], op0=ALU.mult, op1=ALU.add),
                 reads=[H["Ss"][s], H["egl"], pS], writes=[H["Ss"][s]])
            k.dma("sp", O["o_gdn_s"][s, hv], H["Ss"][s][:, :], reads=[H["Ss"][s]], writes=[dbuf["o_gdn_s"]])
        o_finish(a, hv, TP, 1, L)

    hd_next = load_hd(0)
    for j in range(NKH):
        proj_conv_silu(j * 128, cq)
        l2norm_to(cq, 1, 128 ** -0.5)
        proj_conv_silu(KD + j * 128, cq)
        l2norm_to(cq, 0, 1.0)

        def ksrc(bidx, t0, L):
            return kqfm[:, bidx, 0:L]
        ksrc.reads = [kqfm]

        def kevac(g0, gb, pb):
            nfull = sum(1 for (_, L) in gb if L == 64)
            if nfull:
                k.op("act", lambda e: e.copy(out=k_tok[:, g0:g0 + nfull, :],
                                             in_=pb[0:64, 0:nfull * 128].rearrange("p (b c) -> p b c", c=128)),
                     reads=[pb], writes=[k_tok])
            if nfull < len(gb):
                k.op("act", lambda e: e.copy(out=k_tok[0:NSAMP, g0 + nfull, :], in_=pb[0:NSAMP, nfull * 128:(nfull + 1) * 128]),
                     reads=[pb], writes=[k_tok])
        to_tok(ksrc, kevac)
        for a in range(2):
            hv = 2 * j + a
            hd = hd_next
            if hv + 1 < NVH:
                hd_next = load_hd(hv + 1)
            proj_conv_silu(2 * KD + hv * 128, vs)

            def vsrc(bidx, t0, L):
                return vs[:, t0:t0 + L]
            vsrc.reads = [vs]

            def vevac(g0, gb, pb, hd=hd):
                nfull = sum(1 for (_, L) in gb if L == 64)
                if nfull:
                    k.op("dve", lambda e: e.tensor_tensor(
                        out=vb_tok[0][:, g0:g0 + nfull, :],
                        in0=pb[0:64, 0:nfull * 128].rearrange("p (b c) -> p b c", c=128),
                        in1=hd[:, 0, g0:g0 + nfull].unsqueeze(2).to_broadcast([64, nfull, 128]), op=ALU.mult),
                        reads=[pb, hd], writes=[vb_tok[0]])
                if nfull < len(gb):
                    k.op("dve", lambda e: e.tensor_scalar(
                        out=vb_tok[0][0:NSAMP, g0 + nfull, :], in0=pb[0:NSAMP, nfull * 128:(nfull + 1) * 128],
                        scalar1=hd[0:NSAMP, 0, g0 + nfull:g0 + nfull + 1], scalar2=None, op0=ALU.mult),
                        reads=[pb, hd], writes=[vb_tok[0]])
            to_tok(vsrc, vevac)
            wt = load_w(CONVD + hv * 128)

            def zcons(ci, t0, n, ps):
                k.op("act", lambda e: e.activation(out=zsil[0][:, t0:t0 + n], in_=ps[:, 0:n], func=AF.Silu),
                     reads=[ps], writes=[zsil[0]])
            proj(wt, zcons)
            precompute(hv, hd)
            scan(hv, hd)

    nrow = (1 + NS) * 3
    class _CstView:
        def __init__(self, off):
            self.off = off

        def __getitem__(self, idx):
            return prep[0:16, self.off:self.off + 512]
    for g0 in range(0, NCT, 4):
        ps = pf.next()
        coff = ((g0 // 4) % 2) * 512
        for q in range(4):
            ct = g0 + q
            k.op("pe", lambda e: e.transpose(out=ps[0:16, q * 128:(q + 1) * 128], in_=convout[:, ct, :],
                                             identity=ident_f[:, :]),
                 reads=[convout, ident_f], writes=[ps], inc=(q == 3))
        k.op("dve", lambda e: e.tensor_copy(out=prep[0:16, coff:coff + 512], in_=ps[0:16, 0:512]), reads=[ps], writes=[prep])
        k.dma("sp", O["o_gdnc"][:, g0 * 128:(g0 + 4) * 128], prep[0:nrow, coff:coff + 512], reads=[prep], writes=[dbuf["o_gdnc"]])
    if cfg.dbg:
        k.dma("sp", O["dbg_o"], S["oT"], reads=[dbuf["oT"]], writes=[dbuf["dbg_o"]])


_NC_CACHE = {}


def kernel(x_prompt, x_sample, state_gdn, state_gdn_conv, cache_nsa_kv, state_nsa_win, state_ffn_conv,
           page_table, norm_mix, norm_ffn, gdn_w_in, gdn_conv_w, gdn_A_log, gdn_dt_bias, gdn_norm, gdn_w_out,
           nsa_w_in, nsa_q_norm, nsa_k_norm, nsa_cmp_pe, nsa_cmp_w1, nsa_cmp_w2, rel_bias, nsa_w_out,
           ffn_w_up, ffn_conv_w, ffn_conv_b, ffn_w_down):
    cfg = Cfg()
    f = lambda a: np.ascontiguousarray(np.asarray(a))
    x_prompt, x_sample = f(x_prompt), f(x_sample)
    consts = host_consts(cfg)
    nconsts = {k_: v for k_, v in nsa_host_consts(cfg, cfg.PAST).items() if not k_.startswith("dims")}
    cache_r = f(cache_nsa_kv)[0].reshape(cfg.NPOOL * 128, 2048)
    n = 8
    NS = cfg.NS
    in_maps = []
    for c in range(n):
        b = c // 2
        sl = slice(NS * c, NS * (c + 1))
        m = dict(
            xin=np.ascontiguousarray(np.concatenate([x_prompt[b], x_sample[sl].reshape(-1, cfg.D)], 0)),
            norm_mix=f(norm_mix), norm_ffn=f(norm_ffn),
            gdn_w_in=f(gdn_w_in)[0], gdn_conv_w=f(gdn_conv_w)[0], gdn_A_log=f(gdn_A_log), gdn_dt_bias=f(gdn_dt_bias),
            gdn_norm=f(gdn_norm), gdn_w_out=f(gdn_w_out)[0],
            st_gdn=np.ascontiguousarray(f(state_gdn)[0, sl]),
            st_gdn_conv=np.ascontiguousarray(f(state_gdn_conv)[0, sl].reshape(NS * 3, cfg.CONVD)),
            ffn_w_up=f(ffn_w_up), ffn_conv_w=f(ffn_conv_w), ffn_conv_b=f(ffn_conv_b), ffn_w_down=f(ffn_w_down),
            st_ffn_conv=np.ascontiguousarray(f(state_ffn_conv)[:, sl].reshape(2, NS * 2, cfg.DFF)),
            nsa_w_in=f(nsa_w_in)[0], nsa_q_norm=f(nsa_q_norm), nsa_k_norm=f(nsa_k_norm),
            st_win=np.ascontiguousarray(f(state_nsa_win)[0, sl].reshape(NS, 512, 1024)),
            nsa_cmp_pe=f(nsa_cmp_pe)[0], nsa_cmp_w1=f(nsa_cmp_w1)[0], nsa_cmp_w2=f(nsa_cmp_w2)[0],
            rel_bias=f(rel_bias), nsa_w_out=f(nsa_w_out)[0],
            cache=cache_r, page_table=np.ascontiguousarray(f(page_table)[sl]).astype(np.int32),
        )
        m.update(nconsts)
        m.update(consts)
        in_maps.append(m)
    if "nc" not in _NC_CACHE:
        _NC_CACHE["nc"] = build(cfg)
    nc = _NC_CACHE["nc"]
    res = run_bass_kernel_spmd(nc, in_maps, core_ids=list(range(n)))
    R = res.results
    f32 = np.float32
    B, SEQ, DB, DS = 4, 2048, 32, 8
    p_gdn = np.stack([R[2 * b]["o_gdn_p"] for b in range(B)])[None].astype(f32)
    p_gdn_conv = np.stack([R[2 * b]["o_gdnc"][0:3] for b in range(B)])[None].astype(f32)
    s_gdn = np.concatenate([R[c]["o_gdn_s"] for c in range(n)], 0)[None].astype(f32)
    s_gdn_conv = np.concatenate([R[c]["o_gdnc"][3:3 + 3 * NS].reshape(NS, 3, -1) for c in range(n)], 0)[None].astype(f32)
    TP = cfg.TP
    y_prompt = np.stack([R[2 * b]["y"][:TP] for b in range(B)]).astype(f32)
    y_sample = np.concatenate([R[c]["y"][TP:].reshape(NS, DS, cfg.D) for c in range(n)], 0).astype(f32)
    p_nsa_kv = np.stack([R[2 * b]["o_kv"][:TP].reshape(SEQ, 4, 4, 128) for b in range(B)])[None].astype(f32)
    p_nsa_win = np.stack([R[2 * b]["o_pwin"].reshape(512, 2, 4, 128) for b in range(B)])[None].astype(f32)
    p_ffn_conv = np.stack([np.stack([R[2 * b]["o_ffnc"][li][0:2] for b in range(B)]) for li in range(2)]).astype(f32)
    s_nsa_kv = np.concatenate([R[c]["o_kv"][TP:].reshape(NS, DS, 4, 4, 128) for c in range(n)], 0)[None].astype(f32)
    s_nsa_win = np.concatenate([R[c]["o_swin"].reshape(NS, 512, 2, 4, 128) for c in range(n)], 0)[None].astype(f32)
    s_ffn_conv = np.stack([np.concatenate([R[c]["o_ffnc"][li][2:2 + 2 * NS].reshape(NS, 2, -1) for c in range(n)], 0)
                           for li in range(2)]).astype(f32)
    return (y_prompt, y_sample, p_gdn, p_gdn_conv, p_nsa_kv, p_nsa_win, p_ffn_conv,
            s_gdn, s_gdn_conv, s_nsa_kv, s_nsa_win, s_ffn_conv)


class Unpager:
    def __init__(self, k, cfg, I, S, dbuf):
        self.k, self.cfg, self.I, self.S, self.dbuf = k, cfg, I, S, dbuf
        self.NPG = cfg.PAST // 128
        self.total = cfg.NS * self.NPG
        self.done = 0

    def setup(self):
        k, cfg, I = self.k, self.cfg, self.I
        n = self.total
        self.idx = k.sb("u_idx", [128, n], I32)
        ptb = k.sb("u_ptb", [128, n], I32)
        k.dma("sp", ptb[:], I["page_table"].rearrange("s g -> (s g)").partition_broadcast(128), writes=[ptb])
        ptf = k.sb("u_ptf", [128, n], F32)
        iota_f = k.sb("u_iota", [128, 1], F32)
        k.dma("sp", iota_f[:], I["n_iota"], writes=[iota_f])
        k.op("dve", lambda e: e.tensor_copy(out=ptf[:], in_=ptb[:]), reads=[ptb], writes=[ptf])
        k.op("dve", lambda e: e.tensor_scalar(out=ptf[:], in0=ptf[:], scalar1=128.0, scalar2=iota_f[:, 0:1], op0=ALU.mult, op1=ALU.add),
             reads=[ptf, iota_f], writes=[ptf])
        k.op("dve", lambda e: e.tensor_copy(out=self.idx[:], in_=ptf[:]), reads=[ptf], writes=[self.idx])
        self.raw_r = Ring([k.sb(f"u_raw{i}", [128, 2048], F32) for i in range(2)])
        self.bf_r = Ring([k.sb(f"u_bf{i}", [128, 2048], BF16) for i in range(2)])

    def step(self, npages):
        k, I, S = self.k, self.I, self.S
        for _ in range(npages):
            if self.done >= self.total:
                return
            i = self.done
            self.done += 1
            raw = self.raw_r.next()
            k.op("pool", lambda e: e.indirect_dma_start(
                out=raw[:, :], out_offset=None, in_=I["cache"][:, :],
                in_offset=bass.IndirectOffsetOnAxis(ap=self.idx[:, i:i + 1], axis=0)),
                reads=[self.idx, self.dbuf["cache"]], writes=[raw], inc=False, dma=True)
            bf = self.bf_r.next()
            eng = k.ev_eng()
            if eng == "act":
                k.op("act", lambda e: e.copy(out=bf[:, :], in_=raw[:, :]), reads=[raw], writes=[bf])
            else:
                k.op("dve", lambda e: e.tensor_copy(out=bf[:, :], in_=raw[:, :]), reads=[raw], writes=[bf])
            k.dma("sp", S["past"][i * 128:(i + 1) * 128, :], bf[:, :], reads=[bf], writes=[self.dbuf["past"]])

    def finish(self):
        self.step(self.total)


def linear_tm(k, cfg, pf, dbuf, name, src_name, src_ap, w_name, w_ap, Kdim, resid_name, resid_ap, out_name, out_ap):
    D = cfg.D
    NKt = Kdim // 128
    with k.scope():
        wo_r = Ring([k.sb(f"{name}_w{i}", [128, NKt, 512], BF16) for i in range(2)])
        st_r = Ring([k.sb(f"{name}_s{i}", [128, NKt, 128], BF16) for i in range(3)])
        xr_r = Ring([k.sb(f"{name}_x{i}", [128, 512], F32) for i in range(3)])
        ho_r = Ring([k.sb(f"{name}_h{i}", [128, 512], F32) for i in range(3)])
        srcv = src_ap.rearrange("(kt p) t -> p kt t", p=128)
        def load_wo(c):
            wo = wo_r.next()
            k.dma("pool", wo[:], w_ap[:, c * 512:(c + 1) * 512].rearrange("(kt p) n -> p kt n", p=128),
                  reads=[dbuf[w_name]], writes=[wo])
            return wo
        wo_next = load_wo(0)
        for c in range(D // 512):
            wo = wo_next
            if c + 1 < D // 512:
                wo_next = load_wo(c + 1)
            for (r0, R) in cfg.ttiles():
                st = st_r.next()
                k.dma("sp", st[:, :, 0:R], srcv[:, :, r0:r0 + R], reads=[dbuf[src_name]], writes=[st])
                xr = xr_r.next()
                k.dma("sp", xr[0:R, :], resid_ap[r0:r0 + R, c * 512:(c + 1) * 512], reads=[dbuf[resid_name]], writes=[xr])
                ps = pf.next()
                for kt in range(NKt):
                    k.op("pe", lambda e: e.matmul(ps[0:R, :], lhsT=st[:, kt, 0:R], rhs=wo[:, kt, :],
                                                  start=(kt == 0), stop=(kt == NKt - 1)),
                         reads=[st, wo], writes=[ps], inc=(kt == NKt - 1))
                ho = ho_r.next()
                k.op("dve", lambda e: e.tensor_tensor(out=ho[0:R, :], in0=ps[0:R, :], in1=xr[0:R, :], op=ALU.add),
                     reads=[ps, xr], writes=[ho])
                k.dma("act", out_ap[r0:r0 + R, c * 512:(c + 1) * 512], ho[0:R, :], reads=[ho], writes=[dbuf[out_name]])


def ffn_layer(k, cfg, I, O, S, dbuf, pf, pbf, C, li, src_name, src_ap, out_name, out_ap, norm_fm, hook=None):
    D, KT, NT, TP, NS, TS, DFF = cfg.D, cfg.KT, cfg.NT, cfg.TP, cfg.NS, cfg.TS, cfg.DFF
    NSAMP = cfg.NSAMP
    ident_f = C["ident_f"]
    NM = DFF // 128
    W1 = 2
    PADW = W1 + TP + NS * (TS + W1)
    SOFF = W1 + TP
    SW = TS + W1
    chunks = cfg.chunks()
    with k.scope():
        xn = k.sb(f"f{li}_xn", [128, KT, NT], BF16)
        norm_fm(src_name, I["norm_ffn"][li:li + 1, :], xn)
        cw = k.sb(f"f{li}_cw", [128, NM, 3], F32)
        k.dma("sp", cw[:], I["ffn_conv_w"][li].rearrange("(t p) w -> p t w", p=128), writes=[cw])
        cb = k.sb(f"f{li}_cb", [128, NM], F32)
        k.dma("sp", cb[:], I["ffn_conv_b"][li].rearrange("(t p) -> p t", p=128), writes=[cb],
              allow_slow_non_contiguous=True)
        wt_r = Ring([k.sb(f"f{li}_wt{i}", [128, KT, 128], BF16) for i in range(4)])
        stc_r = Ring([k.sb(f"f{li}_stc{i}", [NS * W1, 128], F32) for i in range(2)])
        prep = k.sb(f"f{li}_prep", [128, PADW], F32)
        k.op("pool", lambda e: e.memset(prep[:], 0.0), writes=[prep])
        cq = k.sb(f"f{li}_cq", [128, NT], F32)
        act_r = Ring([k.sb(f"f{li}_act{i}", [128, NT], BF16) for i in range(2)])
        convout = k.sb(f"f{li}_convout", [128, NM, 16], F32)
        k.op("pool", lambda e: e.memset(convout[:], 0.0), writes=[convout])
        wup = I["ffn_w_up"][li]
        if hook is not None:
            hook.setup()

        def load_w(col0):
            wt = wt_r.next()
            k.dma("pool", wt[:], wup[:, col0:col0 + 128].rearrange("(kt p) c -> p kt c", p=128),
                  reads=[dbuf["ffn_w_up"]], writes=[wt])
            return wt

        def proj(wt, consumer):
            for ci, (t0, n) in enumerate(chunks):
                ps = pf.next()
                for kt in range(KT):
                    k.op("pe", lambda e: e.matmul(ps[:, 0:n], lhsT=wt[:, kt, :], rhs=xn[:, kt, t0:t0 + n],
                                                  start=(kt == 0), stop=(kt == KT - 1)),
                         reads=[wt, xn], writes=[ps], inc=(kt == KT - 1))
                consumer(ci, t0, n, ps)

        def sv(lo, hi):
            return prep[:, SOFF:SOFF + NS * SW].rearrange("p (s w) -> p s w", w=SW)[:, :, lo:hi]

        w_next = (load_w(0), load_w(DFF))
        for m in range(NM):
            wg, wv = w_next
            if m + 1 < NM:
                w_next = (load_w((m + 1) * 128), load_w(DFF + (m + 1) * 128))
            stc = stc_r.next()
            k.dma("sp", stc[:], I["st_ffn_conv"][li][:, m * 128:(m + 1) * 128], writes=[stc])
            pst = pf.next()
            k.op("pe", lambda e: e.transpose(out=pst[:, 0:NS * W1], in_=stc[0:NS * W1, 0:128],
                                             identity=ident_f[0:NS * W1, 0:NS * W1]),
                 reads=[stc, ident_f], writes=[pst])
            k.op("dve", lambda e: e.tensor_copy(out=sv(0, W1), in_=pst[:, 0:NS * W1].rearrange("p (s w) -> p s w", w=W1)),
                 reads=[pst], writes=[prep])

            def cons(ci, t0, n, ps):
                eng = k.ev_eng()
                if n == 512:
                    dv = prep[:, W1 + t0:W1 + t0 + n]
                    sv_ = ps[:, 0:n]
                else:
                    dv = sv(W1, SW)
                    sv_ = ps[:, 0:n].rearrange("p (s w) -> p s w", w=TS)
                if eng == "act":
                    k.op("act", lambda e: e.copy(out=dv, in_=sv_), reads=[ps], writes=[prep])
                else:
                    k.op("dve", lambda e: e.tensor_copy(out=dv, in_=sv_), reads=[ps], writes=[prep])
            proj(wg, cons)
            k.op("pool", lambda e: e.tensor_copy(out=convout[:, m, 0:W1], in_=prep[:, W1 + TP - W1:W1 + TP]),
                 reads=[prep], writes=[convout])
            k.op("pool", lambda e: e.tensor_copy(out=convout[:, m, W1:W1 + NS * W1].rearrange("p (s w) -> p s w", w=W1),
                                                 in_=sv(TS, SW)), reads=[prep], writes=[convout])
            for (ov, mkv) in ((cq[:, 0:TP], lambda i: prep[:, i:i + TP]),
                              (cq[:, TP:NT].rearrange("p (s w) -> p s w", w=TS), lambda i: sv(i, i + TS))):
                k.op("dve", lambda e: e.tensor_scalar(out=ov, in0=mkv(2), scalar1=cw[:, m, 2:3], scalar2=cb[:, m:m + 1],
                                                      op0=ALU.mult, op1=ALU.add), reads=[prep, cw, cb], writes=[cq])
                for i in range(2):
                    k.op("dve", lambda e: e.scalar_tensor_tensor(out=ov, in0=mkv(i), scalar=cw[:, m, i:i + 1], in1=ov,
                                                                 op0=ALU.mult, op1=ALU.add),
                         reads=[prep, cw, cq], writes=[cq])
            k.op("act", lambda e: e.activation(out=cq[:], in_=cq[:], func=AF.Silu), reads=[cq], writes=[cq])
            act = act_r.next()

            def vcons(ci, t0, n, ps):
                k.op("dve", lambda e: e.tensor_tensor(out=act[:, t0:t0 + n], in0=ps[:, 0:n], in1=cq[:, t0:t0 + n], op=ALU.mult),
                     reads=[ps, cq], writes=[act])
            proj(wv, vcons)
            k.dma("act", S["actT"][m * 128:(m + 1) * 128, :], act[:], reads=[act], writes=[dbuf["actT"]])
            if hook is not None:
                hook.step(-(-hook.total // NM))
        if hook is not None:
            hook.finish()
        nrow = (1 + NS) * W1
        cst_r = Ring([k.sb(f"f{li}_cst{i}", [16, 512], F32) for i in range(2)])
        for g0 in range(0, NM, 4):
            ps = pf.next()
            cst = cst_r.next()
            ng = min(4, NM - g0)
            for q in range(ng):
                k.op("pe", lambda e: e.transpose(out=ps[0:16, q * 128:(q + 1) * 128], in_=convout[:, g0 + q, :],
                                                 identity=ident_f[:, :]),
                     reads=[convout, ident_f], writes=[ps], inc=(q == ng - 1))
            k.op("dve", lambda e: e.tensor_copy(out=cst[:, 0:ng * 128], in_=ps[0:16, 0:ng * 128]), reads=[ps], writes=[cst])
            k.dma("sp", O["o_ffnc"][li][:, g0 * 128:(g0 + ng) * 128], cst[0:nrow, 0:ng * 128], reads=[cst],
                  writes=[dbuf["o_ffnc"]])
    linear_tm(k, cfg, pf, dbuf, f"fd{li}", "actT", S["actT"], "ffn_w_down", I["ffn_w_down"][li], DFF,
              src_name, src_ap, out_name, out_ap)


NEG = -30000.0
NSA_Z = 2063
NSA_GL = 5120
NSA_OFF = 384
NSA_X = 1920
NSA_XW = 1408


def t5_bucket_np(d):
    d = np.maximum(d, 0)
    f32 = np.float32
    scale = (32 - 16) / math.log(1024 / 16)
    large = 16 + (np.log(np.maximum(d, 1).astype(f32) / f32(16)) * f32(scale)).astype(np.int32)
    return np.where(d < 16, d, np.minimum(large, 31))


def nsa_host_consts(cfg, P):
    c = {}
    TP, TS, NS = cfg.TP, cfg.TS, cfg.NS
    f32 = np.float32
    dd = np.arange(NSA_GL) - NSA_Z
    oh = np.zeros((33, NSA_GL), f32)
    bk = t5_bucket_np(dd)
    for i in range(NSA_GL):
        if dd[i] >= 0:
            oh[bk[i], i] = 1.0
        else:
            oh[32, i] = 1.0
    c["n_oh"] = oh
    jm = np.zeros((128, 128), f32)
    jm[np.arange(128), 127 - np.arange(128)] = 1.0
    c["n_J"] = jm
    kk = np.arange(128)[:, None]
    xx = np.arange(NSA_XW)[None, :]
    c["n_wm"] = np.where((xx - kk - NSA_OFF) < 512, 0.0, NEG).astype(f32)

    def sel_consts(n_all, qpos, name):
        n_sb = -(-n_all // 64)
        ns = n_all // 16
        ncb = ns - 1
        c_end = np.arange(ncb) * 16 + 31
        c_start = c_end - 31
        sb_start = np.arange(n_sb) * 64
        ovl = np.maximum(np.minimum(c_end[:, None], sb_start[None, :] + 63) - np.maximum(c_start[:, None], sb_start[None, :]) + 1, 0).astype(f32) / 32
        ntile = -(-ncb // 128)
        ovl_t = np.zeros((ntile * 128, n_sb), f32)
        ovl_t[:ncb] = ovl
        c[name + "_ovl"] = ovl_t.reshape(ntile, 128, n_sb)
        cur = qpos // 64
        blk = np.arange(n_sb)
        ok = sb_start[None, :] <= qpos[:, None]
        forced = (blk[None, :] == 0) | (blk[None, :] == cur[:, None]) | (blk[None, :] == cur[:, None] - 1)
        c[name + "_add"] = np.where(ok, np.where(forced, 1000.0, 0.0), -1e30).astype(f32)
        c[name + "_valid"] = ok.astype(f32)
        nk = -(-n_all // 128) * 128
        ex = np.zeros((n_sb, nk), f32)
        keys = np.arange(n_all)
        ex[keys // 64, keys] = 1.0
        c[name + "_exp"] = ex.astype(ml_dtypes.bfloat16)
        return n_sb, ncb
    c["dims_p"] = sel_consts(TP, np.arange(TP), "np")
    c["dims_s"] = sel_consts(P + TS, P + np.arange(TS), "ns")
    c["n_iota"] = np.arange(128, dtype=np.float32).reshape(128, 1)
    return c


def nsa_proj(k, cfg, I, O, S, dbuf, pf, pbf, C, norm_fm, src_name):
    D, KT, NT, TP, NS, TS = cfg.D, cfg.KT, cfg.NT, cfg.TP, cfg.NS, cfg.TS
    ones_b = C["ones_b"]
    chunks = cfg.chunks()
    WIN = 512
    with k.scope():
        xn = k.sb("n_xn", [128, KT, NT], BF16)
        norm_fm(src_name, I["norm_mix"][1:2, :], xn)
        epsb = k.sb("n_epsb", [128, 1], F32)
        k.op("pool", lambda e: e.memset(epsb[:], RMS_EPS), writes=[epsb])
        qg = k.sb("n_qg", [128, 1], F32)
        k.dma("sp", qg[:], I["nsa_q_norm"].rearrange("o d -> d o"), writes=[qg])
        k.op("dve", lambda e: e.tensor_scalar(out=qg[:], in0=qg[:], scalar1=128 ** -0.5, scalar2=None, op0=ALU.mult),
             reads=[qg], writes=[qg])
        import os
        parts = os.environ.get("NSA_PARTS", "q,kv,win,g").split(",")
        with k.scope():
            wt_r = Ring([k.sb(f"n_wt{i}", [128, KT, 128], BF16) for i in range(2)])
            qraw = k.sb("n_qraw", [128, NT], F32)
            sqb = k.sb("n_sqb", [128, 512], BF16)
            rinv = k.sb("n_rinv", [128, 512], F32)
            qn_r = Ring([k.sb(f"n_qn{i}", [128, NT], BF16) for i in range(2)])
            for h in range(16 if "q" in parts else 0):
                wt = wt_r.next()
                k.dma("pool", wt[:], I["nsa_w_in"][:, h * 128:(h + 1) * 128].rearrange("(kt p) c -> p kt c", p=128),
                      reads=[dbuf["nsa_w_in"]], writes=[wt])
                qn = qn_r.next()
                for (t0, n) in chunks:
                    ps = pf.next()
                    for kt in range(KT):
                        k.op("pe", lambda e: e.matmul(ps[:, 0:n], lhsT=wt[:, kt, :], rhs=xn[:, kt, t0:t0 + n],
                                                      start=(kt == 0), stop=(kt == KT - 1)),
                             reads=[wt, xn], writes=[ps], inc=(kt == KT - 1))
                    qlvl = int(os.environ.get("NSA_QLVL", "9"))
                    k.op("dve", lambda e: e.tensor_copy(out=qraw[:, t0:t0 + n], in_=ps[:, 0:n]), reads=[ps], writes=[qraw])
                    if qlvl < 1:
                        continue
                    k.op("act", lambda e: e.activation(out=sqb[:, 0:n], in_=ps[:, 0:n], func=AF.Square), reads=[ps], writes=[sqb])
                    ps2 = pf.next()
                    k.op("pe", lambda e: e.matmul(ps2[:, 0:n], lhsT=ones_b[:, :], rhs=sqb[:, 0:n], start=True, stop=True),
                         reads=[ones_b, sqb], writes=[ps2])
                    if qlvl < 2:
                        continue
                    k.op("act", lambda e: e.activation(out=rinv[:, 0:n], in_=ps2[:, 0:n], func=AF.Sqrt, scale=1.0 / 128,
                                                       bias=epsb[:, 0:1]), reads=[ps2, epsb], writes=[rinv])
                    k.op("dve", lambda e: e.reciprocal(out=rinv[:, 0:n], in_=rinv[:, 0:n]), reads=[rinv], writes=[rinv])
                    if qlvl < 3:
                        continue
                    k.op("dve", lambda e: e.scalar_tensor_tensor(out=qn[:, t0:t0 + n], in0=qraw[:, t0:t0 + n], scalar=qg[:, 0:1],
                                                                 in1=rinv[:, 0:n], op0=ALU.mult, op1=ALU.mult),
                         reads=[qraw, qg, rinv], writes=[qn])
                if qlvl >= 4:
                    k.dma("sp", S["qT"][h * 128:(h + 1) * 128, :], qn[:], reads=[qn], writes=[dbuf["qT"]])
        with k.scope():
            wkv_r = Ring([k.sb(f"n_wkv{i}", [128, KT, 512], BF16) for i in range(2)])
            gain_k = k.sb("n_gaink", [128, 3, 128], F32)
            k.dma("sp", gain_k[:].rearrange("p a d -> p (a d)"),
                  I["nsa_k_norm"].rearrange("o a d -> o (a d)").partition_broadcast(128), writes=[gain_k])
            sqt = k.sb("n_sqt", [128, 512], F32)
            ss = k.sb("n_ss", [128, 8], F32)
            rows_r = Ring([k.sb(f"n_rows{i}", [128, 512], F32) for i in range(3)])
            for c in range(6 if "kv" in parts else 0):
                wc = wkv_r.next()
                k.dma("pool", wc[:], I["nsa_w_in"][:, 2048 + c * 512:2048 + (c + 1) * 512].rearrange("(kt p) c -> p kt c", p=128),
                      reads=[dbuf["nsa_w_in"]], writes=[wc])
                for (r0, R) in cfg.ttiles():
                    ps = pf.next()
                    for kt in range(KT):
                        k.op("pe", lambda e: e.matmul(ps[0:R, :], lhsT=xn[:, kt, r0:r0 + R], rhs=wc[:, kt, :],
                                                      start=(kt == 0), stop=(kt == KT - 1)),
                             reads=[xn, wc], writes=[ps], inc=(kt == KT - 1))
                    rows = rows_r.next()
                    if c in (2, 4):
                        gi = 1 if c == 2 else 2
                        k.op("act", lambda e: e.activation(out=sqt[0:R, :], in_=ps[0:R, :], func=AF.Square), reads=[ps], writes=[sqt])
                        k.op("dve", lambda e: e.reduce_sum(out=ss[0:R, 0:4], in_=sqt[0:R, :].rearrange("p (g d) -> p g d", d=128),
                                                           axis=AX.X), reads=[sqt], writes=[ss])
                        k.op("act", lambda e: e.activation(out=ss[0:R, 4:8], in_=ss[0:R, 0:4], func=AF.Sqrt, scale=1.0 / 128,
                                                           bias=epsb[0:R, 0:1]), reads=[ss, epsb], writes=[ss])
                        k.op("dve", lambda e: e.reciprocal(out=ss[0:R, 0:4], in_=ss[0:R, 4:8]), reads=[ss], writes=[ss])
                        k.op("dve", lambda e: e.tensor_tensor(out=sqt[0:R, :].rearrange("p (g d) -> p g d", d=128),
                                                              in0=ps[0:R, :].rearrange("p (g d) -> p g d", d=128),
                                                              in1=ss[0:R, 0:4].unsqueeze(2).to_broadcast([R, 4, 128]), op=ALU.mult),
                             reads=[ps, ss], writes=[sqt])
                        k.op("dve", lambda e: e.tensor_tensor(out=rows[0:R, :].rearrange("p (g d) -> p g d", d=128),
                                                              in0=sqt[0:R, :].rearrange("p (g d) -> p g d", d=128),
                                                              in1=gain_k[0:R, gi, :].unsqueeze(1).to_broadcast([R, 4, 128]), op=ALU.mult),
                             reads=[sqt, gain_k], writes=[rows])
                    else:
                        k.op("act", lambda e: e.copy(out=rows[0:R, :], in_=ps[0:R, :]), reads=[ps], writes=[rows])
                    if c < 4:
                        k.dma("sp", O["o_kv"][r0:r0 + R, c * 512:(c + 1) * 512], rows[0:R, :], reads=[rows], writes=[dbuf["o_kv"]])
                    else:
                        cc = (c - 4) * 512
                        k.dma("sp", S["win_new"][r0:r0 + R, cc:cc + 512], rows[0:R, :], reads=[rows], writes=[dbuf["win_new"]])
                        if r0 < TP and r0 >= TP - WIN:
                            k.dma("sp", O["o_pwin"][r0 - (TP - WIN):r0 - (TP - WIN) + R, cc:cc + 512], rows[0:R, :],
                                  reads=[rows], writes=[dbuf["o_pwin"]])
                        if r0 == TP:
                            for s in range(NS):
                                k.dma("sp", O["o_swin"][s, WIN - TS:WIN, cc:cc + 512], rows[s * TS:(s + 1) * TS, :],
                                      reads=[rows], writes=[dbuf["o_swin"]])
            swb_r = Ring([k.sb(f"n_swb{i}", [126, 4096], F32) for i in range(2)])
            for s in range(NS if "win" in parts else 0):
                swb = swb_r.next()
                k.dma("sp", swb[:, :], I["st_win"][s, TS:WIN, :].rearrange("(p a) c -> p (a c)", a=4),
                      reads=[dbuf["st_win"]], writes=[swb])
                k.dma("sp", O["o_swin"][s, 0:WIN - TS, :].rearrange("(p a) c -> p (a c)", a=4), swb[:, :],
                      reads=[swb], writes=[dbuf["o_swin"]])
            wg = k.sb("n_wg", [128, KT, 48], BF16)
            k.dma("pool", wg[:], I["nsa_w_in"][:, 5120:5168].rearrange("(kt p) c -> p kt c", p=128),
                  reads=[dbuf["nsa_w_in"]], writes=[wg])
            gt_r = Ring([k.sb(f"n_gt{i}", [128, 48], F32) for i in range(2)])
            for (r0, R) in (cfg.ttiles() if "g" in parts else []):
                ps = pf.next()
                for kt in range(KT):
                    k.op("pe", lambda e: e.matmul(ps[0:R, 0:48], lhsT=xn[:, kt, r0:r0 + R], rhs=wg[:, kt, :],
                                                  start=(kt == 0), stop=(kt == KT - 1)),
                         reads=[xn, wg], writes=[ps], inc=(kt == KT - 1))
                gt = gt_r.next()
                k.op("act", lambda e: e.activation(out=gt[0:R, :], in_=ps[0:R, 0:48], func=AF.Sigmoid), reads=[ps], writes=[gt])
                k.dma("sp", S["gates"][r0:r0 + R, :], gt[0:R, :], reads=[gt], writes=[dbuf["gates"]])


def nsa_attn(k, cfg, I, O, S, dbuf, pf, pbf, C):
    D, KT, NT, TP, NS, TS = cfg.D, cfg.KT, cfg.NT, cfg.TP, cfg.NS, cfg.TS
    P = cfg.PAST
    NQT = TP // 128
    NCH = TP // 512
    n_sb_p = TP // 64
    ncb_p = TP // 16 - 1
    Z, GL, OFF, X, XW = NSA_Z, NSA_GL, NSA_OFF, NSA_X, NSA_XW
    ident_f, ident_b, ones_b, ones_f = C["ident_f"], C["ident_b"], C["ones_b"], C["ones_f"]
    ringN = Ring(pf.bufs[0:4])
    ring2 = Ring(pf.bufs[0:2])
    accO = pf.bufs[2:6]
    GELU_C = 1.5957691216057308

    def evac(dst, src, rd, wr):
        eng = k.ev_eng()
        if eng == "act":
            k.op("act", lambda e: e.copy(out=dst, in_=src), reads=rd, writes=wr)
        else:
            k.op("dve", lambda e: e.tensor_copy(out=dst, in_=src), reads=rd, writes=wr)

    with k.scope():
        Jm = k.sb("a_J", [128, 128], F32)
        k.dma("sp", Jm[:], I["n_J"], writes=[Jm])
        relx = k.sb("a_relx", [64, 16], F32)
        k.op("pool", lambda e: e.memset(relx[32:64, :], NEG), writes=[relx])
        k.dma("sp", relx[0:32, :], I["rel_bias"], writes=[relx])
        epsb = k.sb("a_epsb", [128, 1], F32)
        k.op("pool", lambda e: e.memset(epsb[:], RMS_EPS), writes=[epsb])
        with k.scope():
            oh = k.sb("a_oh", [33, GL], F32)
            k.dma("sp", oh[:], I["n_oh"], writes=[oh])
            gsb_r = Ring([k.sb(f"a_gsb{i}", [16, 512], F32) for i in range(2)])
            for c0 in range(0, GL, 512):
                ps = ringN.next()
                k.op("pe", lambda e: e.matmul(ps[0:16, :], lhsT=relx[0:33, :], rhs=oh[:, c0:c0 + 512], start=True, stop=True),
                     reads=[relx, oh], writes=[ps])
                gsb = gsb_r.next()
                k.op("dve", lambda e: e.tensor_copy(out=gsb[:, :], in_=ps[0:16, :]), reads=[ps], writes=[gsb])
                k.dma("sp", S["gvec"][:, c0:c0 + 512], gsb[:, :], reads=[gsb], writes=[dbuf["gvec"]])
        wm = k.sb("a_wm", [128, XW], F32)
        k.dma("sp", wm[:], I["n_wm"], writes=[wm])
        ovl_p = k.sb("a_ovlp", [128, n_sb_p], BF16)
        k.dma("pool", ovl_p[:], I["np_ovl"][0], writes=[ovl_p])
        add_p = k.sb("a_addp", [128, NQT, n_sb_p], F32)
        k.dma("sp", add_p[:], I["np_add"].rearrange("(t p) m -> p t m", p=128), writes=[add_p])
        val_p = k.sb("a_valp", [128, NQT, n_sb_p], F32)
        k.dma("sp", val_p[:], I["np_valid"].rearrange("(t p) m -> p t m", p=128), writes=[val_p])
        exp_p = k.sb("a_expp", [n_sb_p, TP], BF16)
        k.dma("sp", exp_p[:], I["np_exp"], writes=[exp_p])
        gates = k.sb("a_gates", [128, NQT, 48], F32)
        k.dma("sp", gates[:], S["gates"][0:TP, :].rearrange("(t p) c -> p t c", p=128), reads=[dbuf["gates"]], writes=[gates])
        w1 = k.sb("a_w1", [128, 2, 32, 256], BF16)
        w2 = k.sb("a_w2", [128, 2, 2, 128], BF16)
        for wch in range(2):
            k.dma("pool", w1[:, wch], I["nsa_cmp_w1"][wch].rearrange("(s d) e -> d s e", d=128), writes=[w1])
            k.dma("pool", w2[:, wch], I["nsa_cmp_w2"][wch].rearrange("(h e) d -> e h d", e=128), writes=[w2])
        peT = k.sb("a_peT", [128, 2, 32], BF16)
        k.dma("pool", peT[:], I["nsa_cmp_pe"].rearrange("w s d -> d w s"), writes=[peT], allow_slow_non_contiguous=True)
        peb = k.sb("a_peb", [128, 2, 2], F32)
        for wch in range(2):
            for half in range(2):
                ps = ringN.next()
                for s in range(32):
                    k.op("pe", lambda e: e.matmul(ps[:, 0:1], lhsT=w1[:, wch, s, half * 128:(half + 1) * 128],
                                                  rhs=peT[:, wch, s:s + 1], start=(s == 0), stop=(s == 31)),
                         reads=[w1, peT], writes=[ps], inc=(s == 31))
                k.op("dve", lambda e: e.tensor_copy(out=peb[:, wch, half:half + 1], in_=ps[:, 0:1]), reads=[ps], writes=[peb])
        gk0 = k.sb("a_gk0", [128, 1], F32)
        k.dma("sp", gk0[:], I["nsa_k_norm"][0, 0:1, :].rearrange("o d -> d o"), writes=[gk0])

        t1_r = Ring([k.sb(f"a_t1{i}", [128, 2048], F32) for i in range(1)])

        def build_tab(dst_fn, h, base, W, pstride):
            t1 = t1_r.next()
            gv = S["gvec"]
            src = bass.AP(tensor=gv.tensor, offset=h * GL + base, ap=[[pstride, 128], [1, W]])
            k.dma("sp", t1[:, 0:W], src, reads=[dbuf["gvec"]], writes=[t1])
            for c0 in range(0, W, 512):
                n = min(512, W - c0)
                ps = ringN.next()
                k.op("pe", lambda e: e.matmul(ps[:, 0:n], lhsT=Jm[:, :], rhs=t1[:, c0:c0 + n], start=True, stop=True),
                     reads=[Jm, t1], writes=[ps])
                dst, wr = dst_fn(c0, n)
                evac(dst, ps[:, 0:n], [ps], wr)

        def compress(kx, wch, ncb, hg, tmpx, tmp2):
            kxv = kx[:, 0:(ncb + 1) * 16].rearrange("p (n r) -> p n r", r=16)
            for half in range(2):
                ps = ringN.next()
                for s in range(32):
                    rv = kxv[:, 0:ncb, s] if s < 16 else kxv[:, 1:ncb + 1, s - 16]
                    k.op("pe", lambda e: e.matmul(ps[:, 0:ncb], lhsT=w1[:, wch, s, half * 128:(half + 1) * 128], rhs=rv,
                                                  start=(s == 0), stop=(s == 31)),
                         reads=[w1, kx], writes=[ps], inc=(s == 31))
                k.op("dve", lambda e: e.tensor_scalar(out=tmpx[:, 0:ncb], in0=ps[:, 0:ncb], scalar1=peb[:, wch, half:half + 1],
                                                      scalar2=None, op0=ALU.add), reads=[ps, peb], writes=[tmpx])
                k.op("dve", lambda e: e.tensor_tensor(out=tmp2[:, 0:ncb], in0=tmpx[:, 0:ncb], in1=tmpx[:, 0:ncb], op=ALU.mult),
                     reads=[tmpx], writes=[tmp2])
                k.op("dve", lambda e: e.tensor_scalar(out=tmp2[:, 0:ncb], in0=tmp2[:, 0:ncb], scalar1=0.044715, scalar2=1.0,
                                                      op0=ALU.mult, op1=ALU.add), reads=[tmp2], writes=[tmp2])
                k.op("dve", lambda e: e.tensor_tensor(out=tmp2[:, 0:ncb], in0=tmp2[:, 0:ncb], in1=tmpx[:, 0:ncb], op=ALU.mult),
                     reads=[tmp2, tmpx], writes=[tmp2])
                k.op("act", lambda e: e.activation(out=tmp2[:, 0:ncb], in_=tmp2[:, 0:ncb], func=AF.Sigmoid, scale=GELU_C),
                     reads=[tmp2], writes=[tmp2])
                k.op("dve", lambda e: e.tensor_tensor(out=hg[:, half, 0:ncb], in0=tmp2[:, 0:ncb], in1=tmpx[:, 0:ncb], op=ALU.mult),
                     reads=[tmp2, tmpx], writes=[hg])

        def kc_finish(hg, ncb, kcT, tmpx, tmp2, sqb):
            ps = ringN.next()
            for half in range(2):
                k.op("pe", lambda e: e.matmul(ps[:, 0:ncb], lhsT=w2[:, 0, half, :], rhs=hg[:, half, 0:ncb],
                                              start=(half == 0), stop=(half == 1)), reads=[w2, hg], writes=[ps], inc=(half == 1))
            k.op("dve", lambda e: e.tensor_copy(out=tmpx[:, 0:ncb], in_=ps[:, 0:ncb]), reads=[ps], writes=[tmpx])
            k.op("act", lambda e: e.activation(out=sqb[:, 0:ncb], in_=tmpx[:, 0:ncb], func=AF.Square), reads=[tmpx], writes=[sqb])
            ps2 = ringN.next()
            k.op("pe", lambda e: e.matmul(ps2[:, 0:ncb], lhsT=ones_b[:, :], rhs=sqb[:, 0:ncb], start=True, stop=True),
                 reads=[ones_b, sqb], writes=[ps2])
            k.op("act", lambda e: e.activation(out=tmp2[:, 0:ncb], in_=ps2[:, 0:ncb], func=AF.Sqrt, scale=1.0 / 128, bias=epsb[:, 0:1]),
                 reads=[ps2, epsb], writes=[tmp2])
            k.op("dve", lambda e: e.reciprocal(out=tmp2[:, 0:ncb], in_=tmp2[:, 0:ncb]), reads=[tmp2], writes=[tmp2])
            k.op("dve", lambda e: e.scalar_tensor_tensor(out=kcT[:, 0:ncb], in0=tmpx[:, 0:ncb], scalar=gk0[:, 0:1], in1=tmp2[:, 0:ncb],
                                                         op0=ALU.mult, op1=ALU.mult), reads=[tmpx, gk0, tmp2], writes=[kcT])

        def transpose_tiles(tk, ntile, dstT):
            for g0 in range(0, ntile, 8):
                ng = min(8, ntile - g0)
                pb = pbf.next()
                for i in range(ng):
                    k.op("pe", lambda e: e.transpose(out=pb[:, i * 128:(i + 1) * 128], in_=tk[:, g0 + i, :], identity=ident_b[:, :]),
                         reads=[tk, ident_b], writes=[pb], inc=(i == ng - 1))
                evac(dstT[:, g0 * 128:(g0 + ng) * 128], pb[:, 0:ng * 128], [pb], [dstT])

        def fill_sample_bias(j, T, Tw, Bs, Bw):
            NPG_ = P // 128
            kcut = sum(1 for kt in range(NPG_) if P - 128 * kt >= 1024)
            if kcut:
                k.op("pool", lambda e: e.tensor_copy(out=Bs[:, 0:kcut, j, :],
                                                     in_=T[:, 1024 + OFF:1024 + OFF + 8].unsqueeze(1).to_broadcast([128, kcut, 8])),
                     reads=[T], writes=[Bs])
            for kt in range(kcut, NPG_):
                x0 = P - 128 * kt + OFF
                k.op("pool", lambda e: e.tensor_copy(out=Bs[:, kt, j, :], in_=T[:, x0:x0 + 8]), reads=[T], writes=[Bs])
            k.op("pool", lambda e: e.tensor_copy(out=Bs[:, NPG_, j, :], in_=T[:, OFF:OFF + 8]), reads=[T], writes=[Bs])
            for wt in range(4):
                x0 = 512 - 128 * wt + OFF
                k.op("pool", lambda e: e.tensor_copy(out=Bw[:, wt, j, :], in_=Tw[:, x0:x0 + 8]), reads=[Tw], writes=[Bw])
            k.op("pool", lambda e: e.tensor_copy(out=Bw[:, 4, j, :], in_=Tw[:, OFF:OFF + 8]), reads=[Tw], writes=[Bw])

        def prompt_group(g, qh, Bs, Bw):
            if True:
                kslcT = k.sb("a_kslcT", [128, TP], BF16)
                kwinT = k.sb("a_kwinT", [128, TP], BF16)
                vslc = k.sb("a_vslc", [128, NQT, 136], BF16)
                vwin = k.sb("a_vwin", [128, NQT, 136], BF16)
                k.op("pool", lambda e: e.memset(vslc[:, :, 128:129], 1.0), writes=[vslc])
                k.op("pool", lambda e: e.memset(vwin[:, :, 128:129], 1.0), writes=[vwin])
                kcT = k.sb("a_kcT", [128, 128], BF16)
                vc = k.sb("a_vc", [128, 128], BF16)
                prep_scope = k.scope()
                prep_scope.__enter__()
                tk_r = Ring([k.sb(f"a_tk{i}", [128, NQT, 128], BF16) for i in range(2)])
                kx = k.sb("a_kx", [128, TP], BF16)
                vx = k.sb("a_vx", [128, TP], BF16)
                for (nm, ap, dstT, dstV) in (
                        ("o_kv", O["o_kv"][0:TP, 0 * 512 + g * 128:0 * 512 + (g + 1) * 128], kx, None),
                        ("o_kv", O["o_kv"][0:TP, 1 * 512 + g * 128:1 * 512 + (g + 1) * 128], vx, None),
                        ("o_kv", O["o_kv"][0:TP, 2 * 512 + g * 128:2 * 512 + (g + 1) * 128], kslcT, None),
                        ("o_kv", O["o_kv"][0:TP, 3 * 512 + g * 128:3 * 512 + (g + 1) * 128], None, vslc),
                        ("win_new", S["win_new"][0:TP, g * 128:(g + 1) * 128], kwinT, None),
                        ("win_new", S["win_new"][0:TP, 512 + g * 128:512 + (g + 1) * 128], None, vwin)):
                    if dstV is not None:
                        k.dma("pool", dstV[:, :, 0:128], ap.rearrange("(t p) d -> p t d", p=128), reads=[dbuf[nm]], writes=[dstV])
                    else:
                        tk = tk_r.next()
                        k.dma("pool", tk[:], ap.rearrange("(t p) d -> p t d", p=128), reads=[dbuf[nm]], writes=[tk])
                        transpose_tiles(tk, NQT, dstT)
                hg = k.sb("a_hg", [128, 2, 512], BF16)
                tmpx = k.sb("a_tmpx", [128, 512], F32)
                tmp2 = k.sb("a_tmp2", [128, 512], F32)
                sqb = k.sb("a_sqb", [128, 512], BF16)
                compress(kx, 0, ncb_p, hg, tmpx, tmp2)
                kc_finish(hg, ncb_p, kcT, tmpx, tmp2, sqb)
                compress(vx, 1, ncb_p, hg, tmpx, tmp2)
                ps = ringN.next()
                for half in range(2):
                    k.op("pe", lambda e: e.matmul(ps[0:ncb_p, 0:128], lhsT=hg[:, half, 0:ncb_p], rhs=w2[:, 1, half, :],
                                                  start=(half == 0), stop=(half == 1)), reads=[hg, w2], writes=[ps], inc=(half == 1))
                k.op("act", lambda e: e.copy(out=vc[0:ncb_p, :], in_=ps[0:ncb_p, 0:128]), reads=[ps], writes=[vc])
                prep_scope.__exit__(None, None, None)
                o_acc = k.sb("a_oacc", [128, NQT, 4, 128], F32)
                import os
                BR = os.environ.get("NSA_BR", "c,s,w")
                k.op("pool", lambda e: e.memset(o_acc[:], 0.0), writes=[o_acc])
                selT = k.sb("a_selT", [n_sb_p, TP], BF16)
                ssb_r = Ring([k.sb(f"a_ssb{i}", [128, 512], F32) for i in range(2)])
                ebf_r = Ring([k.sb(f"a_ebf{i}", [128, 512], BF16) for i in range(2)])
                eb2_r = Ring([k.sb(f"a_eb2{i}", [128, 512], BF16) for i in range(2)])
                sm = k.sb("a_sm", [128, 8], F32)
                with k.scope():
                    Bc = k.sb("a_Bc", [128, TP], F32)
                    Pall = k.sb("a_Pall", [128, 4, TP], BF16)
                    rec = k.sb("a_rec", [128, 512], F32)
                    sc = k.sb("a_sc", [128, n_sb_p], F32)
                    wk = k.sb("a_wk", [128, n_sb_p], F32)
                    m8 = k.sb("a_m8", [128, 16], F32)
                    sel = k.sb("a_sel", [128, n_sb_p], F32)
                    for j in range(4):
                        build_tab(lambda c0, n: (Bc[:, c0:c0 + n], [Bc]), 4 * g + j, 0, TP, 16)
                        for c in range(NCH):
                            q0 = c * 512
                            ps = ringN.next()
                            k.op("pe", lambda e: e.matmul(ps[0:ncb_p, :], lhsT=kcT[:, 0:ncb_p], rhs=qh[:, j, q0:q0 + 512], start=True, stop=True),
                                 reads=[kcT, qh], writes=[ps])
                            ssb = ssb_r.next()
                            k.op("dve", lambda e: e.tensor_tensor(out=ssb[0:ncb_p, :], in0=ps[0:ncb_p, :], in1=Bc[0:ncb_p, q0:q0 + 512], op=ALU.add),
                                 reads=[ps, Bc], writes=[ssb])
                            ebf = ebf_r.next()
                            k.op("act", lambda e: e.activation(out=ebf[0:ncb_p, :], in_=ssb[0:ncb_p, :], func=AF.Exp), reads=[ssb], writes=[ebf])
                            psD = ringN.next()
                            k.op("pe", lambda e: e.matmul(psD[0:ncb_p, :], lhsT=ones_b[0:ncb_p, 0:ncb_p], rhs=ebf[0:ncb_p, :], start=True, stop=True),
                                 reads=[ones_b, ebf], writes=[psD])
                            k.op("dve", lambda e: e.tensor_scalar(out=rec[0:ncb_p, :], in0=psD[0:ncb_p, :], scalar1=1e-30, scalar2=None, op0=ALU.max),
                                 reads=[psD], writes=[rec])
                            k.op("dve", lambda e: e.reciprocal(out=rec[0:ncb_p, :], in_=rec[0:ncb_p, :]), reads=[rec], writes=[rec])
                            k.op("dve", lambda e: e.tensor_tensor(out=Pall[0:ncb_p, j, q0:q0 + 512], in0=ebf[0:ncb_p, :], in1=rec[0:ncb_p, :], op=ALU.mult),
                                 reads=[ebf, rec], writes=[Pall])
                            for qs in range(4):
                                qt = c * 4 + qs
                                pso = ringN.next()
                                k.op("pe", lambda e: e.matmul(pso[:, 0:128], lhsT=Pall[0:ncb_p, j, q0 + qs * 128:q0 + (qs + 1) * 128], rhs=vc[0:ncb_p, :],
                                                              start=True, stop=True), reads=[Pall, vc], writes=[pso])
                                if "c" in BR:
                                    k.op("act", lambda e: e.activation(out=o_acc[:, qt, j, :], in_=pso[:, 0:128], func=AF.Copy,
                                                                       scale=gates[:, qt, 4 * g + j:4 * g + j + 1]),
                                         reads=[pso, gates], writes=[o_acc])
                    for c in range(NCH):
                        q0 = c * 512
                        for qs in range(4):
                            qt = c * 4 + qs
                            psI = ringN.next()
                            for j in range(4):
                                k.op("pe", lambda e: e.matmul(psI[:, 0:n_sb_p], lhsT=Pall[0:ncb_p, j, q0 + qs * 128:q0 + (qs + 1) * 128], rhs=ovl_p[0:ncb_p, :],
                                                              start=(j == 0), stop=(j == 3)), reads=[Pall, ovl_p], writes=[psI], inc=(j == 3))
                            k.op("dve", lambda e: e.tensor_tensor(out=sc[:, :], in0=psI[:, 0:n_sb_p], in1=add_p[:, qt, :], op=ALU.add),
                                 reads=[psI, add_p], writes=[sc])
                            k.op("dve", lambda e: e.max(out=m8[:, 0:8], in_=sc[:, :]), reads=[sc], writes=[m8])
                            if n_sb_p > 8:
                                k.op("dve", lambda e: e.match_replace(out=wk[:, :], in_to_replace=m8[:, 0:8], in_values=sc[:, :], imm_value=-1e30),
                                     reads=[m8, sc], writes=[wk])
                                k.op("dve", lambda e: e.max(out=m8[:, 8:16], in_=wk[:, :]), reads=[wk], writes=[m8])
                                thr = m8[:, 15:16]
                            else:
                                thr = m8[:, 7:8]
                            k.op("dve", lambda e: e.tensor_scalar(out=sel[:, :], in0=sc[:, :], scalar1=thr, scalar2=None, op0=ALU.is_ge),
                                 reads=[sc, m8], writes=[sel])
                            k.op("dve", lambda e: e.tensor_tensor(out=sel[:, :], in0=sel[:, :], in1=val_p[:, qt, :], op=ALU.mult),
                                 reads=[sel, val_p], writes=[sel])
                            pst = ringN.next()
                            k.op("pe", lambda e: e.transpose(out=pst[0:n_sb_p, 0:128], in_=sel[:, :], identity=ident_f[:, :]),
                                 reads=[sel, ident_f], writes=[pst])
                            k.op("act", lambda e: e.copy(out=selT[:, qt * 128:(qt + 1) * 128], in_=pst[0:n_sb_p, 0:128]), reads=[pst], writes=[selT])
                with k.scope():
                    oT_r = Ring([k.sb(f"a_oTt{i}", [128, 512], BF16) for i in range(2)])
                    T = k.sb("a_T", [128, X], F32)
                    Tw = k.sb("a_Tw", [128, XW], F32)
                    for j in range(4):
                        build_tab(lambda c0, n: (T[:, c0:c0 + n], [T]), 4 * g + j, Z - OFF - 127, X, 1)
                        k.op("pool", lambda e: e.tensor_tensor(out=Tw[:], in0=T[:, 0:XW], in1=wm[:], op=ALU.add), reads=[T, wm], writes=[Tw])
                        fill_sample_bias(j, T, Tw, Bs, Bw)
                        for c in range(NCH):
                            q0 = c * 512
                            for (kT, vv, tab, gi, kts, use_mask) in (
                                    (kslcT, vslc, T, 1, list(range(0, 4 * c + 4)), True),
                                    (kwinT, vwin, Tw, 2, list(range(max(0, 4 * c - 4), 4 * c + 4)), False)):
                                if (use_mask and "s" not in BR) or (not use_mask and "w" not in BR):
                                    continue
                                for kt in kts:
                                    ps = ring2.next()
                                    k.op("pe", lambda e: e.matmul(ps[:, :], lhsT=kT[:, kt * 128:(kt + 1) * 128], rhs=qh[:, j, q0:q0 + 512], start=True, stop=True),
                                         reads=[kT, qh], writes=[ps])
                                    x0 = min(q0 - 128 * kt, 1024) + OFF
                                    ssb = ssb_r.next()
                                    k.op("dve", lambda e: e.tensor_tensor(out=ssb[:, :], in0=ps[:, :], in1=tab[:, x0:x0 + 512], op=ALU.add),
                                         reads=[ps, tab], writes=[ssb])
                                    ebf = ebf_r.next()
                                    k.op("act", lambda e: e.activation(out=ebf[:, :], in_=ssb[:, :], func=AF.Exp), reads=[ssb], writes=[ebf])
                                    if use_mask:
                                        psM = ring2.next()
                                        k.op("pe", lambda e: e.matmul(psM[:, :], lhsT=exp_p[:, kt * 128:(kt + 1) * 128], rhs=selT[:, q0:q0 + 512], start=True, stop=True),
                                             reads=[exp_p, selT], writes=[psM])
                                        eb2 = eb2_r.next()
                                        k.op("dve", lambda e: e.tensor_tensor(out=eb2[:, :], in0=ebf[:, :], in1=psM[:, :], op=ALU.mult),
                                             reads=[ebf, psM], writes=[eb2])
                                    else:
                                        eb2 = ebf
                                    for qs in range(4):
                                        acc = accO[qs]
                                        k.op("pe", lambda e: e.matmul(acc[:, 0:129], lhsT=eb2[:, qs * 128:(qs + 1) * 128],
                                                                      rhs=vv[:, kt, 0:129], start=(kt == kts[0]), stop=(kt == kts[-1])),
                                             reads=[eb2, vv], writes=[acc], inc=(kt == kts[-1]))
                                for qs in range(4):
                                    qt = c * 4 + qs
                                    acc = accO[qs]
                                    a0 = 0
                                    k.op("dve", lambda e: e.reciprocal(out=sm[:, 0:1], in_=acc[:, a0 + 128:a0 + 129]), reads=[acc], writes=[sm])
                                    k.op("dve", lambda e: e.tensor_tensor(out=sm[:, 1:2], in0=sm[:, 0:1],
                                                                          in1=gates[:, qt, gi * 16 + 4 * g + j:gi * 16 + 4 * g + j + 1], op=ALU.mult),
                                         reads=[sm, gates], writes=[sm])
                                    k.op("dve", lambda e: e.scalar_tensor_tensor(out=o_acc[:, qt, j, :], in0=acc[:, a0:a0 + 128], scalar=sm[:, 1:2],
                                                                                 in1=o_acc[:, qt, j, :], op0=ALU.mult, op1=ALU.add),
                                         reads=[acc, sm, o_acc], writes=[o_acc])
                            pst = ring2.next()
                            for qs in range(4):
                                k.op("pe", lambda e: e.transpose(out=pst[:, qs * 128:(qs + 1) * 128], in_=o_acc[:, c * 4 + qs, j, :], identity=ident_f[:, :]),
                                     reads=[o_acc, ident_f], writes=[pst], inc=(qs == 3))
                            oTt = oT_r.next()
                            evac(oTt[:, :], pst[:, :], [pst], [oTt])
                            k.dma("sp", S["oT2"][(4 * g + j) * 128:(4 * g + j + 1) * 128, q0:q0 + 512], oTt[:, :], reads=[oTt], writes=[dbuf["oT2"]])

        def sample_group(g, s, qh, Bs, Bw, Bcs):
            n_all_s = P + TS
            n_sb_s = -(-n_all_s // 64)
            ncb_s = P // 16 - 1
            NT4 = -(-ncb_s // 128)
            NPG = P // 128
            nA = min(128, n_sb_s)
            NTL = NPG + 1
            t0s = TP + s * TS
            pgc_r = Ring([k.sb(f"s_pgc{i}", [128, 16, 128], BF16) for i in range(2)])
            kslcT = k.sb("s_kslcT", [128, NTL * 128], BF16)
            vslc = k.sb("s_vslc", [128, NTL, 136], BF16)
            kwT = k.sb("s_kwT", [128, 5 * 128], BF16)
            vw = k.sb("s_vw", [128, 5, 136], BF16)
            k.op("pool", lambda e: e.memset(vslc[:, :, 128:129], 1.0), writes=[vslc])
            k.op("pool", lambda e: e.memset(vw[:, :, 128:129], 1.0), writes=[vw])
            kcT = k.sb("s_kcT", [128, 512], BF16)
            vc = k.sb("s_vc", [128, NT4, 128], BF16)
            k.op("pool", lambda e: e.memset(vc[:], 0.0), writes=[vc])
            qsm = k.sb("s_qsm", [128, 4, 8], BF16)
            k.op("pool", lambda e: e.tensor_copy(out=qsm[:], in_=qh[:, :, t0s:t0s + TS]), reads=[qh], writes=[qsm])
            qsv = qsm[:].rearrange("p j t -> p (j t)")

            def gather(c, dstT=None, dstV=None):
                col0 = c * 512 + g * 128
                src = S["past"][s * P:(s + 1) * P, col0:col0 + 128].rearrange("(a p) d -> p a d", p=128)
                if dstV is not None:
                    k.dma("sp", dstV[:, 0:NPG, 0:128], src, reads=[dbuf["past"]], writes=[dstV])
                    return
                for p0 in range(0, NPG, 16):
                    npg = min(16, NPG - p0)
                    pgc = pgc_r.next()
                    k.dma("sp", pgc[:, 0:npg, :], src[:, p0:p0 + npg, :], reads=[dbuf["past"]], writes=[pgc])
                    for q0 in range(0, npg, 8):
                        nq = min(8, npg - q0)
                        pb = pbf.next()
                        for i in range(nq):
                            k.op("pe", lambda e: e.transpose(out=pb[:, i * 128:(i + 1) * 128], in_=pgc[:, q0 + i, :], identity=ident_b[:, :]),
                                 reads=[pgc, ident_b], writes=[pb], inc=(i == nq - 1))
                        evac(dstT[:, (p0 + q0) * 128:(p0 + q0 + nq) * 128], pb[:, 0:nq * 128], [pb], [dstT])

            def small_T(src_ap, nm, rows, dstT_ap, dstT_buf):
                tkb = pgc_r.next()
                k.dma("pool", tkb[0:rows, 0, :], src_ap, reads=[dbuf[nm]], writes=[tkb])
                pb = pbf.next()
                k.op("pe", lambda e: e.transpose(out=pb[:, 0:rows], in_=tkb[0:rows, 0, :], identity=ident_b[0:rows, 0:rows]),
                     reads=[tkb, ident_b], writes=[pb])
                evac(dstT_ap, pb[:, 0:rows], [pb], [dstT_buf])

            cscope = k.scope()
            cscope.__enter__()
            kx = k.sb("s_kx", [128, P], BF16)
            hg = k.sb("s_hg", [128, 2, 512], BF16)
            tmpx = k.sb("s_tmpx", [128, 512], F32)
            tmp2 = k.sb("s_tmp2", [128, 512], F32)
            sqb = k.sb("s_sqb", [128, 512], BF16)
            gather(0, dstT=kx)
            compress(kx, 0, ncb_s, hg, tmpx, tmp2)
            kc_finish(hg, ncb_s, kcT, tmpx, tmp2, sqb)
            gather(1, dstT=kx)
            compress(kx, 1, ncb_s, hg, tmpx, tmp2)
            for nt in range(NT4):
                rows = min(128, ncb_s - nt * 128)
                ps = ringN.next()
                for half in range(2):
                    k.op("pe", lambda e: e.matmul(ps[0:rows, 0:128], lhsT=hg[:, half, nt * 128:nt * 128 + rows], rhs=w2[:, 1, half, :],
                                                  start=(half == 0), stop=(half == 1)), reads=[hg, w2], writes=[ps], inc=(half == 1))
                k.op("act", lambda e: e.copy(out=vc[0:rows, nt, :], in_=ps[0:rows, 0:128]), reads=[ps], writes=[vc])
            cscope.__exit__(None, None, None)
            exp_sA = k.sb("s_expsA", [128, (NPG + 1) * 128], BF16)
            k.dma("sp", exp_sA[0:nA, :], I["ns_exp"][0:nA, :], writes=[exp_sA])
            gather(2, dstT=kslcT)
            small_T(O["o_kv"][t0s:t0s + TS, 2 * 512 + g * 128:2 * 512 + (g + 1) * 128], "o_kv", TS, kslcT[:, NPG * 128:NPG * 128 + TS], kslcT)
            gather(3, dstV=vslc)
            k.dma("pool", vslc[0:TS, NPG, 0:128], O["o_kv"][t0s:t0s + TS, 3 * 512 + g * 128:3 * 512 + (g + 1) * 128],
                  reads=[dbuf["o_kv"]], writes=[vslc])
            tkw = k.sb("s_tkw", [128, 4, 128], BF16)
            k.dma("pool", tkw[:], I["st_win"][s, :, g * 128:(g + 1) * 128].rearrange("(t p) d -> p t d", p=128),
                  reads=[dbuf["st_win"]], writes=[tkw])
            transpose_tiles(tkw, 4, kwT)
            small_T(S["win_new"][t0s:t0s + TS, g * 128:(g + 1) * 128], "win_new", TS, kwT[:, 512:512 + TS], kwT)
            k.dma("pool", vw[:, 0:4, 0:128], I["st_win"][s, :, 512 + g * 128:512 + (g + 1) * 128].rearrange("(t p) d -> p t d", p=128),
                  reads=[dbuf["st_win"]], writes=[vw])
            k.dma("pool", vw[0:TS, 4, 0:128], S["win_new"][t0s:t0s + TS, 512 + g * 128:512 + (g + 1) * 128],
                  reads=[dbuf["win_new"]], writes=[vw])

            o_s = k.sb("s_os", [TS, 4, 128], F32)
            sm = k.sb("s_sm", [TS, 8], F32)
            ecs = k.sb("s_ecs", [128, NT4, 32], F32)
            ecb = k.sb("s_ecb", [128, NT4, 32], BF16)
            pcb = k.sb("s_pcb", [128, NT4, 32], BF16)
            recs = k.sb("s_recs", [128, 32], F32)
            ps = ringN.next()
            for nt in range(NT4):
                rows = min(128, ncb_s - nt * 128)
                k.op("pe", lambda e: e.matmul(ps[0:rows, nt * 32:(nt + 1) * 32], lhsT=kcT[:, nt * 128:nt * 128 + rows], rhs=qsv,
                                              start=True, stop=True), reads=[kcT, qsm], writes=[ps], inc=(nt == NT4 - 1))
            k.op("pool", lambda e: e.memset(ecs[:], NEG), writes=[ecs])
            for nt in range(NT4):
                rows = min(128, ncb_s - nt * 128)
                k.op("dve", lambda e: e.tensor_tensor(out=ecs[0:rows, nt, :], in0=ps[0:rows, nt * 32:(nt + 1) * 32],
                                                      in1=Bcs[0:rows, nt, :, :].rearrange("p j t -> p (j t)"), op=ALU.add),
                     reads=[ps, Bcs], writes=[ecs])
            k.op("act", lambda e: e.activation(out=ecb[:], in_=ecs[:], func=AF.Exp), reads=[ecs], writes=[ecb])
            psD = ringN.next()
            for nt in range(NT4):
                k.op("pe", lambda e: e.matmul(psD[:, 0:32], lhsT=ones_b[:, :], rhs=ecb[:, nt, :], start=(nt == 0), stop=(nt == NT4 - 1)),
                     reads=[ones_b, ecb], writes=[psD], inc=(nt == NT4 - 1))
            k.op("dve", lambda e: e.tensor_scalar(out=recs[:], in0=psD[:, 0:32], scalar1=1e-30, scalar2=None, op0=ALU.max), reads=[psD], writes=[recs])
            k.op("dve", lambda e: e.reciprocal(out=recs[:], in_=recs[:]), reads=[recs], writes=[recs])
            k.op("dve", lambda e: e.tensor_tensor(out=pcb[:], in0=ecb[:], in1=recs[:].unsqueeze(1).to_broadcast([128, NT4, 32]), op=ALU.mult),
                 reads=[ecb, recs], writes=[pcb])
            for j in range(4):
                pso = accO[j]
                for nt in range(NT4):
                    k.op("pe", lambda e: e.matmul(pso[0:TS, 0:128], lhsT=pcb[:, nt, j * 8:(j + 1) * 8], rhs=vc[:, nt, :],
                                                  start=(nt == 0), stop=(nt == NT4 - 1)), reads=[pcb, vc], writes=[pso], inc=(nt == NT4 - 1))
                k.op("act", lambda e: e.activation(out=o_s[:, j, :], in_=pso[0:TS, 0:128], func=AF.Copy,
                                                   scale=gates_s[:, s, 4 * g + j:4 * g + j + 1]), reads=[pso, gates_s], writes=[o_s])
            psI = ringN.next()
            first = True
            for j in range(4):
                for nt in range(NT4):
                    k.op("pe", lambda e: e.matmul(psI[0:TS, 0:n_sb_s], lhsT=pcb[:, nt, j * 8:(j + 1) * 8], rhs=ovl_s[:, nt, :],
                                                  start=first, stop=(j == 3 and nt == NT4 - 1)), reads=[pcb, ovl_s], writes=[psI],
                         inc=(j == 3 and nt == NT4 - 1))
                    first = False
            sc = k.sb("s_sc", [TS, n_sb_s], F32)
            wk = k.sb("s_wk", [TS, n_sb_s], F32)
            m8 = k.sb("s_m8", [TS, 16], F32)
            sel = k.sb("s_sel", [TS, n_sb_s], F32)
            k.op("dve", lambda e: e.tensor_tensor(out=sc[:, :], in0=psI[0:TS, 0:n_sb_s], in1=add_s[:, :], op=ALU.add), reads=[psI, add_s], writes=[sc])
            k.op("dve", lambda e: e.max(out=m8[:, 0:8], in_=sc[:, :]), reads=[sc], writes=[m8])
            k.op("dve", lambda e: e.match_replace(out=wk[:, :], in_to_replace=m8[:, 0:8], in_values=sc[:, :], imm_value=-1e30), reads=[m8, sc], writes=[wk])
            k.op("dve", lambda e: e.max(out=m8[:, 8:16], in_=wk[:, :]), reads=[wk], writes=[m8])
            k.op("dve", lambda e: e.tensor_scalar(out=sel[:, :], in0=sc[:, :], scalar1=m8[:, 15:16], scalar2=None, op0=ALU.is_ge), reads=[sc, m8], writes=[sel])
            k.op("dve", lambda e: e.tensor_tensor(out=sel[:, :], in0=sel[:, :], in1=val_s[:, :], op=ALU.mult), reads=[sel, val_s], writes=[sel])
            selTa = k.sb("s_selTa", [128, TS], BF16)
            selTb = k.sb("s_selTb", [1, TS], BF16)
            pst = ringN.next()
            k.op("pe", lambda e: e.transpose(out=pst[0:nA, 0:TS], in_=sel[:, 0:nA], identity=ident_f[0:TS, 0:TS]), reads=[sel, ident_f], writes=[pst])
            k.op("act", lambda e: e.copy(out=selTa[0:nA, :], in_=pst[0:nA, 0:TS]), reads=[pst], writes=[selTa])
            if n_sb_s > 128:
                pst2 = ringN.next()
                k.op("pe", lambda e: e.transpose(out=pst2[0:1, 0:TS], in_=sel[:, 128:129], identity=ident_f[0:TS, 0:TS]), reads=[sel, ident_f], writes=[pst2])
                k.op("act", lambda e: e.copy(out=selTb[0:1, :], in_=pst2[0:1, 0:TS]), reads=[pst2], writes=[selTb])

            def branch(kT, vv, ntile, last_rows, Btab, gi, use_mask):
                ess = k.sb("s_ess", [128, 16, 32], F32)
                esb = k.sb("s_esb", [128, ntile, 32], BF16)
                k.op("pool", lambda e: e.memset(esb[:], 0.0), writes=[esb])
                for t0 in range(0, ntile, 16):
                    nt_ = min(16, ntile - t0)
                    ps = ringN.next()
                    for i in range(nt_):
                        kt = t0 + i
                        rows = last_rows if kt == ntile - 1 else 128
                        k.op("pe", lambda e: e.matmul(ps[0:rows, i * 32:(i + 1) * 32], lhsT=kT[:, kt * 128:kt * 128 + rows], rhs=qsv,
                                                      start=True, stop=True), reads=[kT, qsm], writes=[ps], inc=(i == nt_ - 1))
                    full = nt_ if (t0 + nt_ < ntile) else nt_ - 1
                    if full:
                        k.op("dve", lambda e: e.tensor_tensor(out=ess[:, 0:full, :], in0=ps[:, 0:full * 32].rearrange("p (a b) -> p a b", b=32),
                                                              in1=Btab[:, t0:t0 + full, :, :].rearrange("p a j t -> p a (j t)"), op=ALU.add),
                             reads=[ps, Btab], writes=[ess])
                        k.op("act", lambda e: e.activation(out=esb[:, t0:t0 + full, :], in_=ess[:, 0:full, :], func=AF.Exp), reads=[ess], writes=[esb])
                    if full < nt_:
                        i = nt_ - 1
                        k.op("dve", lambda e: e.tensor_tensor(out=ess[0:last_rows, i, :], in0=ps[0:last_rows, i * 32:(i + 1) * 32],
                                                              in1=Btab[0:last_rows, ntile - 1, :, :].rearrange("p j t -> p (j t)"), op=ALU.add),
                             reads=[ps, Btab], writes=[ess])
                        k.op("act", lambda e: e.activation(out=esb[0:last_rows, ntile - 1, :], in_=ess[0:last_rows, i, :], func=AF.Exp),
                             reads=[ess], writes=[esb])
                if use_mask:
                    for t0 in range(0, ntile, 64):
                        nt_ = min(64, ntile - t0)
                        psM = ringN.next()
                        for i in range(nt_):
                            kt = t0 + i
                            rows = last_rows if kt == ntile - 1 else 128
                            two = n_sb_s > 128 and kt == ntile - 1
                            k.op("pe", lambda e: e.matmul(psM[0:rows, i * 8:(i + 1) * 8], lhsT=exp_sA[0:nA, kt * 128:kt * 128 + rows], rhs=selTa[0:nA, :],
                                                          start=True, stop=not two), reads=[exp_sA, selTa], writes=[psM], inc=(i == nt_ - 1 and not two))
                            if two:
                                k.op("pe", lambda e: e.matmul(psM[0:rows, i * 8:(i + 1) * 8], lhsT=exp_sB[0:1, 0:rows], rhs=selTb[0:1, :],
                                                              start=False, stop=True), reads=[exp_sB, selTb], writes=[psM], inc=(i == nt_ - 1))
                        full = nt_ if (t0 + nt_ < ntile) else nt_ - 1
                        if full:
                            k.op("dve", lambda e: e.tensor_tensor(
                                out=esb[:, t0:t0 + full, :].rearrange("p a (j t) -> p a j t", t=8),
                                in0=esb[:, t0:t0 + full, :].rearrange("p a (j t) -> p a j t", t=8),
                                in1=psM[:, 0:full * 8].rearrange("p (a t) -> p a t", t=8).unsqueeze(2).to_broadcast([128, full, 4, 8]), op=ALU.mult),
                                reads=[esb, psM], writes=[esb])
                        if full < nt_:
                            i = nt_ - 1
                            k.op("dve", lambda e: e.tensor_tensor(
                                out=esb[0:last_rows, ntile - 1, :].rearrange("p (j t) -> p j t", t=8),
                                in0=esb[0:last_rows, ntile - 1, :].rearrange("p (j t) -> p j t", t=8),
                                in1=psM[0:last_rows, i * 8:(i + 1) * 8].unsqueeze(1).to_broadcast([last_rows, 4, 8]), op=ALU.mult),
                                reads=[esb, psM], writes=[esb])
                for j in range(4):
                    acc = accO[j]
                    for kt in range(ntile):
                        rows = last_rows if kt == ntile - 1 else 128
                        k.op("pe", lambda e: e.matmul(acc[0:TS, 0:129], lhsT=esb[0:rows, kt, j * 8:(j + 1) * 8], rhs=vv[0:rows, kt, 0:129],
                                                      start=(kt == 0), stop=(kt == ntile - 1)), reads=[esb, vv], writes=[acc], inc=(kt == ntile - 1))
                    k.op("dve", lambda e: e.reciprocal(out=sm[:, 0:1], in_=acc[0:TS, 128:129]), reads=[acc], writes=[sm])
                    k.op("dve", lambda e: e.tensor_tensor(out=sm[:, 1:2], in0=sm[:, 0:1], in1=gates_s[:, s, gi * 16 + 4 * g + j:gi * 16 + 4 * g + j + 1], op=ALU.mult),
                         reads=[sm, gates_s], writes=[sm])
                    k.op("dve", lambda e: e.scalar_tensor_tensor(out=o_s[:, j, :], in0=acc[0:TS, 0:128], scalar=sm[:, 1:2], in1=o_s[:, j, :],
                                                                 op0=ALU.mult, op1=ALU.add), reads=[acc, sm, o_s], writes=[o_s])

            with k.scope():
                branch(kslcT, vslc, NTL, TS, Bs, 1, True)
            with k.scope():
                branch(kwT, vw, 5, TS, Bw, 2, False)
            pst = ringN.next()
            for j in range(4):
                k.op("pe", lambda e: e.transpose(out=pst[:, j * TS:(j + 1) * TS], in_=o_s[:, j, :], identity=ident_f[0:TS, 0:TS]),
                     reads=[o_s, ident_f], writes=[pst], inc=(j == 3))
            oTs = k.sb("s_oTs", [128, 4, TS], BF16)
            evac(oTs[:].rearrange("p j t -> p (j t)"), pst[:, 0:4 * TS], [pst], [oTs])
            k.dma("sp", S["oT2"][g * 512:(g + 1) * 512, t0s:t0s + TS].rearrange("(j p) t -> p j t", p=128), oTs[:],
                  reads=[oTs], writes=[dbuf["oT2"]])


        n_all_s = P + TS
        n_sb_s = -(-n_all_s // 64)
        ncb_s = P // 16 - 1
        NT4 = -(-ncb_s // 128)
        NPG = P // 128
        nA = min(128, n_sb_s)
        KTOT = (NPG + 1) * 128
        ovl_s = k.sb("a_ovls", [128, NT4, n_sb_s], BF16)
        k.dma("pool", ovl_s[:], I["ns_ovl"].rearrange("t p m -> p t m"), writes=[ovl_s])
        add_s = k.sb("a_adds", [TS, n_sb_s], F32)
        k.dma("sp", add_s[:], I["ns_add"], writes=[add_s])
        val_s = k.sb("a_vals", [TS, n_sb_s], F32)
        k.dma("sp", val_s[:], I["ns_valid"], writes=[val_s])
        exp_sB = k.sb("a_expsB", [1, 128], BF16)
        if n_sb_s > 128:
            k.dma("sp", exp_sB[:, :], I["ns_exp"][128:129, KTOT - 128:KTOT], writes=[exp_sB])
        iota_f = k.sb("a_iota", [128, 1], F32)
        k.dma("sp", iota_f[:], I["n_iota"], writes=[iota_f])
        gates_s = k.sb("a_gatess", [TS, NS, 48], F32)
        k.dma("sp", gates_s[:], S["gates"][TP:NT, :].rearrange("(s t) c -> t s c", t=TS), reads=[dbuf["gates"]], writes=[gates_s])
        for g in range(4):
          with k.scope():
            qh = k.sb("a_qh", [128, 4, NT], BF16)
            k.dma("sp", qh[:], S["qT"][g * 512:(g + 1) * 512, :].rearrange("(j p) t -> p j t", p=128),
                  reads=[dbuf["qT"]], writes=[qh])
            Bs = k.sb("a_Bs", [128, NPG + 1, 4, 8], F32)
            Bw = k.sb("a_Bw", [128, 5, 4, 8], F32)
            Bcs = k.sb("a_Bcs", [128, NT4, 4, 8], F32)
            for j in range(4):
                for nt in range(NT4):
                    off_d = P - 31 - 2048 * nt - 2032
                    base = Z + min(off_d, 790)
                    build_tab(lambda c0, n, j=j, nt=nt: (Bcs[:, nt, j, :], [Bcs]), 4 * g + j, base, 8, 16)
            with k.scope():
                prompt_group(g, qh, Bs, Bw)
            for s in range(NS):
                with k.scope():
                    sample_group(g, s, qh, Bs, Bw, Bcs)
```
